# Optimizing a Trainium2 kernel written in Bass

```python
import jax, jax.numpy as jnp
from jax import lax
import numpy as np

D_MODEL = 2048
BATCH = 4
SEQ = 4096
DEPTH = 2

GRID_W = 64
CTX_LEN = 256
ROPE_BASE = 10000.0
NORM_EPS = 1e-6
NEG_INF = -1e30

SWA_HEADS = 8
SWA_KV_HEADS = 2
SWA_HEAD_DIM = 128
SWA_WINDOW = 128
SWA_BLOCK = 128

MLA_HEADS = 8
MLA_Q_RANK = 512
MLA_KV_RANK = 256
MLA_NOPE_DIM = 128
MLA_ROPE_DIM = 64
MLA_V_DIM = 128
MLA_Q_BLOCK = 128

_AB_SIZES = (SWA_HEADS * SWA_HEAD_DIM, SWA_KV_HEADS * SWA_HEAD_DIM, SWA_KV_HEADS * SWA_HEAD_DIM,
             MLA_Q_RANK, MLA_KV_RANK, MLA_ROPE_DIM)
AB_IN = sum(_AB_SIZES)
AB_SPLITS = tuple(int(s) for s in np.cumsum(_AB_SIZES)[:-1])
AB_OUT = SWA_HEADS * SWA_HEAD_DIM + MLA_HEADS * MLA_V_DIM

RET_HEADS = 8
RET_QK_DIM = D_MODEL // RET_HEADS
RET_V_DIM = 2 * RET_QK_DIM
RET_CHUNK = 128
_RET_SIZES = (RET_HEADS * RET_QK_DIM, RET_HEADS * RET_QK_DIM, RET_HEADS * RET_V_DIM, RET_HEADS * RET_V_DIM)
RET_IN = sum(_RET_SIZES)
RET_SPLITS = tuple(int(s) for s in np.cumsum(_RET_SIZES)[:-1])
RET_OUT = RET_HEADS * RET_V_DIM

FFN_HIDDEN = ((8 * D_MODEL + 3 * 256 - 1) // (3 * 256)) * 256

N_AB = (DEPTH + 1) // 2
N_RET = DEPTH // 2

kernel_name = "hybrid_swa_mla_retention_prefix_dit"


def _rmsnorm(x, g):
    x32 = x.astype(jnp.float32)
    y = x32 * lax.rsqrt(jnp.mean(x32 * x32, axis=-1, keepdims=True) + NORM_EPS)
    return (y * g.astype(jnp.float32)).astype(x.dtype)


def _modulation(cond, w, b):
    return jnp.split(jax.nn.silu(cond) @ w + b, 6, axis=-1)


def _axial_rope_tables(row, col, rot_dim, dtype):
    n_freq = rot_dim // 4
    inv = ROPE_BASE ** (-jnp.arange(n_freq, dtype=jnp.float32) / n_freq)
    ang = jnp.concatenate([row[:, None] * inv, col[:, None] * inv], axis=-1)
    return jnp.cos(ang).astype(dtype), jnp.sin(ang).astype(dtype)


def _apply_rope(x, cos, sin):
    x1, x2 = jnp.split(x, 2, axis=-1)
    c = cos[None, :, None, :]
    s = sin[None, :, None, :]
    return jnp.concatenate([x1 * c - x2 * s, x1 * s + x2 * c], axis=-1)


def _swa_sink_attention(q, k, v, q_c, k_c, v_c, sink):
    B, S, H, d = q.shape
    KV = k.shape[2]
    G = H // KV
    Cn = k_c.shape[1]
    W = SWA_BLOCK
    nb = S // W
    scale = d ** -0.5
    sink_g = sink.reshape(KV, G).astype(jnp.float32)

    qc = q_c.reshape(B, Cn, KV, G, d)
    s_cc = jnp.einsum('bikgd,bjkd->bkgij', qc, k_c).astype(jnp.float32) * scale
    s_cc = jnp.concatenate([s_cc, jnp.broadcast_to(sink_g[None, :, :, None, None], s_cc.shape[:-1] + (1,))], axis=-1)
    p_cc = jax.nn.softmax(s_cc, axis=-1)[..., :Cn].astype(v_c.dtype)
    o_c = jnp.einsum('bkgij,bjkd->bikgd', p_cc, v_c).reshape(B, Cn, H, d)

    pad = ((0, 0), (W, W), (0, 0), (0, 0))
    kp = jnp.pad(k, pad).reshape(B, nb + 2, W, KV, d)
    vp = jnp.pad(v, pad).reshape(B, nb + 2, W, KV, d)
    kb = jnp.concatenate([kp[:, :-2], kp[:, 1:-1], kp[:, 2:]], axis=2)
    vb = jnp.concatenate([vp[:, :-2], vp[:, 1:-1], vp[:, 2:]], axis=2)
    qb = q.reshape(B, nb, W, KV, G, d)
    s_band = jnp.einsum('bnikgd,bnjkd->bnkgij', qb, kb).astype(jnp.float32) * scale
    qi = jnp.arange(W)
    kj = jnp.arange(3 * W)
    blk = jnp.arange(nb)
    rel = kj[None, :] - W - qi[:, None]
    kpos = blk[:, None] * W - W + kj[None, :]
    valid = (jnp.abs(rel) <= SWA_WINDOW)[None] & ((kpos >= 0) & (kpos < S))[:, None, :]
    s_band = jnp.where(valid[None, :, None, None], s_band, NEG_INF)
    s_lc = jnp.einsum('bnikgd,bjkd->bnkgij', qb, k_c).astype(jnp.float32) * scale
    sink_b = jnp.broadcast_to(sink_g[None, None, :, :, None, None], s_band.shape[:-1] + (1,))
    p = jax.nn.softmax(jnp.concatenate([s_band, s_lc, sink_b], axis=-1), axis=-1)
    p_band = p[..., :3 * W].astype(v.dtype)
    p_lc = p[..., 3 * W:3 * W + Cn].astype(v.dtype)
    o = jnp.einsum('bnkgij,bnjkd->bnikgd', p_band, vb) + jnp.einsum('bnkgij,bjkd->bnikgd', p_lc, v_c)
    return o.reshape(B, S, H, d), o_c


def _mla_attend(q_nope, q_rope, k_nope, k_rope, v):
    scale = (MLA_NOPE_DIM + MLA_ROPE_DIM) ** -0.5
    s = (jnp.einsum('bihd,bjhd->bhij', q_nope, k_nope)
         + jnp.einsum('bihd,bjd->bhij', q_rope, k_rope)).astype(jnp.float32) * scale
    p = jax.nn.softmax(s, axis=-1).astype(v.dtype)
    return jnp.einsum('bhij,bjhd->bihd', p, v)


def _ab_mixer(h, hc, w_in, w_out, sink, q_norm_g, w_q_b, kv_norm_g, w_kv_b, cos_a, sin_a, cos_b, sin_b):
    def project(z):
        B, L, _ = z.shape
        qa, ka, va, q_lat, kv_lat, k_r = jnp.split(z @ w_in, AB_SPLITS, axis=-1)
        qa = qa.reshape(B, L, SWA_HEADS, SWA_HEAD_DIM)
        ka = ka.reshape(B, L, SWA_KV_HEADS, SWA_HEAD_DIM)
        va = va.reshape(B, L, SWA_KV_HEADS, SWA_HEAD_DIM)
        qm = (_rmsnorm(q_lat, q_norm_g) @ w_q_b).reshape(B, L, MLA_HEADS, MLA_NOPE_DIM + MLA_ROPE_DIM)
        kvm = (_rmsnorm(kv_lat, kv_norm_g) @ w_kv_b).reshape(B, L, MLA_HEADS, MLA_NOPE_DIM + MLA_V_DIM)
        return (qa, ka, va, qm[..., :MLA_NOPE_DIM], qm[..., MLA_NOPE_DIM:],
                kvm[..., :MLA_NOPE_DIM], k_r, kvm[..., MLA_NOPE_DIM:])

    B, S, _ = h.shape
    qa, ka, va, qn, qr, kn, kr, vm = project(h)
    qa = _apply_rope(qa, cos_a, sin_a)
    ka = _apply_rope(ka, cos_a, sin_a)
    qr = _apply_rope(qr, cos_b, sin_b)
    kr = _apply_rope(kr[:, :, None, :], cos_b, sin_b)[:, :, 0, :]
    qa_c, ka_c, va_c, qn_c, qr_c, kn_c, kr_c, vm_c = project(hc)

    oa, oa_c = _swa_sink_attention(qa, ka, va, qa_c, ka_c, va_c, sink)

    ob_c = _mla_attend(qn_c, qr_c, kn_c, kr_c, vm_c)
    kn_all = jnp.concatenate([kn_c, kn], axis=1)
    kr_all = jnp.concatenate([kr_c, kr], axis=1)
    v_all = jnp.concatenate([vm_c, vm], axis=1)
    nb = S // MLA_Q_BLOCK
    qn_b = qn.reshape(B, nb, MLA_Q_BLOCK, MLA_HEADS, MLA_NOPE_DIM).swapaxes(0, 1)
    qr_b = qr.reshape(B, nb, MLA_Q_BLOCK, MLA_HEADS, MLA_ROPE_DIM).swapaxes(0, 1)
    ob = lax.map(lambda qq: _mla_attend(qq[0], qq[1], kn_all, kr_all, v_all), (qn_b, qr_b))
    ob = ob.swapaxes(0, 1).reshape(B, S, MLA_HEADS * MLA_V_DIM)

    Cn = hc.shape[1]
    out = jnp.concatenate([oa.reshape(B, S, -1), ob], axis=-1) @ w_out
    out_c = jnp.concatenate([oa_c.reshape(B, Cn, -1), ob_c.reshape(B, Cn, -1)], axis=-1) @ w_out
    return out, out_c


def _retention_scan(q, k, v, log_g, s0):
    B, L, H, _ = q.shape
    dv = v.shape[-1]
    C = RET_CHUNK
    n = L // C
    pos = jnp.arange(C, dtype=jnp.float32)
    diff = pos[:, None] - pos[None, :]
    lg = log_g.astype(jnp.float32)
    d_intra = jnp.where(diff[None] >= 0, jnp.exp(lg[:, None, None] * jnp.maximum(diff, 0.0)[None]), 0.0)
    q_dec = jnp.exp(lg[:, None] * (pos + 1.0)[None])[..., None]
    k_dec = jnp.exp(lg[:, None] * (C - 1.0 - pos)[None])[..., None]
    c_dec = jnp.exp(lg * C)[:, None, None]

    def chunks(z):
        return z.astype(jnp.float32).reshape(B, n, C, H, z.shape[-1]).transpose(1, 0, 3, 2, 4)

    def step(s, qkv):
        qi, ki, vi = qkv
        a = jnp.einsum('bhid,bhjd->bhij', qi, ki) * d_intra
        o = jnp.einsum('bhij,bhjv->bhiv', a, vi) + jnp.einsum('bhid,bhdv->bhiv', qi * q_dec, s)
        s = s * c_dec + jnp.einsum('bhjd,bhjv->bhdv', ki * k_dec, vi)
        return s, o

    s_fin, o = lax.scan(step, s0, (chunks(q), chunks(k), chunks(v)))
    o = o.transpose(1, 0, 3, 2, 4).reshape(B, L, H, dv)
    return o.astype(v.dtype), s_fin


def _retention_mixer(h, hc, w_in, logit_f, logit_b, gn_g, w_out, cos_r, sin_r):
    H, dk, dv = RET_HEADS, RET_QK_DIM, RET_V_DIM

    def project(z):
        B, L, _ = z.shape
        q, k, v, g = jnp.split(z @ w_in, RET_SPLITS, axis=-1)
        return q.reshape(B, L, H, dk), k.reshape(B, L, H, dk) * (dk ** -0.5), v.reshape(B, L, H, dv), g

    q, k, v, g = project(h)
    q = _apply_rope(q, cos_r, sin_r)
    k = _apply_rope(k, cos_r, sin_r)
    qc, kc, vc, gc = project(hc)
    lg_f = jax.nn.log_sigmoid(logit_f.astype(jnp.float32))
    lg_b = jax.nn.log_sigmoid(logit_b.astype(jnp.float32))
    s0 = jnp.zeros((h.shape[0], H, dk, dv), jnp.float32)

    def flip(z):
        return jnp.flip(z, axis=1)

    oc_f, s_f = _retention_scan(qc, kc, vc, lg_f, s0)
    oc_b, s_b = _retention_scan(flip(qc), flip(kc), flip(vc), lg_b, s0)
    o_f, _ = _retention_scan(q, k, v, lg_f, s_f)
    o_b, _ = _retention_scan(flip(q), flip(k), flip(v), lg_b, s_b)

    def finish(o, gate):
        B, L = o.shape[:2]
        o32 = o.astype(jnp.float32)
        mu = jnp.mean(o32, axis=-1, keepdims=True)
        var = jnp.mean(jnp.square(o32 - mu), axis=-1, keepdims=True)
        y = ((o32 - mu) * lax.rsqrt(var + NORM_EPS)).reshape(B, L, H * dv) * gn_g.astype(jnp.float32)
        return (jax.nn.silu(gate) * y.astype(gate.dtype)) @ w_out

    return finish(o_f + flip(o_b), g), finish(oc_f + flip(oc_b), gc)


def _swiglu(h, wg, wu, wd):
    return (jax.nn.silu(h @ wg) * (h @ wu)) @ wd


def setup_inputs(seed: int = 0) -> dict:
    key = jax.random.key(seed)
    ks = jax.random.split(key, 32)
    f32 = jnp.float32

    def nrm(i, shape, scale):
        return jax.random.normal(ks[i], shape, f32) * scale

    decay_base = jnp.log(2.0 ** (5.0 + jnp.arange(RET_HEADS, dtype=f32)) - 1.0)
    return {
        "x": nrm(0, (BATCH, SEQ, D_MODEL), 1.0),
        "c": nrm(1, (BATCH, D_MODEL), 1.0),
        "ctx": nrm(2, (BATCH, CTX_LEN, D_MODEL), 1.0),
        "c_ctx": nrm(3, (D_MODEL,), 1.0),
        "mod_w": nrm(4, (DEPTH, D_MODEL, 6 * D_MODEL), 0.5 * D_MODEL ** -0.5),
        "mod_b": nrm(5, (DEPTH, 6 * D_MODEL), 0.02),
        "norm_mix_g": 1.0 + nrm(6, (DEPTH, D_MODEL), 0.02),
        "norm_ffn_g": 1.0 + nrm(7, (DEPTH, D_MODEL), 0.02),
        "ffn_w_gate": nrm(8, (DEPTH, D_MODEL, FFN_HIDDEN), D_MODEL ** -0.5),
        "ffn_w_up": nrm(9, (DEPTH, D_MODEL, FFN_HIDDEN), D_MODEL ** -0.5),
        "ffn_w_down": nrm(10, (DEPTH, FFN_HIDDEN, D_MODEL), FFN_HIDDEN ** -0.5),
        "ab_w_in": nrm(11, (N_AB, D_MODEL, AB_IN), D_MODEL ** -0.5),
        "ab_w_out": nrm(12, (N_AB, AB_OUT, D_MODEL), AB_OUT ** -0.5),
        "swa_sink": nrm(13, (N_AB, SWA_HEADS), 0.5),
        "mla_q_norm_g": 1.0 + nrm(14, (N_AB, MLA_Q_RANK), 0.02),
        "mla_w_q_b": nrm(15, (N_AB, MLA_Q_RANK, MLA_HEADS * (MLA_NOPE_DIM + MLA_ROPE_DIM)), MLA_Q_RANK ** -0.5),
        "mla_kv_norm_g": 1.0 + nrm(16, (N_AB, MLA_KV_RANK), 0.02),
        "mla_w_kv_b": nrm(17, (N_AB, MLA_KV_RANK, MLA_HEADS * (MLA_NOPE_DIM + MLA_V_DIM)), MLA_KV_RANK ** -0.5),
        "ret_w_in": nrm(18, (N_RET, D_MODEL, RET_IN), D_MODEL ** -0.5),
        "ret_decay_logit_fwd": decay_base + nrm(19, (N_RET, RET_HEADS), 0.1),
        "ret_decay_logit_bwd": decay_base + nrm(20, (N_RET, RET_HEADS), 0.1),
        "ret_gn_g": 1.0 + nrm(21, (N_RET, RET_OUT), 0.02),
        "ret_w_out": nrm(22, (N_RET, RET_OUT, D_MODEL), RET_OUT ** -0.5),
        "final_norm_g": 1.0 + nrm(23, (D_MODEL,), 0.02),
    }


def reference(x, c, ctx, c_ctx, mod_w, mod_b, norm_mix_g, norm_ffn_g, ffn_w_gate, ffn_w_up, ffn_w_down,
              ab_w_in, ab_w_out, swa_sink, mla_q_norm_g, mla_w_q_b, mla_kv_norm_g, mla_w_kv_b,
              ret_w_in, ret_decay_logit_fwd, ret_decay_logit_bwd, ret_gn_g, ret_w_out, final_norm_g):
    n_tok = x.shape[1]
    rows = n_tok // GRID_W
    row = jnp.repeat(jnp.arange(rows, dtype=jnp.float32), GRID_W)
    col = (jnp.arange(n_tok) % GRID_W).astype(jnp.float32)
    cos_a, sin_a = _axial_rope_tables(row, col, SWA_HEAD_DIM, x.dtype)
    cos_b, sin_b = _axial_rope_tables(row, col, MLA_ROPE_DIM, x.dtype)
    cos_r, sin_r = _axial_rope_tables(row, col, RET_QK_DIM, x.dtype)

    xc = ctx
    for l in range(DEPTH):
        last = l == DEPTH - 1
        sh1, sc1, g1, sh2, sc2, g2 = [t[:, None, :] for t in _modulation(c, mod_w[l], mod_b[l])]
        csh1, csc1, cg1, csh2, csc2, cg2 = _modulation(c_ctx, mod_w[l], mod_b[l])
        h = _rmsnorm(x, norm_mix_g[l]) * (1.0 + sc1) + sh1
        hc = _rmsnorm(xc, norm_mix_g[l]) * (1.0 + csc1) + csh1
        i = l // 2
        if l % 2 == 0:
            o, oc = _ab_mixer(h, hc, ab_w_in[i], ab_w_out[i], swa_sink[i], mla_q_norm_g[i], mla_w_q_b[i],
                              mla_kv_norm_g[i], mla_w_kv_b[i], cos_a, sin_a, cos_b, sin_b)
        else:
            o, oc = _retention_mixer(h, hc, ret_w_in[i], ret_decay_logit_fwd[i], ret_decay_logit_bwd[i],
                                     ret_gn_g[i], ret_w_out[i], cos_r, sin_r)
        x = x + g1 * o
        h = _rmsnorm(x, norm_ffn_g[l]) * (1.0 + sc2) + sh2
        x = x + g2 * _swiglu(h, ffn_w_gate[l], ffn_w_up[l], ffn_w_down[l])
        if not last:
            xc = xc + cg1 * oc
            hc = _rmsnorm(xc, norm_ffn_g[l]) * (1.0 + csc2) + csh2
            xc = xc + cg2 * _swiglu(hc, ffn_w_gate[l], ffn_w_up[l], ffn_w_down[l])
    return _rmsnorm(x, final_norm_g)
```

```python
from contextlib import ExitStack

import numpy as np
import concourse.bass as bass
import concourse.mybir as mybir
from concourse.bass_utils import run_bass_kernel_spmd

F32 = mybir.dt.float32
BF16 = mybir.dt.bfloat16
AF = mybir.ActivationFunctionType
ALU = mybir.AluOpType

D = 2048
KD = 16
FF = 5632
KF = 44
NCX = 256
T = 512
EPS = 1e-6
COMPUTE = ("pe", "act", "dve", "pool")
ENGS = ("pe", "act", "dve", "pool", "sp")


class Sched:
    def __init__(self, nc, es, n_dma_sems=(("sp", 32), ("pool", 12), ("act", 4))):
        self.nc = nc
        self.streams = {e: [] for e in ENGS}
        self.esem = {e: es.enter_context(nc.semaphore("s_" + e)) for e in COMPUTE}
        self.ecount = {e: 0 for e in COMPUTE}
        self.dsems = {q: [es.enter_context(nc.semaphore("d%s%d" % (q, i))) for i in range(n)] for q, n in n_dma_sems}
        self.dcount = {q: [0] * n for q, n in n_dma_sems}
        self.dnext = {q: 0 for q, n in n_dma_sems}
        self.last_w = {}
        self.readers = {}
        self.seen = {e: {} for e in ENGS}
        self.semobj = {}
        self.n_ops = 0

    def _need(self, eng, tok, waits):
        if tok is None:
            return
        sk, val, teng = tok
        if teng == "pe" and eng == "pe":
            return
        if self.seen[eng].get(sk, 0) >= val:
            return
        self.seen[eng][sk] = val
        waits.append((sk, val))

    def add(self, eng, fn, reads=(), writes=(), dma=False):
        waits = []
        for k in reads:
            self._need(eng, self.last_w.get(k), waits)
        for k in writes:
            self._need(eng, self.last_w.get(k), waits)
            for sk, (val, teng) in self.readers.get(k, {}).items():
                self._need(eng, (sk, val, teng), waits)
        if dma:
            i = self.dnext[eng]
            self.dnext[eng] = (i + 1) % len(self.dsems[eng])
            sk = ("d", eng, i)
            dc = self.dcount[eng]
            if dc[i] > 0:
                self._need(eng, (sk, dc[i], "dma"), waits)
            dc[i] += 16
            tok = (sk, dc[i], "dma")
            self.semobj[sk] = self.dsems[eng][i]
            inc = 16
        else:
            self.ecount[eng] += 1
            sk = ("e", eng)
            tok = (sk, self.ecount[eng], eng)
            self.semobj[sk] = self.esem[eng]
            inc = 1
        self.streams[eng].append((waits, fn, sk, inc))
        self.n_ops += 1
        for k in writes:
            self.last_w[k] = tok
            self.readers[k] = {}
        for k in reads:
            if k in writes:
                continue
            self.readers.setdefault(k, {})[tok[0]] = (tok[1], tok[2])
        return tok

    def fence(self, eng, keys):
        waits = []
        for k in keys:
            self._need(eng, self.last_w.get(k), waits)
        self.streams[eng].append((waits, None, None, 0))

    def emit(self):
        so = self.semobj

        def replay(name, e):
            for waits, fn, sk, inc in self.streams[name]:
                for wk, val in waits:
                    e.wait_ge(so[wk], val)
                if fn is not None:
                    fn(e).then_inc(so[sk], inc)

        with self.nc.Block() as block:
            @block.tensor
            def _(e):
                replay("pe", e)

            @block.scalar
            def _(e):
                replay("act", e)

            @block.vector
            def _(e):
                replay("dve", e)

            @block.gpsimd
            def _(e):
                replay("pool", e)

            @block.sync
            def _(e):
                replay("sp", e)


def _rs(shape):
    names = "abcdefg"[: len(shape)]
    if len(shape) == 1:
        return None, {}
    pat = "p (" + " ".join(names) + ") -> p " + " ".join(names)
    return pat, {n: s for n, s in zip(names[:-1], shape[:-1])}


class Arena:
    def __init__(self, nc, es, nbytes):
        self.t = es.enter_context(nc.sbuf_tensor("arena", [128, nbytes // 2], BF16))
        self.nbytes = nbytes
        self.off = 0

    def alloc(self, nbytes):
        o = self.off
        self.off += (nbytes + 63) // 64 * 64
        assert self.off <= self.nbytes, (self.off, self.nbytes)
        return o

    def view(self, off, shape, dtype, parts=128):
        n = int(np.prod(shape))
        sz = 4 if dtype == F32 else 2
        ap = self.t[0:parts, off // 2: off // 2 + n * sz // 2]
        if dtype == F32:
            ap = ap.bitcast(F32)
        pat, kw = _rs(list(shape))
        if pat is not None:
            ap = ap.rearrange(pat, **kw)
        return ap


def build(NL, stage=9):
    nc = bass.Bass("TRN2", target_bir_lowering=False)
    NTOK = NCX + NL
    NBLK = NTOK // 128
    NCH = NL // 128
    NTL = NL // T

    def din(n, s, dt=F32):
        return nc.dram_tensor(n, s, dt, kind="ExternalInput").ap()

    def dsc(n, s, dt=BF16):
        return nc.dram_tensor(n, s, dt).ap()

    x_in = din("x", [NL, D])
    ctx_in = din("ctx", [NCX, D])
    cv_in = din("cv", [128, 32])
    modw_in = din("mod_w", [2, D, 6 * D])
    modb_in = din("modb", [128, 192])
    nmg_in = din("nmg", [128, 32])
    nfg_in = din("nfg", [128, 32])
    fng_in = din("fng", [128, 16])
    wg_in = din("wg", [2, D, FF])
    wu_in = din("wu", [2, D, FF])
    wd_in = din("wd", [2, FF, D])
    abin_in = din("abin", [D, 2368])
    about_in = din("about", [D, D])
    sink_in = din("sink", [128, 8])
    qng_in = din("qng", [128, 4])
    wqb_in = din("wqb", [512, 1536])
    kvng_in = din("kvng", [128, 2])
    wkvb_in = din("wkvb", [256, 2048])
    retin_in = din("retin", [D, 12288])
    lg_in = din("lg", [128, 16])
    gng_in = din("gng", [128, 32])
    retout_in = din("retout", [4096, D])
    ropeA_in = din("ropeA", [2, 128, NL])
    ropeB_in = din("ropeB", [2, 128, NL])
    ropeR_in = din("ropeR", [2, 128, NL])
    cst_in = din("cst", [128, 7 * 128])
    pc_in = din("pc", [128, 4])
    out = nc.dram_tensor("out", [NL, D], F32, kind="ExternalOutput").ap()
    dbg = nc.dram_tensor("dbg", [KD, 128, NCX + NL], F32, kind="ExternalOutput").ap() if stage <= 2 else None

    abin_b = dsc("abin_b", [D, 2368])
    abin_s = dsc("abin_s", [D, 1280])
    abin_kr = dsc("abin_kr", [D, 256])
    about_b = dsc("about_b", [D, D])
    wqn_b = dsc("wqn_b", [512, 1024])
    wqr_b = dsc("wqr_b", [512, 1024])
    wkn_b = dsc("wkn_b", [256, 1024])
    wkv_b = dsc("wkv_b", [256, 1024])
    wg_b = dsc("wg_b", [2, D, FF])
    wu_b = dsc("wu_b", [2, D, FF])
    wd_b = dsc("wd_b", [2, FF, D])
    retin_b = dsc("retin_b", [D, 12288])
    retout_b = dsc("retout_b", [4096, D])
    XT = dsc("XT", [KD, 128, NTOK], F32)
    QA = dsc("QA", [8, 128, NTOK])
    KA = dsc("KA", [2, 128, NTOK])
    VA = dsc("VA", [NBLK, 128, 2, 129])
    QN = dsc("QN", [8, 128, NTOK])
    QR = dsc("QR", [8, 128, NTOK])
    KN = dsc("KN", [8, 128, NTOK])
    KRS = dsc("KRS", [128, NTOK])
    VM = dsc("VM", [8, 128, NBLK, 129])
    QT1 = dsc("QT1", [NCH, 128, KD, 128])
    KT1 = dsc("KT1", [NCH, 128, KD, 128])
    KP1 = dsc("KP1", [2, NCH, 128, D])
    V1 = dsc("V1", [NCH, 128, 4096])
    G1 = dsc("G1", [NCH, 128, 4096])
    OF1 = dsc("OF1", [NCH, 128, 4096], F32)
    Z1 = dsc("Z1", [NCH, 128, 4096])
    S0 = dsc("S0", [2, 128, 16, 512], F32)

    es = ExitStack()
    with es:
        S = Sched(nc, es)
        ar = Arena(nc, es, 206 * 1024)
        ps = [es.enter_context(nc.psum_tensor("ps%d" % i, [128, 512], F32))[:] for i in range(8)]
        psk = [("ps", i) for i in range(8)]

        WPB = 16384
        NWP = 4
        WP0 = ar.alloc(WPB * NWP)
        XTo = ar.alloc(32768)
        BAo = ar.alloc(16384)
        BIGo = ar.alloc(45056)
        KRo = ar.alloc(max(NTOK * 2, 8192 + 64))
        NTMP = 6
        TMPo = ar.alloc(2048 * NTMP)
        PTo = ar.alloc(1024 * 3)
        RSo = ar.alloc(2048)
        VACo = ar.alloc(2048)
        KR2o = ar.alloc(8192 + 64)
        CSTo = ar.alloc(7 * 128 * 4)
        SMo = ar.alloc(8192)
        sm_off = [SMo]

        def small(ncols, dtype=F32):
            sz = 4 if dtype == F32 else 2
            o = sm_off[0]
            sm_off[0] += (ncols * sz + 31) // 32 * 32
            assert sm_off[0] <= SMo + 8192
            return ar.view(o, [ncols], dtype)

        cst = ar.view(CSTo, [7, 128], F32)
        ident_f = cst[:, 0, :]
        diffF = cst[:, 3, :]
        diffB = cst[:, 4, :]
        posr = cst[:, 5, :]
        ident_b = small(128, BF16)
        maskLo = small(128, BF16)
        maskHi = small(128, BF16)
        ones_m = small(128)
        cv = small(32)
        scv = small(32)
        modb = small(192)
        modT = small(384)
        EFF = small(384)
        nmg = small(32)
        nfg = small(32)
        fng = small(16)
        qng = small(4)
        kvng = small(2)
        esink = small(8)
        lg = small(16)
        lgs = small(16)
        kdec = small(16)
        cdec = small(16)
        pcol = small(2)
        gng = small(32)
        dummy = small(8)
        xT = ar.view(XTo, [KD, T], F32)
        hT = ar.view(BAo, [KD, T], BF16)
        tmp = [ar.view(TMPo + 2048 * i, [T], F32) for i in range(NTMP)]
        tmpk = [("tmp", i) for i in range(NTMP)]
        rsb = ar.view(RSo, [T], F32)
        pt = [ar.view(PTo + 1024 * i, [T], BF16) for i in range(3)]
        ptk = [("pt", i) for i in range(3)]
        ctr = {"tmp": 0, "pt": 0, "wp": 0, "psA": 0, "psB": 0, "psC": 0, "ev": 0}

        def nxt(name, n):
            i = ctr[name]
            ctr[name] = (i + 1) % n
            return i

        def gtmp():
            i = nxt("tmp", NTMP)
            return tmp[i], tmpk[i]

        def gpt():
            i = nxt("pt", 3)
            return pt[i], ptk[i]

        def psA():
            i = nxt("psA", 4)
            return ps[i], psk[i]

        def psB():
            i = 4 + nxt("psB", 2)
            return ps[i], psk[i]

        def psC():
            i = 6 + nxt("psC", 2)
            return ps[i], psk[i]

        def dma(q, o, i, r, w):
            S.add(q, lambda e: e.dma_start(out=o, in_=i), r, w, dma=True)

        def mm(o, lhsT, rhs, start, stop, r, w):
            S.add("pe", lambda e: e.matmul(o, lhsT=lhsT, rhs=rhs, start=start, stop=stop), r, w)

        def tr(o, i, idn, r, w):
            S.add("pe", lambda e: e.transpose(o, i, idn), r, w)

        def act(o, i, func, r, w, bias=None, scale=None, accum=None):
            kw = {}
            if bias is not None:
                kw["bias"] = bias
            if scale is not None:
                kw["scale"] = scale
            if accum is not None:
                kw["accum_out"] = accum
            S.add("act", lambda e: e.activation(out=o, in_=i, func=func, **kw), r, w)

        def tt(eng, o, a, b, op, r, w):
            S.add(eng, lambda e: e.tensor_tensor(out=o, in0=a, in1=b, op=op), r, w)

        def ts(eng, o, a, s1, s2, op0, op1, r, w):
            if s2 is None:
                S.add(eng, lambda e: e.tensor_scalar(out=o, in0=a, scalar1=s1, scalar2=None, op0=op0), r, w)
            else:
                S.add(eng, lambda e: e.tensor_scalar(out=o, in0=a, scalar1=s1, scalar2=s2, op0=op0, op1=op1), r, w)

        def stt(eng, o, a, sc, b, op0, op1, r, w):
            S.add(eng, lambda e: e.scalar_tensor_tensor(out=o, in0=a, scalar=sc, in1=b, op0=op0, op1=op1), r, w)

        def cp(eng, o, i, r, w):
            if eng == "act":
                S.add("act", lambda e: e.copy(out=o, in_=i), r, w)
            else:
                S.add(eng, lambda e: e.tensor_copy(out=o, in_=i), r, w)

        def evac(o, i, r, w):
            cp("act" if nxt("ev", 2) == 0 else "dve", o, i, r, w)

        def rename(old, new):
            S.add("dve", lambda e: e.memset(dummy[:, 0:1], 0.0), list(old), list(new) + ["dummy"])

        def wload(src, shape, rk, dtype=BF16, q="sp"):
            i = nxt("wp", NWP)
            v = ar.view(WP0 + i * WPB, shape, dtype)
            dma(q, v, src, rk, [("wp", i)])
            return v, ("wp", i)

        wkeys = {}

        def conv(name, dst, src, nrows, rb=256):
            ks = []
            for r0 in range(0, nrows, rb):
                k = ("w", name, r0)
                dma("pool", dst[r0:r0 + rb], src[r0:r0 + rb], [], [k])
                ks.append(k)
            wkeys.setdefault(name, []).extend(ks)

        dma("sp", cst, cst_in.rearrange("p (a b) -> p a b", a=7), [], ["cst"])
        for (v, src, k) in [(cv, cv_in, "cv"), (modb, modb_in, "modb"), (nmg, nmg_in, "nmg"), (nfg, nfg_in, "nfg"),
                            (fng, fng_in, "fng"), (qng, qng_in, "qng"), (kvng, kvng_in, "kvng"), (esink, sink_in, "esink"),
                            (lg, lg_in, "lg"), (gng, gng_in, "gng")]:
            dma("sp", v, src, [], [k])
        cp("dve", ident_b, cst[:, 0, :], ["cst"], ["ident_b"])
        cp("dve", maskLo, cst[:, 1, :], ["cst"], ["maskLo"])
        cp("dve", maskHi, cst[:, 2, :], ["cst"], ["maskHi"])
        S.add("dve", lambda e: e.memset(ones_m, 1.0 / D), [], ["ones_m"])
        act(esink, esink, AF.Exp, ["esink"], ["esink"])
        act(scv, cv, AF.Silu, ["cv"], ["scv"])

        conv("abin", abin_b, abin_in, D)
        av = abin_in[:, 0:1280].rearrange("r (h t c) -> r h t c", h=10, t=2)
        sv = abin_s.rearrange("r (h t c) -> r h t c", h=10, t=2)
        for r0 in range(0, D, 128):
            for t_ in range(2):
                k = ("w", "abin_s", r0, t_)
                dma("pool", sv[r0:r0 + 128, :, t_, :], av[r0:r0 + 128, :, 1 - t_, :], [], [k])
                wkeys.setdefault("abin_s", []).append(k)
        for r0 in range(0, D, 512):
            segs = [(0, 64, 2304), (64, 128, 2304), (128, 160, 2336), (160, 192, 2304), (192, 224, 2336), (224, 256, 2304)]
            for (d0, d1, s0) in segs:
                k = ("w", "abin_kr", r0, d0)
                dma("pool", abin_kr[r0:r0 + 512, d0:d1], abin_in[r0:r0 + 512, s0:s0 + (d1 - d0)], [], [k])
                wkeys.setdefault("abin_kr", []).append(k)
        wq3 = wqb_in.rearrange("r (h c) -> r h c", h=8)
        dma("pool", wqn_b.rearrange("r (h c) -> r h c", h=8), wq3[:, :, 0:128], [], [("w", "wqn")])
        wkeys["wqn"] = [("w", "wqn")]
        wr3 = wqr_b.rearrange("r (s h c) -> r s h c", s=2, h=8)
        wkeys["wqr"] = []
        for j, (sl_d, sl_s) in enumerate([((0, slice(0, 64)), slice(128, 192)),
                                          ((1, slice(0, 32)), slice(160, 192)),
                                          ((1, slice(32, 64)), slice(128, 160))]):
            k = ("w", "wqr", j)
            dma("pool", wr3[:, sl_d[0], :, sl_d[1]], wq3[:, :, sl_s], [], [k])
            wkeys["wqr"].append(k)
        wk3 = wkvb_in.rearrange("r (h c) -> r h c", h=8)
        dma("pool", wkn_b.rearrange("r (h c) -> r h c", h=8), wk3[:, :, 0:128], [], [("w", "wkn")])
        dma("pool", wkv_b.rearrange("r (h c) -> r h c", h=8), wk3[:, :, 128:256], [], [("w", "wkv")])
        wkeys["wkn"] = [("w", "wkn")]
        wkeys["wkv"] = [("w", "wkv")]
        conv("about", about_b, about_in, D)
        conv("wg0", wg_b[0], wg_in[0], D)
        conv("wu0", wu_b[0], wu_in[0], D)
        conv("wd0", wd_b[0], wd_in[0], FF)
        conv("retin", retin_b, retin_in, D, rb=128)
        conv("retout", retout_b, retout_in, 4096)
        conv("wg1", wg_b[1], wg_in[1], D)
        conv("wu1", wu_b[1], wu_in[1], D)
        conv("wd1", wd_b[1], wd_in[1], FF)

        scv3 = scv.rearrange("p (k s) -> p k s", s=2)
        for l in range(2):
            pm, pmk = ps[7], psk[7]
            for cb in range(48):
                wv, wk = wload(modw_in[l][:, cb * 256:(cb + 1) * 256].rearrange("(k p) c -> p k c", p=128), [KD, 256], [], F32)
                for j in range(2):
                    oc = cb * 2 + j
                    for kc in range(KD):
                        mm(pm[:, oc * 2:oc * 2 + 2], wv[:, kc, j * 128:(j + 1) * 128], scv3[:, kc, :], kc == 0, kc == KD - 1,
                           [wk, "scv"], [pmk])
            tt("dve", modT[:, l * 192:(l + 1) * 192].rearrange("p (o s) -> p o s", s=2), pm[:, 0:192].rearrange("p (o s) -> p o s", s=2),
               modb[:, l * 96:(l + 1) * 96].unsqueeze(2).to_broadcast([128, 96, 2]), ALU.add, [pmk, "modb"], [("modT", l)])
        modT4 = modT.rearrange("p (l j k s) -> p l j k s", l=2, j=6, k=16)
        EFF4 = EFF.rearrange("p (l j k s) -> p l j k s", l=2, j=6, k=16)
        for l in range(2):
            for (jo, jsc, jsh, jg, gv) in [(0, 1, 0, 2, nmg), (3, 4, 3, 5, nfg)]:
                gcol = gv[:, l * 16:(l + 1) * 16].unsqueeze(2).to_broadcast([128, 16, 2])
                stt("dve", EFF4[:, l, jo], modT4[:, l, jsc], 1.0, gcol, ALU.add, ALU.mult, [("modT", l), "nmg", "nfg"], [("EFF", l, jo)])
                cp("dve", EFF4[:, l, jo + 1], modT4[:, l, jsh], [("modT", l)], [("EFF", l, jo + 1)])
                cp("dve", EFF4[:, l, jo + 2], modT4[:, l, jg], [("modT", l)], [("EFF", l, jo + 2)])

        def effcol(l, j, kc, s):
            return EFF4[:, l, j, kc, s:s + 1]

        def effk(l, j):
            return ("EFF", l, j)

        def norm_fm(xv, n, nk, xk, scale_fn, bias_fn, out_fn, outk_fn, sk, sqs=1.0):
            pn, pnk = psC()
            for kc in range(nk):
                sq, sqk = gtmp()
                if kc % 2 == 0:
                    act(sq[:, 0:n], xv[:, kc, 0:n], AF.Square, [xk], [sqk])
                else:
                    tt("dve", sq[:, 0:n], xv[:, kc, 0:n], xv[:, kc, 0:n], ALU.mult, [xk], [sqk])
                mm(pn[:, 0:n], ones_m, sq[:, 0:n], kc == 0, kc == nk - 1, [sqk, "ones_m"], [pnk])
            rs, rsk = rsb, "rsb"
            act(rs[:, 0:n], pn[:, 0:n], AF.Sqrt, [pnk], [rsk], bias=EPS, scale=sqs)
            S.add("dve", lambda e: e.reciprocal(out=rs[:, 0:n], in_=rs[:, 0:n]), [rsk], [rsk])
            for kc in range(nk):
                t_, tk = gtmp()
                tt("dve", t_[:, 0:n], xv[:, kc, 0:n], rs[:, 0:n], ALU.mult, [xk, rsk], [tk])
                b = bias_fn(kc) if bias_fn is not None else None
                act(out_fn(kc), t_[:, 0:n], AF.Identity, [tk] + sk, [outk_fn(kc)], bias=b, scale=scale_fn(kc))

        def ffn(l, n, mods):
            act_t = ar.view(BIGo, [KF, T], BF16)
            for (c0, c1, s) in mods:
                norm_fm(xT[:, :, c0:c1], c1 - c0, KD, "xT", lambda kc: effcol(l, 3, kc, s), lambda kc: effcol(l, 4, kc, s),
                        lambda kc: hT[:, kc, c0:c1], lambda kc: "hT", [effk(l, 3), effk(l, 4)])
            for hb in range(11):
                gv, gk = wload(wg_b[l][:, hb * 512:(hb + 1) * 512].rearrange("(k p) c -> p k c", p=128), [KD, 512], wkeys["wg%d" % l])
                uv, uk = wload(wu_b[l][:, hb * 512:(hb + 1) * 512].rearrange("(k p) c -> p k c", p=128), [KD, 512], wkeys["wu%d" % l])
                for j in range(4):
                    hc = hb * 4 + j
                    pg, pgk = psA()
                    for kc in range(KD):
                        mm(pg[:, 0:n], gv[:, kc, j * 128:(j + 1) * 128], hT[:, kc, 0:n], kc == 0, kc == KD - 1, [gk, "hT"], [pgk])
                    pu, puk = psA()
                    for kc in range(KD):
                        mm(pu[:, 0:n], uv[:, kc, j * 128:(j + 1) * 128], hT[:, kc, 0:n], kc == 0, kc == KD - 1, [uk, "hT"], [puk])
                    sg, sgk = gtmp()
                    act(sg[:, 0:n], pg[:, 0:n], AF.Silu, [pgk], [sgk])
                    tt("dve", act_t[:, hc, 0:n], sg[:, 0:n], pu[:, 0:n], ALU.mult, [sgk, puk], [("BIG", hc)])
            for dg in range(4):
                pd = [psA() for _ in range(4)]
                for q4 in range(4):
                    dv_, dk = wload(wd_b[l][q4 * 1408:(q4 + 1) * 1408, dg * 512:(dg + 1) * 512].rearrange("(k p) c -> p k c", p=128),
                                    [11, 512], wkeys["wd%d" % l])
                    for j in range(4):
                        for hh in range(11):
                            hc = q4 * 11 + hh
                            mm(pd[j][0][:, 0:n], dv_[:, hh, j * 128:(j + 1) * 128], act_t[:, hc, 0:n], hc == 0, hc == KF - 1,
                               [dk, ("BIG", hc)], [pd[j][1]])
                for j in range(4):
                    dc = dg * 4 + j
                    for (c0, c1, s) in mods:
                        stt("dve", xT[:, dc, c0:c1], pd[j][0][:, c0:c1], effcol(l, 5, dc, s), xT[:, dc, c0:c1], ALU.mult, ALU.add,
                            [pd[j][1], effk(l, 5), "xT"], ["xT"])

        xtok = [ar.view(BIGo + 8192 * i, [D], F32) for i in range(2)]
        xtokk = [("xtok", i) for i in range(2)]

        def load_x_transposed(src_rows, n, bigkeys_old):
            for sub in range(n // 128):
                xb, xbk = xtok[sub % 2], xtokk[sub % 2]
                dma("sp", xb, src_rows[sub * 128:(sub + 1) * 128, :], [], [xbk])
                for g4 in range(4):
                    pp, ppk = psC()
                    for j in range(4):
                        kc = g4 * 4 + j
                        tr(pp[:, j * 128:(j + 1) * 128], xb[:, kc * 128:(kc + 1) * 128], ident_f, [xbk, "cst"], [ppk])
                    evac(xT[:, g4 * 4:(g4 + 1) * 4, sub * 128:(sub + 1) * 128], pp.rearrange("p (j t) -> p j t", j=4), [ppk], ["xT"])

        tiles = [(0, NCX, 1)] + [(NCX + i * T, T, 0) for i in range(NTL)]
        big_all = [("BIG", i) for i in range(KF)]

        ropeTa = [ar.view(BIGo + 16640 + 2048 * i, [T], F32) for i in range(4)]
        qa_t = ar.view(BIGo + 24832, [8, T], BF16)
        kx_t = ar.view(BIGo + 33024, [4, T], BF16)
        va_t = ar.view(BIGo + 37120, [4, 2, 129], BF16)
        lat_t = ar.view(KRo, [4, T], F32)
        rename(big_all, ["xtok0", "ropeT", "qa_t", "kx_t", "va_t", "lat_t"] + xtokk)
        S.add("dve", lambda e: e.memset(va_t[:, :, :, 128:129], 1.0), ["va_t"], ["va_t"])

        def rope_evac(px, pxk, psw, pswk, ct, st, o, n, rk, wk_, sl=slice(0, 128)):
            t1, t1k = gtmp()
            tt("dve", t1[sl, 0:n], px[:, 0:n], ct[:, 0:n], ALU.mult, [pxk] + rk, [t1k])
            t2, t2k = gtmp()
            tt("dve", t2[sl, 0:n], psw[:, 0:n], st[:, 0:n], ALU.mult, [pswk] + rk, [t2k])
            tt("dve", o, t1[sl, 0:n], t2[sl, 0:n], ALU.add, [t1k, t2k], wk_)

        for (t0, n, isc) in tiles:
            l = 0
            if isc:
                load_x_transposed(ctx_in, n, None)
            else:
                load_x_transposed(x_in[t0 - NCX:t0 - NCX + n], n, None)
                p0 = t0 - NCX
                dma("sp", ropeTa[0][:, 0:n], ropeA_in[0][:, p0:p0 + n], [], [("ropeT", 0)])
                dma("sp", ropeTa[1][:, 0:n], ropeA_in[1][:, p0:p0 + n], [], [("ropeT", 1)])
                dma("sp", ropeTa[2][:, 0:n], ropeB_in[0][:, p0:p0 + n], [], [("ropeT", 2)])
                dma("sp", ropeTa[3][:, 0:n], ropeB_in[1][:, p0:p0 + n], [], [("ropeT", 3)])
            dma("sp", XT[:, :, t0:t0 + n].rearrange("k p t -> p k t"), xT[:, :, 0:n], ["xT"], [("XT", t0)])
            norm_fm(xT[:, :, 0:n], n, KD, "xT", lambda kc: effcol(0, 0, kc, isc), lambda kc: effcol(0, 1, kc, isc),
                    lambda kc: hT[:, kc, 0:n], lambda kc: "hT", [effk(0, 0), effk(0, 1)])
            for blk in range(3):
                ncol = 512 if blk < 2 else 256
                wv, wk = wload(abin_b[:, blk * 512:blk * 512 + 512].rearrange("(k p) c -> p k c", p=128), [KD, 512], wkeys["abin"])
                if not isc:
                    sv_, swk = wload(abin_s[:, blk * 512:blk * 512 + ncol].rearrange("(k p) c -> p k c", p=128), [KD, ncol], wkeys["abin_s"])
                for j in range(ncol // 128):
                    oc = blk * 4 + j
                    dst = qa_t[:, oc, 0:n] if oc < 8 else kx_t[:, oc - 8, 0:n]
                    dk_ = "qa_t" if oc < 8 else "kx_t"
                    px, pxk = psA()
                    for kc in range(KD):
                        mm(px[:, 0:n], wv[:, kc, j * 128:(j + 1) * 128], hT[:, kc, 0:n], kc == 0, kc == KD - 1, [wk, "hT"], [pxk])
                    if isc:
                        evac(dst, px[:, 0:n], [pxk], [dk_])
                    else:
                        pw, pwk = psA()
                        for kc in range(KD):
                            mm(pw[:, 0:n], sv_[:, kc, j * 128:(j + 1) * 128], hT[:, kc, 0:n], kc == 0, kc == KD - 1, [swk, "hT"], [pwk])
                        rope_evac(px, pxk, pw, pwk, ropeTa[0], ropeTa[1], dst, n, [("ropeT", 0), ("ropeT", 1)], [dk_])
                if blk == 2:
                    for tb in range(n // 128):
                        pv, pvk = psA()
                        for kc in range(KD):
                            mm(pv[:, 0:256], hT[:, kc, tb * 128:(tb + 1) * 128], wv[:, kc, 256:512], kc == 0, kc == KD - 1, [wk, "hT"], [pvk])
                        evac(va_t[:, tb, :, 0:128], pv[:, 0:256].rearrange("p (g c) -> p g c", g=2), [pvk], ["va_t"])
            dma("sp", QA[:, :, t0:t0 + n].rearrange("h p t -> p h t"), qa_t[:, :, 0:n], ["qa_t"], [("QA", t0)])
            dma("sp", KA[:, :, t0:t0 + n].rearrange("h p t -> p h t"), kx_t[:, 0:2, 0:n], ["kx_t"], [("KA", t0)])
            b0 = t0 // 128
            dma("sp", VA[b0:b0 + n // 128].rearrange("b p g c -> p b g c"), va_t[:, 0:n // 128], ["va_t"], [("VA", t0)])
            wv, wk = wload(abin_b[:, 1536:2048].rearrange("(k p) c -> p k c", p=128), [KD, 512], wkeys["abin"])
            for j in range(4):
                px, pxk = psA()
                for kc in range(KD):
                    mm(px[:, 0:n], wv[:, kc, j * 128:(j + 1) * 128], hT[:, kc, 0:n], kc == 0, kc == KD - 1, [wk, "hT"], [pxk])
                evac(lat_t[:, j, 0:n], px[:, 0:n], [pxk], ["lat_t"])
            qln = ar.view(BIGo + 40960, [4, T], BF16)
            norm_fm(lat_t[:, :, 0:n], n, 4, "lat_t", lambda kc: qng[:, kc:kc + 1], None,
                    lambda kc: qln[:, kc, 0:n], lambda kc: "qln", ["qng"], sqs=4.0)
            qn_t = qa_t
            wv, wk = wload(wqn_b.rearrange("(k p) c -> p k c", p=128), [4, 1024], wkeys["wqn"])
            for h in range(8):
                px, pxk = psA()
                for kc in range(4):
                    mm(px[:, 0:n], wv[:, kc, h * 128:(h + 1) * 128], qln[:, kc, 0:n], kc == 0, kc == 3, [wk, "qln"], [pxk])
                evac(qn_t[:, h, 0:n], px[:, 0:n], [pxk], ["qa_t"])
            dma("sp", QN[:, :, t0:t0 + n].rearrange("h p t -> p h t"), qn_t[:, :, 0:n], ["qa_t"], [("QN", t0)])
            wv, wk = wload(wqr_b.rearrange("(k p) c -> p k c", p=128), [4, 1024], wkeys["wqr"])
            qr_t = qa_t
            S.add("dve", lambda e: e.memset(qr_t[:, :, 0:n], 0.0), [], ["qa_t"])
            for m in range(4):
                px, pxk = psA()
                for kc in range(4):
                    mm(px[:, 0:n], wv[:, kc, m * 128:(m + 1) * 128], qln[:, kc, 0:n], kc == 0, kc == 3, [wk, "qln"], [pxk])
                if isc:
                    for hh in range(2):
                        evac(qr_t[64 * hh:64 * hh + 64, 2 * m + hh, 0:n], px[64 * hh:64 * hh + 64, 0:n], [pxk], ["qa_t"])
                else:
                    pw, pwk = psA()
                    for kc in range(4):
                        mm(pw[:, 0:n], wv[:, kc, 512 + m * 128:512 + (m + 1) * 128], qln[:, kc, 0:n], kc == 0, kc == 3, [wk, "qln"], [pwk])
                    for hh in range(2):
                        sl = slice(64 * hh, 64 * hh + 64)
                        rope_evac(px[sl], pxk, pw[sl], pwk, ropeTa[2][sl], ropeTa[3][sl], qr_t[sl, 2 * m + hh, 0:n], n,
                                  [("ropeT", 2), ("ropeT", 3)], ["qa_t"], sl)
            dma("sp", QR[:, :, t0:t0 + n].rearrange("h p t -> p h t"), qr_t[:, :, 0:n], ["qa_t"], [("QR", t0)])
            wv, wk = wload(abin_b[:, 2048:2304].rearrange("(k p) c -> p k c", p=128), [KD, 256], wkeys["abin"])
            for j in range(2):
                px, pxk = psA()
                for kc in range(KD):
                    mm(px[:, 0:n], wv[:, kc, j * 128:(j + 1) * 128], hT[:, kc, 0:n], kc == 0, kc == KD - 1, [wk, "hT"], [pxk])
                evac(lat_t[:, j, 0:n], px[:, 0:n], [pxk], ["lat_t"])
            kvn = qln
            norm_fm(lat_t[:, 0:2, 0:n], n, 2, "lat_t", lambda kc: kvng[:, kc:kc + 1], None,
                    lambda kc: kvn[:, kc, 0:n], lambda kc: "qln", ["kvng"], sqs=8.0)
            wv, wk = wload(abin_kr.rearrange("(k p) c -> p k c", p=128), [KD, 256], wkeys["abin_kr"])
            px, pxk = psA()
            for kc in range(KD):
                mm(px[:, 0:n], wv[:, kc, 0:128], hT[:, kc, 0:n], kc == 0, kc == KD - 1, [wk, "hT"], [pxk])
            kr_t = qa_t[:, 0, :]
            if isc:
                evac(kr_t[:, 0:n], px[:, 0:n], [pxk], ["qa_t"])
            else:
                pw, pwk = psA()
                for kc in range(KD):
                    mm(pw[:, 0:n], wv[:, kc, 128:256], hT[:, kc, 0:n], kc == 0, kc == KD - 1, [wk, "hT"], [pwk])
                rope_evac(px, pxk, pw, pwk, ropeTa[2], ropeTa[3], kr_t[:, 0:n], n, [("ropeT", 2), ("ropeT", 3)], ["qa_t"])
            dma("sp", KRS[:, t0:t0 + n], kr_t[:, 0:n], ["qa_t"], [("KRS", t0)])
            wv, wk = wload(wkn_b.rearrange("(k p) c -> p k c", p=128), [2, 1024], wkeys["wkn"])
            kn_full = ar.view(BIGo, [8, T], BF16)
            for h in range(8):
                px, pxk = psA()
                for kc in range(2):
                    mm(px[:, 0:n], wv[:, kc, h * 128:(h + 1) * 128], kvn[:, kc, 0:n], kc == 0, kc == 1, [wk, "qln"], [pxk])
                evac(kn_full[:, h, 0:n], px[:, 0:n], [pxk], [xtokk[0]])
            dma("sp", KN[:, :, t0:t0 + n].rearrange("h p t -> p h t"), kn_full[:, :, 0:n], [xtokk[0]], [("KN", t0)])
            vm_t = ar.view(BIGo + 8192, [4, 8, 129], BF16)
            S.add("dve", lambda e: e.memset(vm_t[:, :, :, 128:129], 1.0), [xtokk[1]], [xtokk[1]])
            wv, wk = wload(wkv_b.rearrange("(k p) c -> p k c", p=128), [2, 1024], wkeys["wkv"])
            for tb in range(n // 128):
                for nb in range(2):
                    pv, pvk = psA()
                    for kc in range(2):
                        mm(pv, kvn[:, kc, tb * 128:(tb + 1) * 128], wv[:, kc, nb * 512:(nb + 1) * 512], kc == 0, kc == 1, [wk, "qln"], [pvk])
                    evac(vm_t[:, tb, nb * 4:(nb + 1) * 4, 0:128], pv.rearrange("p (h c) -> p h c", h=4), [pvk], [xtokk[1]])
            for tb in range(n // 128):
                dma("sp", VM[:, :, b0 + tb, :].rearrange("h p c -> p h c"), vm_t[:, tb], [xtokk[1]], [("VM", t0, tb)])

        if stage <= 1:
            pass

        allk = lambda nm: [(nm, t0) for (t0, n, isc) in tiles]
        krs = ar.view(KRo, [NTOK], BF16)
        qa_b = ar.view(BIGo, [8, T], BF16)
        qn_b = ar.view(BIGo + 8192, [8, T], BF16)
        qr_b = ar.view(BIGo + 16384, [8, T], BF16)
        o_t = ar.view(BIGo + 24576, [4, 16, 128], BF16)
        kA_w = ar.view(BIGo + 40960, [2, 768], BF16)
        kA_c = ar.view(BIGo + 44032, [2, 256], BF16)
        vA_c = ar.view(VACo, [2, 2, 129], BF16)
        vA_w = ar.view(TMPo, [6, 2, 129], BF16)
        oT = hT
        dsm = small(8)
        rename(["xT", "hT", "qa_t", "kx_t", "va_t", "lat_t", "qln", "ropeT"] + xtokk + [("ropeT", i) for i in range(4)],
               ["krs", "qa_b", "qn_b", "qr_b", "o_t", "kA_w", "kA_c", "vA_c"] + big_all)
        dma("sp", krs, KRS, allk("KRS"), ["krs"])
        SC_A = float(128 ** -0.5)
        SC_B = float(192 ** -0.5)

        def finish_head(pv, pvk, hidx, qb, sink_col):
            den, dk_ = dsm, "dsm"
            if sink_col is not None:
                ts("dve", den[:, 0:1], pv[:, 128:129], sink_col, None, ALU.add, None, [pvk, "esink"], [dk_])
                S.add("dve", lambda e: e.reciprocal(out=den[:, 0:1], in_=den[:, 0:1]), [dk_], [dk_])
            else:
                S.add("dve", lambda e: e.reciprocal(out=den[:, 0:1], in_=pv[:, 128:129]), [pvk], [dk_])
            act(o_t[:, qb, hidx, :], pv[:, 0:128], AF.Copy, [pvk, dk_], ["o_t"], scale=den[:, 0:1])

        for (t0, n, isc) in tiles:
            nqb = n // 128
            dma("sp", xT[:, :, 0:n], XT[:, :, t0:t0 + n].rearrange("k p t -> p k t"), [("XT", t0)], ["xT"])
            dma("sp", qa_b[:, :, 0:n], QA[:, :, t0:t0 + n].rearrange("h p t -> p h t"), allk("QA"), ["qa_b"])
            dma("sp", qn_b[:, :, 0:n], QN[:, :, t0:t0 + n].rearrange("h p t -> p h t"), allk("QN"), ["qn_b"])
            dma("sp", qr_b[:, :, 0:n], QR[:, :, t0:t0 + n].rearrange("h p t -> p h t"), allk("QR"), ["qr_b"])
            dma("sp", kA_c, KA[:, :, 0:NCX].rearrange("h p t -> p h t"), allk("KA"), ["kA_c"])
            dma("sp", vA_c, VA[0:2].rearrange("b p g c -> p b g c"), allk("VA"), ["vA_c"])
            if not isc:
                lb0 = (t0 - NCX) // 128
                wlo = max(lb0 - 1, 0)
                whi = min(lb0 + nqb + 1, NCH)
                nw = whi - wlo
                dma("sp", kA_w[:, :, 0:nw * 128], KA[:, :, NCX + wlo * 128:NCX + whi * 128].rearrange("h p t -> p h t"), allk("KA"), ["kA_w"])
                dma("sp", vA_w[:, 0:nw], VA[2 + wlo:2 + whi].rearrange("b p g c -> p b g c"), allk("VA"), [tmpk[0], tmpk[1]])
            for g in range(2):
                for qb in range(nqb):
                    kbs = [("c", 0), ("c", 1)]
                    if not isc:
                        lb = lb0 + qb
                        for dlt in (-1, 0, 1):
                            if 0 <= lb + dlt < NCH:
                                kbs.append(("l", lb + dlt - wlo, dlt))
                    pvs = [psA() for _ in range(4)]
                    for ki, kb in enumerate(kbs):
                        sps, spk = psB()
                        if kb[0] == "c":
                            kl = kA_c[:, g, kb[1] * 128:(kb[1] + 1) * 128]
                            vv = vA_c[:, kb[1], g, :]
                            kr_, vr_ = ["kA_c"], ["vA_c"]
                        else:
                            kl = kA_w[:, g, kb[1] * 128:(kb[1] + 1) * 128]
                            vv = vA_w[:, kb[1], g, :]
                            kr_, vr_ = ["kA_w"], [tmpk[0], tmpk[1]]
                        mm(sps.rearrange("p (h q) -> p h q", h=4), kl, qa_b[:, 4 * g:4 * g + 4, qb * 128:(qb + 1) * 128], True, True,
                           kr_ + ["qa_b"], [spk])
                        p_, pk = gpt()
                        act(p_, sps, AF.Exp, [spk], [pk], scale=SC_A)
                        if kb[0] == "l" and kb[2] != 0:
                            mk = maskLo if kb[2] == -1 else maskHi
                            mkk = "maskLo" if kb[2] == -1 else "maskHi"
                            p3 = p_.rearrange("p (h q) -> p h q", h=4)
                            tt("dve", p3, p3, mk.unsqueeze(1).to_broadcast([128, 4, 128]), ALU.mult, [pk, mkk], [pk])
                        for hh in range(4):
                            mm(pvs[hh][0][:, 0:129], p_[:, hh * 128:(hh + 1) * 128], vv, ki == 0, ki == len(kbs) - 1, [pk] + vr_, [pvs[hh][1]])
                    for hh in range(4):
                        finish_head(pvs[hh][0], pvs[hh][1], 4 * g + hh, qb, esink[:, 4 * g + hh:4 * g + hh + 1])
            nkb = 2 if isc else NBLK
            for h in range(8):
                knv, knk = wload(KN[h], [NTOK], allk("KN"))
                vmv, vmk = wload(VM[h], [NBLK, 129], [("VM", t0_, tb_) for (t0_, n_, i_) in tiles for tb_ in range(n_ // 128)])
                pvs = [psA() for _ in range(nqb)]

                def s_stage(kb):
                    sps, spk = psB()
                    mm(sps[:, 0:n], knv[:, kb * 128:(kb + 1) * 128], qn_b[:, h, 0:n], True, False, [knk, "qn_b"], [spk])
                    mm(sps[:, 0:n], krs[:, kb * 128:(kb + 1) * 128], qr_b[:, h, 0:n], False, True, ["krs", "qr_b"], [spk])
                    return sps, spk

                nxt_s = s_stage(0)
                for kb in range(nkb):
                    sps, spk = nxt_s
                    if kb + 1 < nkb:
                        nxt_s = s_stage(kb + 1)
                    p_, pk = gpt()
                    act(p_[:, 0:n], sps[:, 0:n], AF.Exp, [spk], [pk], scale=SC_B)
                    for qb in range(nqb):
                        mm(pvs[qb][0][:, 0:129], p_[:, qb * 128:(qb + 1) * 128], vmv[:, kb, :], kb == 0, kb == nkb - 1, [pk, vmk], [pvs[qb][1]])
                for qb in range(nqb):
                    finish_head(pvs[qb][0], pvs[qb][1], 8 + h, qb, None)
            for qb in range(nqb):
                for half in range(2):
                    pp, ppk = psC()
                    ppb = pp.bitcast(BF16)
                    for j in range(8):
                        fc = half * 8 + j
                        tr(ppb[:, j * 128:(j + 1) * 128], o_t[:, qb, fc, :], ident_b, ["o_t", "ident_b"], [ppk])
                    evac(oT[:, half * 8:(half + 1) * 8, qb * 128:(qb + 1) * 128], ppb.rearrange("p (j t) -> p j t", j=8), [ppk], ["hT"])
            for dg in range(4):
                wv, wk = wload(about_b[:, dg * 512:(dg + 1) * 512].rearrange("(k p) c -> p k c", p=128), [KD, 512], wkeys["about"])
                for j in range(4):
                    dc = dg * 4 + j
                    px, pxk = psA()
                    for fc in range(KD):
                        mm(px[:, 0:n], wv[:, fc, j * 128:(j + 1) * 128], oT[:, fc, 0:n], fc == 0, fc == KD - 1, [wk, "hT"], [pxk])
                    stt("dve", xT[:, dc, 0:n], px[:, 0:n], effcol(0, 2, dc, isc), xT[:, dc, 0:n], ALU.mult, ALU.add,
                        [pxk, effk(0, 2), "xT"], ["xT"])
            rename(["qa_b", "qn_b", "qr_b", "o_t", "kA_w", "kA_c", "vA_c"], big_all)
            ffn(0, n, [(0, n, isc)])
            rename(big_all, ["qa_b", "qn_b", "qr_b", "o_t", "kA_w", "kA_c", "vA_c"])
            dma("sp", XT[:, :, t0:t0 + n].rearrange("k p t -> p k t"), xT[:, :, 0:n], ["xT"], [("XT", t0)])

        if stage <= 2:
            for kc in range(KD):
                dma("sp", dbg[kc], XT[kc], allk("XT"), [("out", kc)])
            S.fence("sp", [("out", kc) for kc in range(KD)])
            S.emit()
            return nc


        pc = small(4)
        qdec = small(16)
        gst = small(8)
        dma("sp", pc, pc_in, [], ["pc"])
        act(lgs, lg, AF.Exp, ["lg"], ["lgs"], scale=-1.0)
        act(lgs, lgs, AF.Ln, ["lgs"], ["lgs"], bias=1.0)
        ts("dve", lgs, lgs, -1.0, None, ALU.mult, None, ["lgs"], ["lgs"])
        for dirn in range(2):
            for h in range(8):
                c = dirn * 8 + h
                act(kdec[:, c:c + 1], pc[:, dirn:dirn + 1], AF.Exp, ["pc", "lgs"], ["kdec"], scale=lgs[:, c:c + 1])
                act(qdec[:, c:c + 1], pc[:, 2 + dirn:3 + dirn], AF.Exp, ["pc", "lgs"], ["qdec"], scale=lgs[:, c:c + 1])
        act(cdec, lgs, AF.Exp, ["lgs"], ["cdec"], scale=128.0)
        DTv = ar.view(KRo, [2, 8, 128], F32)
        rename(["krs", "lat_t"], [("DT", 0), ("DT", 1)])
        for dirn in range(2):
            for h in range(8):
                c = dirn * 8 + h
                act(DTv[:, dirn, h, :], diffF if dirn == 0 else diffB, AF.Exp, ["cst", "lgs"], [("DT", dirn)], scale=lgs[:, c:c + 1])
                tt("dve", DTv[:, dirn, h, :], DTv[:, dirn, h, :], cst[:, 2 if dirn == 0 else 1, :], ALU.mult, [("DT", dirn), "cst"], [("DT", dirn)])

        def wbuf(shape, dtype):
            i = nxt("wp", NWP)
            return ar.view(WP0 + i * WPB, shape, dtype), ("wp", i)

        kp_t = [ar.view(XTo + 16384 * i, [4, D], BF16) for i in range(2)]
        qT_t = ar.view(BIGo, [4, KD, 128], BF16)
        kT_t = ar.view(BIGo + 16384, [4, KD, 128], BF16)
        vctx = ar.view(BIGo, [2, 4096], BF16)
        ev_t = [ar.view(BIGo + 32768 + 4096 * i, [4, 512], BF16) for i in range(2)]
        ropeRt = [ar.view(BIGo + 40960 + 2048 * i, [T], F32) for i in range(2)]
        rename(big_all + ["qa_b", "qn_b", "qr_b", "o_t", "kA_w", "kA_c", "vA_c"], ["qT_t", "kT_t", "ev0", "ev1", "ropeR"])
        allS = [("S", h) for h in range(8)]
        allSb = [("Sb", h) for h in range(8)]

        for (t0, n, isc) in tiles:
            ntb = n // 128
            ch0 = (t0 - NCX) // 128
            dma("sp", xT[:, :, 0:n], XT[:, :, t0:t0 + n].rearrange("k p t -> p k t"), [("XT", t0)], ["xT"])
            if not isc:
                p0 = t0 - NCX
                dma("sp", ropeRt[0][:, 0:n], ropeR_in[0][:, p0:p0 + n], [], ["ropeR"])
                dma("sp", ropeRt[1][:, 0:n], ropeR_in[1][:, p0:p0 + n], [], ["ropeR"])
            norm_fm(xT[:, :, 0:n], n, KD, "xT", lambda kc: effcol(1, 0, kc, isc), lambda kc: effcol(1, 1, kc, isc),
                    lambda kc: hT[:, kc, 0:n], lambda kc: "hT", [effk(1, 0), effk(1, 1)])
            for which in ((1,) if isc else (0, 1)):
                dstT = qT_t if which == 0 else kT_t
                dkey = "qT_t" if which == 0 else "kT_t"
                scl = 1.0 if which == 0 else 0.0625
                for blk in range(4):
                    c0 = which * 2048 + blk * 512
                    wv, wk = wload(retin_b[:, c0:c0 + 512].rearrange("(k p) c -> p k c", p=128), [KD, 512], wkeys["retin"])
                    for hh in range(2):
                        h = blk * 2 + hh
                        pxs = []
                        for c in range(2):
                            px, pxk = psA()
                            j = hh * 2 + c
                            for kc in range(KD):
                                mm(px[:, 0:n], wv[:, kc, j * 128:(j + 1) * 128], hT[:, kc, 0:n], kc == 0, kc == KD - 1, [wk, "hT"], [pxk])
                            pxs.append((px, pxk))
                        d0 = dstT[:, 0:ntb, 2 * h, :]
                        d1 = dstT[:, 0:ntb, 2 * h + 1, :]
                        if isc:
                            for c, dd in ((0, d0), (1, d1)):
                                act(dd, pxs[c][0][:, 0:n].rearrange("p (b t) -> p b t", t=128), AF.Copy, [pxs[c][1]], [dkey], scale=scl)
                        else:
                            (x0, x0k), (x1, x1k) = pxs
                            cR, sR = ropeRt[0], ropeRt[1]
                            ta, tak = gtmp()
                            stt("dve", ta[:, 0:n], x0[:, 0:n], scl, cR[:, 0:n], ALU.mult, ALU.mult, [x0k, "ropeR"], [tak])
                            tb_, tbk = gtmp()
                            stt("dve", tb_[:, 0:n], x1[:, 0:n], scl, sR[:, 0:n], ALU.mult, ALU.mult, [x1k, "ropeR"], [tbk])
                            tt("pool", d0, ta[:, 0:n].rearrange("p (b t) -> p b t", t=128), tb_[:, 0:n].rearrange("p (b t) -> p b t", t=128),
                               ALU.subtract, [tak, tbk], [dkey])
                            tc_, tck = gtmp()
                            stt("dve", tc_[:, 0:n], x0[:, 0:n], scl, sR[:, 0:n], ALU.mult, ALU.mult, [x0k, "ropeR"], [tck])
                            td, tdk = gtmp()
                            stt("dve", td[:, 0:n], x1[:, 0:n], scl, cR[:, 0:n], ALU.mult, ALU.mult, [x1k, "ropeR"], [tdk])
                            tt("pool", d1, tc_[:, 0:n].rearrange("p (b t) -> p b t", t=128), td[:, 0:n].rearrange("p (b t) -> p b t", t=128),
                               ALU.add, [tck, tdk], [dkey])
            if not isc:
                dma("sp", QT1[ch0:ch0 + ntb].rearrange("c p k t -> p c (k t)"), qT_t[:, 0:ntb].rearrange("p c k t -> p c (k t)"), ["qT_t"], [("QT1", t0)])
                dma("sp", KT1[ch0:ch0 + ntb].rearrange("c p k t -> p c (k t)"), kT_t[:, 0:ntb].rearrange("p c k t -> p c (k t)"), ["kT_t"], [("KT1", t0)])
            rename(["xT"], ["kp0", "kp1"])
            for tb in range(ntb):
                for half in range(2):
                    pp, ppk = psC()
                    ppb = pp.bitcast(BF16)
                    for j in range(8):
                        kc = half * 8 + j
                        tr(ppb[:, j * 128:(j + 1) * 128], kT_t[:, tb, kc, :], ident_b, ["kT_t", "ident_b"], [ppk])
                    for dirn in range(2):
                        for hh in range(4):
                            h = half * 4 + hh
                            act(kp_t[dirn][:, tb, h * 256:(h + 1) * 256], ppb[:, hh * 256:(hh + 1) * 256], AF.Copy, [ppk, "kdec"], ["kp%d" % dirn],
                                scale=kdec[:, dirn * 8 + h:dirn * 8 + h + 1])
            if not isc:
                for dirn in range(2):
                    dma("sp", KP1[dirn][ch0:ch0 + ntb].rearrange("c p f -> p c f"), kp_t[dirn][:, 0:ntb], ["kp%d" % dirn], [("KP1", dirn, t0)])
            for which in ((0,) if isc else (0, 1)):
                for nb in range(8):
                    c0 = 4096 + which * 4096 + nb * 512
                    wv, wk = wload(retin_b[:, c0:c0 + 512].rearrange("(k p) c -> p k c", p=128), [KD, 512], wkeys["retin"])
                    ei = nxt("ev", 2)
                    evv, evk = ev_t[ei], "ev%d" % ei
                    for tb in range(ntb):
                        pv, pvk = psA()
                        for kc in range(KD):
                            mm(pv, hT[:, kc, tb * 128:(tb + 1) * 128], wv[:, kc, :], kc == 0, kc == KD - 1, [wk, "hT"], [pvk])
                        if isc:
                            evac(vctx[:, tb, nb * 512:(nb + 1) * 512], pv, [pvk], ["qT_t"])
                        elif which == 0:
                            evac(evv[:, tb, :], pv, [pvk], [evk])
                        else:
                            act(evv[:, tb, :], pv, AF.Silu, [pvk], [evk])
                    if not isc:
                        dst = V1 if which == 0 else G1
                        dma("sp", dst[ch0:ch0 + ntb, :, nb * 512:(nb + 1) * 512].rearrange("c p f -> p c f"), evv[:, 0:ntb], [evk],
                            [("V1" if which == 0 else "G1", t0, nb)])
            if isc:
                for dirn in range(2):
                    halves = [wbuf([8, 512], F32) for _ in range(2)]
                    for hv, hk in halves:
                        S.add("dve", lambda e, hv=hv: e.memset(hv, 0.0), [], [hk])
                    for tb in ((0, 1) if dirn == 0 else (1, 0)):
                        for h in range(8):
                            hv, hk = halves[h // 4]
                            for c in range(2):
                                pst, pstk = psB()
                                mm(pst, kp_t[dirn][:, tb, h * 256 + c * 128:h * 256 + (c + 1) * 128], vctx[:, tb, h * 512:(h + 1) * 512], True, True,
                                   ["kp%d" % dirn, "qT_t"], [pstk])
                                sl = hv[:, (h % 4) * 2 + c, :]
                                stt("dve", sl, sl, cdec[:, dirn * 8 + h:dirn * 8 + h + 1], pst, ALU.mult, ALU.add, [pstk, "cdec", hk], [hk])
                    for i, (hv, hk) in enumerate(halves):
                        dma("sp", S0[dirn][:, i * 8:(i + 1) * 8, :], hv, [hk], [("S0", dirn, i)])
            rename(["kp0", "kp1"], ["xT"])

        S_v = ar.view(XTo, [16, 512], F32)
        Sb_v = ar.view(BAo, [16, 512], BF16)
        cbq = [ar.view(BIGo + 20480 * i, [KD, 128], BF16) for i in range(2)]
        cbk = [ar.view(BIGo + 20480 * i + 4096, [KD, 128], BF16) for i in range(2)]
        cbp = [ar.view(BIGo + 20480 * i + 8192, [D], BF16) for i in range(2)]
        cbv = [ar.view(BIGo + 20480 * i + 12288, [4096], BF16) for i in range(2)]
        QRW = ar.view(KR2o, [2, 8, 128], F32)
        qpb = [ar.view(BIGo + 40960 + 512 * i, [2, 128], BF16) for i in range(4)]
        gsum = small(8)
        gsq = small(8)
        gm = small(8)
        gr = small(8)
        for dirn in range(2):
            for h in range(8):
                c = dirn * 8 + h
                act(QRW[:, dirn, h, :], posr if dirn == 0 else cst[:, 6, :], AF.Exp, ["cst", "lgs"], [("QRW", dirn)], scale=lgs[:, c:c + 1])
        ltiles = [t for t in tiles if not t[2]]
        kQ = [("QT1", t[0]) for t in ltiles]
        kK = [("KT1", t[0]) for t in ltiles]
        kV = [("V1", t[0], nb) for t in ltiles for nb in range(8)]
        kG = [("G1", t[0], nb) for t in ltiles for nb in range(8)]
        rename(["xT", "hT", "qT_t", "kT_t", "ev0", "ev1", "ropeR"],
               allS + allSb + [("cb", i, w) for i in range(2) for w in "qkpv"] + [("qp", i) for i in range(4)])
        ctr["cb"] = 0
        ctr["qp"] = 0
        for dirn in range(2):
            kP = [("KP1", dirn, t[0]) for t in ltiles]
            dma("sp", S_v, S0[dirn], [("S0", dirn, 0), ("S0", dirn, 1)], allS)
            for h in range(8):
                cp("pool", Sb_v[:, 2 * h:2 * h + 2, :], S_v[:, 2 * h:2 * h + 2, :], [("S", h)], [("Sb", h)])
            order = list(range(NCH)) if dirn == 0 else list(range(NCH - 1, -1, -1))
            for oi, n_ in enumerate(order):
                bi = nxt("cb", 2)
                qc, kc_, kpc, vc = cbq[bi], cbk[bi], cbp[bi], cbv[bi]
                qk, kk_, pk_, vk = [("cb", bi, w) for w in "qkpv"]
                dma("sp", qc, QT1[n_], kQ, [qk])
                dma("sp", kc_, KT1[n_], kK, [kk_])
                dma("sp", kpc, KP1[dirn][n_], kP, [pk_])
                dma("sp", vc, V1[n_], kV, [vk])
                if dirn == 1:
                    ofv, ofk = wload(OF1[n_], [4096], [("OF1", n_, h) for h in range(8)], F32)
                    gv_, gk_ = wload(G1[n_], [4096], kG)
                    zc, zk = wbuf([4096], BF16)
                    S.add("dve", lambda e: e.memset(gsum, 0.0), [], [("gs", h) for h in range(8)])
                    S.add("dve", lambda e: e.memset(gsq, 0.0), [], [("gq", h) for h in range(8)])
                last = oi == len(order) - 1

                def st1(h):
                    pa, pak = psB()
                    for c in range(2):
                        mm(pa[:, 0:128], kc_[:, 2 * h + c, :], qc[:, 2 * h + c, :], c == 0, c == 1, [kk_, qk], [pak])
                    qi = nxt("qp", 4)
                    tt("dve", qpb[qi], qc[:, 2 * h:2 * h + 2, :], QRW[:, dirn, h, :].unsqueeze(1).to_broadcast([128, 2, 128]), ALU.mult,
                       [qk, ("QRW", dirn)], [("qp", qi)])
                    return pa, pak, qpb[qi], ("qp", qi)

                def st2(h, pa, pak, qpv, qpk):
                    c16 = dirn * 8 + h
                    am, amk = gpt()
                    tt("dve", am[:, 0:128], pa[:, 0:128], DTv[:, dirn, h, :], ALU.mult, [pak, ("DT", dirn)], [amk])
                    po_, pok = psA()
                    mm(po_, am[:, 0:128], vc[:, h * 512:(h + 1) * 512], True, False, [amk, vk], [pok])
                    for c in range(2):
                        mm(po_, qpv[:, c, :], Sb_v[:, 2 * h + c, :], False, c == 1, [qpk, ("Sb", h)], [pok])
                    psts = []
                    if not last:
                        for c in range(2):
                            pst, pstk = psC()
                            mm(pst, kpc[:, h * 256 + c * 128:h * 256 + (c + 1) * 128], vc[:, h * 512:(h + 1) * 512], True, True, [pk_, vk], [pstk])
                            psts.append((pst, pstk))
                    if dirn == 0:
                        t1, t1k = gtmp()
                        cp("act", t1, po_, [pok], [t1k])
                        dma("sp", OF1[n_][:, h * 512:(h + 1) * 512], t1, [t1k], [("OF1", n_, h)])
                    else:
                        oh = ofv[:, h * 512:(h + 1) * 512]
                        tt("dve", oh, oh, po_, ALU.add, [pok, ofk], [ofk])
                        j1, j1k = gtmp()
                        act(j1, oh, AF.Copy, [ofk, ("gs", h)], [j1k, ("gs", h)], accum=gsum[:, h:h + 1])
                        j2, j2k = gtmp()
                        act(j2, oh, AF.Square, [ofk, ("gq", h)], [j2k, ("gq", h)], accum=gsq[:, h:h + 1])
                    for c, (pst, pstk) in enumerate(psts):
                        stt("dve", S_v[:, 2 * h + c, :], S_v[:, 2 * h + c, :], cdec[:, c16:c16 + 1], pst, ALU.mult, ALU.add,
                            [pstk, "cdec", ("S", h)], [("S", h)])
                    if psts:
                        cp("pool", Sb_v[:, 2 * h:2 * h + 2, :], S_v[:, 2 * h:2 * h + 2, :], [("S", h)], [("Sb", h)])

                cur = st1(0)
                for h in range(8):
                    nx_ = st1(h + 1) if h < 7 else None
                    st2(h, *cur)
                    cur = nx_
                if dirn == 1:
                    gsk = [("gs", h) for h in range(8)] + [("gq", h) for h in range(8)]
                    ts("dve", gm, gsum, 1.0 / 512, None, ALU.mult, None, gsk, ["gm"])
                    tt("dve", gr, gm, gm, ALU.mult, ["gm"], ["gr"])
                    stt("dve", gr, gsq, 1.0 / 512, gr, ALU.mult, ALU.subtract, gsk + ["gr"], ["gr"])
                    act(gr, gr, AF.Sqrt, ["gr"], ["gr"], bias=EPS)
                    S.add("dve", lambda e: e.reciprocal(out=gr, in_=gr), ["gr"], ["gr"])
                    for h in range(8):
                        y_, yk = gtmp()
                        stt("dve", y_, ofv[:, h * 512:(h + 1) * 512], gm[:, h:h + 1], gv_[:, h * 512:(h + 1) * 512], ALU.subtract, ALU.mult,
                            [ofk, "gm", gk_], [yk])
                        act(zc[:, h * 512:(h + 1) * 512], y_, AF.Copy, [yk, "gr"], [zk], scale=gr[:, h:h + 1])
                    dma("sp", Z1[n_], zc, [zk], [("Z1", n_)])

        zT = ar.view(BIGo, [32, T], BF16)
        yT = ar.view(BIGo, [KD, T], F32)
        orow = ar.view(BIGo + 32768, [D], F32)
        rename(allS + allSb + [("cb", i, w) for i in range(2) for w in "qkpv"] + [("qp", i) for i in range(4)], ["xT", "hT", "zT"])
        for (t0, n, isc) in ltiles:
            ntb = n // 128
            ch0 = (t0 - NCX) // 128
            dma("sp", xT[:, :, 0:n], XT[:, :, t0:t0 + n].rearrange("k p t -> p k t"), [("XT", t0)], ["xT"])
            for tb in range(ntb):
                zt, ztk = wload(Z1[ch0 + tb], [4096], [("Z1", ch0 + tb)])
                for g8 in range(4):
                    pp, ppk = psC()
                    ppb = pp.bitcast(BF16)
                    for j in range(8):
                        fc = g8 * 8 + j
                        tr(ppb[:, j * 128:(j + 1) * 128], zt[:, fc * 128:(fc + 1) * 128], ident_b, [ztk, "ident_b"], [ppk])
                    for j in range(8):
                        fc = g8 * 8 + j
                        act(zT[:, fc, tb * 128:(tb + 1) * 128], ppb[:, j * 128:(j + 1) * 128], AF.Copy, [ppk, "gng"], ["zT"], scale=gng[:, fc:fc + 1])
            for dg in range(4):
                pd = [psA() for _ in range(4)]
                for half in range(2):
                    wv, wk = wload(retout_b[half * 2048:(half + 1) * 2048, dg * 512:(dg + 1) * 512].rearrange("(k p) c -> p k c", p=128),
                                   [KD, 512], wkeys["retout"])
                    for j in range(4):
                        for f16 in range(KD):
                            fc = half * 16 + f16
                            mm(pd[j][0][:, 0:n], wv[:, f16, j * 128:(j + 1) * 128], zT[:, fc, 0:n], fc == 0, fc == 31, [wk, "zT"], [pd[j][1]])
                for j in range(4):
                    dc = dg * 4 + j
                    stt("dve", xT[:, dc, 0:n], pd[j][0][:, 0:n], effcol(1, 2, dc, 0), xT[:, dc, 0:n], ALU.mult, ALU.add,
                        [pd[j][1], effk(1, 2), "xT"], ["xT"])
            rename(["zT"], big_all)
            ffn(1, n, [(0, n, 0)])
            rename(big_all, ["yT", "orow"])
            norm_fm(xT[:, :, 0:n], n, KD, "xT", lambda kc: fng[:, kc:kc + 1], None,
                    lambda kc: yT[:, kc, 0:n], lambda kc: "yT", ["fng"])
            for tb in range(ntb):
                for g4 in range(4):
                    pp, ppk = psC()
                    for j in range(4):
                        kc = g4 * 4 + j
                        tr(pp[:, j * 128:(j + 1) * 128], yT[:, kc, tb * 128:(tb + 1) * 128], ident_f, ["yT", "cst"], [ppk])
                    evac(orow[:, g4 * 512:(g4 + 1) * 512], pp, [ppk], ["orow"])
                r0 = t0 - NCX + tb * 128
                dma("sp", out[r0:r0 + 128, :], orow, ["orow"], [("out", r0)])
            rename(["yT", "orow"], ["zT"])
        S.fence("sp", [("out", r0) for r0 in range(0, NL, 128)])
        S.emit()
    return nc


def _pp(v):
    v = np.asarray(v, np.float32)
    return np.ascontiguousarray(v.reshape(-1, 128).T)


def _rope_tables(n_tok, rot_dim, mode):
    GRID_W = 64
    row = np.repeat(np.arange(n_tok // GRID_W, dtype=np.float32), GRID_W)
    col = (np.arange(n_tok) % GRID_W).astype(np.float32)
    n_freq = rot_dim // 4
    inv = (np.float32(10000.0) ** (-np.arange(n_freq, dtype=np.float32) / np.float32(n_freq))).astype(np.float32)
    ang = np.concatenate([row[:, None] * inv, col[:, None] * inv], axis=-1).astype(np.float32)
    c = np.cos(ang).astype(np.float32).T
    s = np.sin(ang).astype(np.float32).T
    if mode == "half":
        C = np.concatenate([c, c], 0)
        Sg = np.concatenate([-s, s], 0)
        rep = 128 // C.shape[0]
        return np.stack([np.tile(C, (rep, 1)), np.tile(Sg, (rep, 1))]).astype(np.float32)
    return np.stack([c, s]).astype(np.float32)


def _consts():
    p = np.arange(128)
    ident = np.eye(128, dtype=np.float32)
    maskLo = (p[:, None] >= p[None, :]).astype(np.float32)
    maskHi = (p[:, None] <= p[None, :]).astype(np.float32)
    diffF = np.maximum(p[None, :] - p[:, None], 0).astype(np.float32)
    diffB = np.maximum(p[:, None] - p[None, :], 0).astype(np.float32)
    pos = np.tile((p + 1).astype(np.float32)[None, :], (128, 1))
    posb = np.tile((128 - p).astype(np.float32)[None, :], (128, 1))
    return np.ascontiguousarray(np.concatenate([ident, maskLo, maskHi, diffF, diffB, pos, posb], 1))


def prep_core(inp, b, NL):
    f = lambda a: np.ascontiguousarray(np.asarray(a, np.float32))
    d = {}
    d["x"] = f(inp["x"][b][:NL])
    d["ctx"] = f(inp["ctx"][b])
    cvv = np.stack([_pp(inp["c"][b]), _pp(inp["c_ctx"])], -1).reshape(128, 32)
    d["cv"] = f(cvv)
    d["mod_w"] = f(inp["mod_w"])
    d["modb"] = f(np.concatenate([_pp(inp["mod_b"][l]) for l in range(2)], 1))
    d["nmg"] = f(np.concatenate([_pp(inp["norm_mix_g"][l]) for l in range(2)], 1))
    d["nfg"] = f(np.concatenate([_pp(inp["norm_ffn_g"][l]) for l in range(2)], 1))
    d["fng"] = _pp(inp["final_norm_g"])
    d["wg"] = f(inp["ffn_w_gate"])
    d["wu"] = f(inp["ffn_w_up"])
    d["wd"] = f(inp["ffn_w_down"])
    d["abin"] = f(inp["ab_w_in"][0])
    d["about"] = f(inp["ab_w_out"][0])
    d["sink"] = f(np.tile(np.asarray(inp["swa_sink"][0], np.float32)[None, :], (128, 1)))
    d["qng"] = _pp(inp["mla_q_norm_g"][0])
    d["wqb"] = f(inp["mla_w_q_b"][0])
    d["kvng"] = _pp(inp["mla_kv_norm_g"][0])
    d["wkvb"] = f(inp["mla_w_kv_b"][0])
    d["retin"] = f(inp["ret_w_in"][0])
    lgv = np.concatenate([np.asarray(inp["ret_decay_logit_fwd"][0], np.float32), np.asarray(inp["ret_decay_logit_bwd"][0], np.float32)])
    d["lg"] = f(np.tile(lgv[None, :], (128, 1)))
    d["gng"] = _pp(inp["ret_gn_g"][0])
    d["retout"] = f(inp["ret_w_out"][0])
    d["ropeA"] = _rope_tables(NL, 128, "half")
    d["ropeB"] = _rope_tables(NL, 64, "half")
    d["ropeR"] = _rope_tables(NL, 256, "plain")
    d["cst"] = _consts()
    p = np.arange(128, dtype=np.float32)
    d["pc"] = np.ascontiguousarray(np.stack([127 - p, p, p + 1, 128 - p], 1).astype(np.float32))
    return d


_CACHE = {}


def kernel(**inputs):
    NL = 4096
    B = 4
    if "nc" not in _CACHE:
        _CACHE["nc"] = build(NL)
    nc = _CACHE["nc"]
    in_maps = [prep_core(inputs, c % B, NL) for c in range(8)]
    res = run_bass_kernel_spmd(nc, in_maps, core_ids=list(range(8)))
    return np.stack([res.results[b]["out"] for b in range(B)], 0).astype(np.float32)
```

```python
from contextlib import ExitStack

import numpy as np
import concourse.bass as bass
import concourse.mybir as mybir
from concourse.bass_utils import run_bass_kernel_spmd

F32 = mybir.dt.float32
BF16 = mybir.dt.bfloat16
AF = mybir.ActivationFunctionType
ALU = mybir.AluOpType

D = 2048
KD = 16
FF = 5632
KF = 44
NCX = 256
T = 512
EPS = 1e-6
COMPUTE = ("pe", "act", "dve", "pool")
ENGS = ("pe", "act", "dve", "pool", "sp")


class Sched:
    def __init__(self, nc, es, n_dma_sems=(("sp", 32), ("pool", 12), ("act", 4))):
        self.nc = nc
        self.streams = {e: [] for e in ENGS}
        self.esem = {e: es.enter_context(nc.semaphore("s_" + e)) for e in COMPUTE}
        self.ecount = {e: 0 for e in COMPUTE}
        self.dsems = {q: [es.enter_context(nc.semaphore("d%s%d" % (q, i))) for i in range(n)] for q, n in n_dma_sems}
        self.dcount = {q: [0] * n for q, n in n_dma_sems}
        self.dnext = {q: 0 for q, n in n_dma_sems}
        self.last_w = {}
        self.readers = {}
        self.seen = {e: {} for e in ENGS}
        self.semobj = {}
        self.n_ops = 0

    def _need(self, eng, tok, waits):
        if tok is None:
            return
        sk, val, teng = tok
        if teng == "pe" and eng == "pe":
            return
        if self.seen[eng].get(sk, 0) >= val:
            return
        self.seen[eng][sk] = val
        waits.append((sk, val))

    def add(self, eng, fn, reads=(), writes=(), dma=False):
        waits = []
        for k in reads:
            self._need(eng, self.last_w.get(k), waits)
        for k in writes:
            self._need(eng, self.last_w.get(k), waits)
            for sk, (val, teng) in self.readers.get(k, {}).items():
                self._need(eng, (sk, val, teng), waits)
        if dma:
            i = self.dnext[eng]
            self.dnext[eng] = (i + 1) % len(self.dsems[eng])
            sk = ("d", eng, i)
            dc = self.dcount[eng]
            if dc[i] > 0:
                self._need(eng, (sk, dc[i], "dma"), waits)
            dc[i] += 16
            tok = (sk, dc[i], "dma")
            self.semobj[sk] = self.dsems[eng][i]
            inc = 16
        else:
            self.ecount[eng] += 1
            sk = ("e", eng)
            tok = (sk, self.ecount[eng], eng)
            self.semobj[sk] = self.esem[eng]
            inc = 1
        self.streams[eng].append((waits, fn, sk, inc))
        self.n_ops += 1
        for k in writes:
            self.last_w[k] = tok
            self.readers[k] = {}
        for k in reads:
            if k in writes:
                continue
            self.readers.setdefault(k, {})[tok[0]] = (tok[1], tok[2])
        return tok

    def fence(self, eng, keys):
        waits = []
        for k in keys:
            self._need(eng, self.last_w.get(k), waits)
        self.streams[eng].append((waits, None, None, 0))

    def emit(self):
        so = self.semobj

        def replay(name, e):
            for waits, fn, sk, inc in self.streams[name]:
                for wk, val in waits:
                    e.wait_ge(so[wk], val)
                if fn is not None:
                    fn(e).then_inc(so[sk], inc)

        with self.nc.Block() as block:
            @block.tensor
            def _(e):
                replay("pe", e)

            @block.scalar
            def _(e):
                replay("act", e)

            @block.vector
            def _(e):
                replay("dve", e)

            @block.gpsimd
            def _(e):
                replay("pool", e)

            @block.sync
            def _(e):
                replay("sp", e)


def _rs(shape):
    names = "abcdefg"[: len(shape)]
    if len(shape) == 1:
        return None, {}
    pat = "p (" + " ".join(names) + ") -> p " + " ".join(names)
    return pat, {n: s for n, s in zip(names[:-1], shape[:-1])}


class Arena:
    def __init__(self, nc, es, nbytes):
        self.t = es.enter_context(nc.sbuf_tensor("arena", [128, nbytes // 2], BF16))
        self.nbytes = nbytes
        self.off = 0

    def alloc(self, nbytes):
        o = self.off
        self.off += (nbytes + 63) // 64 * 64
        assert self.off <= self.nbytes, (self.off, self.nbytes)
        return o

    def view(self, off, shape, dtype, parts=128):
        n = int(np.prod(shape))
        sz = 4 if dtype == F32 else 2
        ap = self.t[0:parts, off // 2: off // 2 + n * sz // 2]
        if dtype == F32:
            ap = ap.bitcast(F32)
        pat, kw = _rs(list(shape))
        if pat is not None:
            ap = ap.rearrange(pat, **kw)
        return ap


def build(NL, stage=9):
    nc = bass.Bass("TRN2", target_bir_lowering=False)
    NTOK = NCX + NL
    NBLK = NTOK // 128
    NCH = NL // 128
    NTL = NL // T

    def din(n, s, dt=F32):
        return nc.dram_tensor(n, s, dt, kind="ExternalInput").ap()

    def dsc(n, s, dt=BF16):
        return nc.dram_tensor(n, s, dt).ap()

    x_in = din("x", [NL, D])
    ctx_in = din("ctx", [NCX, D])
    cv_in = din("cv", [128, 32])
    modw_in = din("mod_w", [2, D, 6 * D])
    modb_in = din("modb", [128, 192])
    nmg_in = din("nmg", [128, 32])
    nfg_in = din("nfg", [128, 32])
    fng_in = din("fng", [128, 16])
    wg_in = din("wg", [2, D, FF])
    wu_in = din("wu", [2, D, FF])
    wd_in = din("wd", [2, FF, D])
    abin_in = din("abin", [D, 2368])
    about_in = din("about", [D, D])
    sink_in = din("sink", [128, 8])
    qng_in = din("qng", [128, 4])
    wqb_in = din("wqb", [512, 1536])
    kvng_in = din("kvng", [128, 2])
    wkvb_in = din("wkvb", [256, 2048])
    retin_in = din("retin", [D, 12288])
    lg_in = din("lg", [128, 16])
    gng_in = din("gng", [128, 32])
    retout_in = din("retout", [4096, D])
    ropeA_in = din("ropeA", [2, 128, NL])
    ropeB_in = din("ropeB", [2, 128, NL])
    ropeR_in = din("ropeR", [2, 128, NL])
    cst_in = din("cst", [128, 7 * 128])
    pc_in = din("pc", [128, 4])
    out = nc.dram_tensor("out", [NL, D], F32, kind="ExternalOutput").ap()
    dbg = nc.dram_tensor("dbg", [KD, 128, NCX + NL], F32, kind="ExternalOutput").ap() if stage <= 2 else None

    abin_b = dsc("abin_b", [D, 2368])
    abin_s = dsc("abin_s", [D, 1280])
    abin_kr = dsc("abin_kr", [D, 256])
    about_b = dsc("about_b", [D, D])
    wqn_b = dsc("wqn_b", [512, 1024])
    wqr_b = dsc("wqr_b", [512, 1024])
    wkn_b = dsc("wkn_b", [256, 1024])
    wkv_b = dsc("wkv_b", [256, 1024])
    wg_b = dsc("wg_b", [2, D, FF])
    wu_b = dsc("wu_b", [2, D, FF])
    wd_b = dsc("wd_b", [2, FF, D])
    retin_b = dsc("retin_b", [D, 12288])
    retout_b = dsc("retout_b", [4096, D])
    XT = dsc("XT", [KD, 128, NTOK], F32)
    QA = dsc("QA", [8, 128, NTOK])
    KA = dsc("KA", [2, 128, NTOK])
    VA = dsc("VA", [NBLK, 128, 2, 129])
    QN = dsc("QN", [8, 128, NTOK])
    QR = dsc("QR", [8, 128, NTOK])
    KN = dsc("KN", [8, 128, NTOK])
    KRS = dsc("KRS", [128, NTOK])
    VM = dsc("VM", [8, 128, NBLK, 129])
    QT1 = dsc("QT1", [NCH, 128, KD, 128])
    KT1 = dsc("KT1", [NCH, 128, KD, 128])
    KP1 = dsc("KP1", [2, NCH, 128, D])
    V1 = dsc("V1", [NCH, 128, 4096])
    G1 = dsc("G1", [NCH, 128, 4096])
    OF1 = dsc("OF1", [NCH, 128, 4096], F32)
    Z1 = dsc("Z1", [NCH, 128, 4096])
    S0 = dsc("S0", [2, 128, 16, 512], F32)

    es = ExitStack()
    with es:
        S = Sched(nc, es)
        ar = Arena(nc, es, 206 * 1024)
        ps = [es.enter_context(nc.psum_tensor("ps%d" % i, [128, 512], F32))[:] for i in range(8)]
        psk = [("ps", i) for i in range(8)]

        WPB = 16384
        NWP = 4
        WP0 = ar.alloc(WPB * NWP)
        XTo = ar.alloc(32768)
        BAo = ar.alloc(16384)
        BIGo = ar.alloc(45056)
        KRo = ar.alloc(max(NTOK * 2, 8192 + 64))
        NTMP = 6
        TMPo = ar.alloc(2048 * NTMP)
        PTo = ar.alloc(1024 * 3)
        RSo = ar.alloc(2048)
        VACo = ar.alloc(2048)
        KR2o = ar.alloc(8192 + 64)
        CSTo = ar.alloc(7 * 128 * 4)
        SMo = ar.alloc(8192)
        sm_off = [SMo]

        def small(ncols, dtype=F32):
            sz = 4 if dtype == F32 else 2
            o = sm_off[0]
            sm_off[0] += (ncols * sz + 31) // 32 * 32
            assert sm_off[0] <= SMo + 8192
            return ar.view(o, [ncols], dtype)

        cst = ar.view(CSTo, [7, 128], F32)
        ident_f = cst[:, 0, :]
        diffF = cst[:, 3, :]
        diffB = cst[:, 4, :]
        posr = cst[:, 5, :]
        ident_b = small(128, BF16)
        maskLo = small(128, BF16)
        maskHi = small(128, BF16)
        ones_m = small(128)
        cv = small(32)
        scv = small(32)
        modb = small(192)
        modT = small(384)
        EFF = small(384)
        nmg = small(32)
        nfg = small(32)
        fng = small(16)
        qng = small(4)
        kvng = small(2)
        esink = small(8)
        lg = small(16)
        lgs = small(16)
        kdec = small(16)
        cdec = small(16)
        pcol = small(2)
        gng = small(32)
        dummy = small(8)
        xT = ar.view(XTo, [KD, T], F32)
        hT = ar.view(BAo, [KD, T], BF16)
        tmp = [ar.view(TMPo + 2048 * i, [T], F32) for i in range(NTMP)]
        tmpk = [("tmp", i) for i in range(NTMP)]
        rsb = ar.view(RSo, [T], F32)
        pt = [ar.view(PTo + 1024 * i, [T], BF16) for i in range(3)]
        ptk = [("pt", i) for i in range(3)]
        ctr = {"tmp": 0, "pt": 0, "wp": 0, "psA": 0, "psB": 0, "psC": 0, "ev": 0}

        def nxt(name, n):
            i = ctr[name]
            ctr[name] = (i + 1) % n
            return i

        def gtmp():
            i = nxt("tmp", NTMP)
            return tmp[i], tmpk[i]

        def gpt():
            i = nxt("pt", 3)
            return pt[i], ptk[i]

        def psA():
            i = nxt("psA", 4)
            return ps[i], psk[i]

        def psB():
            i = 4 + nxt("psB", 2)
            return ps[i], psk[i]

        def psC():
            i = 6 + nxt("psC", 2)
            return ps[i], psk[i]

        def dma(q, o, i, r, w):
            S.add(q, lambda e: e.dma_start(out=o, in_=i), r, w, dma=True)

        def mm(o, lhsT, rhs, start, stop, r, w):
            S.add("pe", lambda e: e.matmul(o, lhsT=lhsT, rhs=rhs, start=start, stop=stop), r, w)

        def tr(o, i, idn, r, w):
            S.add("pe", lambda e: e.transpose(o, i, idn), r, w)

        def act(o, i, func, r, w, bias=None, scale=None, accum=None):
            kw = {}
            if bias is not None:
                kw["bias"] = bias
            if scale is not None:
                kw["scale"] = scale
            if accum is not None:
                kw["accum_out"] = accum
            S.add("act", lambda e: e.activation(out=o, in_=i, func=func, **kw), r, w)

        def tt(eng, o, a, b, op, r, w):
            S.add(eng, lambda e: e.tensor_tensor(out=o, in0=a, in1=b, op=op), r, w)

        def ts(eng, o, a, s1, s2, op0, op1, r, w):
            if s2 is None:
                S.add(eng, lambda e: e.tensor_scalar(out=o, in0=a, scalar1=s1, scalar2=None, op0=op0), r, w)
            else:
                S.add(eng, lambda e: e.tensor_scalar(out=o, in0=a, scalar1=s1, scalar2=s2, op0=op0, op1=op1), r, w)

        def stt(eng, o, a, sc, b, op0, op1, r, w):
            S.add(eng, lambda e: e.scalar_tensor_tensor(out=o, in0=a, scalar=sc, in1=b, op0=op0, op1=op1), r, w)

        def cp(eng, o, i, r, w):
            if eng == "act":
                S.add("act", lambda e: e.copy(out=o, in_=i), r, w)
            else:
                S.add(eng, lambda e: e.tensor_copy(out=o, in_=i), r, w)

        def evac(o, i, r, w):
            cp("act" if nxt("ev", 2) == 0 else "dve", o, i, r, w)

        def rename(old, new):
            S.add("dve", lambda e: e.memset(dummy[:, 0:1], 0.0), list(old), list(new) + ["dummy"])

        def wload(src, shape, rk, dtype=BF16, q="sp"):
            i = nxt("wp", NWP)
            v = ar.view(WP0 + i * WPB, shape, dtype)
            dma(q, v, src, rk, [("wp", i)])
            return v, ("wp", i)

        wkeys = {}

        def conv(name, dst, src, nrows, rb=256):
            ks = []
            for r0 in range(0, nrows, rb):
                k = ("w", name, r0)
                dma("pool", dst[r0:r0 + rb], src[r0:r0 + rb], [], [k])
                ks.append(k)
            wkeys.setdefault(name, []).extend(ks)

        dma("sp", cst, cst_in.rearrange("p (a b) -> p a b", a=7), [], ["cst"])
        for (v, src, k) in [(cv, cv_in, "cv"), (modb, modb_in, "modb"), (nmg, nmg_in, "nmg"), (nfg, nfg_in, "nfg"),
                            (fng, fng_in, "fng"), (qng, qng_in, "qng"), (kvng, kvng_in, "kvng"), (esink, sink_in, "esink"),
                            (lg, lg_in, "lg"), (gng, gng_in, "gng")]:
            dma("sp", v, src, [], [k])
        cp("dve", ident_b, cst[:, 0, :], ["cst"], ["ident_b"])
        cp("dve", maskLo, cst[:, 1, :], ["cst"], ["maskLo"])
        cp("dve", maskHi, cst[:, 2, :], ["cst"], ["maskHi"])
        S.add("dve", lambda e: e.memset(ones_m, 1.0 / D), [], ["ones_m"])
        act(esink, esink, AF.Exp, ["esink"], ["esink"])
        act(scv, cv, AF.Silu, ["cv"], ["scv"])

        conv("abin", abin_b, abin_in, D)
        av = abin_in[:, 0:1280].rearrange("r (h t c) -> r h t c", h=10, t=2)
        sv = abin_s.rearrange("r (h t c) -> r h t c", h=10, t=2)
        for r0 in range(0, D, 128):
            for t_ in range(2):
                k = ("w", "abin_s", r0, t_)
                dma("pool", sv[r0:r0 + 128, :, t_, :], av[r0:r0 + 128, :, 1 - t_, :], [], [k])
                wkeys.setdefault("abin_s", []).append(k)
        for r0 in range(0, D, 512):
            segs = [(0, 64, 2304), (64, 128, 2304), (128, 160, 2336), (160, 192, 2304), (192, 224, 2336), (224, 256, 2304)]
            for (d0, d1, s0) in segs:
                k = ("w", "abin_kr", r0, d0)
                dma("pool", abin_kr[r0:r0 + 512, d0:d1], abin_in[r0:r0 + 512, s0:s0 + (d1 - d0)], [], [k])
                wkeys.setdefault("abin_kr", []).append(k)
        wq3 = wqb_in.rearrange("r (h c) -> r h c", h=8)
        dma("pool", wqn_b.rearrange("r (h c) -> r h c", h=8), wq3[:, :, 0:128], [], [("w", "wqn")])
        wkeys["wqn"] = [("w", "wqn")]
        wr3 = wqr_b.rearrange("r (s h c) -> r s h c", s=2, h=8)
        wkeys["wqr"] = []
        for j, (sl_d, sl_s) in enumerate([((0, slice(0, 64)), slice(128, 192)),
                                          ((1, slice(0, 32)), slice(160, 192)),
                                          ((1, slice(32, 64)), slice(128, 160))]):
            k = ("w", "wqr", j)
            dma("pool", wr3[:, sl_d[0], :, sl_d[1]], wq3[:, :, sl_s], [], [k])
            wkeys["wqr"].append(k)
        wk3 = wkvb_in.rearrange("r (h c) -> r h c", h=8)
        dma("pool", wkn_b.rearrange("r (h c) -> r h c", h=8), wk3[:, :, 0:128], [], [("w", "wkn")])
        dma("pool", wkv_b.rearrange("r (h c) -> r h c", h=8), wk3[:, :, 128:256], [], [("w", "wkv")])
        wkeys["wkn"] = [("w", "wkn")]
        wkeys["wkv"] = [("w", "wkv")]
        conv("about", about_b, about_in, D)
        conv("wg0", wg_b[0], wg_in[0], D)
        conv("wu0", wu_b[0], wu_in[0], D)
        conv("wd0", wd_b[0], wd_in[0], FF)

        scv3 = scv.rearrange("p (k s) -> p k s", s=2)
        for l in range(2):
            pm, pmk = ps[7], psk[7]
            for cb in range(48):
                wv, wk = wload(modw_in[l][:, cb * 256:(cb + 1) * 256].rearrange("(k p) c -> p k c", p=128), [KD, 256], [], F32)
                for j in range(2):
                    oc = cb * 2 + j
                    for kc in range(KD):
                        mm(pm[:, oc * 2:oc * 2 + 2], wv[:, kc, j * 128:(j + 1) * 128], scv3[:, kc, :], kc == 0, kc == KD - 1,
                           [wk, "scv"], [pmk])
            tt("dve", modT[:, l * 192:(l + 1) * 192].rearrange("p (o s) -> p o s", s=2), pm[:, 0:192].rearrange("p (o s) -> p o s", s=2),
               modb[:, l * 96:(l + 1) * 96].unsqueeze(2).to_broadcast([128, 96, 2]), ALU.add, [pmk, "modb"], [("modT", l)])
        modT4 = modT.rearrange("p (l j k s) -> p l j k s", l=2, j=6, k=16)
        EFF4 = EFF.rearrange("p (l j k s) -> p l j k s", l=2, j=6, k=16)
        for l in range(2):
            for (jo, jsc, jsh, jg, gv) in [(0, 1, 0, 2, nmg), (3, 4, 3, 5, nfg)]:
                gcol = gv[:, l * 16:(l + 1) * 16].unsqueeze(2).to_broadcast([128, 16, 2])
                stt("dve", EFF4[:, l, jo], modT4[:, l, jsc], 1.0, gcol, ALU.add, ALU.mult, [("modT", l), "nmg", "nfg"], [("EFF", l, jo)])
                cp("dve", EFF4[:, l, jo + 1], modT4[:, l, jsh], [("modT", l)], [("EFF", l, jo + 1)])
                cp("dve", EFF4[:, l, jo + 2], modT4[:, l, jg], [("modT", l)], [("EFF", l, jo + 2)])

        def effcol(l, j, kc, s):
            return EFF4[:, l, j, kc, s:s + 1]

        def effk(l, j):
            return ("EFF", l, j)

        def norm_fm(xv, n, nk, xk, scale_fn, bias_fn, out_fn, outk_fn, sk, sqs=1.0):
            pn, pnk = psC()
            for kc in range(nk):
                sq, sqk = gtmp()
                if kc % 2 == 0:
                    act(sq[:, 0:n], xv[:, kc, 0:n], AF.Square, [xk], [sqk])
                else:
                    tt("dve", sq[:, 0:n], xv[:, kc, 0:n], xv[:, kc, 0:n], ALU.mult, [xk], [sqk])
                mm(pn[:, 0:n], ones_m, sq[:, 0:n], kc == 0, kc == nk - 1, [sqk, "ones_m"], [pnk])
            rs, rsk = rsb, "rsb"
            act(rs[:, 0:n], pn[:, 0:n], AF.Sqrt, [pnk], [rsk], bias=EPS, scale=sqs)
            S.add("dve", lambda e: e.reciprocal(out=rs[:, 0:n], in_=rs[:, 0:n]), [rsk], [rsk])
            for kc in range(nk):
                t_, tk = gtmp()
                tt("dve", t_[:, 0:n], xv[:, kc, 0:n], rs[:, 0:n], ALU.mult, [xk, rsk], [tk])
                b = bias_fn(kc) if bias_fn is not None else None
                act(out_fn(kc), t_[:, 0:n], AF.Identity, [tk] + sk, [outk_fn(kc)], bias=b, scale=scale_fn(kc))

        def ffn(l, n, mods):
            act_t = ar.view(BIGo, [KF, T], BF16)
            for (c0, c1, s) in mods:
                norm_fm(xT[:, :, c0:c1], c1 - c0, KD, "xT", lambda kc: effcol(l, 3, kc, s), lambda kc: effcol(l, 4, kc, s),
                        lambda kc: hT[:, kc, c0:c1], lambda kc: "hT", [effk(l, 3), effk(l, 4)])
            for hb in range(11):
                gv, gk = wload(wg_b[l][:, hb * 512:(hb + 1) * 512].rearrange("(k p) c -> p k c", p=128), [KD, 512], wkeys["wg%d" % l])
                uv, uk = wload(wu_b[l][:, hb * 512:(hb + 1) * 512].rearrange("(k p) c -> p k c", p=128), [KD, 512], wkeys["wu%d" % l])
                for j in range(4):
                    hc = hb * 4 + j
                    pg, pgk = psA()
                    for kc in range(KD):
                        mm(pg[:, 0:n], gv[:, kc, j * 128:(j + 1) * 128], hT[:, kc, 0:n], kc == 0, kc == KD - 1, [gk, "hT"], [pgk])
                    pu, puk = psA()
                    for kc in range(KD):
                        mm(pu[:, 0:n], uv[:, kc, j * 128:(j + 1) * 128], hT[:, kc, 0:n], kc == 0, kc == KD - 1, [uk, "hT"], [puk])
                    sg, sgk = gtmp()
                    act(sg[:, 0:n], pg[:, 0:n], AF.Silu, [pgk], [sgk])
                    tt("dve", act_t[:, hc, 0:n], sg[:, 0:n], pu[:, 0:n], ALU.mult, [sgk, puk], [("BIG", hc)])
            for dg in range(4):
                pd = [psA() for _ in range(4)]
                for q4 in range(4):
                    dv_, dk = wload(wd_b[l][q4 * 1408:(q4 + 1) * 1408, dg * 512:(dg + 1) * 512].rearrange("(k p) c -> p k c", p=128),
                                    [11, 512], wkeys["wd%d" % l])
                    for j in range(4):
                        for hh in range(11):
                            hc = q4 * 11 + hh
                            mm(pd[j][0][:, 0:n], dv_[:, hh, j * 128:(j + 1) * 128], act_t[:, hc, 0:n], hc == 0, hc == KF - 1,
                               [dk, ("BIG", hc)], [pd[j][1]])
                for j in range(4):
                    dc = dg * 4 + j
                    for (c0, c1, s) in mods:
                        stt("dve", xT[:, dc, c0:c1], pd[j][0][:, c0:c1], effcol(l, 5, dc, s), xT[:, dc, c0:c1], ALU.mult, ALU.add,
                            [pd[j][1], effk(l, 5), "xT"], ["xT"])

        xtok = [ar.view(BIGo + 8192 * i, [D], F32) for i in range(2)]
        xtokk = [("xtok", i) for i in range(2)]

        def load_x_transposed(src_rows, n, bigkeys_old):
            for sub in range(n // 128):
                xb, xbk = xtok[sub % 2], xtokk[sub % 2]
                dma("sp", xb, src_rows[sub * 128:(sub + 1) * 128, :], [], [xbk])
                for g4 in range(4):
                    pp, ppk = psC()
                    for j in range(4):
                        kc = g4 * 4 + j
                        tr(pp[:, j * 128:(j + 1) * 128], xb[:, kc * 128:(kc + 1) * 128], ident_f, [xbk, "cst"], [ppk])
                    evac(xT[:, g4 * 4:(g4 + 1) * 4, sub * 128:(sub + 1) * 128], pp.rearrange("p (j t) -> p j t", j=4), [ppk], ["xT"])

        tiles = [(0, NCX, 1)] + [(NCX + i * T, T, 0) for i in range(NTL)]
        big_all = [("BIG", i) for i in range(KF)]

        ropeTa = [ar.view(BIGo + 16640 + 2048 * i, [T], F32) for i in range(4)]
        qa_t = ar.view(BIGo + 24832, [8, T], BF16)
        kx_t = ar.view(BIGo + 33024, [4, T], BF16)
        va_t = ar.view(BIGo + 37120, [4, 2, 129], BF16)
        lat_t = ar.view(KRo, [4, T], F32)
        rename(big_all, ["xtok0", "ropeT", "qa_t", "kx_t", "va_t", "lat_t"] + xtokk)
        S.add("dve", lambda e: e.memset(va_t[:, :, :, 128:129], 1.0), ["va_t"], ["va_t"])

        def rope_evac(px, pxk, psw, pswk, ct, st, o, n, rk, wk_, sl=slice(0, 128)):
            t1, t1k = gtmp()
            tt("dve", t1[sl, 0:n], px[:, 0:n], ct[:, 0:n], ALU.mult, [pxk] + rk, [t1k])
            t2, t2k = gtmp()
            tt("dve", t2[sl, 0:n], psw[:, 0:n], st[:, 0:n], ALU.mult, [pswk] + rk, [t2k])
            tt("dve", o, t1[sl, 0:n], t2[sl, 0:n], ALU.add, [t1k, t2k], wk_)

        for (t0, n, isc) in tiles:
            l = 0
            if isc:
                load_x_transposed(ctx_in, n, None)
            else:
                load_x_transposed(x_in[t0 - NCX:t0 - NCX + n], n, None)
                p0 = t0 - NCX
                dma("sp", ropeTa[0][:, 0:n], ropeA_in[0][:, p0:p0 + n], [], [("ropeT", 0)])
                dma("sp", ropeTa[1][:, 0:n], ropeA_in[1][:, p0:p0 + n], [], [("ropeT", 1)])
                dma("sp", ropeTa[2][:, 0:n], ropeB_in[0][:, p0:p0 + n], [], [("ropeT", 2)])
                dma("sp", ropeTa[3][:, 0:n], ropeB_in[1][:, p0:p0 + n], [], [("ropeT", 3)])
            dma("pool", XT[:, :, t0:t0 + n].rearrange("k p t -> p k t"), xT[:, :, 0:n], ["xT"], [("XT", t0)])
            norm_fm(xT[:, :, 0:n], n, KD, "xT", lambda kc: effcol(0, 0, kc, isc), lambda kc: effcol(0, 1, kc, isc),
                    lambda kc: hT[:, kc, 0:n], lambda kc: "hT", [effk(0, 0), effk(0, 1)])
            for blk in range(3):
                ncol = 512 if blk < 2 else 256
                wv, wk = wload(abin_b[:, blk * 512:blk * 512 + 512].rearrange("(k p) c -> p k c", p=128), [KD, 512], wkeys["abin"])
                if not isc:
                    sv_, swk = wload(abin_s[:, blk * 512:blk * 512 + ncol].rearrange("(k p) c -> p k c", p=128), [KD, ncol], wkeys["abin_s"])
                for j in range(ncol // 128):
                    oc = blk * 4 + j
                    dst = qa_t[:, oc, 0:n] if oc < 8 else kx_t[:, oc - 8, 0:n]
                    dk_ = "qa_t" if oc < 8 else "kx_t"
                    px, pxk = psA()
                    for kc in range(KD):
                        mm(px[:, 0:n], wv[:, kc, j * 128:(j + 1) * 128], hT[:, kc, 0:n], kc == 0, kc == KD - 1, [wk, "hT"], [pxk])
                    if isc:
                        evac(dst, px[:, 0:n], [pxk], [dk_])
                    else:
                        pw, pwk = psA()
                        for kc in range(KD):
                            mm(pw[:, 0:n], sv_[:, kc, j * 128:(j + 1) * 128], hT[:, kc, 0:n], kc == 0, kc == KD - 1, [swk, "hT"], [pwk])
                        rope_evac(px, pxk, pw, pwk, ropeTa[0], ropeTa[1], dst, n, [("ropeT", 0), ("ropeT", 1)], [dk_])
                if blk == 2:
                    for tb in range(n // 128):
                        pv, pvk = psA()
                        for kc in range(KD):
                            mm(pv[:, 0:256], hT[:, kc, tb * 128:(tb + 1) * 128], wv[:, kc, 256:512], kc == 0, kc == KD - 1, [wk, "hT"], [pvk])
                        evac(va_t[:, tb, :, 0:128], pv[:, 0:256].rearrange("p (g c) -> p g c", g=2), [pvk], ["va_t"])
            dma("pool", QA[:, :, t0:t0 + n].rearrange("h p t -> p h t"), qa_t[:, :, 0:n], ["qa_t"], [("QA", t0)])
            dma("pool", KA[:, :, t0:t0 + n].rearrange("h p t -> p h t"), kx_t[:, 0:2, 0:n], ["kx_t"], [("KA", t0)])
            b0 = t0 // 128
            dma("pool", VA[b0:b0 + n // 128].rearrange("b p g c -> p b g c"), va_t[:, 0:n // 128], ["va_t"], [("VA", t0)])
            wv, wk = wload(abin_b[:, 1536:2048].rearrange("(k p) c -> p k c", p=128), [KD, 512], wkeys["abin"])
            for j in range(4):
                px, pxk = psA()
                for kc in range(KD):
                    mm(px[:, 0:n], wv[:, kc, j * 128:(j + 1) * 128], hT[:, kc, 0:n], kc == 0, kc == KD - 1, [wk, "hT"], [pxk])
                evac(lat_t[:, j, 0:n], px[:, 0:n], [pxk], ["lat_t"])
            qln = ar.view(BIGo + 40960, [4, T], BF16)
            norm_fm(lat_t[:, :, 0:n], n, 4, "lat_t", lambda kc: qng[:, kc:kc + 1], None,
                    lambda kc: qln[:, kc, 0:n], lambda kc: "qln", ["qng"], sqs=4.0)
            qn_t = qa_t
            wv, wk = wload(wqn_b.rearrange("(k p) c -> p k c", p=128), [4, 1024], wkeys["wqn"])
            for h in range(8):
                px, pxk = psA()
                for kc in range(4):
                    mm(px[:, 0:n], wv[:, kc, h * 128:(h + 1) * 128], qln[:, kc, 0:n], kc == 0, kc == 3, [wk, "qln"], [pxk])
                evac(qn_t[:, h, 0:n], px[:, 0:n], [pxk], ["qa_t"])
            dma("pool", QN[:, :, t0:t0 + n].rearrange("h p t -> p h t"), qn_t[:, :, 0:n], ["qa_t"], [("QN", t0)])
            wv, wk = wload(wqr_b.rearrange("(k p) c -> p k c", p=128), [4, 1024], wkeys["wqr"])
            qr_t = qa_t
            S.add("dve", lambda e: e.memset(qr_t[:, :, 0:n], 0.0), [], ["qa_t"])
            for m in range(4):
                px, pxk = psA()
                for kc in range(4):
                    mm(px[:, 0:n], wv[:, kc, m * 128:(m + 1) * 128], qln[:, kc, 0:n], kc == 0, kc == 3, [wk, "qln"], [pxk])
                if isc:
                    for hh in range(2):
                        evac(qr_t[64 * hh:64 * hh + 64, 2 * m + hh, 0:n], px[64 * hh:64 * hh + 64, 0:n], [pxk], ["qa_t"])
                else:
                    pw, pwk = psA()
                    for kc in range(4):
                        mm(pw[:, 0:n], wv[:, kc, 512 + m * 128:512 + (m + 1) * 128], qln[:, kc, 0:n], kc == 0, kc == 3, [wk, "qln"], [pwk])
                    for hh in range(2):
                        sl = slice(64 * hh, 64 * hh + 64)
                        rope_evac(px[sl], pxk, pw[sl], pwk, ropeTa[2][sl], ropeTa[3][sl], qr_t[sl, 2 * m + hh, 0:n], n,
                                  [("ropeT", 2), ("ropeT", 3)], ["qa_t"], sl)
            dma("pool", QR[:, :, t0:t0 + n].rearrange("h p t -> p h t"), qr_t[:, :, 0:n], ["qa_t"], [("QR", t0)])
            wv, wk = wload(abin_b[:, 2048:2304].rearrange("(k p) c -> p k c", p=128), [KD, 256], wkeys["abin"])
            for j in range(2):
                px, pxk = psA()
                for kc in range(KD):
                    mm(px[:, 0:n], wv[:, kc, j * 128:(j + 1) * 128], hT[:, kc, 0:n], kc == 0, kc == KD - 1, [wk, "hT"], [pxk])
                evac(lat_t[:, j, 0:n], px[:, 0:n], [pxk], ["lat_t"])
            kvn = qln
            norm_fm(lat_t[:, 0:2, 0:n], n, 2, "lat_t", lambda kc: kvng[:, kc:kc + 1], None,
                    lambda kc: kvn[:, kc, 0:n], lambda kc: "qln", ["kvng"], sqs=8.0)
            wv, wk = wload(abin_kr.rearrange("(k p) c -> p k c", p=128), [KD, 256], wkeys["abin_kr"])
            px, pxk = psA()
            for kc in range(KD):
                mm(px[:, 0:n], wv[:, kc, 0:128], hT[:, kc, 0:n], kc == 0, kc == KD - 1, [wk, "hT"], [pxk])
            kr_t = qa_t[:, 0, :]
            if isc:
                evac(kr_t[:, 0:n], px[:, 0:n], [pxk], ["qa_t"])
            else:
                pw, pwk = psA()
                for kc in range(KD):
                    mm(pw[:, 0:n], wv[:, kc, 128:256], hT[:, kc, 0:n], kc == 0, kc == KD - 1, [wk, "hT"], [pwk])
                rope_evac(px, pxk, pw, pwk, ropeTa[2], ropeTa[3], kr_t[:, 0:n], n, [("ropeT", 2), ("ropeT", 3)], ["qa_t"])
            dma("pool", KRS[:, t0:t0 + n], kr_t[:, 0:n], ["qa_t"], [("KRS", t0)])
            wv, wk = wload(wkn_b.rearrange("(k p) c -> p k c", p=128), [2, 1024], wkeys["wkn"])
            kn_full = ar.view(BIGo, [8, T], BF16)
            for h in range(8):
                px, pxk = psA()
                for kc in range(2):
                    mm(px[:, 0:n], wv[:, kc, h * 128:(h + 1) * 128], kvn[:, kc, 0:n], kc == 0, kc == 1, [wk, "qln"], [pxk])
                evac(kn_full[:, h, 0:n], px[:, 0:n], [pxk], [xtokk[0]])
            dma("pool", KN[:, :, t0:t0 + n].rearrange("h p t -> p h t"), kn_full[:, :, 0:n], [xtokk[0]], [("KN", t0)])
            vm_t = ar.view(BIGo + 8192, [4, 8, 129], BF16)
            S.add("dve", lambda e: e.memset(vm_t[:, :, :, 128:129], 1.0), [xtokk[1]], [xtokk[1]])
            wv, wk = wload(wkv_b.rearrange("(k p) c -> p k c", p=128), [2, 1024], wkeys["wkv"])
            for tb in range(n // 128):
                for nb in range(2):
                    pv, pvk = psA()
                    for kc in range(2):
                        mm(pv, kvn[:, kc, tb * 128:(tb + 1) * 128], wv[:, kc, nb * 512:(nb + 1) * 512], kc == 0, kc == 1, [wk, "qln"], [pvk])
                    evac(vm_t[:, tb, nb * 4:(nb + 1) * 4, 0:128], pv.rearrange("p (h c) -> p h c", h=4), [pvk], [xtokk[1]])
            for tb in range(n // 128):
                dma("pool", VM[:, :, b0 + tb, :].rearrange("h p c -> p h c"), vm_t[:, tb], [xtokk[1]], [("VM", t0, tb)])

        conv("retin", retin_b, retin_in, D, rb=128)
        conv("retout", retout_b, retout_in, 4096)
        conv("wg1", wg_b[1], wg_in[1], D)
        conv("wu1", wu_b[1], wu_in[1], D)
        conv("wd1", wd_b[1], wd_in[1], FF)

        allk = lambda nm: [(nm, t0) for (t0, n, isc) in tiles]
        krs = ar.view(KRo, [NTOK], BF16)
        qa_b = ar.view(BIGo, [8, T], BF16)
        qn_b = ar.view(BIGo + 8192, [8, T], BF16)
        qr_b = ar.view(BIGo + 16384, [8, T], BF16)
        o_t = ar.view(BIGo + 24576, [4, 16, 128], BF16)
        kA_w = ar.view(BIGo + 40960, [2, 768], BF16)
        kA_c = ar.view(BIGo + 44032, [2, 256], BF16)
        vA_c = ar.view(VACo, [2, 2, 129], BF16)
        vA_w = ar.view(TMPo, [6, 2, 129], BF16)
        oT = hT
        dsm = small(8)
        rename(["xT", "hT", "qa_t", "kx_t", "va_t", "lat_t", "qln", "ropeT"] + xtokk + [("ropeT", i) for i in range(4)],
               ["krs", "qa_b", "qn_b", "qr_b", "o_t", "kA_w", "kA_c", "vA_c"] + big_all)
        dma("sp", krs, KRS, allk("KRS"), ["krs"])
        SC_A = float(128 ** -0.5)
        SC_B = float(192 ** -0.5)

        def finish_head(pv, pvk, hidx, qb, sink_col):
            den, dk_ = dsm, "dsm"
            if sink_col is not None:
                ts("dve", den[:, 0:1], pv[:, 128:129], sink_col, None, ALU.add, None, [pvk, "esink"], [dk_])
                S.add("dve", lambda e: e.reciprocal(out=den[:, 0:1], in_=den[:, 0:1]), [dk_], [dk_])
            else:
                S.add("dve", lambda e: e.reciprocal(out=den[:, 0:1], in_=pv[:, 128:129]), [pvk], [dk_])
            act(o_t[:, qb, hidx, :], pv[:, 0:128], AF.Copy, [pvk, dk_], ["o_t"], scale=den[:, 0:1])

        for (t0, n, isc) in tiles:
            nqb = n // 128
            dma("sp", xT[:, :, 0:n], XT[:, :, t0:t0 + n].rearrange("k p t -> p k t"), [("XT", t0)], ["xT"])
            dma("sp", qa_b[:, :, 0:n], QA[:, :, t0:t0 + n].rearrange("h p t -> p h t"), allk("QA"), ["qa_b"])
            dma("sp", qn_b[:, :, 0:n], QN[:, :, t0:t0 + n].rearrange("h p t -> p h t"), allk("QN"), ["qn_b"])
            dma("sp", qr_b[:, :, 0:n], QR[:, :, t0:t0 + n].rearrange("h p t -> p h t"), allk("QR"), ["qr_b"])
            dma("sp", kA_c, KA[:, :, 0:NCX].rearrange("h p t -> p h t"), allk("KA"), ["kA_c"])
            dma("sp", vA_c, VA[0:2].rearrange("b p g c -> p b g c"), allk("VA"), ["vA_c"])
            if not isc:
                lb0 = (t0 - NCX) // 128
                wlo = max(lb0 - 1, 0)
                whi = min(lb0 + nqb + 1, NCH)
                nw = whi - wlo
                dma("sp", kA_w[:, :, 0:nw * 128], KA[:, :, NCX + wlo * 128:NCX + whi * 128].rearrange("h p t -> p h t"), allk("KA"), ["kA_w"])
                dma("sp", vA_w[:, 0:nw], VA[2 + wlo:2 + whi].rearrange("b p g c -> p b g c"), allk("VA"), [tmpk[0], tmpk[1]])
            for g in range(2):
                for qb in range(nqb):
                    kbs = [("c", 0), ("c", 1)]
                    if not isc:
                        lb = lb0 + qb
                        for dlt in (-1, 0, 1):
                            if 0 <= lb + dlt < NCH:
                                kbs.append(("l", lb + dlt - wlo, dlt))
                    pvs = [psA() for _ in range(4)]
                    for ki, kb in enumerate(kbs):
                        sps, spk = psB()
                        if kb[0] == "c":
                            kl = kA_c[:, g, kb[1] * 128:(kb[1] + 1) * 128]
                            vv = vA_c[:, kb[1], g, :]
                            kr_, vr_ = ["kA_c"], ["vA_c"]
                        else:
                            kl = kA_w[:, g, kb[1] * 128:(kb[1] + 1) * 128]
                            vv = vA_w[:, kb[1], g, :]
                            kr_, vr_ = ["kA_w"], [tmpk[0], tmpk[1]]
                        mm(sps.rearrange("p (h q) -> p h q", h=4), kl, qa_b[:, 4 * g:4 * g + 4, qb * 128:(qb + 1) * 128], True, True,
                           kr_ + ["qa_b"], [spk])
                        p_, pk = gpt()
                        act(p_, sps, AF.Exp, [spk], [pk], scale=SC_A)
                        if kb[0] == "l" and kb[2] != 0:
                            mk = maskLo if kb[2] == -1 else maskHi
                            mkk = "maskLo" if kb[2] == -1 else "maskHi"
                            p3 = p_.rearrange("p (h q) -> p h q", h=4)
                            tt("dve", p3, p3, mk.unsqueeze(1).to_broadcast([128, 4, 128]), ALU.mult, [pk, mkk], [pk])
                        for hh in range(4):
                            mm(pvs[hh][0][:, 0:129], p_[:, hh * 128:(hh + 1) * 128], vv, ki == 0, ki == len(kbs) - 1, [pk] + vr_, [pvs[hh][1]])
                    for hh in range(4):
                        finish_head(pvs[hh][0], pvs[hh][1], 4 * g + hh, qb, esink[:, 4 * g + hh:4 * g + hh + 1])
            nkb = 2 if isc else NBLK
            for h in range(8):
                knv, knk = wload(KN[h], [NTOK], allk("KN"))
                vmv, vmk = wload(VM[h], [NBLK, 129], [("VM", t0_, tb_) for (t0_, n_, i_) in tiles for tb_ in range(n_ // 128)])
                pvs = [psA() for _ in range(nqb)]

                def s_stage(kb):
                    sps, spk = psB()
                    mm(sps[:, 0:n], knv[:, kb * 128:(kb + 1) * 128], qn_b[:, h, 0:n], True, False, [knk, "qn_b"], [spk])
                    mm(sps[:, 0:n], krs[:, kb * 128:(kb + 1) * 128], qr_b[:, h, 0:n], False, True, ["krs", "qr_b"], [spk])
                    return sps, spk

                nxt_s = s_stage(0)
                for kb in range(nkb):
                    sps, spk = nxt_s
                    if kb + 1 < nkb:
                        nxt_s = s_stage(kb + 1)
                    p_, pk = gpt()
                    act(p_[:, 0:n], sps[:, 0:n], AF.Exp, [spk], [pk], scale=SC_B)
                    for qb in range(nqb):
                        mm(pvs[qb][0][:, 0:129], p_[:, qb * 128:(qb + 1) * 128], vmv[:, kb, :], kb == 0, kb == nkb - 1, [pk, vmk], [pvs[qb][1]])
                for qb in range(nqb):
                    finish_head(pvs[qb][0], pvs[qb][1], 8 + h, qb, None)
            for qb in range(nqb):
                for half in range(2):
                    pp, ppk = psC()
                    ppb = pp.bitcast(BF16)
                    for j in range(8):
                        fc = half * 8 + j
                        tr(ppb[:, j * 128:(j + 1) * 128], o_t[:, qb, fc, :], ident_b, ["o_t", "ident_b"], [ppk])
                    evac(oT[:, half * 8:(half + 1) * 8, qb * 128:(qb + 1) * 128], ppb.rearrange("p (j t) -> p j t", j=8), [ppk], ["hT"])
            for dg in range(4):
                wv, wk = wload(about_b[:, dg * 512:(dg + 1) * 512].rearrange("(k p) c -> p k c", p=128), [KD, 512], wkeys["about"])
                for j in range(4):
                    dc = dg * 4 + j
                    px, pxk = psA()
                    for fc in range(KD):
                        mm(px[:, 0:n], wv[:, fc, j * 128:(j + 1) * 128], oT[:, fc, 0:n], fc == 0, fc == KD - 1, [wk, "hT"], [pxk])
                    stt("dve", xT[:, dc, 0:n], px[:, 0:n], effcol(0, 2, dc, isc), xT[:, dc, 0:n], ALU.mult, ALU.add,
                        [pxk, effk(0, 2), "xT"], ["xT"])
            rename(["qa_b", "qn_b", "qr_b", "o_t", "kA_w", "kA_c", "vA_c"], big_all)
            ffn(0, n, [(0, n, isc)])
            rename(big_all, ["qa_b", "qn_b", "qr_b", "o_t", "kA_w", "kA_c", "vA_c"])
            dma("pool", XT[:, :, t0:t0 + n].rearrange("k p t -> p k t"), xT[:, :, 0:n], ["xT"], [("XT", t0)])

        if stage <= 2:
            for kc in range(KD):
                dma("sp", dbg[kc], XT[kc], allk("XT"), [("out", kc)])
            S.fence("sp", [("out", kc) for kc in range(KD)])
            S.emit()
            return nc


        pc = small(4)
        qdec = small(16)
        gst = small(8)
        dma("sp", pc, pc_in, [], ["pc"])
        act(lgs, lg, AF.Exp, ["lg"], ["lgs"], scale=-1.0)
        act(lgs, lgs, AF.Ln, ["lgs"], ["lgs"], bias=1.0)
        ts("dve", lgs, lgs, -1.0, None, ALU.mult, None, ["lgs"], ["lgs"])
        for dirn in range(2):
            for h in range(8):
                c = dirn * 8 + h
                act(kdec[:, c:c + 1], pc[:, dirn:dirn + 1], AF.Exp, ["pc", "lgs"], ["kdec"], scale=lgs[:, c:c + 1])
                act(qdec[:, c:c + 1], pc[:, 2 + dirn:3 + dirn], AF.Exp, ["pc", "lgs"], ["qdec"], scale=lgs[:, c:c + 1])
        act(cdec, lgs, AF.Exp, ["lgs"], ["cdec"], scale=128.0)
        DTv = ar.view(KRo, [2, 8, 128], F32)
        rename(["krs", "lat_t"], [("DT", 0), ("DT", 1)])
        for dirn in range(2):
            for h in range(8):
                c = dirn * 8 + h
                act(DTv[:, dirn, h, :], diffF if dirn == 0 else diffB, AF.Exp, ["cst", "lgs"], [("DT", dirn)], scale=lgs[:, c:c + 1])
                tt("dve", DTv[:, dirn, h, :], DTv[:, dirn, h, :], cst[:, 2 if dirn == 0 else 1, :], ALU.mult, [("DT", dirn), "cst"], [("DT", dirn)])

        def wbuf(shape, dtype):
            i = nxt("wp", NWP)
            return ar.view(WP0 + i * WPB, shape, dtype), ("wp", i)

        kp_t = [ar.view(XTo + 16384 * i, [4, D], BF16) for i in range(2)]
        qT_t = ar.view(BIGo, [4, KD, 128], BF16)
        kT_t = ar.view(BIGo + 16384, [4, KD, 128], BF16)
        vctx = ar.view(BIGo, [2, 4096], BF16)
        ev_t = [ar.view(BIGo + 32768 + 4096 * i, [4, 512], BF16) for i in range(2)]
        ropeRt = [ar.view(BIGo + 40960 + 2048 * i, [T], F32) for i in range(2)]
        rename(big_all + ["qa_b", "qn_b", "qr_b", "o_t", "kA_w", "kA_c", "vA_c"], ["qT_t", "kT_t", "ev0", "ev1", "ropeR"])
        allS = [("S", h) for h in range(8)]
        allSb = [("Sb", h) for h in range(8)]

        for (t0, n, isc) in tiles:
            ntb = n // 128
            ch0 = (t0 - NCX) // 128
            dma("sp", xT[:, :, 0:n], XT[:, :, t0:t0 + n].rearrange("k p t -> p k t"), [("XT", t0)], ["xT"])
            if not isc:
                p0 = t0 - NCX
                dma("sp", ropeRt[0][:, 0:n], ropeR_in[0][:, p0:p0 + n], [], ["ropeR"])
                dma("sp", ropeRt[1][:, 0:n], ropeR_in[1][:, p0:p0 + n], [], ["ropeR"])
            norm_fm(xT[:, :, 0:n], n, KD, "xT", lambda kc: effcol(1, 0, kc, isc), lambda kc: effcol(1, 1, kc, isc),
                    lambda kc: hT[:, kc, 0:n], lambda kc: "hT", [effk(1, 0), effk(1, 1)])
            for which in ((1,) if isc else (0, 1)):
                dstT = qT_t if which == 0 else kT_t
                dkey = "qT_t" if which == 0 else "kT_t"
                scl = 1.0 if which == 0 else 0.0625
                for blk in range(4):
                    c0 = which * 2048 + blk * 512
                    wv, wk = wload(retin_b[:, c0:c0 + 512].rearrange("(k p) c -> p k c", p=128), [KD, 512], wkeys["retin"])
                    for hh in range(2):
                        h = blk * 2 + hh
                        pxs = []
                        for c in range(2):
                            px, pxk = psA()
                            j = hh * 2 + c
                            for kc in range(KD):
                                mm(px[:, 0:n], wv[:, kc, j * 128:(j + 1) * 128], hT[:, kc, 0:n], kc == 0, kc == KD - 1, [wk, "hT"], [pxk])
                            pxs.append((px, pxk))
                        d0 = dstT[:, 0:ntb, 2 * h, :]
                        d1 = dstT[:, 0:ntb, 2 * h + 1, :]
                        if isc:
                            for c, dd in ((0, d0), (1, d1)):
                                act(dd, pxs[c][0][:, 0:n].rearrange("p (b t) -> p b t", t=128), AF.Copy, [pxs[c][1]], [dkey], scale=scl)
                        else:
                            (x0, x0k), (x1, x1k) = pxs
                            cR, sR = ropeRt[0], ropeRt[1]
                            ta, tak = gtmp()
                            stt("dve", ta[:, 0:n], x0[:, 0:n], scl, cR[:, 0:n], ALU.mult, ALU.mult, [x0k, "ropeR"], [tak])
                            tb_, tbk = gtmp()
                            stt("dve", tb_[:, 0:n], x1[:, 0:n], scl, sR[:, 0:n], ALU.mult, ALU.mult, [x1k, "ropeR"], [tbk])
                            tt("pool", d0, ta[:, 0:n].rearrange("p (b t) -> p b t", t=128), tb_[:, 0:n].rearrange("p (b t) -> p b t", t=128),
                               ALU.subtract, [tak, tbk], [dkey])
                            tc_, tck = gtmp()
                            stt("dve", tc_[:, 0:n], x0[:, 0:n], scl, sR[:, 0:n], ALU.mult, ALU.mult, [x0k, "ropeR"], [tck])
                            td, tdk = gtmp()
                            stt("dve", td[:, 0:n], x1[:, 0:n], scl, cR[:, 0:n], ALU.mult, ALU.mult, [x1k, "ropeR"], [tdk])
                            tt("pool", d1, tc_[:, 0:n].rearrange("p (b t) -> p b t", t=128), td[:, 0:n].rearrange("p (b t) -> p b t", t=128),
                               ALU.add, [tck, tdk], [dkey])
            if not isc:
                dma("pool", QT1[ch0:ch0 + ntb].rearrange("c p k t -> p c (k t)"), qT_t[:, 0:ntb].rearrange("p c k t -> p c (k t)"), ["qT_t"], [("QT1", t0)])
                dma("pool", KT1[ch0:ch0 + ntb].rearrange("c p k t -> p c (k t)"), kT_t[:, 0:ntb].rearrange("p c k t -> p c (k t)"), ["kT_t"], [("KT1", t0)])
            rename(["xT"], ["kp0", "kp1"])
            for tb in range(ntb):
                for half in range(2):
                    pp, ppk = psC()
                    ppb = pp.bitcast(BF16)
                    for j in range(8):
                        kc = half * 8 + j
                        tr(ppb[:, j * 128:(j + 1) * 128], kT_t[:, tb, kc, :], ident_b, ["kT_t", "ident_b"], [ppk])
                    for dirn in range(2):
                        for hh in range(4):
                            h = half * 4 + hh
                            act(kp_t[dirn][:, tb, h * 256:(h + 1) * 256], ppb[:, hh * 256:(hh + 1) * 256], AF.Copy, [ppk, "kdec"], ["kp%d" % dirn],
                                scale=kdec[:, dirn * 8 + h:dirn * 8 + h + 1])
            if not isc:
                for dirn in range(2):
                    dma("pool", KP1[dirn][ch0:ch0 + ntb].rearrange("c p f -> p c f"), kp_t[dirn][:, 0:ntb], ["kp%d" % dirn], [("KP1", dirn, t0)])
            for which in ((0,) if isc else (0, 1)):
                for nb in range(8):
                    c0 = 4096 + which * 4096 + nb * 512
                    wv, wk = wload(retin_b[:, c0:c0 + 512].rearrange("(k p) c -> p k c", p=128), [KD, 512], wkeys["retin"])
                    ei = nxt("ev", 2)
                    evv, evk = ev_t[ei], "ev%d" % ei
                    for tb in range(ntb):
                        pv, pvk = psA()
                        for kc in range(KD):
                            mm(pv, hT[:, kc, tb * 128:(tb + 1) * 128], wv[:, kc, :], kc == 0, kc == KD - 1, [wk, "hT"], [pvk])
                        if isc:
                            evac(vctx[:, tb, nb * 512:(nb + 1) * 512], pv, [pvk], ["qT_t"])
                        elif which == 0:
                            evac(evv[:, tb, :], pv, [pvk], [evk])
                        else:
                            act(evv[:, tb, :], pv, AF.Silu, [pvk], [evk])
                    if not isc:
                        dst = V1 if which == 0 else G1
                        dma("pool", dst[ch0:ch0 + ntb, :, nb * 512:(nb + 1) * 512].rearrange("c p f -> p c f"), evv[:, 0:ntb], [evk],
                            [("V1" if which == 0 else "G1", t0, nb)])
            if isc:
                for dirn in range(2):
                    halves = [wbuf([8, 512], F32) for _ in range(2)]
                    for hv, hk in halves:
                        S.add("dve", lambda e, hv=hv: e.memset(hv, 0.0), [], [hk])
                    for tb in ((0, 1) if dirn == 0 else (1, 0)):
                        for h in range(8):
                            hv, hk = halves[h // 4]
                            for c in range(2):
                                pst, pstk = psB()
                                mm(pst, kp_t[dirn][:, tb, h * 256 + c * 128:h * 256 + (c + 1) * 128], vctx[:, tb, h * 512:(h + 1) * 512], True, True,
                                   ["kp%d" % dirn, "qT_t"], [pstk])
                                sl = hv[:, (h % 4) * 2 + c, :]
                                stt("dve", sl, sl, cdec[:, dirn * 8 + h:dirn * 8 + h + 1], pst, ALU.mult, ALU.add, [pstk, "cdec", hk], [hk])
                    for i, (hv, hk) in enumerate(halves):
                        dma("pool", S0[dirn][:, i * 8:(i + 1) * 8, :], hv, [hk], [("S0", dirn, i)])
            rename(["kp0", "kp1"], ["xT"])

        S_v = ar.view(XTo, [16, 512], F32)
        Sb_v = ar.view(BAo, [16, 512], BF16)
        cbq = [ar.view(BIGo + 20480 * i, [KD, 128], BF16) for i in range(2)]
        cbk = [ar.view(BIGo + 20480 * i + 4096, [KD, 128], BF16) for i in range(2)]
        cbp = [ar.view(BIGo + 20480 * i + 8192, [D], BF16) for i in range(2)]
        cbv = [ar.view(BIGo + 20480 * i + 12288, [4096], BF16) for i in range(2)]
        QRW = ar.view(KR2o, [2, 8, 128], F32)
        qpb = [ar.view(BIGo + 40960 + 512 * i, [2, 128], BF16) for i in range(4)]
        gsum = small(8)
        gsq = small(8)
        gm = small(8)
        gr = small(8)
        for dirn in range(2):
            for h in range(8):
                c = dirn * 8 + h
                act(QRW[:, dirn, h, :], posr if dirn == 0 else cst[:, 6, :], AF.Exp, ["cst", "lgs"], [("QRW", dirn)], scale=lgs[:, c:c + 1])
        ltiles = [t for t in tiles if not t[2]]
        kQ = [("QT1", t[0]) for t in ltiles]
        kK = [("KT1", t[0]) for t in ltiles]
        kV = [("V1", t[0], nb) for t in ltiles for nb in range(8)]
        kG = [("G1", t[0], nb) for t in ltiles for nb in range(8)]
        rename(["xT", "hT", "qT_t", "kT_t", "ev0", "ev1", "ropeR"],
               allS + allSb + [("cb", i, w) for i in range(2) for w in "qkpv"] + [("qp", i) for i in range(4)])
        ctr["cb"] = 0
        ctr["qp"] = 0
        for dirn in range(2):
            kP = [("KP1", dirn, t[0]) for t in ltiles]
            dma("sp", S_v, S0[dirn], [("S0", dirn, 0), ("S0", dirn, 1)], allS)
            for h in range(8):
                cp("pool", Sb_v[:, 2 * h:2 * h + 2, :], S_v[:, 2 * h:2 * h + 2, :], [("S", h)], [("Sb", h)])
            order = list(range(NCH)) if dirn == 0 else list(range(NCH - 1, -1, -1))
            for oi, n_ in enumerate(order):
                bi = nxt("cb", 2)
                qc, kc_, kpc, vc = cbq[bi], cbk[bi], cbp[bi], cbv[bi]
                qk, kk_, pk_, vk = [("cb", bi, w) for w in "qkpv"]
                dma("sp", qc, QT1[n_], kQ, [qk])
                dma("sp", kc_, KT1[n_], kK, [kk_])
                dma("sp", kpc, KP1[dirn][n_], kP, [pk_])
                dma("sp", vc, V1[n_], kV, [vk])
                if dirn == 1:
                    ofv, ofk = wload(OF1[n_], [4096], [("OF1", n_, h) for h in range(8)], F32)
                    gv_, gk_ = wload(G1[n_], [4096], kG)
                    zc, zk = wbuf([4096], BF16)
                    S.add("dve", lambda e: e.memset(gsum, 0.0), [], [("gs", h) for h in range(8)])
                    S.add("dve", lambda e: e.memset(gsq, 0.0), [], [("gq", h) for h in range(8)])
                last = oi == len(order) - 1

                def st1(h):
                    pa, pak = psB()
                    for c in range(2):
                        mm(pa[:, 0:128], kc_[:, 2 * h + c, :], qc[:, 2 * h + c, :], c == 0, c == 1, [kk_, qk], [pak])
                    qi = nxt("qp", 4)
                    tt("pool", qpb[qi], qc[:, 2 * h:2 * h + 2, :], QRW[:, dirn, h, :].unsqueeze(1).to_broadcast([128, 2, 128]), ALU.mult,
                       [qk, ("QRW", dirn)], [("qp", qi)])
                    return pa, pak, qpb[qi], ("qp", qi)

                def st2(h, pa, pak, qpv, qpk):
                    c16 = dirn * 8 + h
                    am, amk = gpt()
                    tt("dve", am[:, 0:128], pa[:, 0:128], DTv[:, dirn, h, :], ALU.mult, [pak, ("DT", dirn)], [amk])
                    po_, pok = psA()
                    mm(po_, am[:, 0:128], vc[:, h * 512:(h + 1) * 512], True, False, [amk, vk], [pok])
                    for c in range(2):
                        mm(po_, qpv[:, c, :], Sb_v[:, 2 * h + c, :], False, c == 1, [qpk, ("Sb", h)], [pok])
                    psts = []
                    if not last:
                        for c in range(2):
                            pst, pstk = psC()
                            mm(pst, kpc[:, h * 256 + c * 128:h * 256 + (c + 1) * 128], vc[:, h * 512:(h + 1) * 512], True, True, [pk_, vk], [pstk])
                            psts.append((pst, pstk))
                    if dirn == 0:
                        t1, t1k = gtmp()
                        cp("act", t1, po_, [pok], [t1k])
                        dma("pool", OF1[n_][:, h * 512:(h + 1) * 512], t1, [t1k], [("OF1", n_, h)])
                    else:
                        oh = ofv[:, h * 512:(h + 1) * 512]
                        tt("dve", oh, oh, po_, ALU.add, [pok, ofk], [ofk])
                        j1, j1k = gtmp()
                        act(j1, oh, AF.Copy, [ofk, ("gs", h)], [j1k, ("gs", h)], accum=gsum[:, h:h + 1])
                        j2, j2k = gtmp()
                        act(j2, oh, AF.Square, [ofk, ("gq", h)], [j2k, ("gq", h)], accum=gsq[:, h:h + 1])
                    for c, (pst, pstk) in enumerate(psts):
                        stt("dve", S_v[:, 2 * h + c, :], S_v[:, 2 * h + c, :], cdec[:, c16:c16 + 1], pst, ALU.mult, ALU.add,
                            [pstk, "cdec", ("S", h)], [("S", h)])
                    if psts:
                        cp("act", Sb_v[:, 2 * h:2 * h + 2, :], S_v[:, 2 * h:2 * h + 2, :], [("S", h)], [("Sb", h)])

                cur = st1(0)
                for h in range(8):
                    nx_ = st1(h + 1) if h < 7 else None
                    st2(h, *cur)
                    cur = nx_
                if dirn == 1:
                    gsk = [("gs", h) for h in range(8)] + [("gq", h) for h in range(8)]
                    ts("dve", gm, gsum, 1.0 / 512, None, ALU.mult, None, gsk, ["gm"])
                    tt("dve", gr, gm, gm, ALU.mult, ["gm"], ["gr"])
                    stt("dve", gr, gsq, 1.0 / 512, gr, ALU.mult, ALU.subtract, gsk + ["gr"], ["gr"])
                    act(gr, gr, AF.Sqrt, ["gr"], ["gr"], bias=EPS)
                    S.add("dve", lambda e: e.reciprocal(out=gr, in_=gr), ["gr"], ["gr"])
                    for h in range(8):
                        y_, yk = gtmp()
                        stt("dve", y_, ofv[:, h * 512:(h + 1) * 512], gm[:, h:h + 1], gv_[:, h * 512:(h + 1) * 512], ALU.subtract, ALU.mult,
                            [ofk, "gm", gk_], [yk])
                        act(zc[:, h * 512:(h + 1) * 512], y_, AF.Copy, [yk, "gr"], [zk], scale=gr[:, h:h + 1])
                    dma("pool", Z1[n_], zc, [zk], [("Z1", n_)])

        zT = ar.view(BIGo, [32, T], BF16)
        yT = ar.view(BIGo, [KD, T], F32)
        orow = ar.view(BIGo + 32768, [D], F32)
        rename(allS + allSb + [("cb", i, w) for i in range(2) for w in "qkpv"] + [("qp", i) for i in range(4)], ["xT", "hT", "zT"])
        for (t0, n, isc) in ltiles:
            ntb = n // 128
            ch0 = (t0 - NCX) // 128
            dma("sp", xT[:, :, 0:n], XT[:, :, t0:t0 + n].rearrange("k p t -> p k t"), [("XT", t0)], ["xT"])
            for tb in range(ntb):
                zt, ztk = wload(Z1[ch0 + tb], [4096], [("Z1", ch0 + tb)])
                for g8 in range(4):
                    pp, ppk = psC()
                    ppb = pp.bitcast(BF16)
                    for j in range(8):
                        fc = g8 * 8 + j
                        tr(ppb[:, j * 128:(j + 1) * 128], zt[:, fc * 128:(fc + 1) * 128], ident_b, [ztk, "ident_b"], [ppk])
                    for j in range(8):
                        fc = g8 * 8 + j
                        act(zT[:, fc, tb * 128:(tb + 1) * 128], ppb[:, j * 128:(j + 1) * 128], AF.Copy, [ppk, "gng"], ["zT"], scale=gng[:, fc:fc + 1])
            for dg in range(4):
                pd = [psA() for _ in range(4)]
                for half in range(2):
                    wv, wk = wload(retout_b[half * 2048:(half + 1) * 2048, dg * 512:(dg + 1) * 512].rearrange("(k p) c -> p k c", p=128),
                                   [KD, 512], wkeys["retout"])
                    for j in range(4):
                        for f16 in range(KD):
                            fc = half * 16 + f16
                            mm(pd[j][0][:, 0:n], wv[:, f16, j * 128:(j + 1) * 128], zT[:, fc, 0:n], fc == 0, fc == 31, [wk, "zT"], [pd[j][1]])
                for j in range(4):
                    dc = dg * 4 + j
                    stt("dve", xT[:, dc, 0:n], pd[j][0][:, 0:n], effcol(1, 2, dc, 0), xT[:, dc, 0:n], ALU.mult, ALU.add,
                        [pd[j][1], effk(1, 2), "xT"], ["xT"])
            rename(["zT"], big_all)
            ffn(1, n, [(0, n, 0)])
            rename(big_all, ["yT", "orow"])
            norm_fm(xT[:, :, 0:n], n, KD, "xT", lambda kc: fng[:, kc:kc + 1], None,
                    lambda kc: yT[:, kc, 0:n], lambda kc: "yT", ["fng"])
            for tb in range(ntb):
                for g4 in range(4):
                    pp, ppk = psC()
                    for j in range(4):
                        kc = g4 * 4 + j
                        tr(pp[:, j * 128:(j + 1) * 128], yT[:, kc, tb * 128:(tb + 1) * 128], ident_f, ["yT", "cst"], [ppk])
                    evac(orow[:, g4 * 512:(g4 + 1) * 512], pp, [ppk], ["orow"])
                r0 = t0 - NCX + tb * 128
                dma("pool", out[r0:r0 + 128, :], orow, ["orow"], [("out", r0)])
            rename(["yT", "orow"], ["zT"])
        S.fence("sp", [("out", r0) for r0 in range(0, NL, 128)])
        S.emit()
    return nc


def _pp(v):
    v = np.asarray(v, np.float32)
    return np.ascontiguousarray(v.reshape(-1, 128).T)


def _rope_tables(n_tok, rot_dim, mode):
    GRID_W = 64
    row = np.repeat(np.arange(n_tok // GRID_W, dtype=np.float32), GRID_W)
    col = (np.arange(n_tok) % GRID_W).astype(np.float32)
    n_freq = rot_dim // 4
    inv = (np.float32(10000.0) ** (-np.arange(n_freq, dtype=np.float32) / np.float32(n_freq))).astype(np.float32)
    ang = np.concatenate([row[:, None] * inv, col[:, None] * inv], axis=-1).astype(np.float32)
    c = np.cos(ang).astype(np.float32).T
    s = np.sin(ang).astype(np.float32).T
    if mode == "half":
        C = np.concatenate([c, c], 0)
        Sg = np.concatenate([-s, s], 0)
        rep = 128 // C.shape[0]
        return np.stack([np.tile(C, (rep, 1)), np.tile(Sg, (rep, 1))]).astype(np.float32)
    return np.stack([c, s]).astype(np.float32)


def _consts():
    p = np.arange(128)
    ident = np.eye(128, dtype=np.float32)
    maskLo = (p[:, None] >= p[None, :]).astype(np.float32)
    maskHi = (p[:, None] <= p[None, :]).astype(np.float32)
    diffF = np.maximum(p[None, :] - p[:, None], 0).astype(np.float32)
    diffB = np.maximum(p[:, None] - p[None, :], 0).astype(np.float32)
    pos = np.tile((p + 1).astype(np.float32)[None, :], (128, 1))
    posb = np.tile((128 - p).astype(np.float32)[None, :], (128, 1))
    return np.ascontiguousarray(np.concatenate([ident, maskLo, maskHi, diffF, diffB, pos, posb], 1))


def prep_core(inp, b, NL):
    f = lambda a: np.ascontiguousarray(np.asarray(a, np.float32))
    d = {}
    d["x"] = f(inp["x"][b][:NL])
    d["ctx"] = f(inp["ctx"][b])
    cvv = np.stack([_pp(inp["c"][b]), _pp(inp["c_ctx"])], -1).reshape(128, 32)
    d["cv"] = f(cvv)
    d["mod_w"] = f(inp["mod_w"])
    d["modb"] = f(np.concatenate([_pp(inp["mod_b"][l]) for l in range(2)], 1))
    d["nmg"] = f(np.concatenate([_pp(inp["norm_mix_g"][l]) for l in range(2)], 1))
    d["nfg"] = f(np.concatenate([_pp(inp["norm_ffn_g"][l]) for l in range(2)], 1))
    d["fng"] = _pp(inp["final_norm_g"])
    d["wg"] = f(inp["ffn_w_gate"])
    d["wu"] = f(inp["ffn_w_up"])
    d["wd"] = f(inp["ffn_w_down"])
    d["abin"] = f(inp["ab_w_in"][0])
    d["about"] = f(inp["ab_w_out"][0])
    d["sink"] = f(np.tile(np.asarray(inp["swa_sink"][0], np.float32)[None, :], (128, 1)))
    d["qng"] = _pp(inp["mla_q_norm_g"][0])
    d["wqb"] = f(inp["mla_w_q_b"][0])
    d["kvng"] = _pp(inp["mla_kv_norm_g"][0])
    d["wkvb"] = f(inp["mla_w_kv_b"][0])
    d["retin"] = f(inp["ret_w_in"][0])
    lgv = np.concatenate([np.asarray(inp["ret_decay_logit_fwd"][0], np.float32), np.asarray(inp["ret_decay_logit_bwd"][0], np.float32)])
    d["lg"] = f(np.tile(lgv[None, :], (128, 1)))
    d["gng"] = _pp(inp["ret_gn_g"][0])
    d["retout"] = f(inp["ret_w_out"][0])
    d["ropeA"] = _rope_tables(NL, 128, "half")
    d["ropeB"] = _rope_tables(NL, 64, "half")
    d["ropeR"] = _rope_tables(NL, 256, "plain")
    d["cst"] = _consts()
    p = np.arange(128, dtype=np.float32)
    d["pc"] = np.ascontiguousarray(np.stack([127 - p, p, p + 1, 128 - p], 1).astype(np.float32))
    return d


_CACHE = {}


def kernel(**inputs):
    NL = 4096
    B = 4
    if "nc" not in _CACHE:
        _CACHE["nc"] = build(NL)
    nc = _CACHE["nc"]
    in_maps = [prep_core(inputs, c % B, NL) for c in range(8)]
    res = run_bass_kernel_spmd(nc, in_maps, core_ids=list(range(8)))
    return np.stack([res.results[b]["out"] for b in range(B)], 0).astype(np.float32)
```

```python
from contextlib import ExitStack

import numpy as np
import concourse.bass as bass
import concourse.mybir as mybir
from concourse.bass_utils import run_bass_kernel_spmd

F32 = mybir.dt.float32
BF16 = mybir.dt.bfloat16
AF = mybir.ActivationFunctionType
ALU = mybir.AluOpType

D = 2048
KD = 16
FF = 5632
KF = 44
NCX = 256
T = 512
EPS = 1e-6
COMPUTE = ("pe", "act", "dve", "pool")
ENGS = ("pe", "act", "dve", "pool", "sp")


class Sched:
    def __init__(self, nc, es, n_dma_sems=(("sp", 32), ("pool", 12), ("act", 4))):
        self.nc = nc
        self.streams = {e: [] for e in ENGS}
        self.esem = {e: es.enter_context(nc.semaphore("s_" + e)) for e in COMPUTE}
        self.ecount = {e: 0 for e in COMPUTE}
        self.dsems = {q: [es.enter_context(nc.semaphore("d%s%d" % (q, i))) for i in range(n)] for q, n in n_dma_sems}
        self.dcount = {q: [0] * n for q, n in n_dma_sems}
        self.dnext = {q: 0 for q, n in n_dma_sems}
        self.last_w = {}
        self.readers = {}
        self.seen = {e: {} for e in ENGS}
        self.semobj = {}
        self.n_ops = 0

    def _need(self, eng, tok, waits):
        if tok is None:
            return
        sk, val, teng = tok
        if teng == "pe" and eng == "pe":
            return
        if self.seen[eng].get(sk, 0) >= val:
            return
        self.seen[eng][sk] = val
        waits.append((sk, val))

    def add(self, eng, fn, reads=(), writes=(), dma=False):
        waits = []
        for k in reads:
            self._need(eng, self.last_w.get(k), waits)
        for k in writes:
            self._need(eng, self.last_w.get(k), waits)
            for sk, (val, teng) in self.readers.get(k, {}).items():
                self._need(eng, (sk, val, teng), waits)
        if dma:
            i = self.dnext[eng]
            self.dnext[eng] = (i + 1) % len(self.dsems[eng])
            sk = ("d", eng, i)
            dc = self.dcount[eng]
            if dc[i] > 0:
                self._need(eng, (sk, dc[i], "dma"), waits)
            dc[i] += 16
            tok = (sk, dc[i], "dma")
            self.semobj[sk] = self.dsems[eng][i]
            inc = 16
        else:
            self.ecount[eng] += 1
            sk = ("e", eng)
            tok = (sk, self.ecount[eng], eng)
            self.semobj[sk] = self.esem[eng]
            inc = 1
        self.streams[eng].append((waits, fn, sk, inc))
        self.n_ops += 1
        for k in writes:
            self.last_w[k] = tok
            self.readers[k] = {}
        for k in reads:
            if k in writes:
                continue
            self.readers.setdefault(k, {})[tok[0]] = (tok[1], tok[2])
        return tok

    def fence(self, eng, keys):
        waits = []
        for k in keys:
            self._need(eng, self.last_w.get(k), waits)
        self.streams[eng].append((waits, None, None, 0))

    def emit(self):
        so = self.semobj

        def replay(name, e):
            for waits, fn, sk, inc in self.streams[name]:
                for wk, val in waits:
                    e.wait_ge(so[wk], val)
                if fn is not None:
                    fn(e).then_inc(so[sk], inc)

        with self.nc.Block() as block:
            @block.tensor
            def _(e):
                replay("pe", e)

            @block.scalar
            def _(e):
                replay("act", e)

            @block.vector
            def _(e):
                replay("dve", e)

            @block.gpsimd
            def _(e):
                replay("pool", e)

            @block.sync
            def _(e):
                replay("sp", e)


def _rs(shape):
    names = "abcdefg"[: len(shape)]
    if len(shape) == 1:
        return None, {}
    pat = "p (" + " ".join(names) + ") -> p " + " ".join(names)
    return pat, {n: s for n, s in zip(names[:-1], shape[:-1])}


class Arena:
    def __init__(self, nc, es, nbytes):
        self.t = es.enter_context(nc.sbuf_tensor("arena", [128, nbytes // 2], BF16))
        self.nbytes = nbytes
        self.off = 0

    def alloc(self, nbytes):
        o = self.off
        self.off += (nbytes + 63) // 64 * 64
        assert self.off <= self.nbytes, (self.off, self.nbytes)
        return o

    def view(self, off, shape, dtype, parts=128):
        n = int(np.prod(shape))
        sz = 4 if dtype == F32 else 2
        ap = self.t[0:parts, off // 2: off // 2 + n * sz // 2]
        if dtype == F32:
            ap = ap.bitcast(F32)
        pat, kw = _rs(list(shape))
        if pat is not None:
            ap = ap.rearrange(pat, **kw)
        return ap


def build(NL, stage=9, NOWN=None):
    nc = bass.Bass("TRN2", target_bir_lowering=False)
    NTOK = NCX + NL
    NBLK = NTOK // 128
    NCH = NL // 128
    NOWN = NL if NOWN is None else NOWN
    NOCH = NOWN // 128
    NTL = NL // T

    def din(n, s, dt=F32):
        return nc.dram_tensor(n, s, dt, kind="ExternalInput").ap()

    def dsc(n, s, dt=BF16):
        return nc.dram_tensor(n, s, dt).ap()

    x_in = din("x", [NL, D])
    ctx_in = din("ctx", [NCX, D])
    cv_in = din("cv", [128, 32])
    modw_in = din("mod_w", [2, D, 6 * D])
    modb_in = din("modb", [128, 192])
    nmg_in = din("nmg", [128, 32])
    nfg_in = din("nfg", [128, 32])
    fng_in = din("fng", [128, 16])
    wg_in = din("wg", [2, D, FF])
    wu_in = din("wu", [2, D, FF])
    wd_in = din("wd", [2, FF, D])
    abin_in = din("abin", [D, 2368])
    about_in = din("about", [D, D])
    sink_in = din("sink", [128, 8])
    qng_in = din("qng", [128, 4])
    wqb_in = din("wqb", [512, 1536])
    kvng_in = din("kvng", [128, 2])
    wkvb_in = din("wkvb", [256, 2048])
    retin_in = din("retin", [D, 12288])
    lg_in = din("lg", [128, 16])
    gng_in = din("gng", [128, 32])
    retout_in = din("retout", [4096, D])
    ropeA_in = din("ropeA", [2, 128, NL])
    ropeB_in = din("ropeB", [2, 128, NL])
    ropeR_in = din("ropeR", [2, 128, NL])
    cst_in = din("cst", [128, 7 * 128])
    pc_in = din("pc", [128, 4])
    out = nc.dram_tensor("out", [NOWN, D], F32, kind="ExternalOutput").ap()
    dbg = nc.dram_tensor("dbg", [KD, 128, NCX + NL], F32, kind="ExternalOutput").ap() if stage <= 2 else None

    abin_b = dsc("abin_b", [D, 2368])
    abin_s = dsc("abin_s", [D, 1280])
    abin_kr = dsc("abin_kr", [D, 256])
    about_b = dsc("about_b", [D, D])
    wqn_b = dsc("wqn_b", [512, 1024])
    wqr_b = dsc("wqr_b", [512, 1024])
    wkn_b = dsc("wkn_b", [256, 1024])
    wkv_b = dsc("wkv_b", [256, 1024])
    wg_b = dsc("wg_b", [2, D, FF])
    wu_b = dsc("wu_b", [2, D, FF])
    wd_b = dsc("wd_b", [2, FF, D])
    retin_b = dsc("retin_b", [D, 12288])
    retout_b = dsc("retout_b", [4096, D])
    XT = dsc("XT", [KD, 128, NTOK], F32)
    QA = dsc("QA", [8, 128, NTOK])
    KA = dsc("KA", [2, 128, NTOK])
    VA = dsc("VA", [NBLK, 128, 2, 129])
    QN = dsc("QN", [8, 128, NTOK])
    QR = dsc("QR", [8, 128, NTOK])
    KN = dsc("KN", [8, 128, NTOK])
    KRS = dsc("KRS", [128, NTOK])
    VM = dsc("VM", [8, 128, NBLK, 129])
    QT1 = dsc("QT1", [NOCH, 128, KD, 128])
    KT1 = dsc("KT1", [NOCH, 128, KD, 128])
    KP1 = dsc("KP1", [2, NCH, 128, D])
    V1 = dsc("V1", [NCH, 128, 4096])
    G1 = dsc("G1", [NOCH, 128, 4096])
    OF1 = dsc("OF1", [NOCH, 128, 4096], F32)
    Z1 = dsc("Z1", [NOCH, 128, 4096])
    S0 = dsc("S0", [2, 128, 16, 512], F32)

    es = ExitStack()
    with es:
        S = Sched(nc, es)
        ar = Arena(nc, es, 206 * 1024)
        ps = [es.enter_context(nc.psum_tensor("ps%d" % i, [128, 512], F32))[:] for i in range(8)]
        psk = [("ps", i) for i in range(8)]

        WPB = 16384
        NWP = 4
        WP0 = ar.alloc(WPB * NWP)
        XTo = ar.alloc(32768)
        BAo = ar.alloc(16384)
        BIGo = ar.alloc(45056)
        KRo = ar.alloc(max(NTOK * 2, 8192 + 64))
        NTMP = 6
        TMPo = ar.alloc(2048 * NTMP)
        PTo = ar.alloc(1024 * 3)
        RSo = ar.alloc(2048)
        VACo = ar.alloc(2048)
        KR2o = ar.alloc(8192 + 64)
        CSTo = ar.alloc(7 * 128 * 4)
        SMo = ar.alloc(8192)
        sm_off = [SMo]

        def small(ncols, dtype=F32):
            sz = 4 if dtype == F32 else 2
            o = sm_off[0]
            sm_off[0] += (ncols * sz + 31) // 32 * 32
            assert sm_off[0] <= SMo + 8192
            return ar.view(o, [ncols], dtype)

        cst = ar.view(CSTo, [7, 128], F32)
        ident_f = cst[:, 0, :]
        diffF = cst[:, 3, :]
        diffB = cst[:, 4, :]
        posr = cst[:, 5, :]
        ident_b = small(128, BF16)
        maskLo = small(128, BF16)
        maskHi = small(128, BF16)
        ones_m = small(128)
        cv = small(32)
        scv = small(32)
        modb = small(192)
        modT = small(384)
        EFF = small(384)
        nmg = small(32)
        nfg = small(32)
        fng = small(16)
        qng = small(4)
        kvng = small(2)
        esink = small(8)
        lg = small(16)
        lgs = small(16)
        kdec = small(16)
        cdec = small(16)
        pcol = small(2)
        gng = small(32)
        dummy = small(8)
        xT = ar.view(XTo, [KD, T], F32)
        hT = ar.view(BAo, [KD, T], BF16)
        tmp = [ar.view(TMPo + 2048 * i, [T], F32) for i in range(NTMP)]
        tmpk = [("tmp", i) for i in range(NTMP)]
        rsb = ar.view(RSo, [T], F32)
        pt = [ar.view(PTo + 1024 * i, [T], BF16) for i in range(3)]
        ptk = [("pt", i) for i in range(3)]
        ctr = {"tmp": 0, "pt": 0, "wp": 0, "psA": 0, "psB": 0, "psC": 0, "ev": 0}

        def nxt(name, n):
            i = ctr[name]
            ctr[name] = (i + 1) % n
            return i

        def gtmp():
            i = nxt("tmp", NTMP)
            return tmp[i], tmpk[i]

        def gpt():
            i = nxt("pt", 3)
            return pt[i], ptk[i]

        def psA():
            i = nxt("psA", 4)
            return ps[i], psk[i]

        def psB():
            i = 4 + nxt("psB", 2)
            return ps[i], psk[i]

        def psC():
            i = 6 + nxt("psC", 2)
            return ps[i], psk[i]

        def dma(q, o, i, r, w):
            S.add(q, lambda e: e.dma_start(out=o, in_=i), r, w, dma=True)

        def mm(o, lhsT, rhs, start, stop, r, w):
            S.add("pe", lambda e: e.matmul(o, lhsT=lhsT, rhs=rhs, start=start, stop=stop), r, w)

        def tr(o, i, idn, r, w):
            S.add("pe", lambda e: e.transpose(o, i, idn), r, w)

        def act(o, i, func, r, w, bias=None, scale=None, accum=None):
            kw = {}
            if bias is not None:
                kw["bias"] = bias
            if scale is not None:
                kw["scale"] = scale
            if accum is not None:
                kw["accum_out"] = accum
            S.add("act", lambda e: e.activation(out=o, in_=i, func=func, **kw), r, w)

        def tt(eng, o, a, b, op, r, w):
            S.add(eng, lambda e: e.tensor_tensor(out=o, in0=a, in1=b, op=op), r, w)

        def ts(eng, o, a, s1, s2, op0, op1, r, w):
            if s2 is None:
                S.add(eng, lambda e: e.tensor_scalar(out=o, in0=a, scalar1=s1, scalar2=None, op0=op0), r, w)
            else:
                S.add(eng, lambda e: e.tensor_scalar(out=o, in0=a, scalar1=s1, scalar2=s2, op0=op0, op1=op1), r, w)

        def stt(eng, o, a, sc, b, op0, op1, r, w):
            S.add(eng, lambda e: e.scalar_tensor_tensor(out=o, in0=a, scalar=sc, in1=b, op0=op0, op1=op1), r, w)

        def cp(eng, o, i, r, w):
            if eng == "act":
                S.add("act", lambda e: e.copy(out=o, in_=i), r, w)
            else:
                S.add(eng, lambda e: e.tensor_copy(out=o, in_=i), r, w)

        def evac(o, i, r, w):
            cp("act" if nxt("ev", 2) == 0 else "dve", o, i, r, w)

        def rename(old, new):
            S.add("dve", lambda e: e.memset(dummy[:, 0:1], 0.0), list(old), list(new) + ["dummy"])

        def wload(src, shape, rk, dtype=BF16, q="sp"):
            i = nxt("wp", NWP)
            v = ar.view(WP0 + i * WPB, shape, dtype)
            dma(q, v, src, rk, [("wp", i)])
            return v, ("wp", i)

        wkeys = {}

        def conv(name, dst, src, nrows, rb=256):
            ks = []
            for r0 in range(0, nrows, rb):
                k = ("w", name, r0)
                dma("pool", dst[r0:r0 + rb], src[r0:r0 + rb], [], [k])
                ks.append(k)
            wkeys.setdefault(name, []).extend(ks)

        dma("sp", cst, cst_in.rearrange("p (a b) -> p a b", a=7), [], ["cst"])
        for (v, src, k) in [(cv, cv_in, "cv"), (modb, modb_in, "modb"), (nmg, nmg_in, "nmg"), (nfg, nfg_in, "nfg"),
                            (fng, fng_in, "fng"), (qng, qng_in, "qng"), (kvng, kvng_in, "kvng"), (esink, sink_in, "esink"),
                            (lg, lg_in, "lg"), (gng, gng_in, "gng")]:
            dma("sp", v, src, [], [k])
        cp("dve", ident_b, cst[:, 0, :], ["cst"], ["ident_b"])
        cp("dve", maskLo, cst[:, 1, :], ["cst"], ["maskLo"])
        cp("dve", maskHi, cst[:, 2, :], ["cst"], ["maskHi"])
        S.add("dve", lambda e: e.memset(ones_m, 1.0 / D), [], ["ones_m"])
        act(esink, esink, AF.Exp, ["esink"], ["esink"])
        act(scv, cv, AF.Silu, ["cv"], ["scv"])

        conv("abin", abin_b, abin_in, D)
        av = abin_in[:, 0:1280].rearrange("r (h t c) -> r h t c", h=10, t=2)
        sv = abin_s.rearrange("r (h t c) -> r h t c", h=10, t=2)
        for r0 in range(0, D, 128):
            for t_ in range(2):
                k = ("w", "abin_s", r0, t_)
                dma("pool", sv[r0:r0 + 128, :, t_, :], av[r0:r0 + 128, :, 1 - t_, :], [], [k])
                wkeys.setdefault("abin_s", []).append(k)
        for r0 in range(0, D, 512):
            segs = [(0, 64, 2304), (64, 128, 2304), (128, 160, 2336), (160, 192, 2304), (192, 224, 2336), (224, 256, 2304)]
            for (d0, d1, s0) in segs:
                k = ("w", "abin_kr", r0, d0)
                dma("pool", abin_kr[r0:r0 + 512, d0:d1], abin_in[r0:r0 + 512, s0:s0 + (d1 - d0)], [], [k])
                wkeys.setdefault("abin_kr", []).append(k)
        wq3 = wqb_in.rearrange("r (h c) -> r h c", h=8)
        dma("pool", wqn_b.rearrange("r (h c) -> r h c", h=8), wq3[:, :, 0:128], [], [("w", "wqn")])
        wkeys["wqn"] = [("w", "wqn")]
        wr3 = wqr_b.rearrange("r (s h c) -> r s h c", s=2, h=8)
        wkeys["wqr"] = []
        for j, (sl_d, sl_s) in enumerate([((0, slice(0, 64)), slice(128, 192)),
                                          ((1, slice(0, 32)), slice(160, 192)),
                                          ((1, slice(32, 64)), slice(128, 160))]):
            k = ("w", "wqr", j)
            dma("pool", wr3[:, sl_d[0], :, sl_d[1]], wq3[:, :, sl_s], [], [k])
            wkeys["wqr"].append(k)
        wk3 = wkvb_in.rearrange("r (h c) -> r h c", h=8)
        dma("pool", wkn_b.rearrange("r (h c) -> r h c", h=8), wk3[:, :, 0:128], [], [("w", "wkn")])
        dma("pool", wkv_b.rearrange("r (h c) -> r h c", h=8), wk3[:, :, 128:256], [], [("w", "wkv")])
        wkeys["wkn"] = [("w", "wkn")]
        wkeys["wkv"] = [("w", "wkv")]
        conv("about", about_b, about_in, D)
        conv("wg0", wg_b[0], wg_in[0], D)
        conv("wu0", wu_b[0], wu_in[0], D)
        conv("wd0", wd_b[0], wd_in[0], FF)

        scv3 = scv.rearrange("p (k s) -> p k s", s=2)
        for l in range(2):
            pm, pmk = ps[7], psk[7]
            for cb in range(48):
                wv, wk = wload(modw_in[l][:, cb * 256:(cb + 1) * 256].rearrange("(k p) c -> p k c", p=128), [KD, 256], [], F32)
                for j in range(2):
                    oc = cb * 2 + j
                    for kc in range(KD):
                        mm(pm[:, oc * 2:oc * 2 + 2], wv[:, kc, j * 128:(j + 1) * 128], scv3[:, kc, :], kc == 0, kc == KD - 1,
                           [wk, "scv"], [pmk])
            tt("dve", modT[:, l * 192:(l + 1) * 192].rearrange("p (o s) -> p o s", s=2), pm[:, 0:192].rearrange("p (o s) -> p o s", s=2),
               modb[:, l * 96:(l + 1) * 96].unsqueeze(2).to_broadcast([128, 96, 2]), ALU.add, [pmk, "modb"], [("modT", l)])
        modT4 = modT.rearrange("p (l j k s) -> p l j k s", l=2, j=6, k=16)
        EFF4 = EFF.rearrange("p (l j k s) -> p l j k s", l=2, j=6, k=16)
        for l in range(2):
            for (jo, jsc, jsh, jg, gv) in [(0, 1, 0, 2, nmg), (3, 4, 3, 5, nfg)]:
                gcol = gv[:, l * 16:(l + 1) * 16].unsqueeze(2).to_broadcast([128, 16, 2])
                stt("dve", EFF4[:, l, jo], modT4[:, l, jsc], 1.0, gcol, ALU.add, ALU.mult, [("modT", l), "nmg", "nfg"], [("EFF", l, jo)])
                cp("dve", EFF4[:, l, jo + 1], modT4[:, l, jsh], [("modT", l)], [("EFF", l, jo + 1)])
                cp("dve", EFF4[:, l, jo + 2], modT4[:, l, jg], [("modT", l)], [("EFF", l, jo + 2)])

        def effcol(l, j, kc, s):
            return EFF4[:, l, j, kc, s:s + 1]

        def effk(l, j):
            return ("EFF", l, j)

        def norm_fm(xv, n, nk, xk, scale_fn, bias_fn, out_fn, outk_fn, sk, sqs=1.0):
            pn, pnk = psC()
            for kc in range(nk):
                sq, sqk = gtmp()
                if kc % 2 == 0:
                    act(sq[:, 0:n], xv[:, kc, 0:n], AF.Square, [xk], [sqk])
                else:
                    tt("dve", sq[:, 0:n], xv[:, kc, 0:n], xv[:, kc, 0:n], ALU.mult, [xk], [sqk])
                mm(pn[:, 0:n], ones_m, sq[:, 0:n], kc == 0, kc == nk - 1, [sqk, "ones_m"], [pnk])
            rs, rsk = rsb, "rsb"
            act(rs[:, 0:n], pn[:, 0:n], AF.Sqrt, [pnk], [rsk], bias=EPS, scale=sqs)
            S.add("dve", lambda e: e.reciprocal(out=rs[:, 0:n], in_=rs[:, 0:n]), [rsk], [rsk])
            for kc in range(nk):
                t_, tk = gtmp()
                tt("dve", t_[:, 0:n], xv[:, kc, 0:n], rs[:, 0:n], ALU.mult, [xk, rsk], [tk])
                b = bias_fn(kc) if bias_fn is not None else None
                act(out_fn(kc), t_[:, 0:n], AF.Identity, [tk] + sk, [outk_fn(kc)], bias=b, scale=scale_fn(kc))

        def ffn(l, n, mods):
            act_t = ar.view(BIGo, [KF, T], BF16)
            for (c0, c1, s) in mods:
                norm_fm(xT[:, :, c0:c1], c1 - c0, KD, "xT", lambda kc: effcol(l, 3, kc, s), lambda kc: effcol(l, 4, kc, s),
                        lambda kc: hT[:, kc, c0:c1], lambda kc: "hT", [effk(l, 3), effk(l, 4)])
            for hb in range(11):
                gv, gk = wload(wg_b[l][:, hb * 512:(hb + 1) * 512].rearrange("(k p) c -> p k c", p=128), [KD, 512], wkeys["wg%d" % l])
                uv, uk = wload(wu_b[l][:, hb * 512:(hb + 1) * 512].rearrange("(k p) c -> p k c", p=128), [KD, 512], wkeys["wu%d" % l])
                for j in range(4):
                    hc = hb * 4 + j
                    pg, pgk = psA()
                    for kc in range(KD):
                        mm(pg[:, 0:n], gv[:, kc, j * 128:(j + 1) * 128], hT[:, kc, 0:n], kc == 0, kc == KD - 1, [gk, "hT"], [pgk])
                    pu, puk = psA()
                    for kc in range(KD):
                        mm(pu[:, 0:n], uv[:, kc, j * 128:(j + 1) * 128], hT[:, kc, 0:n], kc == 0, kc == KD - 1, [uk, "hT"], [puk])
                    sg, sgk = gtmp()
                    act(sg[:, 0:n], pg[:, 0:n], AF.Silu, [pgk], [sgk])
                    tt("dve", act_t[:, hc, 0:n], sg[:, 0:n], pu[:, 0:n], ALU.mult, [sgk, puk], [("BIG", hc)])
            for dg in range(4):
                pd = [psA() for _ in range(4)]
                for q4 in range(4):
                    dv_, dk = wload(wd_b[l][q4 * 1408:(q4 + 1) * 1408, dg * 512:(dg + 1) * 512].rearrange("(k p) c -> p k c", p=128),
                                    [11, 512], wkeys["wd%d" % l])
                    for j in range(4):
                        for hh in range(11):
                            hc = q4 * 11 + hh
                            mm(pd[j][0][:, 0:n], dv_[:, hh, j * 128:(j + 1) * 128], act_t[:, hc, 0:n], hc == 0, hc == KF - 1,
                               [dk, ("BIG", hc)], [pd[j][1]])
                for j in range(4):
                    dc = dg * 4 + j
                    for (c0, c1, s) in mods:
                        stt("dve", xT[:, dc, c0:c1], pd[j][0][:, c0:c1], effcol(l, 5, dc, s), xT[:, dc, c0:c1], ALU.mult, ALU.add,
                            [pd[j][1], effk(l, 5), "xT"], ["xT"])

        xtok = [ar.view(BIGo + 8192 * i, [D], F32) for i in range(2)]
        xtokk = [("xtok", i) for i in range(2)]

        def load_x_transposed(src_rows, n, bigkeys_old):
            for sub in range(n // 128):
                xb, xbk = xtok[sub % 2], xtokk[sub % 2]
                dma("sp", xb, src_rows[sub * 128:(sub + 1) * 128, :], [], [xbk])
                for g4 in range(4):
                    pp, ppk = psC()
                    for j in range(4):
                        kc = g4 * 4 + j
                        tr(pp[:, j * 128:(j + 1) * 128], xb[:, kc * 128:(kc + 1) * 128], ident_f, [xbk, "cst"], [ppk])
                    evac(xT[:, g4 * 4:(g4 + 1) * 4, sub * 128:(sub + 1) * 128], pp.rearrange("p (j t) -> p j t", j=4), [ppk], ["xT"])

        tiles = [(0, NCX, 1)] + [(NCX + i * T, T, 0) for i in range(NTL)]
        big_all = [("BIG", i) for i in range(KF)]

        ropeTa = [ar.view(BIGo + 16640 + 2048 * i, [T], F32) for i in range(4)]
        qa_t = ar.view(BIGo + 24832, [8, T], BF16)
        kx_t = ar.view(BIGo + 33024, [4, T], BF16)
        va_t = ar.view(BIGo + 37120, [4, 2, 129], BF16)
        lat_t = ar.view(KRo, [4, T], F32)
        rename(big_all, ["xtok0", "ropeT", "qa_t", "kx_t", "va_t", "lat_t"] + xtokk)
        S.add("dve", lambda e: e.memset(va_t[:, :, :, 128:129], 1.0), ["va_t"], ["va_t"])

        def rope_evac(px, pxk, psw, pswk, ct, st, o, n, rk, wk_, sl=slice(0, 128)):
            t1, t1k = gtmp()
            tt("dve", t1[sl, 0:n], px[:, 0:n], ct[:, 0:n], ALU.mult, [pxk] + rk, [t1k])
            t2, t2k = gtmp()
            tt("dve", t2[sl, 0:n], psw[:, 0:n], st[:, 0:n], ALU.mult, [pswk] + rk, [t2k])
            tt("dve", o, t1[sl, 0:n], t2[sl, 0:n], ALU.add, [t1k, t2k], wk_)

        for (t0, n, isc) in tiles:
            l = 0
            if isc:
                load_x_transposed(ctx_in, n, None)
            else:
                load_x_transposed(x_in[t0 - NCX:t0 - NCX + n], n, None)
                p0 = t0 - NCX
                dma("sp", ropeTa[0][:, 0:n], ropeA_in[0][:, p0:p0 + n], [], [("ropeT", 0)])
                dma("sp", ropeTa[1][:, 0:n], ropeA_in[1][:, p0:p0 + n], [], [("ropeT", 1)])
                dma("sp", ropeTa[2][:, 0:n], ropeB_in[0][:, p0:p0 + n], [], [("ropeT", 2)])
                dma("sp", ropeTa[3][:, 0:n], ropeB_in[1][:, p0:p0 + n], [], [("ropeT", 3)])
            dma("pool", XT[:, :, t0:t0 + n].rearrange("k p t -> p k t"), xT[:, :, 0:n], ["xT"], [("XT", t0)])
            norm_fm(xT[:, :, 0:n], n, KD, "xT", lambda kc: effcol(0, 0, kc, isc), lambda kc: effcol(0, 1, kc, isc),
                    lambda kc: hT[:, kc, 0:n], lambda kc: "hT", [effk(0, 0), effk(0, 1)])
            for blk in range(3):
                ncol = 512 if blk < 2 else 256
                wv, wk = wload(abin_b[:, blk * 512:blk * 512 + 512].rearrange("(k p) c -> p k c", p=128), [KD, 512], wkeys["abin"])
                if not isc:
                    sv_, swk = wload(abin_s[:, blk * 512:blk * 512 + ncol].rearrange("(k p) c -> p k c", p=128), [KD, ncol], wkeys["abin_s"])
                for j in range(ncol // 128):
                    oc = blk * 4 + j
                    dst = qa_t[:, oc, 0:n] if oc < 8 else kx_t[:, oc - 8, 0:n]
                    dk_ = "qa_t" if oc < 8 else "kx_t"
                    px, pxk = psA()
                    for kc in range(KD):
                        mm(px[:, 0:n], wv[:, kc, j * 128:(j + 1) * 128], hT[:, kc, 0:n], kc == 0, kc == KD - 1, [wk, "hT"], [pxk])
                    if isc:
                        evac(dst, px[:, 0:n], [pxk], [dk_])
                    else:
                        pw, pwk = psA()
                        for kc in range(KD):
                            mm(pw[:, 0:n], sv_[:, kc, j * 128:(j + 1) * 128], hT[:, kc, 0:n], kc == 0, kc == KD - 1, [swk, "hT"], [pwk])
                        rope_evac(px, pxk, pw, pwk, ropeTa[0], ropeTa[1], dst, n, [("ropeT", 0), ("ropeT", 1)], [dk_])
                if blk == 2:
                    for tb in range(n // 128):
                        pv, pvk = psA()
                        for kc in range(KD):
                            mm(pv[:, 0:256], hT[:, kc, tb * 128:(tb + 1) * 128], wv[:, kc, 256:512], kc == 0, kc == KD - 1, [wk, "hT"], [pvk])
                        evac(va_t[:, tb, :, 0:128], pv[:, 0:256].rearrange("p (g c) -> p g c", g=2), [pvk], ["va_t"])
            dma("pool", QA[:, :, t0:t0 + n].rearrange("h p t -> p h t"), qa_t[:, :, 0:n], ["qa_t"], [("QA", t0)])
            dma("pool", KA[:, :, t0:t0 + n].rearrange("h p t -> p h t"), kx_t[:, 0:2, 0:n], ["kx_t"], [("KA", t0)])
            b0 = t0 // 128
            dma("pool", VA[b0:b0 + n // 128].rearrange("b p g c -> p b g c"), va_t[:, 0:n // 128], ["va_t"], [("VA", t0)])
            wv, wk = wload(abin_b[:, 1536:2048].rearrange("(k p) c -> p k c", p=128), [KD, 512], wkeys["abin"])
            for j in range(4):
                px, pxk = psA()
                for kc in range(KD):
                    mm(px[:, 0:n], wv[:, kc, j * 128:(j + 1) * 128], hT[:, kc, 0:n], kc == 0, kc == KD - 1, [wk, "hT"], [pxk])
                evac(lat_t[:, j, 0:n], px[:, 0:n], [pxk], ["lat_t"])
            qln = ar.view(BIGo + 40960, [4, T], BF16)
            norm_fm(lat_t[:, :, 0:n], n, 4, "lat_t", lambda kc: qng[:, kc:kc + 1], None,
                    lambda kc: qln[:, kc, 0:n], lambda kc: "qln", ["qng"], sqs=4.0)
            qn_t = qa_t
            wv, wk = wload(wqn_b.rearrange("(k p) c -> p k c", p=128), [4, 1024], wkeys["wqn"])
            for h in range(8):
                px, pxk = psA()
                for kc in range(4):
                    mm(px[:, 0:n], wv[:, kc, h * 128:(h + 1) * 128], qln[:, kc, 0:n], kc == 0, kc == 3, [wk, "qln"], [pxk])
                evac(qn_t[:, h, 0:n], px[:, 0:n], [pxk], ["qa_t"])
            dma("pool", QN[:, :, t0:t0 + n].rearrange("h p t -> p h t"), qn_t[:, :, 0:n], ["qa_t"], [("QN", t0)])
            wv, wk = wload(wqr_b.rearrange("(k p) c -> p k c", p=128), [4, 1024], wkeys["wqr"])
            qr_t = qa_t
            S.add("dve", lambda e: e.memset(qr_t[:, :, 0:n], 0.0), [], ["qa_t"])
            for m in range(4):
                px, pxk = psA()
                for kc in range(4):
                    mm(px[:, 0:n], wv[:, kc, m * 128:(m + 1) * 128], qln[:, kc, 0:n], kc == 0, kc == 3, [wk, "qln"], [pxk])
                if isc:
                    for hh in range(2):
                        evac(qr_t[64 * hh:64 * hh + 64, 2 * m + hh, 0:n], px[64 * hh:64 * hh + 64, 0:n], [pxk], ["qa_t"])
                else:
                    pw, pwk = psA()
                    for kc in range(4):
                        mm(pw[:, 0:n], wv[:, kc, 512 + m * 128:512 + (m + 1) * 128], qln[:, kc, 0:n], kc == 0, kc == 3, [wk, "qln"], [pwk])
                    for hh in range(2):
                        sl = slice(64 * hh, 64 * hh + 64)
                        rope_evac(px[sl], pxk, pw[sl], pwk, ropeTa[2][sl], ropeTa[3][sl], qr_t[sl, 2 * m + hh, 0:n], n,
                                  [("ropeT", 2), ("ropeT", 3)], ["qa_t"], sl)
            dma("pool", QR[:, :, t0:t0 + n].rearrange("h p t -> p h t"), qr_t[:, :, 0:n], ["qa_t"], [("QR", t0)])
            wv, wk = wload(abin_b[:, 2048:2304].rearrange("(k p) c -> p k c", p=128), [KD, 256], wkeys["abin"])
            for j in range(2):
                px, pxk = psA()
                for kc in range(KD):
                    mm(px[:, 0:n], wv[:, kc, j * 128:(j + 1) * 128], hT[:, kc, 0:n], kc == 0, kc == KD - 1, [wk, "hT"], [pxk])
                evac(lat_t[:, j, 0:n], px[:, 0:n], [pxk], ["lat_t"])
            kvn = qln
            norm_fm(lat_t[:, 0:2, 0:n], n, 2, "lat_t", lambda kc: kvng[:, kc:kc + 1], None,
                    lambda kc: kvn[:, kc, 0:n], lambda kc: "qln", ["kvng"], sqs=8.0)
            wv, wk = wload(abin_kr.rearrange("(k p) c -> p k c", p=128), [KD, 256], wkeys["abin_kr"])
            px, pxk = psA()
            for kc in range(KD):
                mm(px[:, 0:n], wv[:, kc, 0:128], hT[:, kc, 0:n], kc == 0, kc == KD - 1, [wk, "hT"], [pxk])
            kr_t = qa_t[:, 0, :]
            if isc:
                evac(kr_t[:, 0:n], px[:, 0:n], [pxk], ["qa_t"])
            else:
                pw, pwk = psA()
                for kc in range(KD):
                    mm(pw[:, 0:n], wv[:, kc, 128:256], hT[:, kc, 0:n], kc == 0, kc == KD - 1, [wk, "hT"], [pwk])
                rope_evac(px, pxk, pw, pwk, ropeTa[2], ropeTa[3], kr_t[:, 0:n], n, [("ropeT", 2), ("ropeT", 3)], ["qa_t"])
            dma("pool", KRS[:, t0:t0 + n], kr_t[:, 0:n], ["qa_t"], [("KRS", t0)])
            wv, wk = wload(wkn_b.rearrange("(k p) c -> p k c", p=128), [2, 1024], wkeys["wkn"])
            kn_full = ar.view(BIGo, [8, T], BF16)
            for h in range(8):
                px, pxk = psA()
                for kc in range(2):
                    mm(px[:, 0:n], wv[:, kc, h * 128:(h + 1) * 128], kvn[:, kc, 0:n], kc == 0, kc == 1, [wk, "qln"], [pxk])
                evac(kn_full[:, h, 0:n], px[:, 0:n], [pxk], [xtokk[0]])
            dma("pool", KN[:, :, t0:t0 + n].rearrange("h p t -> p h t"), kn_full[:, :, 0:n], [xtokk[0]], [("KN", t0)])
            vm_t = ar.view(BIGo + 8192, [4, 8, 129], BF16)
            S.add("dve", lambda e: e.memset(vm_t[:, :, :, 128:129], 1.0), [xtokk[1]], [xtokk[1]])
            wv, wk = wload(wkv_b.rearrange("(k p) c -> p k c", p=128), [2, 1024], wkeys["wkv"])
            for tb in range(n // 128):
                for nb in range(2):
                    pv, pvk = psA()
                    for kc in range(2):
                        mm(pv, kvn[:, kc, tb * 128:(tb + 1) * 128], wv[:, kc, nb * 512:(nb + 1) * 512], kc == 0, kc == 1, [wk, "qln"], [pvk])
                    evac(vm_t[:, tb, nb * 4:(nb + 1) * 4, 0:128], pv.rearrange("p (h c) -> p h c", h=4), [pvk], [xtokk[1]])
            for tb in range(n // 128):
                dma("pool", VM[:, :, b0 + tb, :].rearrange("h p c -> p h c"), vm_t[:, tb], [xtokk[1]], [("VM", t0, tb)])

        conv("retin", retin_b, retin_in, D, rb=128)
        conv("retout", retout_b, retout_in, 4096)
        conv("wg1", wg_b[1], wg_in[1], D)
        conv("wu1", wu_b[1], wu_in[1], D)
        conv("wd1", wd_b[1], wd_in[1], FF)

        allk = lambda nm: [(nm, t0) for (t0, n, isc) in tiles]
        krs = ar.view(KRo, [NTOK], BF16)
        qa_b = ar.view(BIGo, [8, T], BF16)
        qn_b = ar.view(BIGo + 8192, [8, T], BF16)
        qr_b = ar.view(BIGo + 16384, [8, T], BF16)
        o_t = ar.view(BIGo + 24576, [4, 16, 128], BF16)
        kA_w = ar.view(BIGo + 40960, [2, 768], BF16)
        kA_c = ar.view(BIGo + 44032, [2, 256], BF16)
        vA_c = ar.view(VACo, [2, 2, 129], BF16)
        vA_w = ar.view(TMPo, [6, 2, 129], BF16)
        oT = hT
        dsm = small(8)
        rename(["xT", "hT", "qa_t", "kx_t", "va_t", "lat_t", "qln", "ropeT"] + xtokk + [("ropeT", i) for i in range(4)],
               ["krs", "qa_b", "qn_b", "qr_b", "o_t", "kA_w", "kA_c", "vA_c"] + big_all)
        dma("sp", krs, KRS, allk("KRS"), ["krs"])
        SC_A = float(128 ** -0.5)
        SC_B = float(192 ** -0.5)

        def finish_head(pv, pvk, hidx, qb, sink_col):
            den, dk_ = dsm, "dsm"
            if sink_col is not None:
                ts("dve", den[:, 0:1], pv[:, 128:129], sink_col, None, ALU.add, None, [pvk, "esink"], [dk_])
                S.add("dve", lambda e: e.reciprocal(out=den[:, 0:1], in_=den[:, 0:1]), [dk_], [dk_])
            else:
                S.add("dve", lambda e: e.reciprocal(out=den[:, 0:1], in_=pv[:, 128:129]), [pvk], [dk_])
            act(o_t[:, qb, hidx, :], pv[:, 0:128], AF.Copy, [pvk, dk_], ["o_t"], scale=den[:, 0:1])

        for (t0, n, isc) in tiles:
            nqb = n // 128
            dma("sp", xT[:, :, 0:n], XT[:, :, t0:t0 + n].rearrange("k p t -> p k t"), [("XT", t0)], ["xT"])
            dma("sp", qa_b[:, :, 0:n], QA[:, :, t0:t0 + n].rearrange("h p t -> p h t"), allk("QA"), ["qa_b"])
            dma("sp", qn_b[:, :, 0:n], QN[:, :, t0:t0 + n].rearrange("h p t -> p h t"), allk("QN"), ["qn_b"])
            dma("sp", qr_b[:, :, 0:n], QR[:, :, t0:t0 + n].rearrange("h p t -> p h t"), allk("QR"), ["qr_b"])
            dma("sp", kA_c, KA[:, :, 0:NCX].rearrange("h p t -> p h t"), allk("KA"), ["kA_c"])
            dma("sp", vA_c, VA[0:2].rearrange("b p g c -> p b g c"), allk("VA"), ["vA_c"])
            if not isc:
                lb0 = (t0 - NCX) // 128
                wlo = max(lb0 - 1, 0)
                whi = min(lb0 + nqb + 1, NCH)
                nw = whi - wlo
                dma("sp", kA_w[:, :, 0:nw * 128], KA[:, :, NCX + wlo * 128:NCX + whi * 128].rearrange("h p t -> p h t"), allk("KA"), ["kA_w"])
                dma("sp", vA_w[:, 0:nw], VA[2 + wlo:2 + whi].rearrange("b p g c -> p b g c"), allk("VA"), [tmpk[0], tmpk[1]])
            for g in range(2):
                for qb in range(nqb):
                    kbs = [("c", 0), ("c", 1)]
                    if not isc:
                        lb = lb0 + qb
                        for dlt in (-1, 0, 1):
                            if 0 <= lb + dlt < NCH:
                                kbs.append(("l", lb + dlt - wlo, dlt))
                    pvs = [psA() for _ in range(4)]
                    for ki, kb in enumerate(kbs):
                        sps, spk = psB()
                        if kb[0] == "c":
                            kl = kA_c[:, g, kb[1] * 128:(kb[1] + 1) * 128]
                            vv = vA_c[:, kb[1], g, :]
                            kr_, vr_ = ["kA_c"], ["vA_c"]
                        else:
                            kl = kA_w[:, g, kb[1] * 128:(kb[1] + 1) * 128]
                            vv = vA_w[:, kb[1], g, :]
                            kr_, vr_ = ["kA_w"], [tmpk[0], tmpk[1]]
                        mm(sps.rearrange("p (h q) -> p h q", h=4), kl, qa_b[:, 4 * g:4 * g + 4, qb * 128:(qb + 1) * 128], True, True,
                           kr_ + ["qa_b"], [spk])
                        p_, pk = gpt()
                        act(p_, sps, AF.Exp, [spk], [pk], scale=SC_A)
                        if kb[0] == "l" and kb[2] != 0:
                            mk = maskLo if kb[2] == -1 else maskHi
                            mkk = "maskLo" if kb[2] == -1 else "maskHi"
                            p3 = p_.rearrange("p (h q) -> p h q", h=4)
                            tt("dve", p3, p3, mk.unsqueeze(1).to_broadcast([128, 4, 128]), ALU.mult, [pk, mkk], [pk])
                        for hh in range(4):
                            mm(pvs[hh][0][:, 0:129], p_[:, hh * 128:(hh + 1) * 128], vv, ki == 0, ki == len(kbs) - 1, [pk] + vr_, [pvs[hh][1]])
                    for hh in range(4):
                        finish_head(pvs[hh][0], pvs[hh][1], 4 * g + hh, qb, esink[:, 4 * g + hh:4 * g + hh + 1])
            nkb = 2 if isc else NBLK
            for h in range(8):
                knv, knk = wload(KN[h], [NTOK], allk("KN"))
                vmv, vmk = wload(VM[h], [NBLK, 129], [("VM", t0_, tb_) for (t0_, n_, i_) in tiles for tb_ in range(n_ // 128)])
                pvs = [psA() for _ in range(nqb)]

                def s_stage(kb):
                    sps, spk = psB()
                    mm(sps[:, 0:n], knv[:, kb * 128:(kb + 1) * 128], qn_b[:, h, 0:n], True, False, [knk, "qn_b"], [spk])
                    mm(sps[:, 0:n], krs[:, kb * 128:(kb + 1) * 128], qr_b[:, h, 0:n], False, True, ["krs", "qr_b"], [spk])
                    return sps, spk

                nxt_s = s_stage(0)
                for kb in range(nkb):
                    sps, spk = nxt_s
                    if kb + 1 < nkb:
                        nxt_s = s_stage(kb + 1)
                    p_, pk = gpt()
                    act(p_[:, 0:n], sps[:, 0:n], AF.Exp, [spk], [pk], scale=SC_B)
                    for qb in range(nqb):
                        mm(pvs[qb][0][:, 0:129], p_[:, qb * 128:(qb + 1) * 128], vmv[:, kb, :], kb == 0, kb == nkb - 1, [pk, vmk], [pvs[qb][1]])
                for qb in range(nqb):
                    finish_head(pvs[qb][0], pvs[qb][1], 8 + h, qb, None)
            for qb in range(nqb):
                for half in range(2):
                    pp, ppk = psC()
                    ppb = pp.bitcast(BF16)
                    for j in range(8):
                        fc = half * 8 + j
                        tr(ppb[:, j * 128:(j + 1) * 128], o_t[:, qb, fc, :], ident_b, ["o_t", "ident_b"], [ppk])
                    evac(oT[:, half * 8:(half + 1) * 8, qb * 128:(qb + 1) * 128], ppb.rearrange("p (j t) -> p j t", j=8), [ppk], ["hT"])
            for dg in range(4):
                wv, wk = wload(about_b[:, dg * 512:(dg + 1) * 512].rearrange("(k p) c -> p k c", p=128), [KD, 512], wkeys["about"])
                for j in range(4):
                    dc = dg * 4 + j
                    px, pxk = psA()
                    for fc in range(KD):
                        mm(px[:, 0:n], wv[:, fc, j * 128:(j + 1) * 128], oT[:, fc, 0:n], fc == 0, fc == KD - 1, [wk, "hT"], [pxk])
                    stt("dve", xT[:, dc, 0:n], px[:, 0:n], effcol(0, 2, dc, isc), xT[:, dc, 0:n], ALU.mult, ALU.add,
                        [pxk, effk(0, 2), "xT"], ["xT"])
            rename(["qa_b", "qn_b", "qr_b", "o_t", "kA_w", "kA_c", "vA_c"], big_all)
            ffn(0, n, [(0, n, isc)])
            rename(big_all, ["qa_b", "qn_b", "qr_b", "o_t", "kA_w", "kA_c", "vA_c"])
            dma("pool", XT[:, :, t0:t0 + n].rearrange("k p t -> p k t"), xT[:, :, 0:n], ["xT"], [("XT", t0)])

        if stage <= 2:
            for kc in range(KD):
                dma("sp", dbg[kc], XT[kc], allk("XT"), [("out", kc)])
            S.fence("sp", [("out", kc) for kc in range(KD)])
            S.emit()
            return nc


        pc = small(4)
        qdec = small(16)
        gst = small(8)
        dma("sp", pc, pc_in, [], ["pc"])
        act(lgs, lg, AF.Exp, ["lg"], ["lgs"], scale=-1.0)
        act(lgs, lgs, AF.Ln, ["lgs"], ["lgs"], bias=1.0)
        ts("dve", lgs, lgs, -1.0, None, ALU.mult, None, ["lgs"], ["lgs"])
        for dirn in range(2):
            for h in range(8):
                c = dirn * 8 + h
                act(kdec[:, c:c + 1], pc[:, dirn:dirn + 1], AF.Exp, ["pc", "lgs"], ["kdec"], scale=lgs[:, c:c + 1])
                act(qdec[:, c:c + 1], pc[:, 2 + dirn:3 + dirn], AF.Exp, ["pc", "lgs"], ["qdec"], scale=lgs[:, c:c + 1])
        act(cdec, lgs, AF.Exp, ["lgs"], ["cdec"], scale=128.0)
        DTv = ar.view(KRo, [2, 8, 128], F32)
        rename(["krs", "lat_t"], [("DT", 0), ("DT", 1)])
        for dirn in range(2):
            for h in range(8):
                c = dirn * 8 + h
                act(DTv[:, dirn, h, :], diffF if dirn == 0 else diffB, AF.Exp, ["cst", "lgs"], [("DT", dirn)], scale=lgs[:, c:c + 1])
                tt("dve", DTv[:, dirn, h, :], DTv[:, dirn, h, :], cst[:, 2 if dirn == 0 else 1, :], ALU.mult, [("DT", dirn), "cst"], [("DT", dirn)])

        def wbuf(shape, dtype):
            i = nxt("wp", NWP)
            return ar.view(WP0 + i * WPB, shape, dtype), ("wp", i)

        kp_t = [ar.view(XTo + 16384 * i, [4, D], BF16) for i in range(2)]
        qT_t = ar.view(BIGo, [4, KD, 128], BF16)
        kT_t = ar.view(BIGo + 16384, [4, KD, 128], BF16)
        vctx = ar.view(BIGo, [2, 4096], BF16)
        ev_t = [ar.view(BIGo + 32768 + 4096 * i, [4, 512], BF16) for i in range(2)]
        ropeRt = [ar.view(BIGo + 40960 + 2048 * i, [T], F32) for i in range(2)]
        rename(big_all + ["qa_b", "qn_b", "qr_b", "o_t", "kA_w", "kA_c", "vA_c"], ["qT_t", "kT_t", "ev0", "ev1", "ropeR"])
        allS = [("S", h) for h in range(8)]
        allSb = [("Sb", h) for h in range(8)]

        for (t0, n, isc) in tiles:
            ntb = n // 128
            ch0 = (t0 - NCX) // 128
            own = (not isc) and (t0 - NCX) < NOWN
            dma("sp", xT[:, :, 0:n], XT[:, :, t0:t0 + n].rearrange("k p t -> p k t"), [("XT", t0)], ["xT"])
            if not isc:
                p0 = t0 - NCX
                dma("sp", ropeRt[0][:, 0:n], ropeR_in[0][:, p0:p0 + n], [], ["ropeR"])
                dma("sp", ropeRt[1][:, 0:n], ropeR_in[1][:, p0:p0 + n], [], ["ropeR"])
            norm_fm(xT[:, :, 0:n], n, KD, "xT", lambda kc: effcol(1, 0, kc, isc), lambda kc: effcol(1, 1, kc, isc),
                    lambda kc: hT[:, kc, 0:n], lambda kc: "hT", [effk(1, 0), effk(1, 1)])
            for which in ((0, 1) if own else (1,)):
                dstT = qT_t if which == 0 else kT_t
                dkey = "qT_t" if which == 0 else "kT_t"
                scl = 1.0 if which == 0 else 0.0625
                for blk in range(4):
                    c0 = which * 2048 + blk * 512
                    wv, wk = wload(retin_b[:, c0:c0 + 512].rearrange("(k p) c -> p k c", p=128), [KD, 512], wkeys["retin"])
                    for hh in range(2):
                        h = blk * 2 + hh
                        pxs = []
                        for c in range(2):
                            px, pxk = psA()
                            j = hh * 2 + c
                            for kc in range(KD):
                                mm(px[:, 0:n], wv[:, kc, j * 128:(j + 1) * 128], hT[:, kc, 0:n], kc == 0, kc == KD - 1, [wk, "hT"], [pxk])
                            pxs.append((px, pxk))
                        d0 = dstT[:, 0:ntb, 2 * h, :]
                        d1 = dstT[:, 0:ntb, 2 * h + 1, :]
                        if isc:
                            for c, dd in ((0, d0), (1, d1)):
                                act(dd, pxs[c][0][:, 0:n].rearrange("p (b t) -> p b t", t=128), AF.Copy, [pxs[c][1]], [dkey], scale=scl)
                        else:
                            (x0, x0k), (x1, x1k) = pxs
                            cR, sR = ropeRt[0], ropeRt[1]
                            ta, tak = gtmp()
                            stt("dve", ta[:, 0:n], x0[:, 0:n], scl, cR[:, 0:n], ALU.mult, ALU.mult, [x0k, "ropeR"], [tak])
                            tb_, tbk = gtmp()
                            stt("dve", tb_[:, 0:n], x1[:, 0:n], scl, sR[:, 0:n], ALU.mult, ALU.mult, [x1k, "ropeR"], [tbk])
                            tt("pool", d0, ta[:, 0:n].rearrange("p (b t) -> p b t", t=128), tb_[:, 0:n].rearrange("p (b t) -> p b t", t=128),
                               ALU.subtract, [tak, tbk], [dkey])
                            tc_, tck = gtmp()
                            stt("dve", tc_[:, 0:n], x0[:, 0:n], scl, sR[:, 0:n], ALU.mult, ALU.mult, [x0k, "ropeR"], [tck])
                            td, tdk = gtmp()
                            stt("dve", td[:, 0:n], x1[:, 0:n], scl, cR[:, 0:n], ALU.mult, ALU.mult, [x1k, "ropeR"], [tdk])
                            tt("pool", d1, tc_[:, 0:n].rearrange("p (b t) -> p b t", t=128), td[:, 0:n].rearrange("p (b t) -> p b t", t=128),
                               ALU.add, [tck, tdk], [dkey])
            if own:
                dma("pool", QT1[ch0:ch0 + ntb].rearrange("c p k t -> p c (k t)"), qT_t[:, 0:ntb].rearrange("p c k t -> p c (k t)"), ["qT_t"], [("QT1", t0)])
                dma("pool", KT1[ch0:ch0 + ntb].rearrange("c p k t -> p c (k t)"), kT_t[:, 0:ntb].rearrange("p c k t -> p c (k t)"), ["kT_t"], [("KT1", t0)])
            rename(["xT"], ["kp0", "kp1"])
            for tb in range(ntb):
                for half in range(2):
                    pp, ppk = psC()
                    ppb = pp.bitcast(BF16)
                    for j in range(8):
                        kc = half * 8 + j
                        tr(ppb[:, j * 128:(j + 1) * 128], kT_t[:, tb, kc, :], ident_b, ["kT_t", "ident_b"], [ppk])
                    for dirn in ((0, 1) if (own or isc) else (1,)):
                        for hh in range(4):
                            h = half * 4 + hh
                            act(kp_t[dirn][:, tb, h * 256:(h + 1) * 256], ppb[:, hh * 256:(hh + 1) * 256], AF.Copy, [ppk, "kdec"], ["kp%d" % dirn],
                                scale=kdec[:, dirn * 8 + h:dirn * 8 + h + 1])
            if not isc:
                for dirn in ((0, 1) if own else (1,)):
                    dma("pool", KP1[dirn][ch0:ch0 + ntb].rearrange("c p f -> p c f"), kp_t[dirn][:, 0:ntb], ["kp%d" % dirn], [("KP1", dirn, t0)])
            for which in ((0, 1) if own else (0,)):
                for nb in range(8):
                    c0 = 4096 + which * 4096 + nb * 512
                    wv, wk = wload(retin_b[:, c0:c0 + 512].rearrange("(k p) c -> p k c", p=128), [KD, 512], wkeys["retin"])
                    ei = nxt("ev", 2)
                    evv, evk = ev_t[ei], "ev%d" % ei
                    for tb in range(ntb):
                        pv, pvk = psA()
                        for kc in range(KD):
                            mm(pv, hT[:, kc, tb * 128:(tb + 1) * 128], wv[:, kc, :], kc == 0, kc == KD - 1, [wk, "hT"], [pvk])
                        if isc:
                            evac(vctx[:, tb, nb * 512:(nb + 1) * 512], pv, [pvk], ["qT_t"])
                        elif which == 0:
                            evac(evv[:, tb, :], pv, [pvk], [evk])
                        else:
                            act(evv[:, tb, :], pv, AF.Silu, [pvk], [evk])
                    if not isc:
                        dst = V1 if which == 0 else G1
                        dma("pool", dst[ch0:ch0 + ntb, :, nb * 512:(nb + 1) * 512].rearrange("c p f -> p c f"), evv[:, 0:ntb], [evk],
                            [("V1" if which == 0 else "G1", t0, nb)])
            if isc:
                for dirn in range(2):
                    halves = [wbuf([8, 512], F32) for _ in range(2)]
                    for hv, hk in halves:
                        S.add("dve", lambda e, hv=hv: e.memset(hv, 0.0), [], [hk])
                    for tb in ((0, 1) if dirn == 0 else (1, 0)):
                        for h in range(8):
                            hv, hk = halves[h // 4]
                            for c in range(2):
                                pst, pstk = psB()
                                mm(pst, kp_t[dirn][:, tb, h * 256 + c * 128:h * 256 + (c + 1) * 128], vctx[:, tb, h * 512:(h + 1) * 512], True, True,
                                   ["kp%d" % dirn, "qT_t"], [pstk])
                                sl = hv[:, (h % 4) * 2 + c, :]
                                stt("dve", sl, sl, cdec[:, dirn * 8 + h:dirn * 8 + h + 1], pst, ALU.mult, ALU.add, [pstk, "cdec", hk], [hk])
                    for i, (hv, hk) in enumerate(halves):
                        dma("pool", S0[dirn][:, i * 8:(i + 1) * 8, :], hv, [hk], [("S0", dirn, i)])
            rename(["kp0", "kp1"], ["xT"])

        S_v = ar.view(XTo, [16, 512], F32)
        Sb_v = ar.view(BAo, [16, 512], BF16)
        cbq = [ar.view(BIGo + 20480 * i, [KD, 128], BF16) for i in range(2)]
        cbk = [ar.view(BIGo + 20480 * i + 4096, [KD, 128], BF16) for i in range(2)]
        cbp = [ar.view(BIGo + 20480 * i + 8192, [D], BF16) for i in range(2)]
        cbv = [ar.view(BIGo + 20480 * i + 12288, [4096], BF16) for i in range(2)]
        QRW = ar.view(KR2o, [2, 8, 128], F32)
        qpb = [ar.view(BIGo + 40960 + 512 * i, [2, 128], BF16) for i in range(4)]
        gsum = small(8)
        gsq = small(8)
        gm = small(8)
        gr = small(8)
        for dirn in range(2):
            for h in range(8):
                c = dirn * 8 + h
                act(QRW[:, dirn, h, :], posr if dirn == 0 else cst[:, 6, :], AF.Exp, ["cst", "lgs"], [("QRW", dirn)], scale=lgs[:, c:c + 1])
        ltiles = [t for t in tiles if (not t[2]) and (t[0] - NCX) < NOWN]
        altiles = [t for t in tiles if not t[2]]
        kQ = [("QT1", t[0]) for t in ltiles]
        kK = [("KT1", t[0]) for t in ltiles]
        kV = [("V1", t[0], nb) for t in altiles for nb in range(8)]
        kG = [("G1", t[0], nb) for t in ltiles for nb in range(8)]
        rename(["xT", "hT", "qT_t", "kT_t", "ev0", "ev1", "ropeR"],
               allS + allSb + [("cb", i, w) for i in range(2) for w in "qkpv"] + [("qp", i) for i in range(4)])
        ctr["cb"] = 0
        ctr["qp"] = 0
        for dirn in range(2):
            kP = [("KP1", dirn, t[0]) for t in (ltiles if dirn == 0 else altiles)]
            dma("sp", S_v, S0[dirn], [("S0", dirn, 0), ("S0", dirn, 1)], allS)
            for h in range(8):
                cp("pool", Sb_v[:, 2 * h:2 * h + 2, :], S_v[:, 2 * h:2 * h + 2, :], [("S", h)], [("Sb", h)])
            order = list(range(NOCH)) if dirn == 0 else list(range(NCH - 1, -1, -1))
            for oi, n_ in enumerate(order):
                bi = nxt("cb", 2)
                qc, kc_, kpc, vc = cbq[bi], cbk[bi], cbp[bi], cbv[bi]
                qk, kk_, pk_, vk = [("cb", bi, w) for w in "qkpv"]
                dma("sp", kpc, KP1[dirn][n_], kP, [pk_])
                dma("sp", vc, V1[n_], kV, [vk])
                if n_ >= NOCH:
                    for h in range(8):
                        c16 = dirn * 8 + h
                        for c in range(2):
                            pst, pstk = psC()
                            mm(pst, kpc[:, h * 256 + c * 128:h * 256 + (c + 1) * 128], vc[:, h * 512:(h + 1) * 512], True, True, [pk_, vk], [pstk])
                            stt("dve", S_v[:, 2 * h + c, :], S_v[:, 2 * h + c, :], cdec[:, c16:c16 + 1], pst, ALU.mult, ALU.add,
                                [pstk, "cdec", ("S", h)], [("S", h)])
                        if n_ == NOCH:
                            cp("act", Sb_v[:, 2 * h:2 * h + 2, :], S_v[:, 2 * h:2 * h + 2, :], [("S", h)], [("Sb", h)])
                    continue
                dma("sp", qc, QT1[n_], kQ, [qk])
                dma("sp", kc_, KT1[n_], kK, [kk_])
                if dirn == 1:
                    ofv, ofk = wload(OF1[n_], [4096], [("OF1", n_, h) for h in range(8)], F32)
                    gv_, gk_ = wload(G1[n_], [4096], kG)
                    zc, zk = wbuf([4096], BF16)
                    S.add("dve", lambda e: e.memset(gsum, 0.0), [], [("gs", h) for h in range(8)])
                    S.add("dve", lambda e: e.memset(gsq, 0.0), [], [("gq", h) for h in range(8)])
                last = oi == len(order) - 1

                def st1(h):
                    pa, pak = psB()
                    for c in range(2):
                        mm(pa[:, 0:128], kc_[:, 2 * h + c, :], qc[:, 2 * h + c, :], c == 0, c == 1, [kk_, qk], [pak])
                    qi = nxt("qp", 4)
                    tt("pool", qpb[qi], qc[:, 2 * h:2 * h + 2, :], QRW[:, dirn, h, :].unsqueeze(1).to_broadcast([128, 2, 128]), ALU.mult,
                       [qk, ("QRW", dirn)], [("qp", qi)])
                    return pa, pak, qpb[qi], ("qp", qi)

                def st2(h, pa, pak, qpv, qpk):
                    c16 = dirn * 8 + h
                    am, amk = gpt()
                    tt("dve", am[:, 0:128], pa[:, 0:128], DTv[:, dirn, h, :], ALU.mult, [pak, ("DT", dirn)], [amk])
                    po_, pok = psA()
                    mm(po_, am[:, 0:128], vc[:, h * 512:(h + 1) * 512], True, False, [amk, vk], [pok])
                    for c in range(2):
                        mm(po_, qpv[:, c, :], Sb_v[:, 2 * h + c, :], False, c == 1, [qpk, ("Sb", h)], [pok])
                    psts = []
                    if not last:
                        for c in range(2):
                            pst, pstk = psC()
                            mm(pst, kpc[:, h * 256 + c * 128:h * 256 + (c + 1) * 128], vc[:, h * 512:(h + 1) * 512], True, True, [pk_, vk], [pstk])
                            psts.append((pst, pstk))
                    if dirn == 0:
                        t1, t1k = gtmp()
                        cp("act", t1, po_, [pok], [t1k])
                        dma("pool", OF1[n_][:, h * 512:(h + 1) * 512], t1, [t1k], [("OF1", n_, h)])
                    else:
                        oh = ofv[:, h * 512:(h + 1) * 512]
                        tt("dve", oh, oh, po_, ALU.add, [pok, ofk], [ofk])
                        j1, j1k = gtmp()
                        act(j1, oh, AF.Copy, [ofk, ("gs", h)], [j1k, ("gs", h)], accum=gsum[:, h:h + 1])
                        j2, j2k = gtmp()
                        act(j2, oh, AF.Square, [ofk, ("gq", h)], [j2k, ("gq", h)], accum=gsq[:, h:h + 1])
                    for c, (pst, pstk) in enumerate(psts):
                        stt("dve", S_v[:, 2 * h + c, :], S_v[:, 2 * h + c, :], cdec[:, c16:c16 + 1], pst, ALU.mult, ALU.add,
                            [pstk, "cdec", ("S", h)], [("S", h)])
                    if psts:
                        cp("act", Sb_v[:, 2 * h:2 * h + 2, :], S_v[:, 2 * h:2 * h + 2, :], [("S", h)], [("Sb", h)])

                cur = st1(0)
                for h in range(8):
                    nx_ = st1(h + 1) if h < 7 else None
                    st2(h, *cur)
                    cur = nx_
                if dirn == 1:
                    gsk = [("gs", h) for h in range(8)] + [("gq", h) for h in range(8)]
                    ts("dve", gm, gsum, 1.0 / 512, None, ALU.mult, None, gsk, ["gm"])
                    tt("dve", gr, gm, gm, ALU.mult, ["gm"], ["gr"])
                    stt("dve", gr, gsq, 1.0 / 512, gr, ALU.mult, ALU.subtract, gsk + ["gr"], ["gr"])
                    act(gr, gr, AF.Sqrt, ["gr"], ["gr"], bias=EPS)
                    S.add("dve", lambda e: e.reciprocal(out=gr, in_=gr), ["gr"], ["gr"])
                    for h in range(8):
                        y_, yk = gtmp()
                        stt("dve", y_, ofv[:, h * 512:(h + 1) * 512], gm[:, h:h + 1], gv_[:, h * 512:(h + 1) * 512], ALU.subtract, ALU.mult,
                            [ofk, "gm", gk_], [yk])
                        act(zc[:, h * 512:(h + 1) * 512], y_, AF.Copy, [yk, "gr"], [zk], scale=gr[:, h:h + 1])
                    dma("pool", Z1[n_], zc, [zk], [("Z1", n_)])

        zT = ar.view(BIGo, [32, T], BF16)
        yT = ar.view(BIGo, [KD, T], F32)
        orow = ar.view(BIGo + 32768, [D], F32)
        rename(allS + allSb + [("cb", i, w) for i in range(2) for w in "qkpv"] + [("qp", i) for i in range(4)], ["xT", "hT", "zT"])
        for (t0, n, isc) in ltiles:
            ntb = n // 128
            ch0 = (t0 - NCX) // 128
            dma("sp", xT[:, :, 0:n], XT[:, :, t0:t0 + n].rearrange("k p t -> p k t"), [("XT", t0)], ["xT"])
            for tb in range(ntb):
                zt, ztk = wload(Z1[ch0 + tb], [4096], [("Z1", ch0 + tb)])
                for g8 in range(4):
                    pp, ppk = psC()
                    ppb = pp.bitcast(BF16)
                    for j in range(8):
                        fc = g8 * 8 + j
                        tr(ppb[:, j * 128:(j + 1) * 128], zt[:, fc * 128:(fc + 1) * 128], ident_b, [ztk, "ident_b"], [ppk])
                    for j in range(8):
                        fc = g8 * 8 + j
                        act(zT[:, fc, tb * 128:(tb + 1) * 128], ppb[:, j * 128:(j + 1) * 128], AF.Copy, [ppk, "gng"], ["zT"], scale=gng[:, fc:fc + 1])
            for dg in range(4):
                pd = [psA() for _ in range(4)]
                for half in range(2):
                    wv, wk = wload(retout_b[half * 2048:(half + 1) * 2048, dg * 512:(dg + 1) * 512].rearrange("(k p) c -> p k c", p=128),
                                   [KD, 512], wkeys["retout"])
                    for j in range(4):
                        for f16 in range(KD):
                            fc = half * 16 + f16
                            mm(pd[j][0][:, 0:n], wv[:, f16, j * 128:(j + 1) * 128], zT[:, fc, 0:n], fc == 0, fc == 31, [wk, "zT"], [pd[j][1]])
                for j in range(4):
                    dc = dg * 4 + j
                    stt("dve", xT[:, dc, 0:n], pd[j][0][:, 0:n], effcol(1, 2, dc, 0), xT[:, dc, 0:n], ALU.mult, ALU.add,
                        [pd[j][1], effk(1, 2), "xT"], ["xT"])
            rename(["zT"], big_all)
            ffn(1, n, [(0, n, 0)])
            rename(big_all, ["yT", "orow"])
            norm_fm(xT[:, :, 0:n], n, KD, "xT", lambda kc: fng[:, kc:kc + 1], None,
                    lambda kc: yT[:, kc, 0:n], lambda kc: "yT", ["fng"])
            for tb in range(ntb):
                for g4 in range(4):
                    pp, ppk = psC()
                    for j in range(4):
                        kc = g4 * 4 + j
                        tr(pp[:, j * 128:(j + 1) * 128], yT[:, kc, tb * 128:(tb + 1) * 128], ident_f, ["yT", "cst"], [ppk])
                    evac(orow[:, g4 * 512:(g4 + 1) * 512], pp, [ppk], ["orow"])
                r0 = t0 - NCX + tb * 128
                dma("pool", out[r0:r0 + 128, :], orow, ["orow"], [("out", r0)])
            rename(["yT", "orow"], ["zT"])
        S.fence("sp", [("out", r0) for r0 in range(0, NOWN, 128)])
        S.emit()
    return nc


def _pp(v):
    v = np.asarray(v, np.float32)
    return np.ascontiguousarray(v.reshape(-1, 128).T)


def _rope_tables(n_tok, rot_dim, mode):
    GRID_W = 64
    row = np.repeat(np.arange(n_tok // GRID_W, dtype=np.float32), GRID_W)
    col = (np.arange(n_tok) % GRID_W).astype(np.float32)
    n_freq = rot_dim // 4
    inv = (np.float32(10000.0) ** (-np.arange(n_freq, dtype=np.float32) / np.float32(n_freq))).astype(np.float32)
    ang = np.concatenate([row[:, None] * inv, col[:, None] * inv], axis=-1).astype(np.float32)
    c = np.cos(ang).astype(np.float32).T
    s = np.sin(ang).astype(np.float32).T
    if mode == "half":
        C = np.concatenate([c, c], 0)
        Sg = np.concatenate([-s, s], 0)
        rep = 128 // C.shape[0]
        return np.stack([np.tile(C, (rep, 1)), np.tile(Sg, (rep, 1))]).astype(np.float32)
    return np.stack([c, s]).astype(np.float32)


def _consts():
    p = np.arange(128)
    ident = np.eye(128, dtype=np.float32)
    maskLo = (p[:, None] >= p[None, :]).astype(np.float32)
    maskHi = (p[:, None] <= p[None, :]).astype(np.float32)
    diffF = np.maximum(p[None, :] - p[:, None], 0).astype(np.float32)
    diffB = np.maximum(p[:, None] - p[None, :], 0).astype(np.float32)
    pos = np.tile((p + 1).astype(np.float32)[None, :], (128, 1))
    posb = np.tile((128 - p).astype(np.float32)[None, :], (128, 1))
    return np.ascontiguousarray(np.concatenate([ident, maskLo, maskHi, diffF, diffB, pos, posb], 1))


def prep_core(inp, b, NL, flip=False):
    f = lambda a: np.ascontiguousarray(np.asarray(a, np.float32))
    fl = (lambda a, ax: np.flip(a, axis=ax)) if flip else (lambda a, ax: a)
    d = {}
    d["x"] = f(fl(np.asarray(inp["x"][b][:NL]), 0))
    d["ctx"] = f(fl(np.asarray(inp["ctx"][b]), 0))
    cvv = np.stack([_pp(inp["c"][b]), _pp(inp["c_ctx"])], -1).reshape(128, 32)
    d["cv"] = f(cvv)
    d["mod_w"] = f(inp["mod_w"])
    d["modb"] = f(np.concatenate([_pp(inp["mod_b"][l]) for l in range(2)], 1))
    d["nmg"] = f(np.concatenate([_pp(inp["norm_mix_g"][l]) for l in range(2)], 1))
    d["nfg"] = f(np.concatenate([_pp(inp["norm_ffn_g"][l]) for l in range(2)], 1))
    d["fng"] = _pp(inp["final_norm_g"])
    d["wg"] = f(inp["ffn_w_gate"])
    d["wu"] = f(inp["ffn_w_up"])
    d["wd"] = f(inp["ffn_w_down"])
    d["abin"] = f(inp["ab_w_in"][0])
    d["about"] = f(inp["ab_w_out"][0])
    d["sink"] = f(np.tile(np.asarray(inp["swa_sink"][0], np.float32)[None, :], (128, 1)))
    d["qng"] = _pp(inp["mla_q_norm_g"][0])
    d["wqb"] = f(inp["mla_w_q_b"][0])
    d["kvng"] = _pp(inp["mla_kv_norm_g"][0])
    d["wkvb"] = f(inp["mla_w_kv_b"][0])
    d["retin"] = f(inp["ret_w_in"][0])
    lf = np.asarray(inp["ret_decay_logit_fwd"][0], np.float32)
    lb = np.asarray(inp["ret_decay_logit_bwd"][0], np.float32)
    lgv = np.concatenate([lb, lf]) if flip else np.concatenate([lf, lb])
    d["lg"] = f(np.tile(lgv[None, :], (128, 1)))
    d["gng"] = _pp(inp["ret_gn_g"][0])
    d["retout"] = f(inp["ret_w_out"][0])
    d["ropeA"] = f(fl(_rope_tables(NL, 128, "half"), 2))
    d["ropeB"] = f(fl(_rope_tables(NL, 64, "half"), 2))
    d["ropeR"] = f(fl(_rope_tables(NL, 256, "plain"), 2))
    d["cst"] = _consts()
    p = np.arange(128, dtype=np.float32)
    d["pc"] = np.ascontiguousarray(np.stack([127 - p, p, p + 1, 128 - p], 1).astype(np.float32))
    return d


_CACHE = {}


def kernel(**inputs):
    NL = 4096
    NOWN = 2048
    B = 4
    if "nc" not in _CACHE:
        _CACHE["nc"] = build(NL, NOWN=NOWN)
    nc = _CACHE["nc"]
    in_maps = [prep_core(inputs, c // 2, NL, flip=bool(c % 2)) for c in range(8)]
    res = run_bass_kernel_spmd(nc, in_maps, core_ids=list(range(8)))
    outp = np.empty((B, NL, D), np.float32)
    for b in range(B):
        outp[b, :NOWN] = res.results[2 * b]["out"]
        outp[b, NL - NOWN:] = res.results[2 * b + 1]["out"][::-1]
    return outp
```

```python
from contextlib import ExitStack

import numpy as np
import concourse.bass as bass
import concourse.mybir as mybir
from concourse.bass_utils import run_bass_kernel_spmd

F32 = mybir.dt.float32
BF16 = mybir.dt.bfloat16
AF = mybir.ActivationFunctionType
ALU = mybir.AluOpType

D = 2048
KD = 16
FF = 5632
KF = 44
NCX = 256
T = 512
EPS = 1e-6
COMPUTE = ("pe", "act", "dve", "pool")
ENGS = ("pe", "act", "dve", "pool", "sp")


class Sched:
    def __init__(self, nc, es, n_dma_sems=(("sp", 32), ("pool", 12), ("act", 4))):
        self.nc = nc
        self.streams = {e: [] for e in ENGS}
        self.esem = {e: es.enter_context(nc.semaphore("s_" + e)) for e in COMPUTE}
        self.ecount = {e: 0 for e in COMPUTE}
        self.dsems = {q: [es.enter_context(nc.semaphore("d%s%d" % (q, i))) for i in range(n)] for q, n in n_dma_sems}
        self.dcount = {q: [0] * n for q, n in n_dma_sems}
        self.dnext = {q: 0 for q, n in n_dma_sems}
        self.last_w = {}
        self.readers = {}
        self.seen = {e: {} for e in ENGS}
        self.semobj = {}
        self.n_ops = 0

    def _need(self, eng, tok, waits):
        if tok is None:
            return
        sk, val, teng = tok
        if teng == "pe" and eng == "pe":
            return
        if self.seen[eng].get(sk, 0) >= val:
            return
        self.seen[eng][sk] = val
        waits.append((sk, val))

    def add(self, eng, fn, reads=(), writes=(), dma=False):
        waits = []
        for k in reads:
            self._need(eng, self.last_w.get(k), waits)
        for k in writes:
            self._need(eng, self.last_w.get(k), waits)
            for sk, (val, teng) in self.readers.get(k, {}).items():
                self._need(eng, (sk, val, teng), waits)
        if dma:
            i = self.dnext[eng]
            self.dnext[eng] = (i + 1) % len(self.dsems[eng])
            sk = ("d", eng, i)
            dc = self.dcount[eng]
            if dc[i] > 0:
                self._need(eng, (sk, dc[i], "dma"), waits)
            dc[i] += 16
            tok = (sk, dc[i], "dma")
            self.semobj[sk] = self.dsems[eng][i]
            inc = 16
        else:
            self.ecount[eng] += 1
            sk = ("e", eng)
            tok = (sk, self.ecount[eng], eng)
            self.semobj[sk] = self.esem[eng]
            inc = 1
        self.streams[eng].append((waits, fn, sk, inc))
        self.n_ops += 1
        for k in writes:
            self.last_w[k] = tok
            self.readers[k] = {}
        for k in reads:
            if k in writes:
                continue
            self.readers.setdefault(k, {})[tok[0]] = (tok[1], tok[2])
        return tok

    def fence(self, eng, keys):
        waits = []
        for k in keys:
            self._need(eng, self.last_w.get(k), waits)
        self.streams[eng].append((waits, None, None, 0))

    def emit(self):
        so = self.semobj

        def replay(name, e):
            for waits, fn, sk, inc in self.streams[name]:
                for wk, val in waits:
                    e.wait_ge(so[wk], val)
                if fn is not None:
                    fn(e).then_inc(so[sk], inc)

        with self.nc.Block() as block:
            @block.tensor
            def _(e):
                replay("pe", e)

            @block.scalar
            def _(e):
                replay("act", e)

            @block.vector
            def _(e):
                replay("dve", e)

            @block.gpsimd
            def _(e):
                replay("pool", e)

            @block.sync
            def _(e):
                replay("sp", e)


def _rs(shape):
    names = "abcdefg"[: len(shape)]
    if len(shape) == 1:
        return None, {}
    pat = "p (" + " ".join(names) + ") -> p " + " ".join(names)
    return pat, {n: s for n, s in zip(names[:-1], shape[:-1])}


class Arena:
    def __init__(self, nc, es, nbytes):
        self.t = es.enter_context(nc.sbuf_tensor("arena", [128, nbytes // 2], BF16))
        self.nbytes = nbytes
        self.off = 0

    def alloc(self, nbytes):
        o = self.off
        self.off += (nbytes + 63) // 64 * 64
        assert self.off <= self.nbytes, (self.off, self.nbytes)
        return o

    def view(self, off, shape, dtype, parts=128):
        n = int(np.prod(shape))
        sz = 4 if dtype == F32 else 2
        ap = self.t[0:parts, off // 2: off // 2 + n * sz // 2]
        if dtype == F32:
            ap = ap.bitcast(F32)
        pat, kw = _rs(list(shape))
        if pat is not None:
            ap = ap.rearrange(pat, **kw)
        return ap


def build(NL, stage=9, NOWN=None):
    nc = bass.Bass("TRN2", target_bir_lowering=False)
    NTOK = NCX + NL
    NBLK = NTOK // 128
    NCH = NL // 128
    NOWN = NL if NOWN is None else NOWN
    NOCH = NOWN // 128
    NTL = NL // T

    def din(n, s, dt=F32):
        return nc.dram_tensor(n, s, dt, kind="ExternalInput").ap()

    def dsc(n, s, dt=BF16):
        return nc.dram_tensor(n, s, dt).ap()

    x_in = din("x", [NL, D])
    ctx_in = din("ctx", [NCX, D])
    cv_in = din("cv", [128, 32])
    modw_in = din("mod_w", [2, D, 6 * D])
    modb_in = din("modb", [128, 192])
    nmg_in = din("nmg", [128, 32])
    nfg_in = din("nfg", [128, 32])
    fng_in = din("fng", [128, 16])
    wg_in = din("wg", [2, D, FF])
    wu_in = din("wu", [2, D, FF])
    wd_in = din("wd", [2, FF, D])
    abin_in = din("abin", [D, 2368])
    about_in = din("about", [D, D])
    sink_in = din("sink", [128, 8])
    qng_in = din("qng", [128, 4])
    wqb_in = din("wqb", [512, 1536])
    kvng_in = din("kvng", [128, 2])
    wkvb_in = din("wkvb", [256, 2048])
    retin_in = din("retin", [D, 12288])
    lg_in = din("lg", [128, 16])
    gng_in = din("gng", [128, 32])
    retout_in = din("retout", [4096, D])
    ropeA_in = din("ropeA", [2, 128, NL])
    ropeB_in = din("ropeB", [2, 128, NL])
    ropeR_in = din("ropeR", [2, 128, NL])
    cst_in = din("cst", [128, 7 * 128])
    pc_in = din("pc", [128, 4])
    out = nc.dram_tensor("out", [NOWN, D], F32, kind="ExternalOutput").ap()
    dbg = nc.dram_tensor("dbg", [KD, 128, NCX + NL], F32, kind="ExternalOutput").ap() if stage <= 2 else None

    abin_b = dsc("abin_b", [D, 2368])
    abin_s = dsc("abin_s", [D, 1280])
    abin_kr = dsc("abin_kr", [D, 256])
    about_b = dsc("about_b", [4, 128, KD, 512])
    wqn_b = dsc("wqn_b", [512, 1024])
    wqr_b = dsc("wqr_b", [512, 1024])
    wkn_b = dsc("wkn_b", [256, 1024])
    wkv_b = dsc("wkv_b", [256, 1024])
    wg_b = dsc("wg_b", [2, 11, 128, KD, 512])
    wu_b = dsc("wu_b", [2, 11, 128, KD, 512])
    wd_b = dsc("wd_b", [2, 4, 4, 128, 11, 512])
    retin_b = dsc("retin_b", [24, 128, KD, 512])
    retout_b = dsc("retout_b", [4, 2, 128, KD, 512])
    XT = dsc("XT", [KD, 128, NTOK], F32)
    QA = dsc("QA", [8, 128, NTOK])
    KA = dsc("KA", [2, 128, NTOK])
    VA = dsc("VA", [NBLK, 128, 2, 129])
    QN = dsc("QN", [8, 128, NTOK])
    QR = dsc("QR", [8, 128, NTOK])
    KN = dsc("KN", [8, 128, NTOK])
    KRS = dsc("KRS", [128, NTOK])
    VM = dsc("VM", [8, 128, NBLK, 129])
    QT1 = dsc("QT1", [NOCH, 128, KD, 128])
    KT1 = dsc("KT1", [NOCH, 128, KD, 128])
    KP1 = dsc("KP1", [2, NCH, 128, D])
    V1 = dsc("V1", [NCH, 128, 4096])
    G1 = dsc("G1", [NOCH, 128, 4096])
    OF1 = dsc("OF1", [NOCH, 128, 4096], F32)
    Z1 = dsc("Z1", [NOCH, 128, 4096])
    S0 = dsc("S0", [2, 128, 16, 512], F32)

    es = ExitStack()
    with es:
        S = Sched(nc, es)
        ar = Arena(nc, es, 206 * 1024)
        ps = [es.enter_context(nc.psum_tensor("ps%d" % i, [128, 512], F32))[:] for i in range(8)]
        psk = [("ps", i) for i in range(8)]

        WPB = 16384
        NWP = 4
        WP0 = ar.alloc(WPB * NWP)
        XTo = ar.alloc(32768)
        BAo = ar.alloc(16384)
        BIGo = ar.alloc(45056)
        KRo = ar.alloc(max(NTOK * 2, 8192 + 64))
        NTMP = 6
        TMPo = ar.alloc(2048 * NTMP)
        PTo = ar.alloc(1024 * 3)
        RSo = ar.alloc(2048)
        VACo = ar.alloc(2048)
        KR2o = ar.alloc(8192 + 64)
        CSTo = ar.alloc(7 * 128 * 4)
        SMo = ar.alloc(8192)
        sm_off = [SMo]

        def small(ncols, dtype=F32):
            sz = 4 if dtype == F32 else 2
            o = sm_off[0]
            sm_off[0] += (ncols * sz + 31) // 32 * 32
            assert sm_off[0] <= SMo + 8192
            return ar.view(o, [ncols], dtype)

        cst = ar.view(CSTo, [7, 128], F32)
        ident_f = cst[:, 0, :]
        diffF = cst[:, 3, :]
        diffB = cst[:, 4, :]
        posr = cst[:, 5, :]
        ident_b = small(128, BF16)
        maskLo = small(128, BF16)
        maskHi = small(128, BF16)
        ones_m = small(128)
        cv = small(32)
        scv = small(32)
        modb = small(192)
        modT = small(384)
        EFF = small(384)
        nmg = small(32)
        nfg = small(32)
        fng = small(16)
        qng = small(4)
        kvng = small(2)
        esink = small(8)
        lg = small(16)
        lgs = small(16)
        kdec = small(16)
        cdec = small(16)
        pcol = small(2)
        gng = small(32)
        dummy = small(8)
        xT = ar.view(XTo, [KD, T], F32)
        hT = ar.view(BAo, [KD, T], BF16)
        tmp = [ar.view(TMPo + 2048 * i, [T], F32) for i in range(NTMP)]
        tmpk = [("tmp", i) for i in range(NTMP)]
        rsb = ar.view(RSo, [T], F32)
        pt = [ar.view(PTo + 1024 * i, [T], BF16) for i in range(3)]
        ptk = [("pt", i) for i in range(3)]
        ctr = {"tmp": 0, "pt": 0, "wp": 0, "psA": 0, "psB": 0, "psC": 0, "ev": 0}

        def nxt(name, n):
            i = ctr[name]
            ctr[name] = (i + 1) % n
            return i

        def gtmp():
            i = nxt("tmp", NTMP)
            return tmp[i], tmpk[i]

        def gpt():
            i = nxt("pt", 3)
            return pt[i], ptk[i]

        def psA():
            i = nxt("psA", 4)
            return ps[i], psk[i]

        def psB():
            i = 4 + nxt("psB", 2)
            return ps[i], psk[i]

        def psC():
            i = 6 + nxt("psC", 2)
            return ps[i], psk[i]

        def dma(q, o, i, r, w):
            S.add(q, lambda e: e.dma_start(out=o, in_=i), r, w, dma=True)

        def mm(o, lhsT, rhs, start, stop, r, w):
            S.add("pe", lambda e: e.matmul(o, lhsT=lhsT, rhs=rhs, start=start, stop=stop), r, w)

        def tr(o, i, idn, r, w):
            S.add("pe", lambda e: e.transpose(o, i, idn), r, w)

        def act(o, i, func, r, w, bias=None, scale=None, accum=None):
            kw = {}
            if bias is not None:
                kw["bias"] = bias
            if scale is not None:
                kw["scale"] = scale
            if accum is not None:
                kw["accum_out"] = accum
            S.add("act", lambda e: e.activation(out=o, in_=i, func=func, **kw), r, w)

        def tt(eng, o, a, b, op, r, w):
            S.add(eng, lambda e: e.tensor_tensor(out=o, in0=a, in1=b, op=op), r, w)

        def ts(eng, o, a, s1, s2, op0, op1, r, w):
            if s2 is None:
                S.add(eng, lambda e: e.tensor_scalar(out=o, in0=a, scalar1=s1, scalar2=None, op0=op0), r, w)
            else:
                S.add(eng, lambda e: e.tensor_scalar(out=o, in0=a, scalar1=s1, scalar2=s2, op0=op0, op1=op1), r, w)

        def stt(eng, o, a, sc, b, op0, op1, r, w):
            S.add(eng, lambda e: e.scalar_tensor_tensor(out=o, in0=a, scalar=sc, in1=b, op0=op0, op1=op1), r, w)

        def cp(eng, o, i, r, w):
            if eng == "act":
                S.add("act", lambda e: e.copy(out=o, in_=i), r, w)
            else:
                S.add(eng, lambda e: e.tensor_copy(out=o, in_=i), r, w)

        def evac(o, i, r, w):
            cp("act" if nxt("ev", 2) == 0 else "dve", o, i, r, w)

        def rename(old, new):
            S.add("dve", lambda e: e.memset(dummy[:, 0:1], 0.0), list(old), list(new) + ["dummy"])

        def wload(src, shape, rk, dtype=BF16, q="sp"):
            i = nxt("wp", NWP)
            v = ar.view(WP0 + i * WPB, shape, dtype)
            dma(q, v, src, rk, [("wp", i)])
            return v, ("wp", i)

        wkeys = {}

        def conv(name, dst, src, nrows, rb=256):
            ks = []
            for r0 in range(0, nrows, rb):
                k = ("w", name, r0)
                dma("pool", dst[r0:r0 + rb], src[r0:r0 + rb], [], [k])
                ks.append(k)
            wkeys.setdefault(name, []).extend(ks)

        dma("sp", cst, cst_in.rearrange("p (a b) -> p a b", a=7), [], ["cst"])
        for (v, src, k) in [(cv, cv_in, "cv"), (modb, modb_in, "modb"), (nmg, nmg_in, "nmg"), (nfg, nfg_in, "nfg"),
                            (fng, fng_in, "fng"), (qng, qng_in, "qng"), (kvng, kvng_in, "kvng"), (esink, sink_in, "esink"),
                            (lg, lg_in, "lg"), (gng, gng_in, "gng")]:
            dma("sp", v, src, [], [k])
        cp("dve", ident_b, cst[:, 0, :], ["cst"], ["ident_b"])
        cp("dve", maskLo, cst[:, 1, :], ["cst"], ["maskLo"])
        cp("dve", maskHi, cst[:, 2, :], ["cst"], ["maskHi"])
        S.add("dve", lambda e: e.memset(ones_m, 1.0 / D), [], ["ones_m"])
        act(esink, esink, AF.Exp, ["esink"], ["esink"])
        act(scv, cv, AF.Silu, ["cv"], ["scv"])

        conv("abin", abin_b, abin_in, D)
        av = abin_in[:, 0:1280].rearrange("r (h t c) -> r h t c", h=10, t=2)
        sv = abin_s.rearrange("r (h t c) -> r h t c", h=10, t=2)
        for r0 in range(0, D, 128):
            for t_ in range(2):
                k = ("w", "abin_s", r0, t_)
                dma("pool", sv[r0:r0 + 128, :, t_, :], av[r0:r0 + 128, :, 1 - t_, :], [], [k])
                wkeys.setdefault("abin_s", []).append(k)
        for r0 in range(0, D, 512):
            segs = [(0, 64, 2304), (64, 128, 2304), (128, 160, 2336), (160, 192, 2304), (192, 224, 2336), (224, 256, 2304)]
            for (d0, d1, s0) in segs:
                k = ("w", "abin_kr", r0, d0)
                dma("pool", abin_kr[r0:r0 + 512, d0:d1], abin_in[r0:r0 + 512, s0:s0 + (d1 - d0)], [], [k])
                wkeys.setdefault("abin_kr", []).append(k)
        wq3 = wqb_in.rearrange("r (h c) -> r h c", h=8)
        dma("pool", wqn_b.rearrange("r (h c) -> r h c", h=8), wq3[:, :, 0:128], [], [("w", "wqn")])
        wkeys["wqn"] = [("w", "wqn")]
        wr3 = wqr_b.rearrange("r (s h c) -> r s h c", s=2, h=8)
        wkeys["wqr"] = []
        for j, (sl_d, sl_s) in enumerate([((0, slice(0, 64)), slice(128, 192)),
                                          ((1, slice(0, 32)), slice(160, 192)),
                                          ((1, slice(32, 64)), slice(128, 160))]):
            k = ("w", "wqr", j)
            dma("pool", wr3[:, sl_d[0], :, sl_d[1]], wq3[:, :, sl_s], [], [k])
            wkeys["wqr"].append(k)
        wk3 = wkvb_in.rearrange("r (h c) -> r h c", h=8)
        dma("pool", wkn_b.rearrange("r (h c) -> r h c", h=8), wk3[:, :, 0:128], [], [("w", "wkn")])
        dma("pool", wkv_b.rearrange("r (h c) -> r h c", h=8), wk3[:, :, 128:256], [], [("w", "wkv")])
        wkeys["wkn"] = [("w", "wkn")]
        wkeys["wkv"] = [("w", "wkv")]
        cq = []
        cdone = set()

        def creg(key, dst, src):
            cq.append((key, dst, src))

        def cissue(key):
            if key in cdone:
                return
            for (k_, d_, s_) in cq:
                if k_ == key:
                    dma("pool", d_, s_, [], [k_])
                    cdone.add(k_)
                    return
            raise KeyError(key)

        def pump(cnt):
            for (k_, d_, s_) in cq:
                if cnt <= 0:
                    break
                if k_ not in cdone:
                    cissue(k_)
                    cnt -= 1

        def wloadc(key, dstblk, shape):
            cissue(key)
            return wload(dstblk, shape, [key])

        for dg in range(4):
            creg(("w", "about", dg), about_b[dg], about_in[:, dg * 512:(dg + 1) * 512].rearrange("(k p) c -> p k c", p=128))

        def creg_ffn(l):
            for hb in range(11):
                creg(("w", "wg", l, hb), wg_b[l][hb], wg_in[l][:, hb * 512:(hb + 1) * 512].rearrange("(k p) c -> p k c", p=128))
                creg(("w", "wu", l, hb), wu_b[l][hb], wu_in[l][:, hb * 512:(hb + 1) * 512].rearrange("(k p) c -> p k c", p=128))
            for dg in range(4):
                for q4 in range(4):
                    creg(("w", "wd", l, dg, q4), wd_b[l][dg][q4],
                         wd_in[l][q4 * 1408:(q4 + 1) * 1408, dg * 512:(dg + 1) * 512].rearrange("(k p) c -> p k c", p=128))

        creg_ffn(0)
        for blk in list(range(4, 16)) + list(range(0, 4)) + list(range(16, 24)):
            creg(("w", "retin", blk), retin_b[blk], retin_in[:, blk * 512:(blk + 1) * 512].rearrange("(k p) c -> p k c", p=128))
        for dg in range(4):
            for half in range(2):
                creg(("w", "retout", dg, half), retout_b[dg][half],
                     retout_in[half * 2048:(half + 1) * 2048, dg * 512:(dg + 1) * 512].rearrange("(k p) c -> p k c", p=128))
        creg_ffn(1)

        scv3 = scv.rearrange("p (k s) -> p k s", s=2)
        for l in range(2):
            pm, pmk = ps[7], psk[7]
            for cb in range(48):
                wv, wk = wload(modw_in[l][:, cb * 256:(cb + 1) * 256].rearrange("(k p) c -> p k c", p=128), [KD, 256], [], F32)
                for j in range(2):
                    oc = cb * 2 + j
                    for kc in range(KD):
                        mm(pm[:, oc * 2:oc * 2 + 2], wv[:, kc, j * 128:(j + 1) * 128], scv3[:, kc, :], kc == 0, kc == KD - 1,
                           [wk, "scv"], [pmk])
            tt("dve", modT[:, l * 192:(l + 1) * 192].rearrange("p (o s) -> p o s", s=2), pm[:, 0:192].rearrange("p (o s) -> p o s", s=2),
               modb[:, l * 96:(l + 1) * 96].unsqueeze(2).to_broadcast([128, 96, 2]), ALU.add, [pmk, "modb"], [("modT", l)])
        modT4 = modT.rearrange("p (l j k s) -> p l j k s", l=2, j=6, k=16)
        EFF4 = EFF.rearrange("p (l j k s) -> p l j k s", l=2, j=6, k=16)
        for l in range(2):
            for (jo, jsc, jsh, jg, gv) in [(0, 1, 0, 2, nmg), (3, 4, 3, 5, nfg)]:
                gcol = gv[:, l * 16:(l + 1) * 16].unsqueeze(2).to_broadcast([128, 16, 2])
                stt("dve", EFF4[:, l, jo], modT4[:, l, jsc], 1.0, gcol, ALU.add, ALU.mult, [("modT", l), "nmg", "nfg"], [("EFF", l, jo)])
                cp("dve", EFF4[:, l, jo + 1], modT4[:, l, jsh], [("modT", l)], [("EFF", l, jo + 1)])
                cp("dve", EFF4[:, l, jo + 2], modT4[:, l, jg], [("modT", l)], [("EFF", l, jo + 2)])

        def effcol(l, j, kc, s):
            return EFF4[:, l, j, kc, s:s + 1]

        def effk(l, j):
            return ("EFF", l, j)

        def norm_fm(xv, n, nk, xk, scale_fn, bias_fn, out_fn, outk_fn, sk, sqs=1.0):
            pn, pnk = psC()
            for kc in range(nk):
                sq, sqk = gtmp()
                if kc % 2 == 0:
                    act(sq[:, 0:n], xv[:, kc, 0:n], AF.Square, [xk], [sqk])
                else:
                    tt("dve", sq[:, 0:n], xv[:, kc, 0:n], xv[:, kc, 0:n], ALU.mult, [xk], [sqk])
                mm(pn[:, 0:n], ones_m, sq[:, 0:n], kc == 0, kc == nk - 1, [sqk, "ones_m"], [pnk])
            rs, rsk = rsb, "rsb"
            act(rs[:, 0:n], pn[:, 0:n], AF.Sqrt, [pnk], [rsk], bias=EPS, scale=sqs)
            S.add("dve", lambda e: e.reciprocal(out=rs[:, 0:n], in_=rs[:, 0:n]), [rsk], [rsk])
            for kc in range(nk):
                t_, tk = gtmp()
                tt("dve", t_[:, 0:n], xv[:, kc, 0:n], rs[:, 0:n], ALU.mult, [xk, rsk], [tk])
                b = bias_fn(kc) if bias_fn is not None else None
                act(out_fn(kc), t_[:, 0:n], AF.Identity, [tk] + sk, [outk_fn(kc)], bias=b, scale=scale_fn(kc))

        def ffn(l, n, mods):
            act_t = ar.view(BIGo, [KF, T], BF16)
            for (c0, c1, s) in mods:
                norm_fm(xT[:, :, c0:c1], c1 - c0, KD, "xT", lambda kc: effcol(l, 3, kc, s), lambda kc: effcol(l, 4, kc, s),
                        lambda kc: hT[:, kc, c0:c1], lambda kc: "hT", [effk(l, 3), effk(l, 4)])
            for hb in range(11):
                gv, gk = wloadc(("w", "wg", l, hb), wg_b[l][hb], [KD, 512])
                uv, uk = wloadc(("w", "wu", l, hb), wu_b[l][hb], [KD, 512])
                for j in range(4):
                    hc = hb * 4 + j
                    pg, pgk = psA()
                    for kc in range(KD):
                        mm(pg[:, 0:n], gv[:, kc, j * 128:(j + 1) * 128], hT[:, kc, 0:n], kc == 0, kc == KD - 1, [gk, "hT"], [pgk])
                    pu, puk = psA()
                    for kc in range(KD):
                        mm(pu[:, 0:n], uv[:, kc, j * 128:(j + 1) * 128], hT[:, kc, 0:n], kc == 0, kc == KD - 1, [uk, "hT"], [puk])
                    sg, sgk = gtmp()
                    act(sg[:, 0:n], pg[:, 0:n], AF.Silu, [pgk], [sgk])
                    tt("dve", act_t[:, hc, 0:n], sg[:, 0:n], pu[:, 0:n], ALU.mult, [sgk, puk], [("BIG", hc)])
            for dg in range(4):
                pd = [psA() for _ in range(4)]
                for q4 in range(4):
                    dv_, dk = wloadc(("w", "wd", l, dg, q4), wd_b[l][dg][q4], [11, 512])
                    for j in range(4):
                        for hh in range(11):
                            hc = q4 * 11 + hh
                            mm(pd[j][0][:, 0:n], dv_[:, hh, j * 128:(j + 1) * 128], act_t[:, hc, 0:n], hc == 0, hc == KF - 1,
                               [dk, ("BIG", hc)], [pd[j][1]])
                for j in range(4):
                    dc = dg * 4 + j
                    for (c0, c1, s) in mods:
                        stt("dve", xT[:, dc, c0:c1], pd[j][0][:, c0:c1], effcol(l, 5, dc, s), xT[:, dc, c0:c1], ALU.mult, ALU.add,
                            [pd[j][1], effk(l, 5), "xT"], ["xT"])

        xtok = [ar.view(BIGo + 8192 * i, [D], F32) for i in range(2)]
        xtokk = [("xtok", i) for i in range(2)]

        def load_x_transposed(src_rows, n, bigkeys_old):
            for sub in range(n // 128):
                xb, xbk = xtok[sub % 2], xtokk[sub % 2]
                dma("sp", xb, src_rows[sub * 128:(sub + 1) * 128, :], [], [xbk])
                for g4 in range(4):
                    pp, ppk = psC()
                    for j in range(4):
                        kc = g4 * 4 + j
                        tr(pp[:, j * 128:(j + 1) * 128], xb[:, kc * 128:(kc + 1) * 128], ident_f, [xbk, "cst"], [ppk])
                    evac(xT[:, g4 * 4:(g4 + 1) * 4, sub * 128:(sub + 1) * 128], pp.rearrange("p (j t) -> p j t", j=4), [ppk], ["xT"])

        tiles = [(0, NCX, 1)] + [(NCX + i * T, T, 0) for i in range(NTL)]
        big_all = [("BIG", i) for i in range(KF)]

        ropeTa = [ar.view(BIGo + 16640 + 2048 * i, [T], F32) for i in range(4)]
        qa_t = ar.view(BIGo + 24832, [8, T], BF16)
        kx_t = ar.view(BIGo + 33024, [4, T], BF16)
        va_t = ar.view(BIGo + 37120, [4, 2, 129], BF16)
        lat_t = ar.view(KRo, [4, T], F32)
        rename(big_all, ["xtok0", "ropeT", "qa_t", "kx_t", "va_t", "lat_t"] + xtokk)
        S.add("dve", lambda e: e.memset(va_t[:, :, :, 128:129], 1.0), ["va_t"], ["va_t"])

        def rope_evac(px, pxk, psw, pswk, ct, st, o, n, rk, wk_, sl=slice(0, 128)):
            t1, t1k = gtmp()
            tt("dve", t1[sl, 0:n], px[:, 0:n], ct[:, 0:n], ALU.mult, [pxk] + rk, [t1k])
            t2, t2k = gtmp()
            tt("dve", t2[sl, 0:n], psw[:, 0:n], st[:, 0:n], ALU.mult, [pswk] + rk, [t2k])
            tt("dve", o, t1[sl, 0:n], t2[sl, 0:n], ALU.add, [t1k, t2k], wk_)

        for (t0, n, isc) in tiles:
            l = 0
            if isc:
                load_x_transposed(ctx_in, n, None)
            else:
                load_x_transposed(x_in[t0 - NCX:t0 - NCX + n], n, None)
                p0 = t0 - NCX
                dma("sp", ropeTa[0][:, 0:n], ropeA_in[0][:, p0:p0 + n], [], [("ropeT", 0)])
                dma("sp", ropeTa[1][:, 0:n], ropeA_in[1][:, p0:p0 + n], [], [("ropeT", 1)])
                dma("sp", ropeTa[2][:, 0:n], ropeB_in[0][:, p0:p0 + n], [], [("ropeT", 2)])
                dma("sp", ropeTa[3][:, 0:n], ropeB_in[1][:, p0:p0 + n], [], [("ropeT", 3)])
            dma("pool", XT[:, :, t0:t0 + n].rearrange("k p t -> p k t"), xT[:, :, 0:n], ["xT"], [("XT", t0)])
            norm_fm(xT[:, :, 0:n], n, KD, "xT", lambda kc: effcol(0, 0, kc, isc), lambda kc: effcol(0, 1, kc, isc),
                    lambda kc: hT[:, kc, 0:n], lambda kc: "hT", [effk(0, 0), effk(0, 1)])
            for blk in range(3):
                ncol = 512 if blk < 2 else 256
                wv, wk = wload(abin_b[:, blk * 512:blk * 512 + 512].rearrange("(k p) c -> p k c", p=128), [KD, 512], wkeys["abin"])
                if not isc:
                    sv_, swk = wload(abin_s[:, blk * 512:blk * 512 + ncol].rearrange("(k p) c -> p k c", p=128), [KD, ncol], wkeys["abin_s"])
                for j in range(ncol // 128):
                    oc = blk * 4 + j
                    dst = qa_t[:, oc, 0:n] if oc < 8 else kx_t[:, oc - 8, 0:n]
                    dk_ = "qa_t" if oc < 8 else "kx_t"
                    px, pxk = psA()
                    for kc in range(KD):
                        mm(px[:, 0:n], wv[:, kc, j * 128:(j + 1) * 128], hT[:, kc, 0:n], kc == 0, kc == KD - 1, [wk, "hT"], [pxk])
                    if isc:
                        evac(dst, px[:, 0:n], [pxk], [dk_])
                    else:
                        pw, pwk = psA()
                        for kc in range(KD):
                            mm(pw[:, 0:n], sv_[:, kc, j * 128:(j + 1) * 128], hT[:, kc, 0:n], kc == 0, kc == KD - 1, [swk, "hT"], [pwk])
                        rope_evac(px, pxk, pw, pwk, ropeTa[0], ropeTa[1], dst, n, [("ropeT", 0), ("ropeT", 1)], [dk_])
                if blk == 2:
                    for tb in range(n // 128):
                        pv, pvk = psA()
                        for kc in range(KD):
                            mm(pv[:, 0:256], hT[:, kc, tb * 128:(tb + 1) * 128], wv[:, kc, 256:512], kc == 0, kc == KD - 1, [wk, "hT"], [pvk])
                        evac(va_t[:, tb, :, 0:128], pv[:, 0:256].rearrange("p (g c) -> p g c", g=2), [pvk], ["va_t"])
            dma("pool", QA[:, :, t0:t0 + n].rearrange("h p t -> p h t"), qa_t[:, :, 0:n], ["qa_t"], [("QA", t0)])
            dma("pool", KA[:, :, t0:t0 + n].rearrange("h p t -> p h t"), kx_t[:, 0:2, 0:n], ["kx_t"], [("KA", t0)])
            b0 = t0 // 128
            dma("pool", VA[b0:b0 + n // 128].rearrange("b p g c -> p b g c"), va_t[:, 0:n // 128], ["va_t"], [("VA", t0)])
            wv, wk = wload(abin_b[:, 1536:2048].rearrange("(k p) c -> p k c", p=128), [KD, 512], wkeys["abin"])
            for j in range(4):
                px, pxk = psA()
                for kc in range(KD):
                    mm(px[:, 0:n], wv[:, kc, j * 128:(j + 1) * 128], hT[:, kc, 0:n], kc == 0, kc == KD - 1, [wk, "hT"], [pxk])
                evac(lat_t[:, j, 0:n], px[:, 0:n], [pxk], ["lat_t"])
            qln = ar.view(BIGo + 40960, [4, T], BF16)
            norm_fm(lat_t[:, :, 0:n], n, 4, "lat_t", lambda kc: qng[:, kc:kc + 1], None,
                    lambda kc: qln[:, kc, 0:n], lambda kc: "qln", ["qng"], sqs=4.0)
            qn_t = qa_t
            wv, wk = wload(wqn_b.rearrange("(k p) c -> p k c", p=128), [4, 1024], wkeys["wqn"])
            for h in range(8):
                px, pxk = psA()
                for kc in range(4):
                    mm(px[:, 0:n], wv[:, kc, h * 128:(h + 1) * 128], qln[:, kc, 0:n], kc == 0, kc == 3, [wk, "qln"], [pxk])
                evac(qn_t[:, h, 0:n], px[:, 0:n], [pxk], ["qa_t"])
            dma("pool", QN[:, :, t0:t0 + n].rearrange("h p t -> p h t"), qn_t[:, :, 0:n], ["qa_t"], [("QN", t0)])
            wv, wk = wload(wqr_b.rearrange("(k p) c -> p k c", p=128), [4, 1024], wkeys["wqr"])
            qr_t = qa_t
            S.add("dve", lambda e: e.memset(qr_t[:, :, 0:n], 0.0), [], ["qa_t"])
            for m in range(4):
                px, pxk = psA()
                for kc in range(4):
                    mm(px[:, 0:n], wv[:, kc, m * 128:(m + 1) * 128], qln[:, kc, 0:n], kc == 0, kc == 3, [wk, "qln"], [pxk])
                if isc:
                    for hh in range(2):
                        evac(qr_t[64 * hh:64 * hh + 64, 2 * m + hh, 0:n], px[64 * hh:64 * hh + 64, 0:n], [pxk], ["qa_t"])
                else:
                    pw, pwk = psA()
                    for kc in range(4):
                        mm(pw[:, 0:n], wv[:, kc, 512 + m * 128:512 + (m + 1) * 128], qln[:, kc, 0:n], kc == 0, kc == 3, [wk, "qln"], [pwk])
                    for hh in range(2):
                        sl = slice(64 * hh, 64 * hh + 64)
                        rope_evac(px[sl], pxk, pw[sl], pwk, ropeTa[2][sl], ropeTa[3][sl], qr_t[sl, 2 * m + hh, 0:n], n,
                                  [("ropeT", 2), ("ropeT", 3)], ["qa_t"], sl)
            dma("pool", QR[:, :, t0:t0 + n].rearrange("h p t -> p h t"), qr_t[:, :, 0:n], ["qa_t"], [("QR", t0)])
            wv, wk = wload(abin_b[:, 2048:2304].rearrange("(k p) c -> p k c", p=128), [KD, 256], wkeys["abin"])
            for j in range(2):
                px, pxk = psA()
                for kc in range(KD):
                    mm(px[:, 0:n], wv[:, kc, j * 128:(j + 1) * 128], hT[:, kc, 0:n], kc == 0, kc == KD - 1, [wk, "hT"], [pxk])
                evac(lat_t[:, j, 0:n], px[:, 0:n], [pxk], ["lat_t"])
            kvn = qln
            norm_fm(lat_t[:, 0:2, 0:n], n, 2, "lat_t", lambda kc: kvng[:, kc:kc + 1], None,
                    lambda kc: kvn[:, kc, 0:n], lambda kc: "qln", ["kvng"], sqs=8.0)
            wv, wk = wload(abin_kr.rearrange("(k p) c -> p k c", p=128), [KD, 256], wkeys["abin_kr"])
            px, pxk = psA()
            for kc in range(KD):
                mm(px[:, 0:n], wv[:, kc, 0:128], hT[:, kc, 0:n], kc == 0, kc == KD - 1, [wk, "hT"], [pxk])
            kr_t = qa_t[:, 0, :]
            if isc:
                evac(kr_t[:, 0:n], px[:, 0:n], [pxk], ["qa_t"])
            else:
                pw, pwk = psA()
                for kc in range(KD):
                    mm(pw[:, 0:n], wv[:, kc, 128:256], hT[:, kc, 0:n], kc == 0, kc == KD - 1, [wk, "hT"], [pwk])
                rope_evac(px, pxk, pw, pwk, ropeTa[2], ropeTa[3], kr_t[:, 0:n], n, [("ropeT", 2), ("ropeT", 3)], ["qa_t"])
            dma("pool", KRS[:, t0:t0 + n], kr_t[:, 0:n], ["qa_t"], [("KRS", t0)])
            wv, wk = wload(wkn_b.rearrange("(k p) c -> p k c", p=128), [2, 1024], wkeys["wkn"])
            kn_full = ar.view(BIGo, [8, T], BF16)
            for h in range(8):
                px, pxk = psA()
                for kc in range(2):
                    mm(px[:, 0:n], wv[:, kc, h * 128:(h + 1) * 128], kvn[:, kc, 0:n], kc == 0, kc == 1, [wk, "qln"], [pxk])
                evac(kn_full[:, h, 0:n], px[:, 0:n], [pxk], [xtokk[0]])
            dma("pool", KN[:, :, t0:t0 + n].rearrange("h p t -> p h t"), kn_full[:, :, 0:n], [xtokk[0]], [("KN", t0)])
            vm_t = ar.view(BIGo + 8192, [4, 8, 129], BF16)
            S.add("dve", lambda e: e.memset(vm_t[:, :, :, 128:129], 1.0), [xtokk[1]], [xtokk[1]])
            wv, wk = wload(wkv_b.rearrange("(k p) c -> p k c", p=128), [2, 1024], wkeys["wkv"])
            for tb in range(n // 128):
                for nb in range(2):
                    pv, pvk = psA()
                    for kc in range(2):
                        mm(pv, kvn[:, kc, tb * 128:(tb + 1) * 128], wv[:, kc, nb * 512:(nb + 1) * 512], kc == 0, kc == 1, [wk, "qln"], [pvk])
                    evac(vm_t[:, tb, nb * 4:(nb + 1) * 4, 0:128], pv.rearrange("p (h c) -> p h c", h=4), [pvk], [xtokk[1]])
            for tb in range(n // 128):
                dma("pool", VM[:, :, b0 + tb, :].rearrange("h p c -> p h c"), vm_t[:, tb], [xtokk[1]], [("VM", t0, tb)])
            pump(5)


        allk = lambda nm: [(nm, t0) for (t0, n, isc) in tiles]
        krs = ar.view(KRo, [NTOK], BF16)
        qa_b = ar.view(BIGo, [8, T], BF16)
        qn_b = ar.view(BIGo + 8192, [8, T], BF16)
        qr_b = ar.view(BIGo + 16384, [8, T], BF16)
        o_t = ar.view(BIGo + 24576, [4, 16, 128], BF16)
        kA_w = ar.view(BIGo + 40960, [2, 768], BF16)
        kA_c = ar.view(BIGo + 44032, [2, 256], BF16)
        vA_c = ar.view(VACo, [2, 2, 129], BF16)
        vA_w = ar.view(TMPo, [6, 2, 129], BF16)
        oT = hT
        dsm = small(8)
        rename(["xT", "hT", "qa_t", "kx_t", "va_t", "lat_t", "qln", "ropeT"] + xtokk + [("ropeT", i) for i in range(4)],
               ["krs", "qa_b", "qn_b", "qr_b", "o_t", "kA_w", "kA_c", "vA_c"] + big_all)
        dma("sp", krs, KRS, allk("KRS"), ["krs"])
        SC_A = float(128 ** -0.5)
        SC_B = float(192 ** -0.5)

        def finish_head(pv, pvk, hidx, qb, sink_col):
            den, dk_ = dsm, "dsm"
            if sink_col is not None:
                ts("dve", den[:, 0:1], pv[:, 128:129], sink_col, None, ALU.add, None, [pvk, "esink"], [dk_])
                S.add("dve", lambda e: e.reciprocal(out=den[:, 0:1], in_=den[:, 0:1]), [dk_], [dk_])
            else:
                S.add("dve", lambda e: e.reciprocal(out=den[:, 0:1], in_=pv[:, 128:129]), [pvk], [dk_])
            act(o_t[:, qb, hidx, :], pv[:, 0:128], AF.Copy, [pvk, dk_], ["o_t"], scale=den[:, 0:1])

        for (t0, n, isc) in tiles:
            nqb = n // 128
            dma("sp", xT[:, :, 0:n], XT[:, :, t0:t0 + n].rearrange("k p t -> p k t"), [("XT", t0)], ["xT"])
            dma("sp", qa_b[:, :, 0:n], QA[:, :, t0:t0 + n].rearrange("h p t -> p h t"), allk("QA"), ["qa_b"])
            dma("sp", qn_b[:, :, 0:n], QN[:, :, t0:t0 + n].rearrange("h p t -> p h t"), allk("QN"), ["qn_b"])
            dma("sp", qr_b[:, :, 0:n], QR[:, :, t0:t0 + n].rearrange("h p t -> p h t"), allk("QR"), ["qr_b"])
            dma("sp", kA_c, KA[:, :, 0:NCX].rearrange("h p t -> p h t"), allk("KA"), ["kA_c"])
            dma("sp", vA_c, VA[0:2].rearrange("b p g c -> p b g c"), allk("VA"), ["vA_c"])
            if not isc:
                lb0 = (t0 - NCX) // 128
                wlo = max(lb0 - 1, 0)
                whi = min(lb0 + nqb + 1, NCH)
                nw = whi - wlo
                dma("sp", kA_w[:, :, 0:nw * 128], KA[:, :, NCX + wlo * 128:NCX + whi * 128].rearrange("h p t -> p h t"), allk("KA"), ["kA_w"])
                dma("sp", vA_w[:, 0:nw], VA[2 + wlo:2 + whi].rearrange("b p g c -> p b g c"), allk("VA"), [tmpk[0], tmpk[1]])
            for g in range(2):
                for qb in range(nqb):
                    kbs = [("c", 0), ("c", 1)]
                    if not isc:
                        lb = lb0 + qb
                        for dlt in (-1, 0, 1):
                            if 0 <= lb + dlt < NCH:
                                kbs.append(("l", lb + dlt - wlo, dlt))
                    pvs = [psA() for _ in range(4)]
                    for ki, kb in enumerate(kbs):
                        sps, spk = psB()
                        if kb[0] == "c":
                            kl = kA_c[:, g, kb[1] * 128:(kb[1] + 1) * 128]
                            vv = vA_c[:, kb[1], g, :]
                            kr_, vr_ = ["kA_c"], ["vA_c"]
                        else:
                            kl = kA_w[:, g, kb[1] * 128:(kb[1] + 1) * 128]
                            vv = vA_w[:, kb[1], g, :]
                            kr_, vr_ = ["kA_w"], [tmpk[0], tmpk[1]]
                        mm(sps.rearrange("p (h q) -> p h q", h=4), kl, qa_b[:, 4 * g:4 * g + 4, qb * 128:(qb + 1) * 128], True, True,
                           kr_ + ["qa_b"], [spk])
                        p_, pk = gpt()
                        act(p_, sps, AF.Exp, [spk], [pk], scale=SC_A)
                        if kb[0] == "l" and kb[2] != 0:
                            mk = maskLo if kb[2] == -1 else maskHi
                            mkk = "maskLo" if kb[2] == -1 else "maskHi"
                            p3 = p_.rearrange("p (h q) -> p h q", h=4)
                            tt("dve", p3, p3, mk.unsqueeze(1).to_broadcast([128, 4, 128]), ALU.mult, [pk, mkk], [pk])
                        for hh in range(4):
                            mm(pvs[hh][0][:, 0:129], p_[:, hh * 128:(hh + 1) * 128], vv, ki == 0, ki == len(kbs) - 1, [pk] + vr_, [pvs[hh][1]])
                    for hh in range(4):
                        finish_head(pvs[hh][0], pvs[hh][1], 4 * g + hh, qb, esink[:, 4 * g + hh:4 * g + hh + 1])
            nkb = 2 if isc else NBLK
            for h in range(8):
                knv, knk = wload(KN[h], [NTOK], allk("KN"))
                vmv, vmk = wload(VM[h], [NBLK, 129], [("VM", t0_, tb_) for (t0_, n_, i_) in tiles for tb_ in range(n_ // 128)])
                pvs = [psA() for _ in range(nqb)]

                def s_stage(kb):
                    sps, spk = psB()
                    mm(sps[:, 0:n], knv[:, kb * 128:(kb + 1) * 128], qn_b[:, h, 0:n], True, False, [knk, "qn_b"], [spk])
                    mm(sps[:, 0:n], krs[:, kb * 128:(kb + 1) * 128], qr_b[:, h, 0:n], False, True, ["krs", "qr_b"], [spk])
                    return sps, spk

                nxt_s = s_stage(0)
                for kb in range(nkb):
                    sps, spk = nxt_s
                    if kb + 1 < nkb:
                        nxt_s = s_stage(kb + 1)
                    p_, pk = gpt()
                    act(p_[:, 0:n], sps[:, 0:n], AF.Exp, [spk], [pk], scale=SC_B)
                    for qb in range(nqb):
                        mm(pvs[qb][0][:, 0:129], p_[:, qb * 128:(qb + 1) * 128], vmv[:, kb, :], kb == 0, kb == nkb - 1, [pk, vmk], [pvs[qb][1]])
                for qb in range(nqb):
                    finish_head(pvs[qb][0], pvs[qb][1], 8 + h, qb, None)
            for qb in range(nqb):
                for half in range(2):
                    pp, ppk = psC()
                    ppb = pp.bitcast(BF16)
                    for j in range(8):
                        fc = half * 8 + j
                        tr(ppb[:, j * 128:(j + 1) * 128], o_t[:, qb, fc, :], ident_b, ["o_t", "ident_b"], [ppk])
                    evac(oT[:, half * 8:(half + 1) * 8, qb * 128:(qb + 1) * 128], ppb.rearrange("p (j t) -> p j t", j=8), [ppk], ["hT"])
            for dg in range(4):
                wv, wk = wloadc(("w", "about", dg), about_b[dg], [KD, 512])
                for j in range(4):
                    dc = dg * 4 + j
                    px, pxk = psA()
                    for fc in range(KD):
                        mm(px[:, 0:n], wv[:, fc, j * 128:(j + 1) * 128], oT[:, fc, 0:n], fc == 0, fc == KD - 1, [wk, "hT"], [pxk])
                    stt("dve", xT[:, dc, 0:n], px[:, 0:n], effcol(0, 2, dc, isc), xT[:, dc, 0:n], ALU.mult, ALU.add,
                        [pxk, effk(0, 2), "xT"], ["xT"])
            rename(["qa_b", "qn_b", "qr_b", "o_t", "kA_w", "kA_c", "vA_c"], big_all)
            ffn(0, n, [(0, n, isc)])
            rename(big_all, ["qa_b", "qn_b", "qr_b", "o_t", "kA_w", "kA_c", "vA_c"])
            dma("pool", XT[:, :, t0:t0 + n].rearrange("k p t -> p k t"), xT[:, :, 0:n], ["xT"], [("XT", t0)])
            pump(8)

        if stage <= 2:
            for kc in range(KD):
                dma("sp", dbg[kc], XT[kc], allk("XT"), [("out", kc)])
            S.fence("sp", [("out", kc) for kc in range(KD)])
            S.emit()
            return nc


        pc = small(4)
        qdec = small(16)
        gst = small(8)
        dma("sp", pc, pc_in, [], ["pc"])
        act(lgs, lg, AF.Exp, ["lg"], ["lgs"], scale=-1.0)
        act(lgs, lgs, AF.Ln, ["lgs"], ["lgs"], bias=1.0)
        ts("dve", lgs, lgs, -1.0, None, ALU.mult, None, ["lgs"], ["lgs"])
        for dirn in range(2):
            for h in range(8):
                c = dirn * 8 + h
                act(kdec[:, c:c + 1], pc[:, dirn:dirn + 1], AF.Exp, ["pc", "lgs"], ["kdec"], scale=lgs[:, c:c + 1])
                act(qdec[:, c:c + 1], pc[:, 2 + dirn:3 + dirn], AF.Exp, ["pc", "lgs"], ["qdec"], scale=lgs[:, c:c + 1])
        act(cdec, lgs, AF.Exp, ["lgs"], ["cdec"], scale=128.0)
        DTv = ar.view(KRo, [2, 8, 128], F32)
        rename(["krs", "lat_t"], [("DT", 0), ("DT", 1)])
        for dirn in range(2):
            for h in range(8):
                c = dirn * 8 + h
                act(DTv[:, dirn, h, :], diffF if dirn == 0 else diffB, AF.Exp, ["cst", "lgs"], [("DT", dirn)], scale=lgs[:, c:c + 1])
                tt("dve", DTv[:, dirn, h, :], DTv[:, dirn, h, :], cst[:, 2 if dirn == 0 else 1, :], ALU.mult, [("DT", dirn), "cst"], [("DT", dirn)])

        def wbuf(shape, dtype):
            i = nxt("wp", NWP)
            return ar.view(WP0 + i * WPB, shape, dtype), ("wp", i)

        kp_t = [ar.view(XTo + 16384 * i, [4, D], BF16) for i in range(2)]
        qT_t = ar.view(BIGo, [4, KD, 128], BF16)
        kT_t = ar.view(BIGo + 16384, [4, KD, 128], BF16)
        vctx = ar.view(BIGo, [2, 4096], BF16)
        ev_t = [ar.view(BIGo + 32768 + 4096 * i, [4, 512], BF16) for i in range(2)]
        ropeRt = [ar.view(BIGo + 40960 + 2048 * i, [T], F32) for i in range(2)]
        rename(big_all + ["qa_b", "qn_b", "qr_b", "o_t", "kA_w", "kA_c", "vA_c"], ["qT_t", "kT_t", "ev0", "ev1", "ropeR"])
        allS = [("S", h) for h in range(8)]
        allSb = [("Sb", h) for h in range(8)]

        for (t0, n, isc) in tiles:
            ntb = n // 128
            ch0 = (t0 - NCX) // 128
            own = (not isc) and (t0 - NCX) < NOWN
            dma("sp", xT[:, :, 0:n], XT[:, :, t0:t0 + n].rearrange("k p t -> p k t"), [("XT", t0)], ["xT"])
            if not isc:
                p0 = t0 - NCX
                dma("sp", ropeRt[0][:, 0:n], ropeR_in[0][:, p0:p0 + n], [], ["ropeR"])
                dma("sp", ropeRt[1][:, 0:n], ropeR_in[1][:, p0:p0 + n], [], ["ropeR"])
            norm_fm(xT[:, :, 0:n], n, KD, "xT", lambda kc: effcol(1, 0, kc, isc), lambda kc: effcol(1, 1, kc, isc),
                    lambda kc: hT[:, kc, 0:n], lambda kc: "hT", [effk(1, 0), effk(1, 1)])
            for which in ((0, 1) if own else (1,)):
                dstT = qT_t if which == 0 else kT_t
                dkey = "qT_t" if which == 0 else "kT_t"
                scl = 1.0 if which == 0 else 0.0625
                for blk in range(4):
                    c0 = which * 2048 + blk * 512
                    wv, wk = wloadc(("w", "retin", c0 // 512), retin_b[c0 // 512], [KD, 512])
                    for hh in range(2):
                        h = blk * 2 + hh
                        pxs = []
                        for c in range(2):
                            px, pxk = psA()
                            j = hh * 2 + c
                            for kc in range(KD):
                                mm(px[:, 0:n], wv[:, kc, j * 128:(j + 1) * 128], hT[:, kc, 0:n], kc == 0, kc == KD - 1, [wk, "hT"], [pxk])
                            pxs.append((px, pxk))
                        d0 = dstT[:, 0:ntb, 2 * h, :]
                        d1 = dstT[:, 0:ntb, 2 * h + 1, :]
                        if isc:
                            for c, dd in ((0, d0), (1, d1)):
                                act(dd, pxs[c][0][:, 0:n].rearrange("p (b t) -> p b t", t=128), AF.Copy, [pxs[c][1]], [dkey], scale=scl)
                        else:
                            (x0, x0k), (x1, x1k) = pxs
                            cR, sR = ropeRt[0], ropeRt[1]
                            ta, tak = gtmp()
                            stt("dve", ta[:, 0:n], x0[:, 0:n], scl, cR[:, 0:n], ALU.mult, ALU.mult, [x0k, "ropeR"], [tak])
                            tb_, tbk = gtmp()
                            stt("dve", tb_[:, 0:n], x1[:, 0:n], scl, sR[:, 0:n], ALU.mult, ALU.mult, [x1k, "ropeR"], [tbk])
                            tt("pool", d0, ta[:, 0:n].rearrange("p (b t) -> p b t", t=128), tb_[:, 0:n].rearrange("p (b t) -> p b t", t=128),
                               ALU.subtract, [tak, tbk], [dkey])
                            tc_, tck = gtmp()
                            stt("dve", tc_[:, 0:n], x0[:, 0:n], scl, sR[:, 0:n], ALU.mult, ALU.mult, [x0k, "ropeR"], [tck])
                            td, tdk = gtmp()
                            stt("dve", td[:, 0:n], x1[:, 0:n], scl, cR[:, 0:n], ALU.mult, ALU.mult, [x1k, "ropeR"], [tdk])
                            tt("pool", d1, tc_[:, 0:n].rearrange("p (b t) -> p b t", t=128), td[:, 0:n].rearrange("p (b t) -> p b t", t=128),
                               ALU.add, [tck, tdk], [dkey])
            if own:
                dma("pool", QT1[ch0:ch0 + ntb].rearrange("c p k t -> p c (k t)"), qT_t[:, 0:ntb].rearrange("p c k t -> p c (k t)"), ["qT_t"], [("QT1", t0)])
                dma("pool", KT1[ch0:ch0 + ntb].rearrange("c p k t -> p c (k t)"), kT_t[:, 0:ntb].rearrange("p c k t -> p c (k t)"), ["kT_t"], [("KT1", t0)])
            rename(["xT"], ["kp0", "kp1"])
            for tb in range(ntb):
                for half in range(2):
                    pp, ppk = psC()
                    ppb = pp.bitcast(BF16)
                    for j in range(8):
                        kc = half * 8 + j
                        tr(ppb[:, j * 128:(j + 1) * 128], kT_t[:, tb, kc, :], ident_b, ["kT_t", "ident_b"], [ppk])
                    for dirn in ((0, 1) if (own or isc) else (1,)):
                        for hh in range(4):
                            h = half * 4 + hh
                            act(kp_t[dirn][:, tb, h * 256:(h + 1) * 256], ppb[:, hh * 256:(hh + 1) * 256], AF.Copy, [ppk, "kdec"], ["kp%d" % dirn],
                                scale=kdec[:, dirn * 8 + h:dirn * 8 + h + 1])
            if not isc:
                for dirn in ((0, 1) if own else (1,)):
                    dma("pool", KP1[dirn][ch0:ch0 + ntb].rearrange("c p f -> p c f"), kp_t[dirn][:, 0:ntb], ["kp%d" % dirn], [("KP1", dirn, t0)])
            for which in ((0, 1) if own else (0,)):
                for nb in range(8):
                    c0 = 4096 + which * 4096 + nb * 512
                    wv, wk = wloadc(("w", "retin", c0 // 512), retin_b[c0 // 512], [KD, 512])
                    ei = nxt("ev", 2)
                    evv, evk = ev_t[ei], "ev%d" % ei
                    for tb in range(ntb):
                        pv, pvk = psA()
                        for kc in range(KD):
                            mm(pv, hT[:, kc, tb * 128:(tb + 1) * 128], wv[:, kc, :], kc == 0, kc == KD - 1, [wk, "hT"], [pvk])
                        if isc:
                            evac(vctx[:, tb, nb * 512:(nb + 1) * 512], pv, [pvk], ["qT_t"])
                        elif which == 0:
                            evac(evv[:, tb, :], pv, [pvk], [evk])
                        else:
                            act(evv[:, tb, :], pv, AF.Silu, [pvk], [evk])
                    if not isc:
                        dst = V1 if which == 0 else G1
                        dma("pool", dst[ch0:ch0 + ntb, :, nb * 512:(nb + 1) * 512].rearrange("c p f -> p c f"), evv[:, 0:ntb], [evk],
                            [("V1" if which == 0 else "G1", t0, nb)])
            if isc:
                for dirn in range(2):
                    halves = [wbuf([8, 512], F32) for _ in range(2)]
                    for hv, hk in halves:
                        S.add("dve", lambda e, hv=hv: e.memset(hv, 0.0), [], [hk])
                    for tb in ((0, 1) if dirn == 0 else (1, 0)):
                        for h in range(8):
                            hv, hk = halves[h // 4]
                            for c in range(2):
                                pst, pstk = psB()
                                mm(pst, kp_t[dirn][:, tb, h * 256 + c * 128:h * 256 + (c + 1) * 128], vctx[:, tb, h * 512:(h + 1) * 512], True, True,
                                   ["kp%d" % dirn, "qT_t"], [pstk])
                                sl = hv[:, (h % 4) * 2 + c, :]
                                stt("dve", sl, sl, cdec[:, dirn * 8 + h:dirn * 8 + h + 1], pst, ALU.mult, ALU.add, [pstk, "cdec", hk], [hk])
                    for i, (hv, hk) in enumerate(halves):
                        dma("pool", S0[dirn][:, i * 8:(i + 1) * 8, :], hv, [hk], [("S0", dirn, i)])
            rename(["kp0", "kp1"], ["xT"])

        S_v = ar.view(XTo, [16, 512], F32)
        Sb_v = ar.view(BAo, [16, 512], BF16)
        cbq = [ar.view(BIGo + 20480 * i, [KD, 128], BF16) for i in range(2)]
        cbk = [ar.view(BIGo + 20480 * i + 4096, [KD, 128], BF16) for i in range(2)]
        cbp = [ar.view(BIGo + 20480 * i + 8192, [D], BF16) for i in range(2)]
        cbv = [ar.view(BIGo + 20480 * i + 12288, [4096], BF16) for i in range(2)]
        QRW = ar.view(KR2o, [2, 8, 128], F32)
        qpb = [ar.view(BIGo + 40960 + 512 * i, [2, 128], BF16) for i in range(4)]
        gsum = small(8)
        gsq = small(8)
        gm = small(8)
        gr = small(8)
        for dirn in range(2):
            for h in range(8):
                c = dirn * 8 + h
                act(QRW[:, dirn, h, :], posr if dirn == 0 else cst[:, 6, :], AF.Exp, ["cst", "lgs"], [("QRW", dirn)], scale=lgs[:, c:c + 1])
        ltiles = [t for t in tiles if (not t[2]) and (t[0] - NCX) < NOWN]
        altiles = [t for t in tiles if not t[2]]
        kQ = [("QT1", t[0]) for t in ltiles]
        kK = [("KT1", t[0]) for t in ltiles]
        kV = [("V1", t[0], nb) for t in altiles for nb in range(8)]
        kG = [("G1", t[0], nb) for t in ltiles for nb in range(8)]
        rename(["xT", "hT", "qT_t", "kT_t", "ev0", "ev1", "ropeR"],
               allS + allSb + [("cb", i, w) for i in range(2) for w in "qkpv"] + [("qp", i) for i in range(4)])
        ctr["cb"] = 0
        ctr["qp"] = 0
        for dirn in range(2):
            kP = [("KP1", dirn, t[0]) for t in (ltiles if dirn == 0 else altiles)]
            dma("sp", S_v, S0[dirn], [("S0", dirn, 0), ("S0", dirn, 1)], allS)
            for h in range(8):
                cp("pool", Sb_v[:, 2 * h:2 * h + 2, :], S_v[:, 2 * h:2 * h + 2, :], [("S", h)], [("Sb", h)])
            order = list(range(NOCH)) if dirn == 0 else list(range(NCH - 1, -1, -1))
            for oi, n_ in enumerate(order):
                bi = nxt("cb", 2)
                qc, kc_, kpc, vc = cbq[bi], cbk[bi], cbp[bi], cbv[bi]
                qk, kk_, pk_, vk = [("cb", bi, w) for w in "qkpv"]
                dma("sp", kpc, KP1[dirn][n_], kP, [pk_])
                dma("sp", vc, V1[n_], kV, [vk])
                if n_ >= NOCH:
                    for h in range(8):
                        c16 = dirn * 8 + h
                        for c in range(2):
                            pst, pstk = psC()
                            mm(pst, kpc[:, h * 256 + c * 128:h * 256 + (c + 1) * 128], vc[:, h * 512:(h + 1) * 512], True, True, [pk_, vk], [pstk])
                            stt("dve", S_v[:, 2 * h + c, :], S_v[:, 2 * h + c, :], cdec[:, c16:c16 + 1], pst, ALU.mult, ALU.add,
                                [pstk, "cdec", ("S", h)], [("S", h)])
                        if n_ == NOCH:
                            cp("act", Sb_v[:, 2 * h:2 * h + 2, :], S_v[:, 2 * h:2 * h + 2, :], [("S", h)], [("Sb", h)])
                    continue
                dma("sp", qc, QT1[n_], kQ, [qk])
                dma("sp", kc_, KT1[n_], kK, [kk_])
                if dirn == 1:
                    ofv, ofk = wload(OF1[n_], [4096], [("OF1", n_, h) for h in range(8)], F32)
                    gv_, gk_ = wload(G1[n_], [4096], kG)
                    zc, zk = wbuf([4096], BF16)
                    S.add("dve", lambda e: e.memset(gsum, 0.0), [], [("gs", h) for h in range(8)])
                    S.add("dve", lambda e: e.memset(gsq, 0.0), [], [("gq", h) for h in range(8)])
                last = oi == len(order) - 1

                def st1(h):
                    pa, pak = psB()
                    for c in range(2):
                        mm(pa[:, 0:128], kc_[:, 2 * h + c, :], qc[:, 2 * h + c, :], c == 0, c == 1, [kk_, qk], [pak])
                    qi = nxt("qp", 4)
                    tt("pool", qpb[qi], qc[:, 2 * h:2 * h + 2, :], QRW[:, dirn, h, :].unsqueeze(1).to_broadcast([128, 2, 128]), ALU.mult,
                       [qk, ("QRW", dirn)], [("qp", qi)])
                    return pa, pak, qpb[qi], ("qp", qi)

                def st2(h, pa, pak, qpv, qpk):
                    c16 = dirn * 8 + h
                    am, amk = gpt()
                    tt("dve", am[:, 0:128], pa[:, 0:128], DTv[:, dirn, h, :], ALU.mult, [pak, ("DT", dirn)], [amk])
                    po_, pok = psA()
                    mm(po_, am[:, 0:128], vc[:, h * 512:(h + 1) * 512], True, False, [amk, vk], [pok])
                    for c in range(2):
                        mm(po_, qpv[:, c, :], Sb_v[:, 2 * h + c, :], False, c == 1, [qpk, ("Sb", h)], [pok])
                    psts = []
                    if not last:
                        for c in range(2):
                            pst, pstk = psC()
                            mm(pst, kpc[:, h * 256 + c * 128:h * 256 + (c + 1) * 128], vc[:, h * 512:(h + 1) * 512], True, True, [pk_, vk], [pstk])
                            psts.append((pst, pstk))
                    if dirn == 0:
                        t1, t1k = gtmp()
                        cp("act", t1, po_, [pok], [t1k])
                        dma("pool", OF1[n_][:, h * 512:(h + 1) * 512], t1, [t1k], [("OF1", n_, h)])
                    else:
                        oh = ofv[:, h * 512:(h + 1) * 512]
                        tt("dve", oh, oh, po_, ALU.add, [pok, ofk], [ofk])
                        j1, j1k = gtmp()
                        act(j1, oh, AF.Copy, [ofk, ("gs", h)], [j1k, ("gs", h)], accum=gsum[:, h:h + 1])
                        j2, j2k = gtmp()
                        act(j2, oh, AF.Square, [ofk, ("gq", h)], [j2k, ("gq", h)], accum=gsq[:, h:h + 1])
                    for c, (pst, pstk) in enumerate(psts):
                        stt("dve", S_v[:, 2 * h + c, :], S_v[:, 2 * h + c, :], cdec[:, c16:c16 + 1], pst, ALU.mult, ALU.add,
                            [pstk, "cdec", ("S", h)], [("S", h)])
                    if psts:
                        cp("act", Sb_v[:, 2 * h:2 * h + 2, :], S_v[:, 2 * h:2 * h + 2, :], [("S", h)], [("Sb", h)])

                cur = st1(0)
                for h in range(8):
                    nx_ = st1(h + 1) if h < 7 else None
                    st2(h, *cur)
                    cur = nx_
                if dirn == 1:
                    gsk = [("gs", h) for h in range(8)] + [("gq", h) for h in range(8)]
                    ts("dve", gm, gsum, 1.0 / 512, None, ALU.mult, None, gsk, ["gm"])
                    tt("dve", gr, gm, gm, ALU.mult, ["gm"], ["gr"])
                    stt("dve", gr, gsq, 1.0 / 512, gr, ALU.mult, ALU.subtract, gsk + ["gr"], ["gr"])
                    act(gr, gr, AF.Sqrt, ["gr"], ["gr"], bias=EPS)
                    S.add("dve", lambda e: e.reciprocal(out=gr, in_=gr), ["gr"], ["gr"])
                    for h in range(8):
                        y_, yk = gtmp()
                        stt("dve", y_, ofv[:, h * 512:(h + 1) * 512], gm[:, h:h + 1], gv_[:, h * 512:(h + 1) * 512], ALU.subtract, ALU.mult,
                            [ofk, "gm", gk_], [yk])
                        act(zc[:, h * 512:(h + 1) * 512], y_, AF.Copy, [yk, "gr"], [zk], scale=gr[:, h:h + 1])
                    dma("pool", Z1[n_], zc, [zk], [("Z1", n_)])

        zT = ar.view(BIGo, [32, T], BF16)
        yT = ar.view(BIGo, [KD, T], F32)
        orow = ar.view(BIGo + 32768, [D], F32)
        rename(allS + allSb + [("cb", i, w) for i in range(2) for w in "qkpv"] + [("qp", i) for i in range(4)], ["xT", "hT", "zT"])
        for (t0, n, isc) in ltiles:
            ntb = n // 128
            ch0 = (t0 - NCX) // 128
            dma("sp", xT[:, :, 0:n], XT[:, :, t0:t0 + n].rearrange("k p t -> p k t"), [("XT", t0)], ["xT"])
            for tb in range(ntb):
                zt, ztk = wload(Z1[ch0 + tb], [4096], [("Z1", ch0 + tb)])
                for g8 in range(4):
                    pp, ppk = psC()
                    ppb = pp.bitcast(BF16)
                    for j in range(8):
                        fc = g8 * 8 + j
                        tr(ppb[:, j * 128:(j + 1) * 128], zt[:, fc * 128:(fc + 1) * 128], ident_b, [ztk, "ident_b"], [ppk])
                    for j in range(8):
                        fc = g8 * 8 + j
                        act(zT[:, fc, tb * 128:(tb + 1) * 128], ppb[:, j * 128:(j + 1) * 128], AF.Copy, [ppk, "gng"], ["zT"], scale=gng[:, fc:fc + 1])
            for dg in range(4):
                pd = [psA() for _ in range(4)]
                for half in range(2):
                    wv, wk = wloadc(("w", "retout", dg, half), retout_b[dg][half], [KD, 512])
                    for j in range(4):
                        for f16 in range(KD):
                            fc = half * 16 + f16
                            mm(pd[j][0][:, 0:n], wv[:, f16, j * 128:(j + 1) * 128], zT[:, fc, 0:n], fc == 0, fc == 31, [wk, "zT"], [pd[j][1]])
                for j in range(4):
                    dc = dg * 4 + j
                    stt("dve", xT[:, dc, 0:n], pd[j][0][:, 0:n], effcol(1, 2, dc, 0), xT[:, dc, 0:n], ALU.mult, ALU.add,
                        [pd[j][1], effk(1, 2), "xT"], ["xT"])
            rename(["zT"], big_all)
            ffn(1, n, [(0, n, 0)])
            rename(big_all, ["yT", "orow"])
            norm_fm(xT[:, :, 0:n], n, KD, "xT", lambda kc: fng[:, kc:kc + 1], None,
                    lambda kc: yT[:, kc, 0:n], lambda kc: "yT", ["fng"])
            for tb in range(ntb):
                for g4 in range(4):
                    pp, ppk = psC()
                    for j in range(4):
                        kc = g4 * 4 + j
                        tr(pp[:, j * 128:(j + 1) * 128], yT[:, kc, tb * 128:(tb + 1) * 128], ident_f, ["yT", "cst"], [ppk])
                    evac(orow[:, g4 * 512:(g4 + 1) * 512], pp, [ppk], ["orow"])
                r0 = t0 - NCX + tb * 128
                dma("pool", out[r0:r0 + 128, :], orow, ["orow"], [("out", r0)])
            rename(["yT", "orow"], ["zT"])
        S.fence("sp", [("out", r0) for r0 in range(0, NOWN, 128)])
        S.emit()
    return nc


def _pp(v):
    v = np.asarray(v, np.float32)
    return np.ascontiguousarray(v.reshape(-1, 128).T)


def _rope_tables(n_tok, rot_dim, mode):
    GRID_W = 64
    row = np.repeat(np.arange(n_tok // GRID_W, dtype=np.float32), GRID_W)
    col = (np.arange(n_tok) % GRID_W).astype(np.float32)
    n_freq = rot_dim // 4
    inv = (np.float32(10000.0) ** (-np.arange(n_freq, dtype=np.float32) / np.float32(n_freq))).astype(np.float32)
    ang = np.concatenate([row[:, None] * inv, col[:, None] * inv], axis=-1).astype(np.float32)
    c = np.cos(ang).astype(np.float32).T
    s = np.sin(ang).astype(np.float32).T
    if mode == "half":
        C = np.concatenate([c, c], 0)
        Sg = np.concatenate([-s, s], 0)
        rep = 128 // C.shape[0]
        return np.stack([np.tile(C, (rep, 1)), np.tile(Sg, (rep, 1))]).astype(np.float32)
    return np.stack([c, s]).astype(np.float32)


def _consts():
    p = np.arange(128)
    ident = np.eye(128, dtype=np.float32)
    maskLo = (p[:, None] >= p[None, :]).astype(np.float32)
    maskHi = (p[:, None] <= p[None, :]).astype(np.float32)
    diffF = np.maximum(p[None, :] - p[:, None], 0).astype(np.float32)
    diffB = np.maximum(p[:, None] - p[None, :], 0).astype(np.float32)
    pos = np.tile((p + 1).astype(np.float32)[None, :], (128, 1))
    posb = np.tile((128 - p).astype(np.float32)[None, :], (128, 1))
    return np.ascontiguousarray(np.concatenate([ident, maskLo, maskHi, diffF, diffB, pos, posb], 1))


def prep_core(inp, b, NL, flip=False):
    f = lambda a: np.ascontiguousarray(np.asarray(a, np.float32))
    fl = (lambda a, ax: np.flip(a, axis=ax)) if flip else (lambda a, ax: a)
    d = {}
    d["x"] = f(fl(np.asarray(inp["x"][b][:NL]), 0))
    d["ctx"] = f(fl(np.asarray(inp["ctx"][b]), 0))
    cvv = np.stack([_pp(inp["c"][b]), _pp(inp["c_ctx"])], -1).reshape(128, 32)
    d["cv"] = f(cvv)
    d["mod_w"] = f(inp["mod_w"])
    d["modb"] = f(np.concatenate([_pp(inp["mod_b"][l]) for l in range(2)], 1))
    d["nmg"] = f(np.concatenate([_pp(inp["norm_mix_g"][l]) for l in range(2)], 1))
    d["nfg"] = f(np.concatenate([_pp(inp["norm_ffn_g"][l]) for l in range(2)], 1))
    d["fng"] = _pp(inp["final_norm_g"])
    d["wg"] = f(inp["ffn_w_gate"])
    d["wu"] = f(inp["ffn_w_up"])
    d["wd"] = f(inp["ffn_w_down"])
    d["abin"] = f(inp["ab_w_in"][0])
    d["about"] = f(inp["ab_w_out"][0])
    d["sink"] = f(np.tile(np.asarray(inp["swa_sink"][0], np.float32)[None, :], (128, 1)))
    d["qng"] = _pp(inp["mla_q_norm_g"][0])
    d["wqb"] = f(inp["mla_w_q_b"][0])
    d["kvng"] = _pp(inp["mla_kv_norm_g"][0])
    d["wkvb"] = f(inp["mla_w_kv_b"][0])
    d["retin"] = f(inp["ret_w_in"][0])
    lf = np.asarray(inp["ret_decay_logit_fwd"][0], np.float32)
    lb = np.asarray(inp["ret_decay_logit_bwd"][0], np.float32)
    lgv = np.concatenate([lb, lf]) if flip else np.concatenate([lf, lb])
    d["lg"] = f(np.tile(lgv[None, :], (128, 1)))
    d["gng"] = _pp(inp["ret_gn_g"][0])
    d["retout"] = f(inp["ret_w_out"][0])
    d["ropeA"] = f(fl(_rope_tables(NL, 128, "half"), 2))
    d["ropeB"] = f(fl(_rope_tables(NL, 64, "half"), 2))
    d["ropeR"] = f(fl(_rope_tables(NL, 256, "plain"), 2))
    d["cst"] = _consts()
    p = np.arange(128, dtype=np.float32)
    d["pc"] = np.ascontiguousarray(np.stack([127 - p, p, p + 1, 128 - p], 1).astype(np.float32))
    return d


_CACHE = {}


def kernel(**inputs):
    NL = 4096
    NOWN = 2048
    B = 4
    if "nc" not in _CACHE:
        _CACHE["nc"] = build(NL, NOWN=NOWN)
    nc = _CACHE["nc"]
    in_maps = [prep_core(inputs, c // 2, NL, flip=bool(c % 2)) for c in range(8)]
    res = run_bass_kernel_spmd(nc, in_maps, core_ids=list(range(8)))
    outp = np.empty((B, NL, D), np.float32)
    for b in range(B):
        outp[b, :NOWN] = res.results[2 * b]["out"]
        outp[b, NL - NOWN:] = res.results[2 * b + 1]["out"][::-1]
    return outp
```

```python
from contextlib import ExitStack

import numpy as np
import concourse.bass as bass
import concourse.mybir as mybir
from concourse.bass_utils import run_bass_kernel_spmd

F32 = mybir.dt.float32
BF16 = mybir.dt.bfloat16
AF = mybir.ActivationFunctionType
ALU = mybir.AluOpType

D = 2048
KD = 16
FF = 5632
KF = 44
NCX = 256
T = 512
EPS = 1e-6
COMPUTE = ("pe", "act", "dve", "pool")
ENGS = ("pe", "act", "dve", "pool", "sp")


class Sched:
    def __init__(self, nc, es, n_dma_sems=(("sp", 32), ("pool", 12), ("act", 4))):
        self.nc = nc
        self.streams = {e: [] for e in ENGS}
        self.esem = {e: es.enter_context(nc.semaphore("s_" + e)) for e in COMPUTE}
        self.ecount = {e: 0 for e in COMPUTE}
        self.dsems = {q: [es.enter_context(nc.semaphore("d%s%d" % (q, i))) for i in range(n)] for q, n in n_dma_sems}
        self.dcount = {q: [0] * n for q, n in n_dma_sems}
        self.dnext = {q: 0 for q, n in n_dma_sems}
        self.last_w = {}
        self.readers = {}
        self.seen = {e: {} for e in ENGS}
        self.semobj = {}
        self.n_ops = 0

    def _need(self, eng, tok, waits):
        if tok is None:
            return
        sk, val, teng = tok
        if teng == "pe" and eng == "pe":
            return
        if self.seen[eng].get(sk, 0) >= val:
            return
        self.seen[eng][sk] = val
        waits.append((sk, val))

    def add(self, eng, fn, reads=(), writes=(), dma=False):
        waits = []
        for k in reads:
            self._need(eng, self.last_w.get(k), waits)
        for k in writes:
            self._need(eng, self.last_w.get(k), waits)
            for sk, (val, teng) in self.readers.get(k, {}).items():
                self._need(eng, (sk, val, teng), waits)
        if dma:
            i = self.dnext[eng]
            self.dnext[eng] = (i + 1) % len(self.dsems[eng])
            sk = ("d", eng, i)
            dc = self.dcount[eng]
            if dc[i] > 0:
                self._need(eng, (sk, dc[i], "dma"), waits)
            dc[i] += 16
            tok = (sk, dc[i], "dma")
            self.semobj[sk] = self.dsems[eng][i]
            inc = 16
        else:
            self.ecount[eng] += 1
            sk = ("e", eng)
            tok = (sk, self.ecount[eng], eng)
            self.semobj[sk] = self.esem[eng]
            inc = 1
        self.streams[eng].append((waits, fn, sk, inc))
        self.n_ops += 1
        for k in writes:
            self.last_w[k] = tok
            self.readers[k] = {}
        for k in reads:
            if k in writes:
                continue
            self.readers.setdefault(k, {})[tok[0]] = (tok[1], tok[2])
        return tok

    def fence(self, eng, keys):
        waits = []
        for k in keys:
            self._need(eng, self.last_w.get(k), waits)
        self.streams[eng].append((waits, None, None, 0))

    def emit(self):
        so = self.semobj

        def replay(name, e):
            for waits, fn, sk, inc in self.streams[name]:
                for wk, val in waits:
                    e.wait_ge(so[wk], val)
                if fn is not None:
                    fn(e).then_inc(so[sk], inc)

        with self.nc.Block() as block:
            @block.tensor
            def _(e):
                replay("pe", e)

            @block.scalar
            def _(e):
                replay("act", e)

            @block.vector
            def _(e):
                replay("dve", e)

            @block.gpsimd
            def _(e):
                replay("pool", e)

            @block.sync
            def _(e):
                replay("sp", e)


def _rs(shape):
    names = "abcdefg"[: len(shape)]
    if len(shape) == 1:
        return None, {}
    pat = "p (" + " ".join(names) + ") -> p " + " ".join(names)
    return pat, {n: s for n, s in zip(names[:-1], shape[:-1])}


class Arena:
    def __init__(self, nc, es, nbytes):
        self.t = es.enter_context(nc.sbuf_tensor("arena", [128, nbytes // 2], BF16))
        self.nbytes = nbytes
        self.off = 0

    def alloc(self, nbytes):
        o = self.off
        self.off += (nbytes + 63) // 64 * 64
        assert self.off <= self.nbytes, (self.off, self.nbytes)
        return o

    def view(self, off, shape, dtype, parts=128):
        n = int(np.prod(shape))
        sz = 4 if dtype == F32 else 2
        ap = self.t[0:parts, off // 2: off // 2 + n * sz // 2]
        if dtype == F32:
            ap = ap.bitcast(F32)
        pat, kw = _rs(list(shape))
        if pat is not None:
            ap = ap.rearrange(pat, **kw)
        return ap


def build(NL, stage=9, NOWN=None):
    nc = bass.Bass("TRN2", target_bir_lowering=False)
    NTOK = NCX + NL
    NBLK = NTOK // 128
    NCH = NL // 128
    NOWN = NL if NOWN is None else NOWN
    NOCH = NOWN // 128
    NTL = NL // T

    def din(n, s, dt=F32):
        return nc.dram_tensor(n, s, dt, kind="ExternalInput").ap()

    def dsc(n, s, dt=BF16):
        return nc.dram_tensor(n, s, dt).ap()

    x_in = din("x", [NL, D])
    ctx_in = din("ctx", [NCX, D])
    cv_in = din("cv", [128, 32])
    modw_in = din("mod_w", [2, D, 6 * D])
    modb_in = din("modb", [128, 192])
    nmg_in = din("nmg", [128, 32])
    nfg_in = din("nfg", [128, 32])
    fng_in = din("fng", [128, 16])
    wg_in = din("wg", [2, D, FF])
    wu_in = din("wu", [2, D, FF])
    wd_in = din("wd", [2, FF, D])
    abin_in = din("abin", [D, 2368])
    about_in = din("about", [D, D])
    sink_in = din("sink", [128, 8])
    qng_in = din("qng", [128, 4])
    wqb_in = din("wqb", [512, 1536])
    kvng_in = din("kvng", [128, 2])
    wkvb_in = din("wkvb", [256, 2048])
    retin_in = din("retin", [D, 12288])
    lg_in = din("lg", [128, 16])
    gng_in = din("gng", [128, 32])
    retout_in = din("retout", [4096, D])
    ropeA_in = din("ropeA", [2, 128, NL])
    ropeB_in = din("ropeB", [2, 128, NL])
    ropeR_in = din("ropeR", [2, 128, NL])
    cst_in = din("cst", [128, 7 * 128])
    pc_in = din("pc", [128, 4])
    out = nc.dram_tensor("out", [NOWN, D], F32, kind="ExternalOutput").ap()
    dbg = nc.dram_tensor("dbg", [KD, 128, NCX + NL], F32, kind="ExternalOutput").ap() if stage <= 2 else None

    abin_b = dsc("abin_b", [D, 2368])
    abin_s = dsc("abin_s", [D, 1280])
    abin_kr = dsc("abin_kr", [D, 256])
    about_b = dsc("about_b", [4, 128, KD, 512])
    wqn_b = dsc("wqn_b", [512, 1024])
    wqr_b = dsc("wqr_b", [512, 1024])
    wkn_b = dsc("wkn_b", [256, 1024])
    wkv_b = dsc("wkv_b", [256, 1024])
    wg_b = dsc("wg_b", [2, 11, 128, KD, 512])
    wu_b = dsc("wu_b", [2, 11, 128, KD, 512])
    wd_b = dsc("wd_b", [2, 4, 4, 128, 11, 512])
    retin_b = dsc("retin_b", [24, 128, KD, 512])
    retout_b = dsc("retout_b", [4, 2, 128, KD, 512])
    XT = dsc("XT", [KD, 128, NTOK], F32)
    QA = dsc("QA", [8, 128, NTOK])
    KA = dsc("KA", [2, 128, NTOK])
    VA = dsc("VA", [NBLK, 128, 2, 129])
    QN = dsc("QN", [8, 128, NTOK])
    QR = dsc("QR", [8, 128, NTOK])
    KN = dsc("KN", [8, 128, NTOK])
    KRS = dsc("KRS", [128, NTOK])
    VM = dsc("VM", [8, 128, NBLK, 129])
    QT1 = dsc("QT1", [NOCH, 128, KD, 128])
    KT1 = dsc("KT1", [NOCH, 128, KD, 128])
    KP1 = dsc("KP1", [2, NCH, 128, D])
    V1 = dsc("V1", [NCH, 128, 4096])
    G1 = dsc("G1", [NOCH, 128, 4096])
    OF1 = dsc("OF1", [NOCH, 128, 4096], F32)
    Z1 = dsc("Z1", [NOCH, 128, 4096])
    S0 = dsc("S0", [2, 128, 16, 512], F32)

    es = ExitStack()
    with es:
        S = Sched(nc, es)
        ar = Arena(nc, es, 206 * 1024)
        ps = [es.enter_context(nc.psum_tensor("ps%d" % i, [128, 512], F32))[:] for i in range(8)]
        psk = [("ps", i) for i in range(8)]

        WPB = 16384
        NWP = 4
        WP0 = ar.alloc(WPB * NWP)
        XTo = ar.alloc(32768)
        BAo = ar.alloc(16384)
        BIGo = ar.alloc(45056)
        KRo = ar.alloc(max(NTOK * 2, 8192 + 64))
        NTMP = 6
        TMPo = ar.alloc(2048 * NTMP)
        PTo = ar.alloc(1024 * 3)
        RSo = ar.alloc(2048)
        VACo = ar.alloc(2048)
        KR2o = ar.alloc(8192 + 64)
        CSTo = ar.alloc(7 * 128 * 4)
        SMo = ar.alloc(8192)
        sm_off = [SMo]

        def small(ncols, dtype=F32):
            sz = 4 if dtype == F32 else 2
            o = sm_off[0]
            sm_off[0] += (ncols * sz + 31) // 32 * 32
            assert sm_off[0] <= SMo + 8192
            return ar.view(o, [ncols], dtype)

        cst = ar.view(CSTo, [7, 128], F32)
        ident_f = cst[:, 0, :]
        diffF = cst[:, 3, :]
        diffB = cst[:, 4, :]
        posr = cst[:, 5, :]
        ident_b = small(128, BF16)
        maskLo = small(128, BF16)
        maskHi = small(128, BF16)
        ones_m = small(128)
        cv = small(32)
        scv = small(32)
        modb = small(192)
        modT = small(384)
        EFF = small(384)
        nmg = small(32)
        nfg = small(32)
        fng = small(16)
        qng = small(4)
        kvng = small(2)
        esink = small(8)
        lg = small(16)
        lgs = small(16)
        kdec = small(16)
        cdec = small(16)
        pcol = small(2)
        gng = small(32)
        dummy = small(8)
        xT = ar.view(XTo, [KD, T], F32)
        hT = ar.view(BAo, [KD, T], BF16)
        tmp = [ar.view(TMPo + 2048 * i, [T], F32) for i in range(NTMP)]
        tmpk = [("tmp", i) for i in range(NTMP)]
        rsb = ar.view(RSo, [T], F32)
        pt = [ar.view(PTo + 1024 * i, [T], BF16) for i in range(3)]
        ptk = [("pt", i) for i in range(3)]
        ctr = {"tmp": 0, "pt": 0, "wp": 0, "psA": 0, "psB": 0, "psC": 0, "ev": 0}

        def nxt(name, n):
            i = ctr[name]
            ctr[name] = (i + 1) % n
            return i

        def gtmp():
            i = nxt("tmp", NTMP)
            return tmp[i], tmpk[i]

        def gpt():
            i = nxt("pt", 3)
            return pt[i], ptk[i]

        def psA():
            i = nxt("psA", 4)
            return ps[i], psk[i]

        def psB():
            i = 4 + nxt("psB", 2)
            return ps[i], psk[i]

        def psC():
            i = 6 + nxt("psC", 2)
            return ps[i], psk[i]

        def dma(q, o, i, r, w):
            S.add(q, lambda e: e.dma_start(out=o, in_=i), r, w, dma=True)

        def mm(o, lhsT, rhs, start, stop, r, w):
            S.add("pe", lambda e: e.matmul(o, lhsT=lhsT, rhs=rhs, start=start, stop=stop), r, w)

        def tr(o, i, idn, r, w):
            S.add("pe", lambda e: e.transpose(o, i, idn), r, w)

        def act(o, i, func, r, w, bias=None, scale=None, accum=None):
            kw = {}
            if bias is not None:
                kw["bias"] = bias
            if scale is not None:
                kw["scale"] = scale
            if accum is not None:
                kw["accum_out"] = accum
            S.add("act", lambda e: e.activation(out=o, in_=i, func=func, **kw), r, w)

        def tt(eng, o, a, b, op, r, w):
            S.add(eng, lambda e: e.tensor_tensor(out=o, in0=a, in1=b, op=op), r, w)

        def ts(eng, o, a, s1, s2, op0, op1, r, w):
            if s2 is None:
                S.add(eng, lambda e: e.tensor_scalar(out=o, in0=a, scalar1=s1, scalar2=None, op0=op0), r, w)
            else:
                S.add(eng, lambda e: e.tensor_scalar(out=o, in0=a, scalar1=s1, scalar2=s2, op0=op0, op1=op1), r, w)

        def stt(eng, o, a, sc, b, op0, op1, r, w):
            S.add(eng, lambda e: e.scalar_tensor_tensor(out=o, in0=a, scalar=sc, in1=b, op0=op0, op1=op1), r, w)

        def cp(eng, o, i, r, w):
            if eng == "act":
                S.add("act", lambda e: e.copy(out=o, in_=i), r, w)
            else:
                S.add(eng, lambda e: e.tensor_copy(out=o, in_=i), r, w)

        def evac(o, i, r, w):
            cp("act" if nxt("ev", 2) == 0 else "dve", o, i, r, w)

        def rename(old, new):
            S.add("dve", lambda e: e.memset(dummy[:, 0:1], 0.0), list(old), list(new) + ["dummy"])

        def wload(src, shape, rk, dtype=BF16, q="sp"):
            i = nxt("wp", NWP)
            v = ar.view(WP0 + i * WPB, shape, dtype)
            dma(q, v, src, rk, [("wp", i)])
            return v, ("wp", i)

        wkeys = {}

        def conv(name, dst, src, nrows, rb=256):
            ks = []
            for r0 in range(0, nrows, rb):
                k = ("w", name, r0)
                dma("pool", dst[r0:r0 + rb], src[r0:r0 + rb], [], [k])
                ks.append(k)
            wkeys.setdefault(name, []).extend(ks)

        dma("sp", cst, cst_in.rearrange("p (a b) -> p a b", a=7), [], ["cst"])
        for (v, src, k) in [(cv, cv_in, "cv"), (modb, modb_in, "modb"), (nmg, nmg_in, "nmg"), (nfg, nfg_in, "nfg"),
                            (fng, fng_in, "fng"), (qng, qng_in, "qng"), (kvng, kvng_in, "kvng"), (esink, sink_in, "esink"),
                            (lg, lg_in, "lg"), (gng, gng_in, "gng")]:
            dma("sp", v, src, [], [k])
        cp("dve", ident_b, cst[:, 0, :], ["cst"], ["ident_b"])
        cp("dve", maskLo, cst[:, 1, :], ["cst"], ["maskLo"])
        cp("dve", maskHi, cst[:, 2, :], ["cst"], ["maskHi"])
        S.add("dve", lambda e: e.memset(ones_m, 1.0 / D), [], ["ones_m"])
        act(esink, esink, AF.Exp, ["esink"], ["esink"])
        act(scv, cv, AF.Silu, ["cv"], ["scv"])

        conv("abin", abin_b, abin_in, D)
        av = abin_in[:, 0:1280].rearrange("r (h t c) -> r h t c", h=10, t=2)
        sv = abin_s.rearrange("r (h t c) -> r h t c", h=10, t=2)
        for r0 in range(0, D, 128):
            for t_ in range(2):
                k = ("w", "abin_s", r0, t_)
                dma("pool", sv[r0:r0 + 128, :, t_, :], av[r0:r0 + 128, :, 1 - t_, :], [], [k])
                wkeys.setdefault("abin_s", []).append(k)
        for r0 in range(0, D, 512):
            segs = [(0, 64, 2304), (64, 128, 2304), (128, 160, 2336), (160, 192, 2304), (192, 224, 2336), (224, 256, 2304)]
            for (d0, d1, s0) in segs:
                k = ("w", "abin_kr", r0, d0)
                dma("pool", abin_kr[r0:r0 + 512, d0:d1], abin_in[r0:r0 + 512, s0:s0 + (d1 - d0)], [], [k])
                wkeys.setdefault("abin_kr", []).append(k)
        wq3 = wqb_in.rearrange("r (h c) -> r h c", h=8)
        dma("pool", wqn_b.rearrange("r (h c) -> r h c", h=8), wq3[:, :, 0:128], [], [("w", "wqn")])
        wkeys["wqn"] = [("w", "wqn")]
        wr3 = wqr_b.rearrange("r (s h c) -> r s h c", s=2, h=8)
        wkeys["wqr"] = []
        for j, (sl_d, sl_s) in enumerate([((0, slice(0, 64)), slice(128, 192)),
                                          ((1, slice(0, 32)), slice(160, 192)),
                                          ((1, slice(32, 64)), slice(128, 160))]):
            k = ("w", "wqr", j)
            dma("pool", wr3[:, sl_d[0], :, sl_d[1]], wq3[:, :, sl_s], [], [k])
            wkeys["wqr"].append(k)
        wk3 = wkvb_in.rearrange("r (h c) -> r h c", h=8)
        dma("pool", wkn_b.rearrange("r (h c) -> r h c", h=8), wk3[:, :, 0:128], [], [("w", "wkn")])
        dma("pool", wkv_b.rearrange("r (h c) -> r h c", h=8), wk3[:, :, 128:256], [], [("w", "wkv")])
        wkeys["wkn"] = [("w", "wkn")]
        wkeys["wkv"] = [("w", "wkv")]
        cq = []
        cdone = set()

        def creg(key, dst, src):
            cq.append((key, dst, src))

        def cissue(key):
            if key in cdone:
                return
            for (k_, d_, s_) in cq:
                if k_ == key:
                    dma("pool", d_, s_, [], [k_])
                    cdone.add(k_)
                    return
            raise KeyError(key)

        def pump(cnt):
            for (k_, d_, s_) in cq:
                if cnt <= 0:
                    break
                if k_ not in cdone:
                    cissue(k_)
                    cnt -= 1

        def wloadc(key, dstblk, shape):
            cissue(key)
            return wload(dstblk, shape, [key])

        for dg in range(4):
            creg(("w", "about", dg), about_b[dg], about_in[:, dg * 512:(dg + 1) * 512].rearrange("(k p) c -> p k c", p=128))

        def creg_ffn(l):
            for hb in range(11):
                creg(("w", "wg", l, hb), wg_b[l][hb], wg_in[l][:, hb * 512:(hb + 1) * 512].rearrange("(k p) c -> p k c", p=128))
                creg(("w", "wu", l, hb), wu_b[l][hb], wu_in[l][:, hb * 512:(hb + 1) * 512].rearrange("(k p) c -> p k c", p=128))
            for dg in range(4):
                for q4 in range(4):
                    creg(("w", "wd", l, dg, q4), wd_b[l][dg][q4],
                         wd_in[l][q4 * 1408:(q4 + 1) * 1408, dg * 512:(dg + 1) * 512].rearrange("(k p) c -> p k c", p=128))

        creg_ffn(0)
        for blk in list(range(4, 16)) + list(range(0, 4)) + list(range(16, 24)):
            creg(("w", "retin", blk), retin_b[blk], retin_in[:, blk * 512:(blk + 1) * 512].rearrange("(k p) c -> p k c", p=128))
        for dg in range(4):
            for half in range(2):
                creg(("w", "retout", dg, half), retout_b[dg][half],
                     retout_in[half * 2048:(half + 1) * 2048, dg * 512:(dg + 1) * 512].rearrange("(k p) c -> p k c", p=128))
        creg_ffn(1)

        scv3 = scv.rearrange("p (k s) -> p k s", s=2)
        for l in range(2):
            pm, pmk = ps[7], psk[7]
            for cb in range(48):
                wv, wk = wload(modw_in[l][:, cb * 256:(cb + 1) * 256].rearrange("(k p) c -> p k c", p=128), [KD, 256], [], F32)
                for j in range(2):
                    oc = cb * 2 + j
                    for kc in range(KD):
                        mm(pm[:, oc * 2:oc * 2 + 2], wv[:, kc, j * 128:(j + 1) * 128], scv3[:, kc, :], kc == 0, kc == KD - 1,
                           [wk, "scv"], [pmk])
            tt("dve", modT[:, l * 192:(l + 1) * 192].rearrange("p (o s) -> p o s", s=2), pm[:, 0:192].rearrange("p (o s) -> p o s", s=2),
               modb[:, l * 96:(l + 1) * 96].unsqueeze(2).to_broadcast([128, 96, 2]), ALU.add, [pmk, "modb"], [("modT", l)])
        modT4 = modT.rearrange("p (l j k s) -> p l j k s", l=2, j=6, k=16)
        EFF4 = EFF.rearrange("p (l j k s) -> p l j k s", l=2, j=6, k=16)
        for l in range(2):
            for (jo, jsc, jsh, jg, gv) in [(0, 1, 0, 2, nmg), (3, 4, 3, 5, nfg)]:
                gcol = gv[:, l * 16:(l + 1) * 16].unsqueeze(2).to_broadcast([128, 16, 2])
                stt("dve", EFF4[:, l, jo], modT4[:, l, jsc], 1.0, gcol, ALU.add, ALU.mult, [("modT", l), "nmg", "nfg"], [("EFF", l, jo)])
                cp("dve", EFF4[:, l, jo + 1], modT4[:, l, jsh], [("modT", l)], [("EFF", l, jo + 1)])
                cp("dve", EFF4[:, l, jo + 2], modT4[:, l, jg], [("modT", l)], [("EFF", l, jo + 2)])

        def effcol(l, j, kc, s):
            return EFF4[:, l, j, kc, s:s + 1]

        def effk(l, j):
            return ("EFF", l, j)

        def norm_fm(xv, n, nk, xk, scale_fn, bias_fn, out_fn, outk_fn, sk, sqs=1.0):
            pn, pnk = psC()
            for kc in range(nk):
                sq, sqk = gtmp()
                if kc % 2 == 0:
                    act(sq[:, 0:n], xv[:, kc, 0:n], AF.Square, [xk], [sqk])
                else:
                    tt("dve", sq[:, 0:n], xv[:, kc, 0:n], xv[:, kc, 0:n], ALU.mult, [xk], [sqk])
                mm(pn[:, 0:n], ones_m, sq[:, 0:n], kc == 0, kc == nk - 1, [sqk, "ones_m"], [pnk])
            rs, rsk = rsb, "rsb"
            act(rs[:, 0:n], pn[:, 0:n], AF.Sqrt, [pnk], [rsk], bias=EPS, scale=sqs)
            S.add("dve", lambda e: e.reciprocal(out=rs[:, 0:n], in_=rs[:, 0:n]), [rsk], [rsk])
            for kc in range(nk):
                t_, tk = gtmp()
                tt("pool", t_[:, 0:n], xv[:, kc, 0:n], rs[:, 0:n], ALU.mult, [xk, rsk], [tk])
                b = bias_fn(kc) if bias_fn is not None else None
                act(out_fn(kc), t_[:, 0:n], AF.Identity, [tk] + sk, [outk_fn(kc)], bias=b, scale=scale_fn(kc))

        def ffn(l, n, mods):
            act_t = ar.view(BIGo, [KF, T], BF16)
            for (c0, c1, s) in mods:
                norm_fm(xT[:, :, c0:c1], c1 - c0, KD, "xT", lambda kc: effcol(l, 3, kc, s), lambda kc: effcol(l, 4, kc, s),
                        lambda kc: hT[:, kc, c0:c1], lambda kc: "hT", [effk(l, 3), effk(l, 4)])
            for hb in range(11):
                gv, gk = wloadc(("w", "wg", l, hb), wg_b[l][hb], [KD, 512])
                uv, uk = wloadc(("w", "wu", l, hb), wu_b[l][hb], [KD, 512])
                for j in range(4):
                    hc = hb * 4 + j
                    pg, pgk = psA()
                    for kc in range(KD):
                        mm(pg[:, 0:n], gv[:, kc, j * 128:(j + 1) * 128], hT[:, kc, 0:n], kc == 0, kc == KD - 1, [gk, "hT"], [pgk])
                    pu, puk = psA()
                    for kc in range(KD):
                        mm(pu[:, 0:n], uv[:, kc, j * 128:(j + 1) * 128], hT[:, kc, 0:n], kc == 0, kc == KD - 1, [uk, "hT"], [puk])
                    sg, sgk = gtmp()
                    act(sg[:, 0:n], pg[:, 0:n], AF.Silu, [pgk], [sgk])
                    tt("dve", act_t[:, hc, 0:n], sg[:, 0:n], pu[:, 0:n], ALU.mult, [sgk, puk], [("BIG", hc)])
            for dg in range(4):
                pd = [psA() for _ in range(4)]
                for q4 in range(4):
                    dv_, dk = wloadc(("w", "wd", l, dg, q4), wd_b[l][dg][q4], [11, 512])
                    for j in range(4):
                        for hh in range(11):
                            hc = q4 * 11 + hh
                            mm(pd[j][0][:, 0:n], dv_[:, hh, j * 128:(j + 1) * 128], act_t[:, hc, 0:n], hc == 0, hc == KF - 1,
                               [dk, ("BIG", hc)], [pd[j][1]])
                for j in range(4):
                    dc = dg * 4 + j
                    for (c0, c1, s) in mods:
                        stt("dve", xT[:, dc, c0:c1], pd[j][0][:, c0:c1], effcol(l, 5, dc, s), xT[:, dc, c0:c1], ALU.mult, ALU.add,
                            [pd[j][1], effk(l, 5), "xT"], ["xT"])

        xtok = [ar.view(BIGo + 8192 * i, [D], F32) for i in range(2)]
        xtokk = [("xtok", i) for i in range(2)]

        def load_x_transposed(src_rows, n, bigkeys_old):
            for sub in range(n // 128):
                xb, xbk = xtok[sub % 2], xtokk[sub % 2]
                dma("sp", xb, src_rows[sub * 128:(sub + 1) * 128, :], [], [xbk])
                for g4 in range(4):
                    pp, ppk = psC()
                    for j in range(4):
                        kc = g4 * 4 + j
                        tr(pp[:, j * 128:(j + 1) * 128], xb[:, kc * 128:(kc + 1) * 128], ident_f, [xbk, "cst"], [ppk])
                    evac(xT[:, g4 * 4:(g4 + 1) * 4, sub * 128:(sub + 1) * 128], pp.rearrange("p (j t) -> p j t", j=4), [ppk], ["xT"])

        tiles = [(0, NCX, 1)] + [(NCX + i * T, T, 0) for i in range(NTL)]
        big_all = [("BIG", i) for i in range(KF)]

        ropeTa = [ar.view(BIGo + 16640 + 2048 * i, [T], F32) for i in range(4)]
        qa_t = ar.view(BIGo + 24832, [8, T], BF16)
        kx_t = ar.view(BIGo + 33024, [4, T], BF16)
        va_t = ar.view(BIGo + 37120, [4, 2, 129], BF16)
        lat_t = ar.view(KRo, [4, T], F32)
        rename(big_all, ["xtok0", "ropeT", "qa_t", "kx_t", "va_t", "lat_t"] + xtokk)
        S.add("dve", lambda e: e.memset(va_t[:, :, :, 128:129], 1.0), ["va_t"], ["va_t"])

        def rope_evac(px, pxk, psw, pswk, ct, st, o, n, rk, wk_, sl=slice(0, 128)):
            t1, t1k = gtmp()
            tt("dve", t1[sl, 0:n], px[:, 0:n], ct[:, 0:n], ALU.mult, [pxk] + rk, [t1k])
            t2, t2k = gtmp()
            tt("dve", t2[sl, 0:n], psw[:, 0:n], st[:, 0:n], ALU.mult, [pswk] + rk, [t2k])
            tt("pool", o, t1[sl, 0:n], t2[sl, 0:n], ALU.add, [t1k, t2k], wk_)

        for (t0, n, isc) in tiles:
            l = 0
            if isc:
                load_x_transposed(ctx_in, n, None)
            else:
                load_x_transposed(x_in[t0 - NCX:t0 - NCX + n], n, None)
                p0 = t0 - NCX
                dma("sp", ropeTa[0][:, 0:n], ropeA_in[0][:, p0:p0 + n], [], [("ropeT", 0)])
                dma("sp", ropeTa[1][:, 0:n], ropeA_in[1][:, p0:p0 + n], [], [("ropeT", 1)])
                dma("sp", ropeTa[2][:, 0:n], ropeB_in[0][:, p0:p0 + n], [], [("ropeT", 2)])
                dma("sp", ropeTa[3][:, 0:n], ropeB_in[1][:, p0:p0 + n], [], [("ropeT", 3)])
            dma("pool", XT[:, :, t0:t0 + n].rearrange("k p t -> p k t"), xT[:, :, 0:n], ["xT"], [("XT", t0)])
            norm_fm(xT[:, :, 0:n], n, KD, "xT", lambda kc: effcol(0, 0, kc, isc), lambda kc: effcol(0, 1, kc, isc),
                    lambda kc: hT[:, kc, 0:n], lambda kc: "hT", [effk(0, 0), effk(0, 1)])
            for blk in range(3):
                ncol = 512 if blk < 2 else 256
                wv, wk = wload(abin_b[:, blk * 512:blk * 512 + 512].rearrange("(k p) c -> p k c", p=128), [KD, 512], wkeys["abin"])
                if not isc:
                    sv_, swk = wload(abin_s[:, blk * 512:blk * 512 + ncol].rearrange("(k p) c -> p k c", p=128), [KD, ncol], wkeys["abin_s"])
                for j in range(ncol // 128):
                    oc = blk * 4 + j
                    dst = qa_t[:, oc, 0:n] if oc < 8 else kx_t[:, oc - 8, 0:n]
                    dk_ = "qa_t" if oc < 8 else "kx_t"
                    px, pxk = psA()
                    for kc in range(KD):
                        mm(px[:, 0:n], wv[:, kc, j * 128:(j + 1) * 128], hT[:, kc, 0:n], kc == 0, kc == KD - 1, [wk, "hT"], [pxk])
                    if isc:
                        evac(dst, px[:, 0:n], [pxk], [dk_])
                    else:
                        pw, pwk = psA()
                        for kc in range(KD):
                            mm(pw[:, 0:n], sv_[:, kc, j * 128:(j + 1) * 128], hT[:, kc, 0:n], kc == 0, kc == KD - 1, [swk, "hT"], [pwk])
                        rope_evac(px, pxk, pw, pwk, ropeTa[0], ropeTa[1], dst, n, [("ropeT", 0), ("ropeT", 1)], [dk_])
                if blk == 2:
                    for tb in range(n // 128):
                        pv, pvk = psA()
                        for kc in range(KD):
                            mm(pv[:, 0:256], hT[:, kc, tb * 128:(tb + 1) * 128], wv[:, kc, 256:512], kc == 0, kc == KD - 1, [wk, "hT"], [pvk])
                        evac(va_t[:, tb, :, 0:128], pv[:, 0:256].rearrange("p (g c) -> p g c", g=2), [pvk], ["va_t"])
            dma("pool", QA[:, :, t0:t0 + n].rearrange("h p t -> p h t"), qa_t[:, :, 0:n], ["qa_t"], [("QA", t0)])
            dma("pool", KA[:, :, t0:t0 + n].rearrange("h p t -> p h t"), kx_t[:, 0:2, 0:n], ["kx_t"], [("KA", t0)])
            b0 = t0 // 128
            dma("pool", VA[b0:b0 + n // 128].rearrange("b p g c -> p b g c"), va_t[:, 0:n // 128], ["va_t"], [("VA", t0)])
            wv, wk = wload(abin_b[:, 1536:2048].rearrange("(k p) c -> p k c", p=128), [KD, 512], wkeys["abin"])
            for j in range(4):
                px, pxk = psA()
                for kc in range(KD):
                    mm(px[:, 0:n], wv[:, kc, j * 128:(j + 1) * 128], hT[:, kc, 0:n], kc == 0, kc == KD - 1, [wk, "hT"], [pxk])
                evac(lat_t[:, j, 0:n], px[:, 0:n], [pxk], ["lat_t"])
            qln = ar.view(BIGo + 40960, [4, T], BF16)
            norm_fm(lat_t[:, :, 0:n], n, 4, "lat_t", lambda kc: qng[:, kc:kc + 1], None,
                    lambda kc: qln[:, kc, 0:n], lambda kc: "qln", ["qng"], sqs=4.0)
            qn_t = qa_t
            wv, wk = wload(wqn_b.rearrange("(k p) c -> p k c", p=128), [4, 1024], wkeys["wqn"])
            for h in range(8):
                px, pxk = psA()
                for kc in range(4):
                    mm(px[:, 0:n], wv[:, kc, h * 128:(h + 1) * 128], qln[:, kc, 0:n], kc == 0, kc == 3, [wk, "qln"], [pxk])
                evac(qn_t[:, h, 0:n], px[:, 0:n], [pxk], ["qa_t"])
            dma("pool", QN[:, :, t0:t0 + n].rearrange("h p t -> p h t"), qn_t[:, :, 0:n], ["qa_t"], [("QN", t0)])
            wv, wk = wload(wqr_b.rearrange("(k p) c -> p k c", p=128), [4, 1024], wkeys["wqr"])
            qr_t = qa_t
            S.add("dve", lambda e: e.memset(qr_t[:, :, 0:n], 0.0), [], ["qa_t"])
            for m in range(4):
                px, pxk = psA()
                for kc in range(4):
                    mm(px[:, 0:n], wv[:, kc, m * 128:(m + 1) * 128], qln[:, kc, 0:n], kc == 0, kc == 3, [wk, "qln"], [pxk])
                if isc:
                    for hh in range(2):
                        evac(qr_t[64 * hh:64 * hh + 64, 2 * m + hh, 0:n], px[64 * hh:64 * hh + 64, 0:n], [pxk], ["qa_t"])
                else:
                    pw, pwk = psA()
                    for kc in range(4):
                        mm(pw[:, 0:n], wv[:, kc, 512 + m * 128:512 + (m + 1) * 128], qln[:, kc, 0:n], kc == 0, kc == 3, [wk, "qln"], [pwk])
                    for hh in range(2):
                        sl = slice(64 * hh, 64 * hh + 64)
                        rope_evac(px[sl], pxk, pw[sl], pwk, ropeTa[2][sl], ropeTa[3][sl], qr_t[sl, 2 * m + hh, 0:n], n,
                                  [("ropeT", 2), ("ropeT", 3)], ["qa_t"], sl)
            dma("pool", QR[:, :, t0:t0 + n].rearrange("h p t -> p h t"), qr_t[:, :, 0:n], ["qa_t"], [("QR", t0)])
            wv, wk = wload(abin_b[:, 2048:2304].rearrange("(k p) c -> p k c", p=128), [KD, 256], wkeys["abin"])
            for j in range(2):
                px, pxk = psA()
                for kc in range(KD):
                    mm(px[:, 0:n], wv[:, kc, j * 128:(j + 1) * 128], hT[:, kc, 0:n], kc == 0, kc == KD - 1, [wk, "hT"], [pxk])
                evac(lat_t[:, j, 0:n], px[:, 0:n], [pxk], ["lat_t"])
            kvn = qln
            norm_fm(lat_t[:, 0:2, 0:n], n, 2, "lat_t", lambda kc: kvng[:, kc:kc + 1], None,
                    lambda kc: kvn[:, kc, 0:n], lambda kc: "qln", ["kvng"], sqs=8.0)
            wv, wk = wload(abin_kr.rearrange("(k p) c -> p k c", p=128), [KD, 256], wkeys["abin_kr"])
            px, pxk = psA()
            for kc in range(KD):
                mm(px[:, 0:n], wv[:, kc, 0:128], hT[:, kc, 0:n], kc == 0, kc == KD - 1, [wk, "hT"], [pxk])
            kr_t = qa_t[:, 0, :]
            if isc:
                evac(kr_t[:, 0:n], px[:, 0:n], [pxk], ["qa_t"])
            else:
                pw, pwk = psA()
                for kc in range(KD):
                    mm(pw[:, 0:n], wv[:, kc, 128:256], hT[:, kc, 0:n], kc == 0, kc == KD - 1, [wk, "hT"], [pwk])
                rope_evac(px, pxk, pw, pwk, ropeTa[2], ropeTa[3], kr_t[:, 0:n], n, [("ropeT", 2), ("ropeT", 3)], ["qa_t"])
            dma("pool", KRS[:, t0:t0 + n], kr_t[:, 0:n], ["qa_t"], [("KRS", t0)])
            wv, wk = wload(wkn_b.rearrange("(k p) c -> p k c", p=128), [2, 1024], wkeys["wkn"])
            kn_full = ar.view(BIGo, [8, T], BF16)
            for h in range(8):
                px, pxk = psA()
                for kc in range(2):
                    mm(px[:, 0:n], wv[:, kc, h * 128:(h + 1) * 128], kvn[:, kc, 0:n], kc == 0, kc == 1, [wk, "qln"], [pxk])
                evac(kn_full[:, h, 0:n], px[:, 0:n], [pxk], [xtokk[0]])
            dma("pool", KN[:, :, t0:t0 + n].rearrange("h p t -> p h t"), kn_full[:, :, 0:n], [xtokk[0]], [("KN", t0)])
            vm_t = ar.view(BIGo + 8192, [4, 8, 129], BF16)
            S.add("dve", lambda e: e.memset(vm_t[:, :, :, 128:129], 1.0), [xtokk[1]], [xtokk[1]])
            wv, wk = wload(wkv_b.rearrange("(k p) c -> p k c", p=128), [2, 1024], wkeys["wkv"])
            for tb in range(n // 128):
                for nb in range(2):
                    pv, pvk = psA()
                    for kc in range(2):
                        mm(pv, kvn[:, kc, tb * 128:(tb + 1) * 128], wv[:, kc, nb * 512:(nb + 1) * 512], kc == 0, kc == 1, [wk, "qln"], [pvk])
                    evac(vm_t[:, tb, nb * 4:(nb + 1) * 4, 0:128], pv.rearrange("p (h c) -> p h c", h=4), [pvk], [xtokk[1]])
            for tb in range(n // 128):
                dma("pool", VM[:, :, b0 + tb, :].rearrange("h p c -> p h c"), vm_t[:, tb], [xtokk[1]], [("VM", t0, tb)])
            pump(5)


        allk = lambda nm: [(nm, t0) for (t0, n, isc) in tiles]
        krs = ar.view(KRo, [NTOK], BF16)
        qa_b = ar.view(BIGo, [8, T], BF16)
        qn_b = ar.view(BIGo + 8192, [8, T], BF16)
        qr_b = ar.view(BIGo + 16384, [8, T], BF16)
        o_t = ar.view(BIGo + 24576, [4, 16, 128], BF16)
        kA_w = ar.view(BIGo + 40960, [2, 768], BF16)
        kA_c = ar.view(BIGo + 44032, [2, 256], BF16)
        vA_c = ar.view(VACo, [2, 2, 129], BF16)
        vA_w = ar.view(TMPo, [6, 2, 129], BF16)
        oT = hT
        dsm = small(8)
        rename(["xT", "hT", "qa_t", "kx_t", "va_t", "lat_t", "qln", "ropeT"] + xtokk + [("ropeT", i) for i in range(4)],
               ["krs", "qa_b", "qn_b", "qr_b", "o_t", "kA_w", "kA_c", "vA_c"] + big_all)
        dma("sp", krs, KRS, allk("KRS"), ["krs"])
        SC_A = float(128 ** -0.5)
        SC_B = float(192 ** -0.5)

        def finish_head(pv, pvk, hidx, qb, sink_col):
            den, dk_ = dsm, "dsm"
            if sink_col is not None:
                ts("dve", den[:, 0:1], pv[:, 128:129], sink_col, None, ALU.add, None, [pvk, "esink"], [dk_])
                S.add("dve", lambda e: e.reciprocal(out=den[:, 0:1], in_=den[:, 0:1]), [dk_], [dk_])
            else:
                S.add("dve", lambda e: e.reciprocal(out=den[:, 0:1], in_=pv[:, 128:129]), [pvk], [dk_])
            act(o_t[:, qb, hidx, :], pv[:, 0:128], AF.Copy, [pvk, dk_], ["o_t"], scale=den[:, 0:1])

        for (t0, n, isc) in tiles:
            nqb = n // 128
            dma("sp", xT[:, :, 0:n], XT[:, :, t0:t0 + n].rearrange("k p t -> p k t"), [("XT", t0)], ["xT"])
            dma("sp", qa_b[:, :, 0:n], QA[:, :, t0:t0 + n].rearrange("h p t -> p h t"), allk("QA"), ["qa_b"])
            dma("sp", qn_b[:, :, 0:n], QN[:, :, t0:t0 + n].rearrange("h p t -> p h t"), allk("QN"), ["qn_b"])
            dma("sp", qr_b[:, :, 0:n], QR[:, :, t0:t0 + n].rearrange("h p t -> p h t"), allk("QR"), ["qr_b"])
            dma("sp", kA_c, KA[:, :, 0:NCX].rearrange("h p t -> p h t"), allk("KA"), ["kA_c"])
            dma("sp", vA_c, VA[0:2].rearrange("b p g c -> p b g c"), allk("VA"), ["vA_c"])
            if not isc:
                lb0 = (t0 - NCX) // 128
                wlo = max(lb0 - 1, 0)
                whi = min(lb0 + nqb + 1, NCH)
                nw = whi - wlo
                dma("sp", kA_w[:, :, 0:nw * 128], KA[:, :, NCX + wlo * 128:NCX + whi * 128].rearrange("h p t -> p h t"), allk("KA"), ["kA_w"])
                dma("sp", vA_w[:, 0:nw], VA[2 + wlo:2 + whi].rearrange("b p g c -> p b g c"), allk("VA"), [tmpk[0], tmpk[1]])
            for g in range(2):
                for qb in range(nqb):
                    kbs = [("c", 0), ("c", 1)]
                    if not isc:
                        lb = lb0 + qb
                        for dlt in (-1, 0, 1):
                            if 0 <= lb + dlt < NCH:
                                kbs.append(("l", lb + dlt - wlo, dlt))
                    pvs = [psA() for _ in range(4)]
                    for ki, kb in enumerate(kbs):
                        sps, spk = psB()
                        if kb[0] == "c":
                            kl = kA_c[:, g, kb[1] * 128:(kb[1] + 1) * 128]
                            vv = vA_c[:, kb[1], g, :]
                            kr_, vr_ = ["kA_c"], ["vA_c"]
                        else:
                            kl = kA_w[:, g, kb[1] * 128:(kb[1] + 1) * 128]
                            vv = vA_w[:, kb[1], g, :]
                            kr_, vr_ = ["kA_w"], [tmpk[0], tmpk[1]]
                        mm(sps.rearrange("p (h q) -> p h q", h=4), kl, qa_b[:, 4 * g:4 * g + 4, qb * 128:(qb + 1) * 128], True, True,
                           kr_ + ["qa_b"], [spk])
                        p_, pk = gpt()
                        act(p_, sps, AF.Exp, [spk], [pk], scale=SC_A)
                        if kb[0] == "l" and kb[2] != 0:
                            mk = maskLo if kb[2] == -1 else maskHi
                            mkk = "maskLo" if kb[2] == -1 else "maskHi"
                            p3 = p_.rearrange("p (h q) -> p h q", h=4)
                            tt("dve", p3, p3, mk.unsqueeze(1).to_broadcast([128, 4, 128]), ALU.mult, [pk, mkk], [pk])
                        for hh in range(4):
                            mm(pvs[hh][0][:, 0:129], p_[:, hh * 128:(hh + 1) * 128], vv, ki == 0, ki == len(kbs) - 1, [pk] + vr_, [pvs[hh][1]])
                    for hh in range(4):
                        finish_head(pvs[hh][0], pvs[hh][1], 4 * g + hh, qb, esink[:, 4 * g + hh:4 * g + hh + 1])
            nkb = 2 if isc else NBLK
            for h in range(8):
                knv, knk = wload(KN[h], [NTOK], allk("KN"))
                vmv, vmk = wload(VM[h], [NBLK, 129], [("VM", t0_, tb_) for (t0_, n_, i_) in tiles for tb_ in range(n_ // 128)])
                pvs = [psA() for _ in range(nqb)]

                def s_stage(kb):
                    sps, spk = psB()
                    mm(sps[:, 0:n], knv[:, kb * 128:(kb + 1) * 128], qn_b[:, h, 0:n], True, False, [knk, "qn_b"], [spk])
                    mm(sps[:, 0:n], krs[:, kb * 128:(kb + 1) * 128], qr_b[:, h, 0:n], False, True, ["krs", "qr_b"], [spk])
                    return sps, spk

                nxt_s = s_stage(0)
                for kb in range(nkb):
                    sps, spk = nxt_s
                    if kb + 1 < nkb:
                        nxt_s = s_stage(kb + 1)
                    p_, pk = gpt()
                    act(p_[:, 0:n], sps[:, 0:n], AF.Exp, [spk], [pk], scale=SC_B)
                    for qb in range(nqb):
                        mm(pvs[qb][0][:, 0:129], p_[:, qb * 128:(qb + 1) * 128], vmv[:, kb, :], kb == 0, kb == nkb - 1, [pk, vmk], [pvs[qb][1]])
                for qb in range(nqb):
                    finish_head(pvs[qb][0], pvs[qb][1], 8 + h, qb, None)
            for qb in range(nqb):
                for half in range(2):
                    pp, ppk = psC()
                    ppb = pp.bitcast(BF16)
                    for j in range(8):
                        fc = half * 8 + j
                        tr(ppb[:, j * 128:(j + 1) * 128], o_t[:, qb, fc, :], ident_b, ["o_t", "ident_b"], [ppk])
                    evac(oT[:, half * 8:(half + 1) * 8, qb * 128:(qb + 1) * 128], ppb.rearrange("p (j t) -> p j t", j=8), [ppk], ["hT"])
            for dg in range(4):
                wv, wk = wloadc(("w", "about", dg), about_b[dg], [KD, 512])
                for j in range(4):
                    dc = dg * 4 + j
                    px, pxk = psA()
                    for fc in range(KD):
                        mm(px[:, 0:n], wv[:, fc, j * 128:(j + 1) * 128], oT[:, fc, 0:n], fc == 0, fc == KD - 1, [wk, "hT"], [pxk])
                    stt("dve", xT[:, dc, 0:n], px[:, 0:n], effcol(0, 2, dc, isc), xT[:, dc, 0:n], ALU.mult, ALU.add,
                        [pxk, effk(0, 2), "xT"], ["xT"])
            rename(["qa_b", "qn_b", "qr_b", "o_t", "kA_w", "kA_c", "vA_c"], big_all)
            ffn(0, n, [(0, n, isc)])
            rename(big_all, ["qa_b", "qn_b", "qr_b", "o_t", "kA_w", "kA_c", "vA_c"])
            dma("pool", XT[:, :, t0:t0 + n].rearrange("k p t -> p k t"), xT[:, :, 0:n], ["xT"], [("XT", t0)])
            pump(8)

        if stage <= 2:
            for kc in range(KD):
                dma("sp", dbg[kc], XT[kc], allk("XT"), [("out", kc)])
            S.fence("sp", [("out", kc) for kc in range(KD)])
            S.emit()
            return nc


        pc = small(4)
        qdec = small(16)
        gst = small(8)
        dma("sp", pc, pc_in, [], ["pc"])
        act(lgs, lg, AF.Exp, ["lg"], ["lgs"], scale=-1.0)
        act(lgs, lgs, AF.Ln, ["lgs"], ["lgs"], bias=1.0)
        ts("dve", lgs, lgs, -1.0, None, ALU.mult, None, ["lgs"], ["lgs"])
        for dirn in range(2):
            for h in range(8):
                c = dirn * 8 + h
                act(kdec[:, c:c + 1], pc[:, dirn:dirn + 1], AF.Exp, ["pc", "lgs"], ["kdec"], scale=lgs[:, c:c + 1])
                act(qdec[:, c:c + 1], pc[:, 2 + dirn:3 + dirn], AF.Exp, ["pc", "lgs"], ["qdec"], scale=lgs[:, c:c + 1])
        act(cdec, lgs, AF.Exp, ["lgs"], ["cdec"], scale=128.0)
        DTv = ar.view(KRo, [2, 8, 128], F32)
        rename(["krs", "lat_t"], [("DT", 0), ("DT", 1)])
        for dirn in range(2):
            for h in range(8):
                c = dirn * 8 + h
                act(DTv[:, dirn, h, :], diffF if dirn == 0 else diffB, AF.Exp, ["cst", "lgs"], [("DT", dirn)], scale=lgs[:, c:c + 1])
                tt("dve", DTv[:, dirn, h, :], DTv[:, dirn, h, :], cst[:, 2 if dirn == 0 else 1, :], ALU.mult, [("DT", dirn), "cst"], [("DT", dirn)])

        def wbuf(shape, dtype):
            i = nxt("wp", NWP)
            return ar.view(WP0 + i * WPB, shape, dtype), ("wp", i)

        kp_t = [ar.view(XTo + 16384 * i, [4, D], BF16) for i in range(2)]
        qT_t = ar.view(BIGo, [4, KD, 128], BF16)
        kT_t = ar.view(BIGo + 16384, [4, KD, 128], BF16)
        vctx = ar.view(BIGo, [2, 4096], BF16)
        ev_t = [ar.view(BIGo + 32768 + 4096 * i, [4, 512], BF16) for i in range(2)]
        ropeRt = [ar.view(BIGo + 40960 + 2048 * i, [T], F32) for i in range(2)]
        rename(big_all + ["qa_b", "qn_b", "qr_b", "o_t", "kA_w", "kA_c", "vA_c"], ["qT_t", "kT_t", "ev0", "ev1", "ropeR"])
        allS = [("S", h) for h in range(8)]
        allSb = [("Sb", h) for h in range(8)]

        for (t0, n, isc) in tiles:
            ntb = n // 128
            ch0 = (t0 - NCX) // 128
            own = (not isc) and (t0 - NCX) < NOWN
            dma("sp", xT[:, :, 0:n], XT[:, :, t0:t0 + n].rearrange("k p t -> p k t"), [("XT", t0)], ["xT"])
            if not isc:
                p0 = t0 - NCX
                dma("sp", ropeRt[0][:, 0:n], ropeR_in[0][:, p0:p0 + n], [], ["ropeR"])
                dma("sp", ropeRt[1][:, 0:n], ropeR_in[1][:, p0:p0 + n], [], ["ropeR"])
            norm_fm(xT[:, :, 0:n], n, KD, "xT", lambda kc: effcol(1, 0, kc, isc), lambda kc: effcol(1, 1, kc, isc),
                    lambda kc: hT[:, kc, 0:n], lambda kc: "hT", [effk(1, 0), effk(1, 1)])
            for which in ((0, 1) if own else (1,)):
                dstT = qT_t if which == 0 else kT_t
                dkey = "qT_t" if which == 0 else "kT_t"
                scl = 1.0 if which == 0 else 0.0625
                for blk in range(4):
                    c0 = which * 2048 + blk * 512
                    wv, wk = wloadc(("w", "retin", c0 // 512), retin_b[c0 // 512], [KD, 512])
                    for hh in range(2):
                        h = blk * 2 + hh
                        pxs = []
                        for c in range(2):
                            px, pxk = psA()
                            j = hh * 2 + c
                            for kc in range(KD):
                                mm(px[:, 0:n], wv[:, kc, j * 128:(j + 1) * 128], hT[:, kc, 0:n], kc == 0, kc == KD - 1, [wk, "hT"], [pxk])
                            pxs.append((px, pxk))
                        d0 = dstT[:, 0:ntb, 2 * h, :]
                        d1 = dstT[:, 0:ntb, 2 * h + 1, :]
                        if isc:
                            for c, dd in ((0, d0), (1, d1)):
                                act(dd, pxs[c][0][:, 0:n].rearrange("p (b t) -> p b t", t=128), AF.Copy, [pxs[c][1]], [dkey], scale=scl)
                        else:
                            (x0, x0k), (x1, x1k) = pxs
                            cR, sR = ropeRt[0], ropeRt[1]
                            ta, tak = gtmp()
                            stt("dve", ta[:, 0:n], x0[:, 0:n], scl, cR[:, 0:n], ALU.mult, ALU.mult, [x0k, "ropeR"], [tak])
                            tb_, tbk = gtmp()
                            stt("dve", tb_[:, 0:n], x1[:, 0:n], scl, sR[:, 0:n], ALU.mult, ALU.mult, [x1k, "ropeR"], [tbk])
                            tt("pool", d0, ta[:, 0:n].rearrange("p (b t) -> p b t", t=128), tb_[:, 0:n].rearrange("p (b t) -> p b t", t=128),
                               ALU.subtract, [tak, tbk], [dkey])
                            tc_, tck = gtmp()
                            stt("dve", tc_[:, 0:n], x0[:, 0:n], scl, sR[:, 0:n], ALU.mult, ALU.mult, [x0k, "ropeR"], [tck])
                            td, tdk = gtmp()
                            stt("dve", td[:, 0:n], x1[:, 0:n], scl, cR[:, 0:n], ALU.mult, ALU.mult, [x1k, "ropeR"], [tdk])
                            tt("pool", d1, tc_[:, 0:n].rearrange("p (b t) -> p b t", t=128), td[:, 0:n].rearrange("p (b t) -> p b t", t=128),
                               ALU.add, [tck, tdk], [dkey])
            if own:
                dma("pool", QT1[ch0:ch0 + ntb].rearrange("c p k t -> p c (k t)"), qT_t[:, 0:ntb].rearrange("p c k t -> p c (k t)"), ["qT_t"], [("QT1", t0)])
                dma("pool", KT1[ch0:ch0 + ntb].rearrange("c p k t -> p c (k t)"), kT_t[:, 0:ntb].rearrange("p c k t -> p c (k t)"), ["kT_t"], [("KT1", t0)])
            rename(["xT"], ["kp0", "kp1"])
            for tb in range(ntb):
                for half in range(2):
                    pp, ppk = psC()
                    ppb = pp.bitcast(BF16)
                    for j in range(8):
                        kc = half * 8 + j
                        tr(ppb[:, j * 128:(j + 1) * 128], kT_t[:, tb, kc, :], ident_b, ["kT_t", "ident_b"], [ppk])
                    for dirn in ((0, 1) if (own or isc) else (1,)):
                        for hh in range(4):
                            h = half * 4 + hh
                            act(kp_t[dirn][:, tb, h * 256:(h + 1) * 256], ppb[:, hh * 256:(hh + 1) * 256], AF.Copy, [ppk, "kdec"], ["kp%d" % dirn],
                                scale=kdec[:, dirn * 8 + h:dirn * 8 + h + 1])
            if not isc:
                for dirn in ((0, 1) if own else (1,)):
                    dma("pool", KP1[dirn][ch0:ch0 + ntb].rearrange("c p f -> p c f"), kp_t[dirn][:, 0:ntb], ["kp%d" % dirn], [("KP1", dirn, t0)])
            for which in ((0, 1) if own else (0,)):
                for nb in range(8):
                    c0 = 4096 + which * 4096 + nb * 512
                    wv, wk = wloadc(("w", "retin", c0 // 512), retin_b[c0 // 512], [KD, 512])
                    ei = nxt("ev", 2)
                    evv, evk = ev_t[ei], "ev%d" % ei
                    for tb in range(ntb):
                        pv, pvk = psA()
                        for kc in range(KD):
                            mm(pv, hT[:, kc, tb * 128:(tb + 1) * 128], wv[:, kc, :], kc == 0, kc == KD - 1, [wk, "hT"], [pvk])
                        if isc:
                            evac(vctx[:, tb, nb * 512:(nb + 1) * 512], pv, [pvk], ["qT_t"])
                        elif which == 0:
                            evac(evv[:, tb, :], pv, [pvk], [evk])
                        else:
                            act(evv[:, tb, :], pv, AF.Silu, [pvk], [evk])
                    if not isc:
                        dst = V1 if which == 0 else G1
                        dma("pool", dst[ch0:ch0 + ntb, :, nb * 512:(nb + 1) * 512].rearrange("c p f -> p c f"), evv[:, 0:ntb], [evk],
                            [("V1" if which == 0 else "G1", t0, nb)])
            if isc:
                for dirn in range(2):
                    halves = [wbuf([8, 512], F32) for _ in range(2)]
                    for hv, hk in halves:
                        S.add("dve", lambda e, hv=hv: e.memset(hv, 0.0), [], [hk])
                    for tb in ((0, 1) if dirn == 0 else (1, 0)):
                        for h in range(8):
                            hv, hk = halves[h // 4]
                            for c in range(2):
                                pst, pstk = psB()
                                mm(pst, kp_t[dirn][:, tb, h * 256 + c * 128:h * 256 + (c + 1) * 128], vctx[:, tb, h * 512:(h + 1) * 512], True, True,
                                   ["kp%d" % dirn, "qT_t"], [pstk])
                                sl = hv[:, (h % 4) * 2 + c, :]
                                stt("dve", sl, sl, cdec[:, dirn * 8 + h:dirn * 8 + h + 1], pst, ALU.mult, ALU.add, [pstk, "cdec", hk], [hk])
                    for i, (hv, hk) in enumerate(halves):
                        dma("pool", S0[dirn][:, i * 8:(i + 1) * 8, :], hv, [hk], [("S0", dirn, i)])
            rename(["kp0", "kp1"], ["xT"])

        S_v = ar.view(XTo, [16, 512], F32)
        Sb_v = ar.view(BAo, [16, 512], BF16)
        cbq = [ar.view(BIGo + 20480 * i, [KD, 128], BF16) for i in range(2)]
        cbk = [ar.view(BIGo + 20480 * i + 4096, [KD, 128], BF16) for i in range(2)]
        cbp = [ar.view(BIGo + 20480 * i + 8192, [D], BF16) for i in range(2)]
        cbv = [ar.view(BIGo + 20480 * i + 12288, [4096], BF16) for i in range(2)]
        QRW = ar.view(KR2o, [2, 8, 128], F32)
        qpb = [ar.view(BIGo + 40960 + 512 * i, [2, 128], BF16) for i in range(4)]
        gsum = small(8)
        gsq = small(8)
        gm = small(8)
        gr = small(8)
        for dirn in range(2):
            for h in range(8):
                c = dirn * 8 + h
                act(QRW[:, dirn, h, :], posr if dirn == 0 else cst[:, 6, :], AF.Exp, ["cst", "lgs"], [("QRW", dirn)], scale=lgs[:, c:c + 1])
        ltiles = [t for t in tiles if (not t[2]) and (t[0] - NCX) < NOWN]
        altiles = [t for t in tiles if not t[2]]
        kQ = [("QT1", t[0]) for t in ltiles]
        kK = [("KT1", t[0]) for t in ltiles]
        kV = [("V1", t[0], nb) for t in altiles for nb in range(8)]
        kG = [("G1", t[0], nb) for t in ltiles for nb in range(8)]
        rename(["xT", "hT", "qT_t", "kT_t", "ev0", "ev1", "ropeR"],
               allS + allSb + [("cb", i, w) for i in range(2) for w in "qkpv"] + [("qp", i) for i in range(4)])
        ctr["cb"] = 0
        ctr["qp"] = 0
        for dirn in range(2):
            kP = [("KP1", dirn, t[0]) for t in (ltiles if dirn == 0 else altiles)]
            dma("sp", S_v, S0[dirn], [("S0", dirn, 0), ("S0", dirn, 1)], allS)
            for h in range(8):
                cp("pool", Sb_v[:, 2 * h:2 * h + 2, :], S_v[:, 2 * h:2 * h + 2, :], [("S", h)], [("Sb", h)])
            order = list(range(NOCH)) if dirn == 0 else list(range(NCH - 1, -1, -1))
            for oi, n_ in enumerate(order):
                bi = nxt("cb", 2)
                qc, kc_, kpc, vc = cbq[bi], cbk[bi], cbp[bi], cbv[bi]
                qk, kk_, pk_, vk = [("cb", bi, w) for w in "qkpv"]
                dma("sp", kpc, KP1[dirn][n_], kP, [pk_])
                dma("sp", vc, V1[n_], kV, [vk])
                if n_ >= NOCH:
                    for h in range(8):
                        c16 = dirn * 8 + h
                        for c in range(2):
                            pst, pstk = psC()
                            mm(pst, kpc[:, h * 256 + c * 128:h * 256 + (c + 1) * 128], vc[:, h * 512:(h + 1) * 512], True, True, [pk_, vk], [pstk])
                            stt("dve", S_v[:, 2 * h + c, :], S_v[:, 2 * h + c, :], cdec[:, c16:c16 + 1], pst, ALU.mult, ALU.add,
                                [pstk, "cdec", ("S", h)], [("S", h)])
                        if n_ == NOCH:
                            cp("act", Sb_v[:, 2 * h:2 * h + 2, :], S_v[:, 2 * h:2 * h + 2, :], [("S", h)], [("Sb", h)])
                    continue
                dma("sp", qc, QT1[n_], kQ, [qk])
                dma("sp", kc_, KT1[n_], kK, [kk_])
                if dirn == 1:
                    ofv, ofk = wload(OF1[n_], [4096], [("OF1", n_, h) for h in range(8)], F32)
                    gv_, gk_ = wload(G1[n_], [4096], kG)
                    zc, zk = wbuf([4096], BF16)
                    S.add("dve", lambda e: e.memset(gsum, 0.0), [], [("gs", h) for h in range(8)])
                    S.add("dve", lambda e: e.memset(gsq, 0.0), [], [("gq", h) for h in range(8)])
                last = oi == len(order) - 1

                def st1(h):
                    pa, pak = psB()
                    for c in range(2):
                        mm(pa[:, 0:128], kc_[:, 2 * h + c, :], qc[:, 2 * h + c, :], c == 0, c == 1, [kk_, qk], [pak])
                    qi = nxt("qp", 4)
                    tt("pool", qpb[qi], qc[:, 2 * h:2 * h + 2, :], QRW[:, dirn, h, :].unsqueeze(1).to_broadcast([128, 2, 128]), ALU.mult,
                       [qk, ("QRW", dirn)], [("qp", qi)])
                    return pa, pak, qpb[qi], ("qp", qi)

                def st2(h, pa, pak, qpv, qpk):
                    c16 = dirn * 8 + h
                    am, amk = gpt()
                    tt("dve", am[:, 0:128], pa[:, 0:128], DTv[:, dirn, h, :], ALU.mult, [pak, ("DT", dirn)], [amk])
                    po_, pok = psA()
                    mm(po_, am[:, 0:128], vc[:, h * 512:(h + 1) * 512], True, False, [amk, vk], [pok])
                    for c in range(2):
                        mm(po_, qpv[:, c, :], Sb_v[:, 2 * h + c, :], False, c == 1, [qpk, ("Sb", h)], [pok])
                    psts = []
                    if not last:
                        for c in range(2):
                            pst, pstk = psC()
                            mm(pst, kpc[:, h * 256 + c * 128:h * 256 + (c + 1) * 128], vc[:, h * 512:(h + 1) * 512], True, True, [pk_, vk], [pstk])
                            psts.append((pst, pstk))
                    if dirn == 0:
                        t1, t1k = gtmp()
                        cp("act", t1, po_, [pok], [t1k])
                        dma("pool", OF1[n_][:, h * 512:(h + 1) * 512], t1, [t1k], [("OF1", n_, h)])
                    else:
                        oh = ofv[:, h * 512:(h + 1) * 512]
                        tt("dve", oh, oh, po_, ALU.add, [pok, ofk], [ofk])
                        j1, j1k = gtmp()
                        act(j1, oh, AF.Copy, [ofk, ("gs", h)], [j1k, ("gs", h)], accum=gsum[:, h:h + 1])
                        j2, j2k = gtmp()
                        act(j2, oh, AF.Square, [ofk, ("gq", h)], [j2k, ("gq", h)], accum=gsq[:, h:h + 1])
                    for c, (pst, pstk) in enumerate(psts):
                        stt("dve", S_v[:, 2 * h + c, :], S_v[:, 2 * h + c, :], cdec[:, c16:c16 + 1], pst, ALU.mult, ALU.add,
                            [pstk, "cdec", ("S", h)], [("S", h)])
                    if psts:
                        cp("act", Sb_v[:, 2 * h:2 * h + 2, :], S_v[:, 2 * h:2 * h + 2, :], [("S", h)], [("Sb", h)])

                cur = st1(0)
                for h in range(8):
                    nx_ = st1(h + 1) if h < 7 else None
                    st2(h, *cur)
                    cur = nx_
                if dirn == 1:
                    gsk = [("gs", h) for h in range(8)] + [("gq", h) for h in range(8)]
                    ts("dve", gm, gsum, 1.0 / 512, None, ALU.mult, None, gsk, ["gm"])
                    tt("dve", gr, gm, gm, ALU.mult, ["gm"], ["gr"])
                    stt("dve", gr, gsq, 1.0 / 512, gr, ALU.mult, ALU.subtract, gsk + ["gr"], ["gr"])
                    act(gr, gr, AF.Sqrt, ["gr"], ["gr"], bias=EPS)
                    S.add("dve", lambda e: e.reciprocal(out=gr, in_=gr), ["gr"], ["gr"])
                    for h in range(8):
                        y_, yk = gtmp()
                        stt("dve", y_, ofv[:, h * 512:(h + 1) * 512], gm[:, h:h + 1], gv_[:, h * 512:(h + 1) * 512], ALU.subtract, ALU.mult,
                            [ofk, "gm", gk_], [yk])
                        act(zc[:, h * 512:(h + 1) * 512], y_, AF.Copy, [yk, "gr"], [zk], scale=gr[:, h:h + 1])
                    dma("pool", Z1[n_], zc, [zk], [("Z1", n_)])

        zT = ar.view(BIGo, [32, T], BF16)
        yT = ar.view(BIGo, [KD, T], F32)
        orow = ar.view(BIGo + 32768, [D], F32)
        rename(allS + allSb + [("cb", i, w) for i in range(2) for w in "qkpv"] + [("qp", i) for i in range(4)], ["xT", "hT", "zT"])
        for (t0, n, isc) in ltiles:
            ntb = n // 128
            ch0 = (t0 - NCX) // 128
            dma("sp", xT[:, :, 0:n], XT[:, :, t0:t0 + n].rearrange("k p t -> p k t"), [("XT", t0)], ["xT"])
            for tb in range(ntb):
                zt, ztk = wload(Z1[ch0 + tb], [4096], [("Z1", ch0 + tb)])
                for g8 in range(4):
                    pp, ppk = psC()
                    ppb = pp.bitcast(BF16)
                    for j in range(8):
                        fc = g8 * 8 + j
                        tr(ppb[:, j * 128:(j + 1) * 128], zt[:, fc * 128:(fc + 1) * 128], ident_b, [ztk, "ident_b"], [ppk])
                    for j in range(8):
                        fc = g8 * 8 + j
                        act(zT[:, fc, tb * 128:(tb + 1) * 128], ppb[:, j * 128:(j + 1) * 128], AF.Copy, [ppk, "gng"], ["zT"], scale=gng[:, fc:fc + 1])
            for dg in range(4):
                pd = [psA() for _ in range(4)]
                for half in range(2):
                    wv, wk = wloadc(("w", "retout", dg, half), retout_b[dg][half], [KD, 512])
                    for j in range(4):
                        for f16 in range(KD):
                            fc = half * 16 + f16
                            mm(pd[j][0][:, 0:n], wv[:, f16, j * 128:(j + 1) * 128], zT[:, fc, 0:n], fc == 0, fc == 31, [wk, "zT"], [pd[j][1]])
                for j in range(4):
                    dc = dg * 4 + j
                    stt("dve", xT[:, dc, 0:n], pd[j][0][:, 0:n], effcol(1, 2, dc, 0), xT[:, dc, 0:n], ALU.mult, ALU.add,
                        [pd[j][1], effk(1, 2), "xT"], ["xT"])
            rename(["zT"], big_all)
            ffn(1, n, [(0, n, 0)])
            rename(big_all, ["yT", "orow"])
            norm_fm(xT[:, :, 0:n], n, KD, "xT", lambda kc: fng[:, kc:kc + 1], None,
                    lambda kc: yT[:, kc, 0:n], lambda kc: "yT", ["fng"])
            for tb in range(ntb):
                for g4 in range(4):
                    pp, ppk = psC()
                    for j in range(4):
                        kc = g4 * 4 + j
                        tr(pp[:, j * 128:(j + 1) * 128], yT[:, kc, tb * 128:(tb + 1) * 128], ident_f, ["yT", "cst"], [ppk])
                    evac(orow[:, g4 * 512:(g4 + 1) * 512], pp, [ppk], ["orow"])
                r0 = t0 - NCX + tb * 128
                dma("pool", out[r0:r0 + 128, :], orow, ["orow"], [("out", r0)])
            rename(["yT", "orow"], ["zT"])
        S.fence("sp", [("out", r0) for r0 in range(0, NOWN, 128)])
        S.emit()
    return nc


def _pp(v):
    v = np.asarray(v, np.float32)
    return np.ascontiguousarray(v.reshape(-1, 128).T)


def _rope_tables(n_tok, rot_dim, mode):
    GRID_W = 64
    row = np.repeat(np.arange(n_tok // GRID_W, dtype=np.float32), GRID_W)
    col = (np.arange(n_tok) % GRID_W).astype(np.float32)
    n_freq = rot_dim // 4
    inv = (np.float32(10000.0) ** (-np.arange(n_freq, dtype=np.float32) / np.float32(n_freq))).astype(np.float32)
    ang = np.concatenate([row[:, None] * inv, col[:, None] * inv], axis=-1).astype(np.float32)
    c = np.cos(ang).astype(np.float32).T
    s = np.sin(ang).astype(np.float32).T
    if mode == "half":
        C = np.concatenate([c, c], 0)
        Sg = np.concatenate([-s, s], 0)
        rep = 128 // C.shape[0]
        return np.stack([np.tile(C, (rep, 1)), np.tile(Sg, (rep, 1))]).astype(np.float32)
    return np.stack([c, s]).astype(np.float32)


def _consts():
    p = np.arange(128)
    ident = np.eye(128, dtype=np.float32)
    maskLo = (p[:, None] >= p[None, :]).astype(np.float32)
    maskHi = (p[:, None] <= p[None, :]).astype(np.float32)
    diffF = np.maximum(p[None, :] - p[:, None], 0).astype(np.float32)
    diffB = np.maximum(p[:, None] - p[None, :], 0).astype(np.float32)
    pos = np.tile((p + 1).astype(np.float32)[None, :], (128, 1))
    posb = np.tile((128 - p).astype(np.float32)[None, :], (128, 1))
    return np.ascontiguousarray(np.concatenate([ident, maskLo, maskHi, diffF, diffB, pos, posb], 1))


def prep_core(inp, b, NL, flip=False):
    f = lambda a: np.ascontiguousarray(np.asarray(a, np.float32))
    fl = (lambda a, ax: np.flip(a, axis=ax)) if flip else (lambda a, ax: a)
    d = {}
    d["x"] = f(fl(np.asarray(inp["x"][b][:NL]), 0))
    d["ctx"] = f(fl(np.asarray(inp["ctx"][b]), 0))
    cvv = np.stack([_pp(inp["c"][b]), _pp(inp["c_ctx"])], -1).reshape(128, 32)
    d["cv"] = f(cvv)
    d["mod_w"] = f(inp["mod_w"])
    d["modb"] = f(np.concatenate([_pp(inp["mod_b"][l]) for l in range(2)], 1))
    d["nmg"] = f(np.concatenate([_pp(inp["norm_mix_g"][l]) for l in range(2)], 1))
    d["nfg"] = f(np.concatenate([_pp(inp["norm_ffn_g"][l]) for l in range(2)], 1))
    d["fng"] = _pp(inp["final_norm_g"])
    d["wg"] = f(inp["ffn_w_gate"])
    d["wu"] = f(inp["ffn_w_up"])
    d["wd"] = f(inp["ffn_w_down"])
    d["abin"] = f(inp["ab_w_in"][0])
    d["about"] = f(inp["ab_w_out"][0])
    d["sink"] = f(np.tile(np.asarray(inp["swa_sink"][0], np.float32)[None, :], (128, 1)))
    d["qng"] = _pp(inp["mla_q_norm_g"][0])
    d["wqb"] = f(inp["mla_w_q_b"][0])
    d["kvng"] = _pp(inp["mla_kv_norm_g"][0])
    d["wkvb"] = f(inp["mla_w_kv_b"][0])
    d["retin"] = f(inp["ret_w_in"][0])
    lf = np.asarray(inp["ret_decay_logit_fwd"][0], np.float32)
    lb = np.asarray(inp["ret_decay_logit_bwd"][0], np.float32)
    lgv = np.concatenate([lb, lf]) if flip else np.concatenate([lf, lb])
    d["lg"] = f(np.tile(lgv[None, :], (128, 1)))
    d["gng"] = _pp(inp["ret_gn_g"][0])
    d["retout"] = f(inp["ret_w_out"][0])
    d["ropeA"] = f(fl(_rope_tables(NL, 128, "half"), 2))
    d["ropeB"] = f(fl(_rope_tables(NL, 64, "half"), 2))
    d["ropeR"] = f(fl(_rope_tables(NL, 256, "plain"), 2))
    d["cst"] = _consts()
    p = np.arange(128, dtype=np.float32)
    d["pc"] = np.ascontiguousarray(np.stack([127 - p, p, p + 1, 128 - p], 1).astype(np.float32))
    return d


_CACHE = {}


def kernel(**inputs):
    NL = 4096
    NOWN = 2048
    B = 4
    if "nc" not in _CACHE:
        _CACHE["nc"] = build(NL, NOWN=NOWN)
    nc = _CACHE["nc"]
    in_maps = [prep_core(inputs, c // 2, NL, flip=bool(c % 2)) for c in range(8)]
    res = run_bass_kernel_spmd(nc, in_maps, core_ids=list(range(8)))
    outp = np.empty((B, NL, D), np.float32)
    for b in range(B):
        outp[b, :NOWN] = res.results[2 * b]["out"]
        outp[b, NL - NOWN:] = res.results[2 * b + 1]["out"][::-1]
    return outp
```

```python
from contextlib import ExitStack

import numpy as np
import concourse.bass as bass
import concourse.mybir as mybir
from concourse.bass_utils import run_bass_kernel_spmd

F32 = mybir.dt.float32
BF16 = mybir.dt.bfloat16
AF = mybir.ActivationFunctionType
ALU = mybir.AluOpType

D = 2048
KD = 16
FF = 5632
KF = 44
NCX = 256
T = 512
EPS = 1e-6
COMPUTE = ("pe", "act", "dve", "pool")
ENGS = ("pe", "act", "dve", "pool", "sp")


class Sched:
    def __init__(self, nc, es, n_dma_sems=(("sp", 32), ("pool", 12), ("act", 4))):
        self.nc = nc
        self.streams = {e: [] for e in ENGS}
        self.esem = {e: es.enter_context(nc.semaphore("s_" + e)) for e in COMPUTE}
        self.ecount = {e: 0 for e in COMPUTE}
        self.dsems = {q: [es.enter_context(nc.semaphore("d%s%d" % (q, i))) for i in range(n)] for q, n in n_dma_sems}
        self.dcount = {q: [0] * n for q, n in n_dma_sems}
        self.dnext = {q: 0 for q, n in n_dma_sems}
        self.last_w = {}
        self.readers = {}
        self.seen = {e: {} for e in ENGS}
        self.semobj = {}
        self.n_ops = 0

    def _need(self, eng, tok, waits):
        if tok is None:
            return
        sk, val, teng = tok
        if teng == "pe" and eng == "pe":
            return
        if self.seen[eng].get(sk, 0) >= val:
            return
        self.seen[eng][sk] = val
        waits.append((sk, val))

    def add(self, eng, fn, reads=(), writes=(), dma=False):
        waits = []
        for k in reads:
            self._need(eng, self.last_w.get(k), waits)
        for k in writes:
            self._need(eng, self.last_w.get(k), waits)
            for sk, (val, teng) in self.readers.get(k, {}).items():
                self._need(eng, (sk, val, teng), waits)
        if dma:
            i = self.dnext[eng]
            self.dnext[eng] = (i + 1) % len(self.dsems[eng])
            sk = ("d", eng, i)
            dc = self.dcount[eng]
            if dc[i] > 0:
                self._need(eng, (sk, dc[i], "dma"), waits)
            dc[i] += 16
            tok = (sk, dc[i], "dma")
            self.semobj[sk] = self.dsems[eng][i]
            inc = 16
        else:
            self.ecount[eng] += 1
            sk = ("e", eng)
            tok = (sk, self.ecount[eng], eng)
            self.semobj[sk] = self.esem[eng]
            inc = 1
        self.streams[eng].append((waits, fn, sk, inc))
        self.n_ops += 1
        for k in writes:
            self.last_w[k] = tok
            self.readers[k] = {}
        for k in reads:
            if k in writes:
                continue
            self.readers.setdefault(k, {})[tok[0]] = (tok[1], tok[2])
        return tok

    def fence(self, eng, keys):
        waits = []
        for k in keys:
            self._need(eng, self.last_w.get(k), waits)
        self.streams[eng].append((waits, None, None, 0))

    def emit(self):
        so = self.semobj

        def replay(name, e):
            for waits, fn, sk, inc in self.streams[name]:
                for wk, val in waits:
                    e.wait_ge(so[wk], val)
                if fn is not None:
                    fn(e).then_inc(so[sk], inc)

        with self.nc.Block() as block:
            @block.tensor
            def _(e):
                replay("pe", e)

            @block.scalar
            def _(e):
                replay("act", e)

            @block.vector
            def _(e):
                replay("dve", e)

            @block.gpsimd
            def _(e):
                replay("pool", e)

            @block.sync
            def _(e):
                replay("sp", e)


def _rs(shape):
    names = "abcdefg"[: len(shape)]
    if len(shape) == 1:
        return None, {}
    pat = "p (" + " ".join(names) + ") -> p " + " ".join(names)
    return pat, {n: s for n, s in zip(names[:-1], shape[:-1])}


class Arena:
    def __init__(self, nc, es, nbytes):
        self.t = es.enter_context(nc.sbuf_tensor("arena", [128, nbytes // 2], BF16))
        self.nbytes = nbytes
        self.off = 0

    def alloc(self, nbytes):
        o = self.off
        self.off += (nbytes + 63) // 64 * 64
        assert self.off <= self.nbytes, (self.off, self.nbytes)
        return o

    def view(self, off, shape, dtype, parts=128):
        n = int(np.prod(shape))
        sz = 4 if dtype == F32 else 2
        ap = self.t[0:parts, off // 2: off // 2 + n * sz // 2]
        if dtype == F32:
            ap = ap.bitcast(F32)
        pat, kw = _rs(list(shape))
        if pat is not None:
            ap = ap.rearrange(pat, **kw)
        return ap


def build(NL, stage=9, NOWN=None):
    nc = bass.Bass("TRN2", target_bir_lowering=False)
    NTOK = NCX + NL
    NBLK = NTOK // 128
    NCH = NL // 128
    NOWN = NL if NOWN is None else NOWN
    NOCH = NOWN // 128
    NTL = NL // T

    def din(n, s, dt=F32):
        return nc.dram_tensor(n, s, dt, kind="ExternalInput").ap()

    def dsc(n, s, dt=BF16):
        return nc.dram_tensor(n, s, dt).ap()

    x_in = din("x", [NL, D])
    ctx_in = din("ctx", [NCX, D])
    cv_in = din("cv", [128, 32])
    modw_in = din("mod_w", [2, D, 6 * D])
    modb_in = din("modb", [128, 192])
    nmg_in = din("nmg", [128, 32])
    nfg_in = din("nfg", [128, 32])
    fng_in = din("fng", [128, 16])
    wg_in = din("wg", [2, D, FF])
    wu_in = din("wu", [2, D, FF])
    wd_in = din("wd", [2, FF, D])
    abin_in = din("abin", [D, 2368])
    about_in = din("about", [D, D])
    sink_in = din("sink", [128, 8])
    qng_in = din("qng", [128, 4])
    wqb_in = din("wqb", [512, 1536])
    kvng_in = din("kvng", [128, 2])
    wkvb_in = din("wkvb", [256, 2048])
    retin_in = din("retin", [D, 12288])
    lg_in = din("lg", [128, 16])
    gng_in = din("gng", [128, 32])
    retout_in = din("retout", [4096, D])
    ropeA_in = din("ropeA", [2, 128, NL])
    ropeB_in = din("ropeB", [2, 128, NL])
    ropeR_in = din("ropeR", [2, 128, NL])
    cst_in = din("cst", [128, 7 * 128])
    pc_in = din("pc", [128, 4])
    out = nc.dram_tensor("out", [NOWN, D], F32, kind="ExternalOutput").ap()
    dbg = nc.dram_tensor("dbg", [KD, 128, NCX + NL], F32, kind="ExternalOutput").ap() if stage <= 2 else None

    abin_b = dsc("abin_b", [D, 2368])
    abin_s = dsc("abin_s", [D, 1280])
    abin_kr = dsc("abin_kr", [D, 256])
    about_b = dsc("about_b", [4, 128, KD, 512])
    wqn_b = dsc("wqn_b", [512, 1024])
    wqr_b = dsc("wqr_b", [512, 1024])
    wkn_b = dsc("wkn_b", [256, 1024])
    wkv_b = dsc("wkv_b", [256, 1024])
    wg_b = dsc("wg_b", [2, 11, 128, KD, 512])
    wu_b = dsc("wu_b", [2, 11, 128, KD, 512])
    wd_b = dsc("wd_b", [2, 4, 4, 128, 11, 512])
    retin_b = dsc("retin_b", [24, 128, KD, 512])
    retout_b = dsc("retout_b", [4, 2, 128, KD, 512])
    XT = dsc("XT", [KD, 128, NTOK], F32)
    QA = dsc("QA", [8, 128, NTOK])
    KA = dsc("KA", [2, 128, NTOK])
    VA = dsc("VA", [NBLK, 128, 2, 129])
    QN = dsc("QN", [8, 128, NTOK])
    QR = dsc("QR", [8, 128, NTOK])
    KN = dsc("KN", [8, 128, NTOK])
    KRS = dsc("KRS", [128, NTOK])
    VM = dsc("VM", [8, 128, NBLK, 129])
    QT1 = dsc("QT1", [NOCH, 128, KD, 128])
    KT1 = dsc("KT1", [NOCH, 128, KD, 128])
    KP1 = dsc("KP1", [2, NCH, 128, D])
    V1 = dsc("V1", [NCH, 128, 4096])
    G1 = dsc("G1", [NOCH, 128, 4096])
    OF1 = dsc("OF1", [NOCH, 128, 4096], F32)
    Z1 = dsc("Z1", [NOCH, 128, 4096])
    S0 = dsc("S0", [2, 128, 16, 512], F32)

    es = ExitStack()
    with es:
        S = Sched(nc, es)
        ar = Arena(nc, es, 206 * 1024)
        ps = [es.enter_context(nc.psum_tensor("ps%d" % i, [128, 512], F32))[:] for i in range(8)]
        psk = [("ps", i) for i in range(8)]

        WPB = 16384
        NWP = 4
        WP0 = ar.alloc(WPB * NWP)
        XTo = ar.alloc(32768)
        BAo = ar.alloc(16384)
        BIGo = ar.alloc(45056)
        KRo = ar.alloc(max(NTOK * 2, 8192 + 64))
        NTMP = 6
        TMPo = ar.alloc(2048 * NTMP)
        PTo = ar.alloc(1024 * 3)
        RSo = ar.alloc(2048)
        VACo = ar.alloc(2048)
        KR2o = ar.alloc(8192 + 64)
        CSTo = ar.alloc(7 * 128 * 4)
        SMo = ar.alloc(8192)
        sm_off = [SMo]

        def small(ncols, dtype=F32):
            sz = 4 if dtype == F32 else 2
            o = sm_off[0]
            sm_off[0] += (ncols * sz + 31) // 32 * 32
            assert sm_off[0] <= SMo + 8192
            return ar.view(o, [ncols], dtype)

        cst = ar.view(CSTo, [7, 128], F32)
        ident_f = cst[:, 0, :]
        diffF = cst[:, 3, :]
        diffB = cst[:, 4, :]
        posr = cst[:, 5, :]
        ident_b = small(128, BF16)
        maskLo = small(128, BF16)
        maskHi = small(128, BF16)
        ones_m = small(128, BF16)
        cv = small(32)
        scv = small(32)
        modb = small(192)
        modT = small(384)
        EFF = small(384)
        nmg = small(32)
        nfg = small(32)
        fng = small(16)
        qng = small(4)
        kvng = small(2)
        esink = small(8)
        lg = small(16)
        lgs = small(16)
        kdec = small(16)
        cdec = small(16)
        pcol = small(2)
        gng = small(32)
        dummy = small(8)
        xT = ar.view(XTo, [KD, T], F32)
        hT = ar.view(BAo, [KD, T], BF16)
        tmp = [ar.view(TMPo + 2048 * i, [T], F32) for i in range(NTMP)]
        tmpk = [("tmp", i) for i in range(NTMP)]
        rsb = ar.view(RSo, [T], F32)
        pt = [ar.view(PTo + 1024 * i, [T], BF16) for i in range(3)]
        ptk = [("pt", i) for i in range(3)]
        ctr = {"tmp": 0, "pt": 0, "wp": 0, "psA": 0, "psB": 0, "psC": 0, "ev": 0}

        def nxt(name, n):
            i = ctr[name]
            ctr[name] = (i + 1) % n
            return i

        def gtmp():
            i = nxt("tmp", NTMP)
            return tmp[i], tmpk[i]

        def gpt():
            i = nxt("pt", 3)
            return pt[i], ptk[i]

        def gsq_():
            i = nxt("tmp", NTMP)
            return ar.view(TMPo + 2048 * i, [T], BF16), tmpk[i]

        def psA():
            i = nxt("psA", 4)
            return ps[i], psk[i]

        def psB():
            i = 4 + nxt("psB", 2)
            return ps[i], psk[i]

        def psC():
            i = 6 + nxt("psC", 2)
            return ps[i], psk[i]

        def dma(q, o, i, r, w):
            S.add(q, lambda e: e.dma_start(out=o, in_=i), r, w, dma=True)

        def mm(o, lhsT, rhs, start, stop, r, w):
            S.add("pe", lambda e: e.matmul(o, lhsT=lhsT, rhs=rhs, start=start, stop=stop), r, w)

        def tr(o, i, idn, r, w):
            S.add("pe", lambda e: e.transpose(o, i, idn), r, w)

        def act(o, i, func, r, w, bias=None, scale=None, accum=None):
            kw = {}
            if bias is not None:
                kw["bias"] = bias
            if scale is not None:
                kw["scale"] = scale
            if accum is not None:
                kw["accum_out"] = accum
            S.add("act", lambda e: e.activation(out=o, in_=i, func=func, **kw), r, w)

        def tt(eng, o, a, b, op, r, w):
            S.add(eng, lambda e: e.tensor_tensor(out=o, in0=a, in1=b, op=op), r, w)

        def ts(eng, o, a, s1, s2, op0, op1, r, w):
            if s2 is None:
                S.add(eng, lambda e: e.tensor_scalar(out=o, in0=a, scalar1=s1, scalar2=None, op0=op0), r, w)
            else:
                S.add(eng, lambda e: e.tensor_scalar(out=o, in0=a, scalar1=s1, scalar2=s2, op0=op0, op1=op1), r, w)

        def stt(eng, o, a, sc, b, op0, op1, r, w):
            S.add(eng, lambda e: e.scalar_tensor_tensor(out=o, in0=a, scalar=sc, in1=b, op0=op0, op1=op1), r, w)

        def cp(eng, o, i, r, w):
            if eng == "act":
                S.add("act", lambda e: e.copy(out=o, in_=i), r, w)
            else:
                S.add(eng, lambda e: e.tensor_copy(out=o, in_=i), r, w)

        def evac(o, i, r, w):
            cp("act" if nxt("ev", 2) == 0 else "dve", o, i, r, w)

        def rename(old, new):
            S.add("dve", lambda e: e.memset(dummy[:, 0:1], 0.0), list(old), list(new) + ["dummy"])

        def wload(src, shape, rk, dtype=BF16, q="sp"):
            i = nxt("wp", NWP)
            v = ar.view(WP0 + i * WPB, shape, dtype)
            dma(q, v, src, rk, [("wp", i)])
            return v, ("wp", i)

        wkeys = {}

        def conv(name, dst, src, nrows, rb=256):
            ks = []
            for r0 in range(0, nrows, rb):
                k = ("w", name, r0)
                dma("pool", dst[r0:r0 + rb], src[r0:r0 + rb], [], [k])
                ks.append(k)
            wkeys.setdefault(name, []).extend(ks)

        dma("sp", cst, cst_in.rearrange("p (a b) -> p a b", a=7), [], ["cst"])
        for (v, src, k) in [(cv, cv_in, "cv"), (modb, modb_in, "modb"), (nmg, nmg_in, "nmg"), (nfg, nfg_in, "nfg"),
                            (fng, fng_in, "fng"), (qng, qng_in, "qng"), (kvng, kvng_in, "kvng"), (esink, sink_in, "esink"),
                            (lg, lg_in, "lg"), (gng, gng_in, "gng")]:
            dma("sp", v, src, [], [k])
        cp("dve", ident_b, cst[:, 0, :], ["cst"], ["ident_b"])
        cp("dve", maskLo, cst[:, 1, :], ["cst"], ["maskLo"])
        cp("dve", maskHi, cst[:, 2, :], ["cst"], ["maskHi"])
        S.add("dve", lambda e: e.memset(ones_m, 1.0 / D), [], ["ones_m"])
        act(esink, esink, AF.Exp, ["esink"], ["esink"])
        act(scv, cv, AF.Silu, ["cv"], ["scv"])

        conv("abin", abin_b, abin_in, D)
        av = abin_in[:, 0:1280].rearrange("r (h t c) -> r h t c", h=10, t=2)
        sv = abin_s.rearrange("r (h t c) -> r h t c", h=10, t=2)
        for r0 in range(0, D, 128):
            for t_ in range(2):
                k = ("w", "abin_s", r0, t_)
                dma("pool", sv[r0:r0 + 128, :, t_, :], av[r0:r0 + 128, :, 1 - t_, :], [], [k])
                wkeys.setdefault("abin_s", []).append(k)
        for r0 in range(0, D, 512):
            segs = [(0, 64, 2304), (64, 128, 2304), (128, 160, 2336), (160, 192, 2304), (192, 224, 2336), (224, 256, 2304)]
            for (d0, d1, s0) in segs:
                k = ("w", "abin_kr", r0, d0)
                dma("pool", abin_kr[r0:r0 + 512, d0:d1], abin_in[r0:r0 + 512, s0:s0 + (d1 - d0)], [], [k])
                wkeys.setdefault("abin_kr", []).append(k)
        wq3 = wqb_in.rearrange("r (h c) -> r h c", h=8)
        dma("pool", wqn_b.rearrange("r (h c) -> r h c", h=8), wq3[:, :, 0:128], [], [("w", "wqn")])
        wkeys["wqn"] = [("w", "wqn")]
        wr3 = wqr_b.rearrange("r (s h c) -> r s h c", s=2, h=8)
        wkeys["wqr"] = []
        for j, (sl_d, sl_s) in enumerate([((0, slice(0, 64)), slice(128, 192)),
                                          ((1, slice(0, 32)), slice(160, 192)),
                                          ((1, slice(32, 64)), slice(128, 160))]):
            k = ("w", "wqr", j)
            dma("pool", wr3[:, sl_d[0], :, sl_d[1]], wq3[:, :, sl_s], [], [k])
            wkeys["wqr"].append(k)
        wk3 = wkvb_in.rearrange("r (h c) -> r h c", h=8)
        dma("pool", wkn_b.rearrange("r (h c) -> r h c", h=8), wk3[:, :, 0:128], [], [("w", "wkn")])
        dma("pool", wkv_b.rearrange("r (h c) -> r h c", h=8), wk3[:, :, 128:256], [], [("w", "wkv")])
        wkeys["wkn"] = [("w", "wkn")]
        wkeys["wkv"] = [("w", "wkv")]
        cq = []
        cdone = set()

        def creg(key, dst, src):
            cq.append((key, dst, src))

        def cissue(key):
            if key in cdone:
                return
            for (k_, d_, s_) in cq:
                if k_ == key:
                    dma("pool", d_, s_, [], [k_])
                    cdone.add(k_)
                    return
            raise KeyError(key)

        def pump(cnt):
            for (k_, d_, s_) in cq:
                if cnt <= 0:
                    break
                if k_ not in cdone:
                    cissue(k_)
                    cnt -= 1

        def wloadc(key, dstblk, shape):
            cissue(key)
            return wload(dstblk, shape, [key])

        for dg in range(4):
            creg(("w", "about", dg), about_b[dg], about_in[:, dg * 512:(dg + 1) * 512].rearrange("(k p) c -> p k c", p=128))

        def creg_ffn(l):
            for hb in range(11):
                creg(("w", "wg", l, hb), wg_b[l][hb], wg_in[l][:, hb * 512:(hb + 1) * 512].rearrange("(k p) c -> p k c", p=128))
                creg(("w", "wu", l, hb), wu_b[l][hb], wu_in[l][:, hb * 512:(hb + 1) * 512].rearrange("(k p) c -> p k c", p=128))
            for dg in range(4):
                for q4 in range(4):
                    creg(("w", "wd", l, dg, q4), wd_b[l][dg][q4],
                         wd_in[l][q4 * 1408:(q4 + 1) * 1408, dg * 512:(dg + 1) * 512].rearrange("(k p) c -> p k c", p=128))

        creg_ffn(0)
        for blk in list(range(4, 16)) + list(range(0, 4)) + list(range(16, 24)):
            creg(("w", "retin", blk), retin_b[blk], retin_in[:, blk * 512:(blk + 1) * 512].rearrange("(k p) c -> p k c", p=128))
        for dg in range(4):
            for half in range(2):
                creg(("w", "retout", dg, half), retout_b[dg][half],
                     retout_in[half * 2048:(half + 1) * 2048, dg * 512:(dg + 1) * 512].rearrange("(k p) c -> p k c", p=128))
        creg_ffn(1)

        scv3 = scv.rearrange("p (k s) -> p k s", s=2)
        modT4 = modT.rearrange("p (l j k s) -> p l j k s", l=2, j=6, k=16)
        EFF4 = EFF.rearrange("p (l j k s) -> p l j k s", l=2, j=6, k=16)

        def mod_block(l, cb):
            wv, wk = wload(modw_in[l][:, cb * 256:(cb + 1) * 256].rearrange("(k p) c -> p k c", p=128), [KD, 256], [], F32)
            pm, pmk = psB()
            for j in range(2):
                for kc in range(KD):
                    mm(pm[:, j * 2:j * 2 + 2], wv[:, kc, j * 128:(j + 1) * 128], scv3[:, kc, :], kc == 0, kc == KD - 1, [wk, "scv"], [pmk])
            oc = cb * 2
            tt("dve", modT[:, l * 192 + oc * 2:l * 192 + oc * 2 + 4].rearrange("p (o s) -> p o s", s=2), pm[:, 0:4].rearrange("p (o s) -> p o s", s=2),
               modb[:, l * 96 + oc:l * 96 + oc + 2].unsqueeze(2).to_broadcast([128, 2, 2]), ALU.add, [pmk, "modb"], [("modT", l)])

        def mod_finish(l):
            for (jo, jsc, jsh, jg, gv) in [(0, 1, 0, 2, nmg), (3, 4, 3, 5, nfg)]:
                gcol = gv[:, l * 16:(l + 1) * 16].unsqueeze(2).to_broadcast([128, 16, 2])
                stt("dve", EFF4[:, l, jo], modT4[:, l, jsc], 1.0, gcol, ALU.add, ALU.mult, [("modT", l), "nmg", "nfg"], [("EFF", l, jo)])
                cp("dve", EFF4[:, l, jo + 1], modT4[:, l, jsh], [("modT", l)], [("EFF", l, jo + 1)])
                cp("dve", EFF4[:, l, jo + 2], modT4[:, l, jg], [("modT", l)], [("EFF", l, jo + 2)])

        for cb in range(48):
            mod_block(0, cb)
        mod_finish(0)
        mod1_next = [0]

        def mod1_pump(cnt):
            while cnt > 0 and mod1_next[0] < 48:
                mod_block(1, mod1_next[0])
                mod1_next[0] += 1
                cnt -= 1
                if mod1_next[0] == 48:
                    mod_finish(1)

        def effcol(l, j, kc, s):
            return EFF4[:, l, j, kc, s:s + 1]

        def effk(l, j):
            return ("EFF", l, j)

        def norm_fm(xv, n, nk, xk, scale_fn, bias_fn, out_fn, outk_fn, sk, sqs=1.0):
            pn, pnk = psC()
            for kc in range(nk):
                sq, sqk = gsq_()
                if kc % 2 == 0:
                    act(sq[:, 0:n], xv[:, kc, 0:n], AF.Square, [xk], [sqk])
                else:
                    tt("dve", sq[:, 0:n], xv[:, kc, 0:n], xv[:, kc, 0:n], ALU.mult, [xk], [sqk])
                mm(pn[:, 0:n], ones_m, sq[:, 0:n], kc == 0, kc == nk - 1, [sqk, "ones_m"], [pnk])
            rs, rsk = rsb, "rsb"
            act(rs[:, 0:n], pn[:, 0:n], AF.Sqrt, [pnk], [rsk], bias=EPS, scale=sqs)
            S.add("dve", lambda e: e.reciprocal(out=rs[:, 0:n], in_=rs[:, 0:n]), [rsk], [rsk])
            for kc in range(nk):
                t_, tk = gtmp()
                tt("dve", t_[:, 0:n], xv[:, kc, 0:n], rs[:, 0:n], ALU.mult, [xk, rsk], [tk])
                b = bias_fn(kc) if bias_fn is not None else None
                act(out_fn(kc), t_[:, 0:n], AF.Identity, [tk] + sk, [outk_fn(kc)], bias=b, scale=scale_fn(kc))

        def ffn(l, n, mods):
            act_t = ar.view(BIGo, [KF, T], BF16)
            for (c0, c1, s) in mods:
                norm_fm(xT[:, :, c0:c1], c1 - c0, KD, "xT", lambda kc: effcol(l, 3, kc, s), lambda kc: effcol(l, 4, kc, s),
                        lambda kc: hT[:, kc, c0:c1], lambda kc: "hT", [effk(l, 3), effk(l, 4)])
            for hb in range(11):
                gv, gk = wloadc(("w", "wg", l, hb), wg_b[l][hb], [KD, 512])
                uv, uk = wloadc(("w", "wu", l, hb), wu_b[l][hb], [KD, 512])
                for j in range(4):
                    hc = hb * 4 + j
                    pg, pgk = psA()
                    for kc in range(KD):
                        mm(pg[:, 0:n], gv[:, kc, j * 128:(j + 1) * 128], hT[:, kc, 0:n], kc == 0, kc == KD - 1, [gk, "hT"], [pgk])
                    pu, puk = psA()
                    for kc in range(KD):
                        mm(pu[:, 0:n], uv[:, kc, j * 128:(j + 1) * 128], hT[:, kc, 0:n], kc == 0, kc == KD - 1, [uk, "hT"], [puk])
                    sg, sgk = gtmp()
                    act(sg[:, 0:n], pg[:, 0:n], AF.Silu, [pgk], [sgk])
                    tt("dve", act_t[:, hc, 0:n], sg[:, 0:n], pu[:, 0:n], ALU.mult, [sgk, puk], [("BIG", hc)])
            for dg in range(4):
                pd = [psA() for _ in range(4)]
                for q4 in range(4):
                    dv_, dk = wloadc(("w", "wd", l, dg, q4), wd_b[l][dg][q4], [11, 512])
                    for j in range(4):
                        for hh in range(11):
                            hc = q4 * 11 + hh
                            mm(pd[j][0][:, 0:n], dv_[:, hh, j * 128:(j + 1) * 128], act_t[:, hc, 0:n], hc == 0, hc == KF - 1,
                               [dk, ("BIG", hc)], [pd[j][1]])
                for j in range(4):
                    dc = dg * 4 + j
                    for (c0, c1, s) in mods:
                        stt("dve", xT[:, dc, c0:c1], pd[j][0][:, c0:c1], effcol(l, 5, dc, s), xT[:, dc, c0:c1], ALU.mult, ALU.add,
                            [pd[j][1], effk(l, 5), "xT"], ["xT"])

        xtok = [ar.view(BIGo + 8192 * i, [D], F32) for i in range(2)]
        xtokk = [("xtok", i) for i in range(2)]

        def load_x_transposed(src_rows, n, bigkeys_old):
            for sub in range(n // 128):
                xb, xbk = xtok[sub % 2], xtokk[sub % 2]
                dma("sp", xb, src_rows[sub * 128:(sub + 1) * 128, :], [], [xbk])
                for g4 in range(4):
                    pp, ppk = psC()
                    for j in range(4):
                        kc = g4 * 4 + j
                        tr(pp[:, j * 128:(j + 1) * 128], xb[:, kc * 128:(kc + 1) * 128], ident_f, [xbk, "cst"], [ppk])
                    evac(xT[:, g4 * 4:(g4 + 1) * 4, sub * 128:(sub + 1) * 128], pp.rearrange("p (j t) -> p j t", j=4), [ppk], ["xT"])

        tiles = [(0, NCX, 1)] + [(NCX + i * T, T, 0) for i in range(NTL)]
        big_all = [("BIG", i) for i in range(KF)]

        ropeTa = [ar.view(BIGo + 16640 + 2048 * i, [T], F32) for i in range(4)]
        qa_t = ar.view(BIGo + 24832, [8, T], BF16)
        kx_t = ar.view(BIGo + 33024, [4, T], BF16)
        va_t = ar.view(BIGo + 37120, [4, 2, 129], BF16)
        lat_t = ar.view(KRo, [4, T], F32)
        rename(big_all, ["xtok0", "ropeT", "qa_t", "kx_t", "va_t", "lat_t"] + xtokk)
        S.add("dve", lambda e: e.memset(va_t[:, :, :, 128:129], 1.0), ["va_t"], ["va_t"])

        def rope_evac(px, pxk, psw, pswk, ct, st, o, n, rk, wk_, sl=slice(0, 128)):
            t1, t1k = gtmp()
            tt("dve", t1[sl, 0:n], px[:, 0:n], ct[:, 0:n], ALU.mult, [pxk] + rk, [t1k])
            t2, t2k = gtmp()
            tt("dve", t2[sl, 0:n], psw[:, 0:n], st[:, 0:n], ALU.mult, [pswk] + rk, [t2k])
            tt("dve", o, t1[sl, 0:n], t2[sl, 0:n], ALU.add, [t1k, t2k], wk_)

        for (t0, n, isc) in tiles:
            l = 0
            if isc:
                load_x_transposed(ctx_in, n, None)
            else:
                load_x_transposed(x_in[t0 - NCX:t0 - NCX + n], n, None)
                p0 = t0 - NCX
                dma("sp", ropeTa[0][:, 0:n], ropeA_in[0][:, p0:p0 + n], [], [("ropeT", 0)])
                dma("sp", ropeTa[1][:, 0:n], ropeA_in[1][:, p0:p0 + n], [], [("ropeT", 1)])
                dma("sp", ropeTa[2][:, 0:n], ropeB_in[0][:, p0:p0 + n], [], [("ropeT", 2)])
                dma("sp", ropeTa[3][:, 0:n], ropeB_in[1][:, p0:p0 + n], [], [("ropeT", 3)])
            dma("pool", XT[:, :, t0:t0 + n].rearrange("k p t -> p k t"), xT[:, :, 0:n], ["xT"], [("XT", t0)])
            norm_fm(xT[:, :, 0:n], n, KD, "xT", lambda kc: effcol(0, 0, kc, isc), lambda kc: effcol(0, 1, kc, isc),
                    lambda kc: hT[:, kc, 0:n], lambda kc: "hT", [effk(0, 0), effk(0, 1)])
            for blk in range(3):
                ncol = 512 if blk < 2 else 256
                wv, wk = wload(abin_b[:, blk * 512:blk * 512 + 512].rearrange("(k p) c -> p k c", p=128), [KD, 512], wkeys["abin"])
                if not isc:
                    sv_, swk = wload(abin_s[:, blk * 512:blk * 512 + ncol].rearrange("(k p) c -> p k c", p=128), [KD, ncol], wkeys["abin_s"])
                for j in range(ncol // 128):
                    oc = blk * 4 + j
                    dst = qa_t[:, oc, 0:n] if oc < 8 else kx_t[:, oc - 8, 0:n]
                    dk_ = "qa_t" if oc < 8 else "kx_t"
                    px, pxk = psA()
                    for kc in range(KD):
                        mm(px[:, 0:n], wv[:, kc, j * 128:(j + 1) * 128], hT[:, kc, 0:n], kc == 0, kc == KD - 1, [wk, "hT"], [pxk])
                    if isc:
                        evac(dst, px[:, 0:n], [pxk], [dk_])
                    else:
                        pw, pwk = psA()
                        for kc in range(KD):
                            mm(pw[:, 0:n], sv_[:, kc, j * 128:(j + 1) * 128], hT[:, kc, 0:n], kc == 0, kc == KD - 1, [swk, "hT"], [pwk])
                        rope_evac(px, pxk, pw, pwk, ropeTa[0], ropeTa[1], dst, n, [("ropeT", 0), ("ropeT", 1)], [dk_])
                if blk == 2:
                    for tb in range(n // 128):
                        pv, pvk = psA()
                        for kc in range(KD):
                            mm(pv[:, 0:256], hT[:, kc, tb * 128:(tb + 1) * 128], wv[:, kc, 256:512], kc == 0, kc == KD - 1, [wk, "hT"], [pvk])
                        evac(va_t[:, tb, :, 0:128], pv[:, 0:256].rearrange("p (g c) -> p g c", g=2), [pvk], ["va_t"])
            dma("pool", QA[:, :, t0:t0 + n].rearrange("h p t -> p h t"), qa_t[:, :, 0:n], ["qa_t"], [("QA", t0)])
            dma("pool", KA[:, :, t0:t0 + n].rearrange("h p t -> p h t"), kx_t[:, 0:2, 0:n], ["kx_t"], [("KA", t0)])
            b0 = t0 // 128
            dma("pool", VA[b0:b0 + n // 128].rearrange("b p g c -> p b g c"), va_t[:, 0:n // 128], ["va_t"], [("VA", t0)])
            wv, wk = wload(abin_b[:, 1536:2048].rearrange("(k p) c -> p k c", p=128), [KD, 512], wkeys["abin"])
            for j in range(4):
                px, pxk = psA()
                for kc in range(KD):
                    mm(px[:, 0:n], wv[:, kc, j * 128:(j + 1) * 128], hT[:, kc, 0:n], kc == 0, kc == KD - 1, [wk, "hT"], [pxk])
                evac(lat_t[:, j, 0:n], px[:, 0:n], [pxk], ["lat_t"])
            qln = ar.view(BIGo + 40960, [4, T], BF16)
            norm_fm(lat_t[:, :, 0:n], n, 4, "lat_t", lambda kc: qng[:, kc:kc + 1], None,
                    lambda kc: qln[:, kc, 0:n], lambda kc: "qln", ["qng"], sqs=4.0)
            qn_t = qa_t
            wv, wk = wload(wqn_b.rearrange("(k p) c -> p k c", p=128), [4, 1024], wkeys["wqn"])
            for h in range(8):
                px, pxk = psA()
                for kc in range(4):
                    mm(px[:, 0:n], wv[:, kc, h * 128:(h + 1) * 128], qln[:, kc, 0:n], kc == 0, kc == 3, [wk, "qln"], [pxk])
                evac(qn_t[:, h, 0:n], px[:, 0:n], [pxk], ["qa_t"])
            dma("pool", QN[:, :, t0:t0 + n].rearrange("h p t -> p h t"), qn_t[:, :, 0:n], ["qa_t"], [("QN", t0)])
            wv, wk = wload(wqr_b.rearrange("(k p) c -> p k c", p=128), [4, 1024], wkeys["wqr"])
            qr_t = qa_t
            S.add("dve", lambda e: e.memset(qr_t[:, :, 0:n], 0.0), [], ["qa_t"])
            for m in range(4):
                px, pxk = psA()
                for kc in range(4):
                    mm(px[:, 0:n], wv[:, kc, m * 128:(m + 1) * 128], qln[:, kc, 0:n], kc == 0, kc == 3, [wk, "qln"], [pxk])
                if isc:
                    for hh in range(2):
                        evac(qr_t[64 * hh:64 * hh + 64, 2 * m + hh, 0:n], px[64 * hh:64 * hh + 64, 0:n], [pxk], ["qa_t"])
                else:
                    pw, pwk = psA()
                    for kc in range(4):
                        mm(pw[:, 0:n], wv[:, kc, 512 + m * 128:512 + (m + 1) * 128], qln[:, kc, 0:n], kc == 0, kc == 3, [wk, "qln"], [pwk])
                    for hh in range(2):
                        sl = slice(64 * hh, 64 * hh + 64)
                        rope_evac(px[sl], pxk, pw[sl], pwk, ropeTa[2][sl], ropeTa[3][sl], qr_t[sl, 2 * m + hh, 0:n], n,
                                  [("ropeT", 2), ("ropeT", 3)], ["qa_t"], sl)
            dma("pool", QR[:, :, t0:t0 + n].rearrange("h p t -> p h t"), qr_t[:, :, 0:n], ["qa_t"], [("QR", t0)])
            wv, wk = wload(abin_b[:, 2048:2304].rearrange("(k p) c -> p k c", p=128), [KD, 256], wkeys["abin"])
            for j in range(2):
                px, pxk = psA()
                for kc in range(KD):
                    mm(px[:, 0:n], wv[:, kc, j * 128:(j + 1) * 128], hT[:, kc, 0:n], kc == 0, kc == KD - 1, [wk, "hT"], [pxk])
                evac(lat_t[:, j, 0:n], px[:, 0:n], [pxk], ["lat_t"])
            kvn = qln
            norm_fm(lat_t[:, 0:2, 0:n], n, 2, "lat_t", lambda kc: kvng[:, kc:kc + 1], None,
                    lambda kc: kvn[:, kc, 0:n], lambda kc: "qln", ["kvng"], sqs=8.0)
            wv, wk = wload(abin_kr.rearrange("(k p) c -> p k c", p=128), [KD, 256], wkeys["abin_kr"])
            px, pxk = psA()
            for kc in range(KD):
                mm(px[:, 0:n], wv[:, kc, 0:128], hT[:, kc, 0:n], kc == 0, kc == KD - 1, [wk, "hT"], [pxk])
            kr_t = qa_t[:, 0, :]
            if isc:
                evac(kr_t[:, 0:n], px[:, 0:n], [pxk], ["qa_t"])
            else:
                pw, pwk = psA()
                for kc in range(KD):
                    mm(pw[:, 0:n], wv[:, kc, 128:256], hT[:, kc, 0:n], kc == 0, kc == KD - 1, [wk, "hT"], [pwk])
                rope_evac(px, pxk, pw, pwk, ropeTa[2], ropeTa[3], kr_t[:, 0:n], n, [("ropeT", 2), ("ropeT", 3)], ["qa_t"])
            dma("pool", KRS[:, t0:t0 + n], kr_t[:, 0:n], ["qa_t"], [("KRS", t0)])
            wv, wk = wload(wkn_b.rearrange("(k p) c -> p k c", p=128), [2, 1024], wkeys["wkn"])
            kn_full = ar.view(BIGo, [8, T], BF16)
            for h in range(8):
                px, pxk = psA()
                for kc in range(2):
                    mm(px[:, 0:n], wv[:, kc, h * 128:(h + 1) * 128], kvn[:, kc, 0:n], kc == 0, kc == 1, [wk, "qln"], [pxk])
                evac(kn_full[:, h, 0:n], px[:, 0:n], [pxk], [xtokk[0]])
            dma("pool", KN[:, :, t0:t0 + n].rearrange("h p t -> p h t"), kn_full[:, :, 0:n], [xtokk[0]], [("KN", t0)])
            vm_t = ar.view(BIGo + 8192, [4, 8, 129], BF16)
            S.add("dve", lambda e: e.memset(vm_t[:, :, :, 128:129], 1.0), [xtokk[1]], [xtokk[1]])
            wv, wk = wload(wkv_b.rearrange("(k p) c -> p k c", p=128), [2, 1024], wkeys["wkv"])
            for tb in range(n // 128):
                for nb in range(2):
                    pv, pvk = psA()
                    for kc in range(2):
                        mm(pv, kvn[:, kc, tb * 128:(tb + 1) * 128], wv[:, kc, nb * 512:(nb + 1) * 512], kc == 0, kc == 1, [wk, "qln"], [pvk])
                    evac(vm_t[:, tb, nb * 4:(nb + 1) * 4, 0:128], pv.rearrange("p (h c) -> p h c", h=4), [pvk], [xtokk[1]])
            for tb in range(n // 128):
                dma("pool", VM[:, :, b0 + tb, :].rearrange("h p c -> p h c"), vm_t[:, tb], [xtokk[1]], [("VM", t0, tb)])
            pump(5)
            mod1_pump(6)

        mod1_pump(48)

        allk = lambda nm: [(nm, t0) for (t0, n, isc) in tiles]
        krs = ar.view(KRo, [NTOK], BF16)
        qa_b = ar.view(BIGo, [8, T], BF16)
        qn_b = ar.view(BIGo + 8192, [8, T], BF16)
        qr_b = ar.view(BIGo + 16384, [8, T], BF16)
        o_t = ar.view(BIGo + 24576, [4, 16, 128], BF16)
        kA_w = ar.view(BIGo + 40960, [2, 768], BF16)
        kA_c = ar.view(BIGo + 44032, [2, 256], BF16)
        vA_c = ar.view(VACo, [2, 2, 129], BF16)
        vA_w = ar.view(TMPo, [6, 2, 129], BF16)
        oT = hT
        dsm = small(8)
        rename(["xT", "hT", "qa_t", "kx_t", "va_t", "lat_t", "qln", "ropeT"] + xtokk + [("ropeT", i) for i in range(4)],
               ["krs", "qa_b", "qn_b", "qr_b", "o_t", "kA_w", "kA_c", "vA_c"] + big_all)
        dma("sp", krs, KRS, allk("KRS"), ["krs"])
        SC_A = float(128 ** -0.5)
        SC_B = float(192 ** -0.5)

        def finish_head(pv, pvk, hidx, qb, sink_col):
            den, dk_ = dsm, "dsm"
            if sink_col is not None:
                ts("dve", den[:, 0:1], pv[:, 128:129], sink_col, None, ALU.add, None, [pvk, "esink"], [dk_])
                S.add("dve", lambda e: e.reciprocal(out=den[:, 0:1], in_=den[:, 0:1]), [dk_], [dk_])
            else:
                S.add("dve", lambda e: e.reciprocal(out=den[:, 0:1], in_=pv[:, 128:129]), [pvk], [dk_])
            act(o_t[:, qb, hidx, :], pv[:, 0:128], AF.Copy, [pvk, dk_], ["o_t"], scale=den[:, 0:1])

        for (t0, n, isc) in tiles:
            nqb = n // 128
            dma("sp", xT[:, :, 0:n], XT[:, :, t0:t0 + n].rearrange("k p t -> p k t"), [("XT", t0)], ["xT"])
            dma("sp", qa_b[:, :, 0:n], QA[:, :, t0:t0 + n].rearrange("h p t -> p h t"), allk("QA"), ["qa_b"])
            dma("sp", qn_b[:, :, 0:n], QN[:, :, t0:t0 + n].rearrange("h p t -> p h t"), allk("QN"), ["qn_b"])
            dma("sp", qr_b[:, :, 0:n], QR[:, :, t0:t0 + n].rearrange("h p t -> p h t"), allk("QR"), ["qr_b"])
            dma("sp", kA_c, KA[:, :, 0:NCX].rearrange("h p t -> p h t"), allk("KA"), ["kA_c"])
            dma("sp", vA_c, VA[0:2].rearrange("b p g c -> p b g c"), allk("VA"), ["vA_c"])
            if not isc:
                lb0 = (t0 - NCX) // 128
                wlo = max(lb0 - 1, 0)
                whi = min(lb0 + nqb + 1, NCH)
                nw = whi - wlo
                dma("sp", kA_w[:, :, 0:nw * 128], KA[:, :, NCX + wlo * 128:NCX + whi * 128].rearrange("h p t -> p h t"), allk("KA"), ["kA_w"])
                dma("sp", vA_w[:, 0:nw], VA[2 + wlo:2 + whi].rearrange("b p g c -> p b g c"), allk("VA"), [tmpk[0], tmpk[1]])
            for g in range(2):
                for qb in range(nqb):
                    kbs = [("c", 0), ("c", 1)]
                    if not isc:
                        lb = lb0 + qb
                        for dlt in (-1, 0, 1):
                            if 0 <= lb + dlt < NCH:
                                kbs.append(("l", lb + dlt - wlo, dlt))
                    pvs = [psA() for _ in range(4)]
                    for ki, kb in enumerate(kbs):
                        sps, spk = psB()
                        if kb[0] == "c":
                            kl = kA_c[:, g, kb[1] * 128:(kb[1] + 1) * 128]
                            vv = vA_c[:, kb[1], g, :]
                            kr_, vr_ = ["kA_c"], ["vA_c"]
                        else:
                            kl = kA_w[:, g, kb[1] * 128:(kb[1] + 1) * 128]
                            vv = vA_w[:, kb[1], g, :]
                            kr_, vr_ = ["kA_w"], [tmpk[0], tmpk[1]]
                        mm(sps.rearrange("p (h q) -> p h q", h=4), kl, qa_b[:, 4 * g:4 * g + 4, qb * 128:(qb + 1) * 128], True, True,
                           kr_ + ["qa_b"], [spk])
                        p_, pk = gpt()
                        act(p_, sps, AF.Exp, [spk], [pk], scale=SC_A)
                        if kb[0] == "l" and kb[2] != 0:
                            mk = maskLo if kb[2] == -1 else maskHi
                            mkk = "maskLo" if kb[2] == -1 else "maskHi"
                            p3 = p_.rearrange("p (h q) -> p h q", h=4)
                            tt("dve", p3, p3, mk.unsqueeze(1).to_broadcast([128, 4, 128]), ALU.mult, [pk, mkk], [pk])
                        for hh in range(4):
                            mm(pvs[hh][0][:, 0:129], p_[:, hh * 128:(hh + 1) * 128], vv, ki == 0, ki == len(kbs) - 1, [pk] + vr_, [pvs[hh][1]])
                    for hh in range(4):
                        finish_head(pvs[hh][0], pvs[hh][1], 4 * g + hh, qb, esink[:, 4 * g + hh:4 * g + hh + 1])
            nkb = 2 if isc else NBLK
            for h in range(8):
                knv, knk = wload(KN[h], [NTOK], allk("KN"))
                vmv, vmk = wload(VM[h], [NBLK, 129], [("VM", t0_, tb_) for (t0_, n_, i_) in tiles for tb_ in range(n_ // 128)])
                pvs = [psA() for _ in range(nqb)]

                def s_stage(kb):
                    sps, spk = psB()
                    mm(sps[:, 0:n], knv[:, kb * 128:(kb + 1) * 128], qn_b[:, h, 0:n], True, False, [knk, "qn_b"], [spk])
                    mm(sps[:, 0:n], krs[:, kb * 128:(kb + 1) * 128], qr_b[:, h, 0:n], False, True, ["krs", "qr_b"], [spk])
                    return sps, spk

                nxt_s = s_stage(0)
                for kb in range(nkb):
                    sps, spk = nxt_s
                    if kb + 1 < nkb:
                        nxt_s = s_stage(kb + 1)
                    p_, pk = gpt()
                    act(p_[:, 0:n], sps[:, 0:n], AF.Exp, [spk], [pk], scale=SC_B)
                    for qb in range(nqb):
                        mm(pvs[qb][0][:, 0:129], p_[:, qb * 128:(qb + 1) * 128], vmv[:, kb, :], kb == 0, kb == nkb - 1, [pk, vmk], [pvs[qb][1]])
                for qb in range(nqb):
                    finish_head(pvs[qb][0], pvs[qb][1], 8 + h, qb, None)
            for qb in range(nqb):
                for half in range(2):
                    pp, ppk = psC()
                    ppb = pp.bitcast(BF16)
                    for j in range(8):
                        fc = half * 8 + j
                        tr(ppb[:, j * 128:(j + 1) * 128], o_t[:, qb, fc, :], ident_b, ["o_t", "ident_b"], [ppk])
                    evac(oT[:, half * 8:(half + 1) * 8, qb * 128:(qb + 1) * 128], ppb.rearrange("p (j t) -> p j t", j=8), [ppk], ["hT"])
            for dg in range(4):
                wv, wk = wloadc(("w", "about", dg), about_b[dg], [KD, 512])
                for j in range(4):
                    dc = dg * 4 + j
                    px, pxk = psA()
                    for fc in range(KD):
                        mm(px[:, 0:n], wv[:, fc, j * 128:(j + 1) * 128], oT[:, fc, 0:n], fc == 0, fc == KD - 1, [wk, "hT"], [pxk])
                    stt("dve", xT[:, dc, 0:n], px[:, 0:n], effcol(0, 2, dc, isc), xT[:, dc, 0:n], ALU.mult, ALU.add,
                        [pxk, effk(0, 2), "xT"], ["xT"])
            rename(["qa_b", "qn_b", "qr_b", "o_t", "kA_w", "kA_c", "vA_c"], big_all)
            ffn(0, n, [(0, n, isc)])
            rename(big_all, ["qa_b", "qn_b", "qr_b", "o_t", "kA_w", "kA_c", "vA_c"])
            dma("pool", XT[:, :, t0:t0 + n].rearrange("k p t -> p k t"), xT[:, :, 0:n], ["xT"], [("XT", t0)])
            pump(8)

        if stage <= 2:
            for kc in range(KD):
                dma("sp", dbg[kc], XT[kc], allk("XT"), [("out", kc)])
            S.fence("sp", [("out", kc) for kc in range(KD)])
            S.emit()
            return nc


        pc = small(4)
        qdec = small(16)
        gst = small(8)
        dma("sp", pc, pc_in, [], ["pc"])
        act(lgs, lg, AF.Exp, ["lg"], ["lgs"], scale=-1.0)
        act(lgs, lgs, AF.Ln, ["lgs"], ["lgs"], bias=1.0)
        ts("dve", lgs, lgs, -1.0, None, ALU.mult, None, ["lgs"], ["lgs"])
        for dirn in range(2):
            for h in range(8):
                c = dirn * 8 + h
                act(kdec[:, c:c + 1], pc[:, dirn:dirn + 1], AF.Exp, ["pc", "lgs"], ["kdec"], scale=lgs[:, c:c + 1])
                act(qdec[:, c:c + 1], pc[:, 2 + dirn:3 + dirn], AF.Exp, ["pc", "lgs"], ["qdec"], scale=lgs[:, c:c + 1])
        act(cdec, lgs, AF.Exp, ["lgs"], ["cdec"], scale=128.0)
        DTv = ar.view(KRo, [2, 8, 128], F32)
        rename(["krs", "lat_t"], [("DT", 0), ("DT", 1)])
        for dirn in range(2):
            for h in range(8):
                c = dirn * 8 + h
                act(DTv[:, dirn, h, :], diffF if dirn == 0 else diffB, AF.Exp, ["cst", "lgs"], [("DT", dirn)], scale=lgs[:, c:c + 1])
                tt("dve", DTv[:, dirn, h, :], DTv[:, dirn, h, :], cst[:, 2 if dirn == 0 else 1, :], ALU.mult, [("DT", dirn), "cst"], [("DT", dirn)])

        def wbuf(shape, dtype):
            i = nxt("wp", NWP)
            return ar.view(WP0 + i * WPB, shape, dtype), ("wp", i)

        kp_t = [ar.view(XTo + 16384 * i, [4, D], BF16) for i in range(2)]
        qT_t = ar.view(BIGo, [4, KD, 128], BF16)
        kT_t = ar.view(BIGo + 16384, [4, KD, 128], BF16)
        vctx = ar.view(BIGo, [2, 4096], BF16)
        ev_t = [ar.view(BIGo + 32768 + 4096 * i, [4, 512], BF16) for i in range(2)]
        ropeRt = [ar.view(BIGo + 40960 + 2048 * i, [T], F32) for i in range(2)]
        rename(big_all + ["qa_b", "qn_b", "qr_b", "o_t", "kA_w", "kA_c", "vA_c"], ["qT_t", "kT_t", "ev0", "ev1", "ropeR"])
        allS = [("S", h) for h in range(8)]
        allSb = [("Sb", h) for h in range(8)]

        for (t0, n, isc) in tiles:
            ntb = n // 128
            ch0 = (t0 - NCX) // 128
            own = (not isc) and (t0 - NCX) < NOWN
            dma("sp", xT[:, :, 0:n], XT[:, :, t0:t0 + n].rearrange("k p t -> p k t"), [("XT", t0)], ["xT"])
            if not isc:
                p0 = t0 - NCX
                dma("sp", ropeRt[0][:, 0:n], ropeR_in[0][:, p0:p0 + n], [], ["ropeR"])
                dma("sp", ropeRt[1][:, 0:n], ropeR_in[1][:, p0:p0 + n], [], ["ropeR"])
            norm_fm(xT[:, :, 0:n], n, KD, "xT", lambda kc: effcol(1, 0, kc, isc), lambda kc: effcol(1, 1, kc, isc),
                    lambda kc: hT[:, kc, 0:n], lambda kc: "hT", [effk(1, 0), effk(1, 1)])
            for which in ((0, 1) if own else (1,)):
                dstT = qT_t if which == 0 else kT_t
                dkey = "qT_t" if which == 0 else "kT_t"
                scl = 1.0 if which == 0 else 0.0625
                for blk in range(4):
                    c0 = which * 2048 + blk * 512
                    wv, wk = wloadc(("w", "retin", c0 // 512), retin_b[c0 // 512], [KD, 512])
                    for hh in range(2):
                        h = blk * 2 + hh
                        pxs = []
                        for c in range(2):
                            px, pxk = psA()
                            j = hh * 2 + c
                            for kc in range(KD):
                                mm(px[:, 0:n], wv[:, kc, j * 128:(j + 1) * 128], hT[:, kc, 0:n], kc == 0, kc == KD - 1, [wk, "hT"], [pxk])
                            pxs.append((px, pxk))
                        d0 = dstT[:, 0:ntb, 2 * h, :]
                        d1 = dstT[:, 0:ntb, 2 * h + 1, :]
                        if isc:
                            for c, dd in ((0, d0), (1, d1)):
                                act(dd, pxs[c][0][:, 0:n].rearrange("p (b t) -> p b t", t=128), AF.Copy, [pxs[c][1]], [dkey], scale=scl)
                        else:
                            (x0, x0k), (x1, x1k) = pxs
                            cR, sR = ropeRt[0], ropeRt[1]
                            ta, tak = gtmp()
                            stt("dve", ta[:, 0:n], x0[:, 0:n], scl, cR[:, 0:n], ALU.mult, ALU.mult, [x0k, "ropeR"], [tak])
                            tb_, tbk = gtmp()
                            stt("dve", tb_[:, 0:n], x1[:, 0:n], scl, sR[:, 0:n], ALU.mult, ALU.mult, [x1k, "ropeR"], [tbk])
                            tt("pool", d0, ta[:, 0:n].rearrange("p (b t) -> p b t", t=128), tb_[:, 0:n].rearrange("p (b t) -> p b t", t=128),
                               ALU.subtract, [tak, tbk], [dkey])
                            tc_, tck = gtmp()
                            stt("dve", tc_[:, 0:n], x0[:, 0:n], scl, sR[:, 0:n], ALU.mult, ALU.mult, [x0k, "ropeR"], [tck])
                            td, tdk = gtmp()
                            stt("dve", td[:, 0:n], x1[:, 0:n], scl, cR[:, 0:n], ALU.mult, ALU.mult, [x1k, "ropeR"], [tdk])
                            tt("pool", d1, tc_[:, 0:n].rearrange("p (b t) -> p b t", t=128), td[:, 0:n].rearrange("p (b t) -> p b t", t=128),
                               ALU.add, [tck, tdk], [dkey])
            if own:
                dma("pool", QT1[ch0:ch0 + ntb].rearrange("c p k t -> p c (k t)"), qT_t[:, 0:ntb].rearrange("p c k t -> p c (k t)"), ["qT_t"], [("QT1", t0)])
                dma("pool", KT1[ch0:ch0 + ntb].rearrange("c p k t -> p c (k t)"), kT_t[:, 0:ntb].rearrange("p c k t -> p c (k t)"), ["kT_t"], [("KT1", t0)])
            rename(["xT"], ["kp0", "kp1"])
            for tb in range(ntb):
                for half in range(2):
                    pp, ppk = psC()
                    ppb = pp.bitcast(BF16)
                    for j in range(8):
                        kc = half * 8 + j
                        tr(ppb[:, j * 128:(j + 1) * 128], kT_t[:, tb, kc, :], ident_b, ["kT_t", "ident_b"], [ppk])
                    for dirn in ((0, 1) if (own or isc) else (1,)):
                        for hh in range(4):
                            h = half * 4 + hh
                            act(kp_t[dirn][:, tb, h * 256:(h + 1) * 256], ppb[:, hh * 256:(hh + 1) * 256], AF.Copy, [ppk, "kdec"], ["kp%d" % dirn],
                                scale=kdec[:, dirn * 8 + h:dirn * 8 + h + 1])
            if not isc:
                for dirn in ((0, 1) if own else (1,)):
                    dma("pool", KP1[dirn][ch0:ch0 + ntb].rearrange("c p f -> p c f"), kp_t[dirn][:, 0:ntb], ["kp%d" % dirn], [("KP1", dirn, t0)])
            for which in ((0, 1) if own else (0,)):
                for nb in range(8):
                    c0 = 4096 + which * 4096 + nb * 512
                    wv, wk = wloadc(("w", "retin", c0 // 512), retin_b[c0 // 512], [KD, 512])
                    ei = nxt("ev", 2)
                    evv, evk = ev_t[ei], "ev%d" % ei
                    for tb in range(ntb):
                        pv, pvk = psA()
                        for kc in range(KD):
                            mm(pv, hT[:, kc, tb * 128:(tb + 1) * 128], wv[:, kc, :], kc == 0, kc == KD - 1, [wk, "hT"], [pvk])
                        if isc:
                            evac(vctx[:, tb, nb * 512:(nb + 1) * 512], pv, [pvk], ["qT_t"])
                        elif which == 0:
                            evac(evv[:, tb, :], pv, [pvk], [evk])
                        else:
                            act(evv[:, tb, :], pv, AF.Silu, [pvk], [evk])
                    if not isc:
                        dst = V1 if which == 0 else G1
                        dma("pool", dst[ch0:ch0 + ntb, :, nb * 512:(nb + 1) * 512].rearrange("c p f -> p c f"), evv[:, 0:ntb], [evk],
                            [("V1" if which == 0 else "G1", t0, nb)])
            if isc:
                for dirn in range(2):
                    halves = [wbuf([8, 512], F32) for _ in range(2)]
                    for hv, hk in halves:
                        S.add("dve", lambda e, hv=hv: e.memset(hv, 0.0), [], [hk])
                    for tb in ((0, 1) if dirn == 0 else (1, 0)):
                        for h in range(8):
                            hv, hk = halves[h // 4]
                            for c in range(2):
                                pst, pstk = psB()
                                mm(pst, kp_t[dirn][:, tb, h * 256 + c * 128:h * 256 + (c + 1) * 128], vctx[:, tb, h * 512:(h + 1) * 512], True, True,
                                   ["kp%d" % dirn, "qT_t"], [pstk])
                                sl = hv[:, (h % 4) * 2 + c, :]
                                stt("dve", sl, sl, cdec[:, dirn * 8 + h:dirn * 8 + h + 1], pst, ALU.mult, ALU.add, [pstk, "cdec", hk], [hk])
                    for i, (hv, hk) in enumerate(halves):
                        dma("pool", S0[dirn][:, i * 8:(i + 1) * 8, :], hv, [hk], [("S0", dirn, i)])
            rename(["kp0", "kp1"], ["xT"])

        S_v = ar.view(XTo, [16, 512], F32)
        Sb_v = ar.view(BAo, [16, 512], BF16)
        cbq = [ar.view(BIGo + 20480 * i, [KD, 128], BF16) for i in range(2)]
        cbk = [ar.view(BIGo + 20480 * i + 4096, [KD, 128], BF16) for i in range(2)]
        cbp = [ar.view(BIGo + 20480 * i + 8192, [D], BF16) for i in range(2)]
        cbv = [ar.view(BIGo + 20480 * i + 12288, [4096], BF16) for i in range(2)]
        QRW = ar.view(KR2o, [2, 8, 128], F32)
        qpb = [ar.view(BIGo + 40960 + 512 * i, [2, 128], BF16) for i in range(4)]
        gsum = small(8)
        gsq = small(8)
        gm = small(8)
        gr = small(8)
        for dirn in range(2):
            for h in range(8):
                c = dirn * 8 + h
                act(QRW[:, dirn, h, :], posr if dirn == 0 else cst[:, 6, :], AF.Exp, ["cst", "lgs"], [("QRW", dirn)], scale=lgs[:, c:c + 1])
        ltiles = [t for t in tiles if (not t[2]) and (t[0] - NCX) < NOWN]
        altiles = [t for t in tiles if not t[2]]
        kQ = [("QT1", t[0]) for t in ltiles]
        kK = [("KT1", t[0]) for t in ltiles]
        kV = [("V1", t[0], nb) for t in altiles for nb in range(8)]
        kG = [("G1", t[0], nb) for t in ltiles for nb in range(8)]
        rename(["xT", "hT", "qT_t", "kT_t", "ev0", "ev1", "ropeR"],
               allS + allSb + [("cb", i, w) for i in range(2) for w in "qkpv"] + [("qp", i) for i in range(4)])
        ctr["cb"] = 0
        ctr["qp"] = 0
        for dirn in range(2):
            kP = [("KP1", dirn, t[0]) for t in (ltiles if dirn == 0 else altiles)]
            dma("sp", S_v, S0[dirn], [("S0", dirn, 0), ("S0", dirn, 1)], allS)
            for h in range(8):
                cp("pool", Sb_v[:, 2 * h:2 * h + 2, :], S_v[:, 2 * h:2 * h + 2, :], [("S", h)], [("Sb", h)])
            order = list(range(NOCH)) if dirn == 0 else list(range(NCH - 1, -1, -1))
            for oi, n_ in enumerate(order):
                bi = nxt("cb", 2)
                qc, kc_, kpc, vc = cbq[bi], cbk[bi], cbp[bi], cbv[bi]
                qk, kk_, pk_, vk = [("cb", bi, w) for w in "qkpv"]
                dma("sp", kpc, KP1[dirn][n_], kP, [pk_])
                dma("sp", vc, V1[n_], kV, [vk])
                if n_ >= NOCH:
                    for h in range(8):
                        c16 = dirn * 8 + h
                        for c in range(2):
                            pst, pstk = psC()
                            mm(pst, kpc[:, h * 256 + c * 128:h * 256 + (c + 1) * 128], vc[:, h * 512:(h + 1) * 512], True, True, [pk_, vk], [pstk])
                            stt("dve", S_v[:, 2 * h + c, :], S_v[:, 2 * h + c, :], cdec[:, c16:c16 + 1], pst, ALU.mult, ALU.add,
                                [pstk, "cdec", ("S", h)], [("S", h)])
                        if n_ == NOCH:
                            cp("act", Sb_v[:, 2 * h:2 * h + 2, :], S_v[:, 2 * h:2 * h + 2, :], [("S", h)], [("Sb", h)])
                    continue
                dma("sp", qc, QT1[n_], kQ, [qk])
                dma("sp", kc_, KT1[n_], kK, [kk_])
                if dirn == 1:
                    ofv, ofk = wload(OF1[n_], [4096], [("OF1", n_, h) for h in range(8)], F32)
                    gv_, gk_ = wload(G1[n_], [4096], kG)
                    zc, zk = wbuf([4096], BF16)
                    S.add("dve", lambda e: e.memset(gsum, 0.0), [], [("gs", h) for h in range(8)])
                    S.add("dve", lambda e: e.memset(gsq, 0.0), [], [("gq", h) for h in range(8)])
                last = oi == len(order) - 1

                def st1(h):
                    pa, pak = psB()
                    for c in range(2):
                        mm(pa[:, 0:128], kc_[:, 2 * h + c, :], qc[:, 2 * h + c, :], c == 0, c == 1, [kk_, qk], [pak])
                    qi = nxt("qp", 4)
                    tt("pool", qpb[qi], qc[:, 2 * h:2 * h + 2, :], QRW[:, dirn, h, :].unsqueeze(1).to_broadcast([128, 2, 128]), ALU.mult,
                       [qk, ("QRW", dirn)], [("qp", qi)])
                    return pa, pak, qpb[qi], ("qp", qi)

                def st2(h, pa, pak, qpv, qpk):
                    c16 = dirn * 8 + h
                    am, amk = gpt()
                    tt("dve", am[:, 0:128], pa[:, 0:128], DTv[:, dirn, h, :], ALU.mult, [pak, ("DT", dirn)], [amk])
                    po_, pok = psA()
                    mm(po_, am[:, 0:128], vc[:, h * 512:(h + 1) * 512], True, False, [amk, vk], [pok])
                    for c in range(2):
                        mm(po_, qpv[:, c, :], Sb_v[:, 2 * h + c, :], False, c == 1, [qpk, ("Sb", h)], [pok])
                    psts = []
                    if not last:
                        for c in range(2):
                            pst, pstk = psC()
                            mm(pst, kpc[:, h * 256 + c * 128:h * 256 + (c + 1) * 128], vc[:, h * 512:(h + 1) * 512], True, True, [pk_, vk], [pstk])
                            psts.append((pst, pstk))
                    if dirn == 0:
                        t1, t1k = gtmp()
                        cp("act", t1, po_, [pok], [t1k])
                        dma("pool", OF1[n_][:, h * 512:(h + 1) * 512], t1, [t1k], [("OF1", n_, h)])
                    else:
                        oh = ofv[:, h * 512:(h + 1) * 512]
                        tt("dve", oh, oh, po_, ALU.add, [pok, ofk], [ofk])
                        j1, j1k = gtmp()
                        act(j1, oh, AF.Copy, [ofk, ("gs", h)], [j1k, ("gs", h)], accum=gsum[:, h:h + 1])
                        j2, j2k = gtmp()
                        act(j2, oh, AF.Square, [ofk, ("gq", h)], [j2k, ("gq", h)], accum=gsq[:, h:h + 1])
                    for c, (pst, pstk) in enumerate(psts):
                        stt("dve", S_v[:, 2 * h + c, :], S_v[:, 2 * h + c, :], cdec[:, c16:c16 + 1], pst, ALU.mult, ALU.add,
                            [pstk, "cdec", ("S", h)], [("S", h)])
                    if psts:
                        cp("act", Sb_v[:, 2 * h:2 * h + 2, :], S_v[:, 2 * h:2 * h + 2, :], [("S", h)], [("Sb", h)])

                cur = st1(0)
                for h in range(8):
                    nx_ = st1(h + 1) if h < 7 else None
                    st2(h, *cur)
                    cur = nx_
                if dirn == 1:
                    gsk = [("gs", h) for h in range(8)] + [("gq", h) for h in range(8)]
                    ts("dve", gm, gsum, 1.0 / 512, None, ALU.mult, None, gsk, ["gm"])
                    tt("dve", gr, gm, gm, ALU.mult, ["gm"], ["gr"])
                    stt("dve", gr, gsq, 1.0 / 512, gr, ALU.mult, ALU.subtract, gsk + ["gr"], ["gr"])
                    act(gr, gr, AF.Sqrt, ["gr"], ["gr"], bias=EPS)
                    S.add("dve", lambda e: e.reciprocal(out=gr, in_=gr), ["gr"], ["gr"])
                    for h in range(8):
                        y_, yk = gtmp()
                        stt("dve", y_, ofv[:, h * 512:(h + 1) * 512], gm[:, h:h + 1], gv_[:, h * 512:(h + 1) * 512], ALU.subtract, ALU.mult,
                            [ofk, "gm", gk_], [yk])
                        act(zc[:, h * 512:(h + 1) * 512], y_, AF.Copy, [yk, "gr"], [zk], scale=gr[:, h:h + 1])
                    dma("pool", Z1[n_], zc, [zk], [("Z1", n_)])

        zT = ar.view(BIGo, [32, T], BF16)
        yT = ar.view(BIGo, [KD, T], F32)
        orow = ar.view(BIGo + 32768, [D], F32)
        rename(allS + allSb + [("cb", i, w) for i in range(2) for w in "qkpv"] + [("qp", i) for i in range(4)], ["xT", "hT", "zT"])
        for (t0, n, isc) in ltiles:
            ntb = n // 128
            ch0 = (t0 - NCX) // 128
            dma("sp", xT[:, :, 0:n], XT[:, :, t0:t0 + n].rearrange("k p t -> p k t"), [("XT", t0)], ["xT"])
            for tb in range(ntb):
                zt, ztk = wload(Z1[ch0 + tb], [4096], [("Z1", ch0 + tb)])
                for g8 in range(4):
                    pp, ppk = psC()
                    ppb = pp.bitcast(BF16)
                    for j in range(8):
                        fc = g8 * 8 + j
                        tr(ppb[:, j * 128:(j + 1) * 128], zt[:, fc * 128:(fc + 1) * 128], ident_b, [ztk, "ident_b"], [ppk])
                    for j in range(8):
                        fc = g8 * 8 + j
                        act(zT[:, fc, tb * 128:(tb + 1) * 128], ppb[:, j * 128:(j + 1) * 128], AF.Copy, [ppk, "gng"], ["zT"], scale=gng[:, fc:fc + 1])
            for dg in range(4):
                pd = [psA() for _ in range(4)]
                for half in range(2):
                    wv, wk = wloadc(("w", "retout", dg, half), retout_b[dg][half], [KD, 512])
                    for j in range(4):
                        for f16 in range(KD):
                            fc = half * 16 + f16
                            mm(pd[j][0][:, 0:n], wv[:, f16, j * 128:(j + 1) * 128], zT[:, fc, 0:n], fc == 0, fc == 31, [wk, "zT"], [pd[j][1]])
                for j in range(4):
                    dc = dg * 4 + j
                    stt("dve", xT[:, dc, 0:n], pd[j][0][:, 0:n], effcol(1, 2, dc, 0), xT[:, dc, 0:n], ALU.mult, ALU.add,
                        [pd[j][1], effk(1, 2), "xT"], ["xT"])
            rename(["zT"], big_all)
            ffn(1, n, [(0, n, 0)])
            rename(big_all, ["yT", "orow"])
            norm_fm(xT[:, :, 0:n], n, KD, "xT", lambda kc: fng[:, kc:kc + 1], None,
                    lambda kc: yT[:, kc, 0:n], lambda kc: "yT", ["fng"])
            for tb in range(ntb):
                for g4 in range(4):
                    pp, ppk = psC()
                    for j in range(4):
                        kc = g4 * 4 + j
                        tr(pp[:, j * 128:(j + 1) * 128], yT[:, kc, tb * 128:(tb + 1) * 128], ident_f, ["yT", "cst"], [ppk])
                    evac(orow[:, g4 * 512:(g4 + 1) * 512], pp, [ppk], ["orow"])
                r0 = t0 - NCX + tb * 128
                dma("pool", out[r0:r0 + 128, :], orow, ["orow"], [("out", r0)])
            rename(["yT", "orow"], ["zT"])
        S.fence("sp", [("out", r0) for r0 in range(0, NOWN, 128)])
        S.emit()
    return nc


def _pp(v):
    v = np.asarray(v, np.float32)
    return np.ascontiguousarray(v.reshape(-1, 128).T)


def _rope_tables(n_tok, rot_dim, mode):
    GRID_W = 64
    row = np.repeat(np.arange(n_tok // GRID_W, dtype=np.float32), GRID_W)
    col = (np.arange(n_tok) % GRID_W).astype(np.float32)
    n_freq = rot_dim // 4
    inv = (np.float32(10000.0) ** (-np.arange(n_freq, dtype=np.float32) / np.float32(n_freq))).astype(np.float32)
    ang = np.concatenate([row[:, None] * inv, col[:, None] * inv], axis=-1).astype(np.float32)
    c = np.cos(ang).astype(np.float32).T
    s = np.sin(ang).astype(np.float32).T
    if mode == "half":
        C = np.concatenate([c, c], 0)
        Sg = np.concatenate([-s, s], 0)
        rep = 128 // C.shape[0]
        return np.stack([np.tile(C, (rep, 1)), np.tile(Sg, (rep, 1))]).astype(np.float32)
    return np.stack([c, s]).astype(np.float32)


def _consts():
    p = np.arange(128)
    ident = np.eye(128, dtype=np.float32)
    maskLo = (p[:, None] >= p[None, :]).astype(np.float32)
    maskHi = (p[:, None] <= p[None, :]).astype(np.float32)
    diffF = np.maximum(p[None, :] - p[:, None], 0).astype(np.float32)
    diffB = np.maximum(p[:, None] - p[None, :], 0).astype(np.float32)
    pos = np.tile((p + 1).astype(np.float32)[None, :], (128, 1))
    posb = np.tile((128 - p).astype(np.float32)[None, :], (128, 1))
    return np.ascontiguousarray(np.concatenate([ident, maskLo, maskHi, diffF, diffB, pos, posb], 1))


def prep_core(inp, b, NL, flip=False):
    f = lambda a: np.ascontiguousarray(np.asarray(a, np.float32))
    fl = (lambda a, ax: np.flip(a, axis=ax)) if flip else (lambda a, ax: a)
    d = {}
    d["x"] = f(fl(np.asarray(inp["x"][b][:NL]), 0))
    d["ctx"] = f(fl(np.asarray(inp["ctx"][b]), 0))
    cvv = np.stack([_pp(inp["c"][b]), _pp(inp["c_ctx"])], -1).reshape(128, 32)
    d["cv"] = f(cvv)
    d["mod_w"] = f(inp["mod_w"])
    d["modb"] = f(np.concatenate([_pp(inp["mod_b"][l]) for l in range(2)], 1))
    d["nmg"] = f(np.concatenate([_pp(inp["norm_mix_g"][l]) for l in range(2)], 1))
    d["nfg"] = f(np.concatenate([_pp(inp["norm_ffn_g"][l]) for l in range(2)], 1))
    d["fng"] = _pp(inp["final_norm_g"])
    d["wg"] = f(inp["ffn_w_gate"])
    d["wu"] = f(inp["ffn_w_up"])
    d["wd"] = f(inp["ffn_w_down"])
    d["abin"] = f(inp["ab_w_in"][0])
    d["about"] = f(inp["ab_w_out"][0])
    d["sink"] = f(np.tile(np.asarray(inp["swa_sink"][0], np.float32)[None, :], (128, 1)))
    d["qng"] = _pp(inp["mla_q_norm_g"][0])
    d["wqb"] = f(inp["mla_w_q_b"][0])
    d["kvng"] = _pp(inp["mla_kv_norm_g"][0])
    d["wkvb"] = f(inp["mla_w_kv_b"][0])
    d["retin"] = f(inp["ret_w_in"][0])
    lf = np.asarray(inp["ret_decay_logit_fwd"][0], np.float32)
    lb = np.asarray(inp["ret_decay_logit_bwd"][0], np.float32)
    lgv = np.concatenate([lb, lf]) if flip else np.concatenate([lf, lb])
    d["lg"] = f(np.tile(lgv[None, :], (128, 1)))
    d["gng"] = _pp(inp["ret_gn_g"][0])
    d["retout"] = f(inp["ret_w_out"][0])
    d["ropeA"] = f(fl(_rope_tables(NL, 128, "half"), 2))
    d["ropeB"] = f(fl(_rope_tables(NL, 64, "half"), 2))
    d["ropeR"] = f(fl(_rope_tables(NL, 256, "plain"), 2))
    d["cst"] = _consts()
    p = np.arange(128, dtype=np.float32)
    d["pc"] = np.ascontiguousarray(np.stack([127 - p, p, p + 1, 128 - p], 1).astype(np.float32))
    return d


_CACHE = {}


def kernel(**inputs):
    NL = 4096
    NOWN = 2048
    B = 4
    if "nc" not in _CACHE:
        _CACHE["nc"] = build(NL, NOWN=NOWN)
    nc = _CACHE["nc"]
    in_maps = [prep_core(inputs, c // 2, NL, flip=bool(c % 2)) for c in range(8)]
    res = run_bass_kernel_spmd(nc, in_maps, core_ids=list(range(8)))
    outp = np.empty((B, NL, D), np.float32)
    for b in range(B):
        outp[b, :NOWN] = res.results[2 * b]["out"]
        outp[b, NL - NOWN:] = res.results[2 * b + 1]["out"][::-1]
    return outp
```
